# Optimizing a Trainium2 kernel written in Bass

```python
import jax, jax.numpy as jnp
from jax import lax
import numpy as np


D_MODEL = 1024
BATCH = 32
SEQ = 2048
DEPTH = 2

CHUNK = 64
N_META = 16
EPS = 1e-6
N_EVEN = (DEPTH + 1) // 2
N_ODD = DEPTH // 2
CONV_W = 4

LRU_WIDTH = D_MODEL
LRU_BLOCKS = 4
LRU_BLOCK = LRU_WIDTH // LRU_BLOCKS
RG_LRU_C = 8.0

SSD_WIDTH = D_MODEL
SSD_HEAD_DIM = 64
SSD_HEADS = SSD_WIDTH // SSD_HEAD_DIM
SSD_GROUPS = 2
SSD_HPG = SSD_HEADS // SSD_GROUPS
SSD_STATE = 128
SSD_CHUNK = CHUNK
SSD_CONV_DIM = SSD_WIDTH + 2 * SSD_GROUPS * SSD_STATE

EVEN_SPLITS = (LRU_WIDTH, 2 * LRU_WIDTH, 2 * LRU_WIDTH + SSD_WIDTH,
               2 * LRU_WIDTH + SSD_WIDTH + SSD_CONV_DIM)
EVEN_IN = 2 * LRU_WIDTH + SSD_WIDTH + SSD_CONV_DIM + SSD_HEADS
EVEN_MIX = LRU_WIDTH + SSD_WIDTH

SB_HEADS = 16
SB_HEAD_DIM = D_MODEL // SB_HEADS
SB_WIDTH = SB_HEADS * SB_HEAD_DIM
SB_BLOCK = 128
ODD_IN = 4 * SB_WIDTH

kernel_name = 'hybrid_rglru_ssd_stickbreaking_meta'


def rmsnorm(x, w):
    xf = x.astype(jnp.float32)
    y = xf * lax.rsqrt(jnp.mean(xf * xf, axis=-1, keepdims=True) + EPS)
    return (y * w.astype(jnp.float32)).astype(x.dtype)


def causal_dwconv(u, w, b):
    out = lax.conv_general_dilated(
        u, w[:, None, :].astype(u.dtype), window_strides=(1,),
        padding=[(CONV_W - 1, 0)], dimension_numbers=('NWC', 'WIO', 'NWC'),
        feature_group_count=u.shape[-1])
    return out + b


def linear_scan(a, b):
    def combine(left, right):
        al, bl = left
        ar, br = right
        return al * ar, ar * bl + br
    _, h = lax.associative_scan(combine, (a, b), axis=1)
    return h


def rg_lru(u, w_a, b_a, w_x, b_x, lam):
    bsz, L, _ = u.shape
    uf = u.astype(jnp.float32)
    ub = uf.reshape(bsz, L, LRU_BLOCKS, LRU_BLOCK)
    r = jax.nn.sigmoid(jnp.einsum('blgi,gij->blgj', ub, w_a).reshape(bsz, L, LRU_WIDTH) + b_a)
    i = jax.nn.sigmoid(jnp.einsum('blgi,gij->blgj', ub, w_x).reshape(bsz, L, LRU_WIDTH) + b_x)
    log_a = -RG_LRU_C * r * jax.nn.softplus(-lam)
    a = jnp.exp(log_a)
    mult = jnp.sqrt(-jnp.expm1(2.0 * log_a))
    return linear_scan(a, mult * i * uf)


def ssd_scan(xh, dt, a, bmat, cmat):
    bsz, L, _, _ = xh.shape
    pad = (-L) % SSD_CHUNK
    def padf(t):
        return jnp.pad(t, [(0, 0), (pad, 0)] + [(0, 0)] * (t.ndim - 2))
    f32 = jnp.float32
    xdt = padf((xh * dt[..., None]).astype(f32))
    adt = padf((dt * a).astype(f32))
    bm = padf(bmat.astype(f32))
    cm = padf(cmat.astype(f32))
    nc = (L + pad) // SSD_CHUNK
    X = xdt.reshape(bsz, nc, SSD_CHUNK, SSD_GROUPS, SSD_HPG, SSD_HEAD_DIM)
    A = adt.reshape(bsz, nc, SSD_CHUNK, SSD_GROUPS, SSD_HPG).transpose(0, 3, 4, 1, 2)
    Bc = bm.reshape(bsz, nc, SSD_CHUNK, SSD_GROUPS, SSD_STATE)
    Cc = cm.reshape(bsz, nc, SSD_CHUNK, SSD_GROUPS, SSD_STATE)
    a_cum = jnp.cumsum(A, axis=-1)
    tri = jnp.tril(jnp.ones((SSD_CHUNK, SSD_CHUNK), bool))
    seg = a_cum[..., :, None] - a_cum[..., None, :]
    decay = jnp.exp(jnp.where(tri, seg, -jnp.inf))
    cb = jnp.einsum('bclgn,bcsgn->bcgls', Cc, Bc)
    y_diag = jnp.einsum('bcgls,bgecls,bcsgep->bclgep', cb, decay, X)
    decay_states = jnp.exp(a_cum[..., -1:] - a_cum)
    states = jnp.einsum('bclgn,bgecl,bclgep->bcgepn', Bc, decay_states, X)
    chunk_tot = jnp.pad(a_cum[..., -1], [(0, 0), (0, 0), (0, 0), (1, 0)])
    cs = jnp.cumsum(chunk_tot, axis=-1)
    tri_c = jnp.tril(jnp.ones((nc + 1, nc + 1), bool))
    decay_chunk = jnp.exp(jnp.where(tri_c, cs[..., :, None] - cs[..., None, :], -jnp.inf))
    states = jnp.concatenate([jnp.zeros_like(states[:, :1]), states], axis=1)
    states = jnp.einsum('bgezc,bcgepn->bzgepn', decay_chunk, states)[:, :-1]
    y_off = jnp.einsum('bclgn,bcgepn,bgecl->bclgep', Cc, states, jnp.exp(a_cum))
    y = (y_diag + y_off).reshape(bsz, L + pad, SSD_HEADS, SSD_HEAD_DIM)
    return y[:, pad:]


def gated_group_rmsnorm(y, z, w):
    bsz, L, W = y.shape
    g = (y * jax.nn.silu(z.astype(jnp.float32))).reshape(bsz, L, SSD_GROUPS, W // SSD_GROUPS)
    g = g * lax.rsqrt(jnp.mean(g * g, axis=-1, keepdims=True) + EPS)
    return g.reshape(bsz, L, W) * w.astype(jnp.float32)


def rglru_ssd_layer(h, norm_w, w_in, lru_conv_w, lru_conv_b, lru_w_a, lru_b_a,
                    lru_w_x, lru_b_x, lru_lambda, ssd_conv_w, ssd_conv_b,
                    ssd_dt_bias, ssd_a_log, ssd_d, ssd_norm, w_out):
    bsz, L, _ = h.shape
    u = rmsnorm(h, norm_w)
    proj = u @ w_in
    lru_x, lru_g, ssd_z, ssd_xbc, ssd_dt = jnp.split(proj, EVEN_SPLITS, axis=-1)
    lx = causal_dwconv(lru_x, lru_conv_w, lru_conv_b)
    y_a = rg_lru(lx, lru_w_a, lru_b_a, lru_w_x, lru_b_x, lru_lambda) * jax.nn.silu(lru_g.astype(jnp.float32))
    xbc = jax.nn.silu(causal_dwconv(ssd_xbc, ssd_conv_w, ssd_conv_b))
    xs, bm, cm = jnp.split(xbc, (SSD_WIDTH, SSD_WIDTH + SSD_GROUPS * SSD_STATE), axis=-1)
    dt = jax.nn.softplus(ssd_dt.astype(jnp.float32) + ssd_dt_bias)
    a = -jnp.exp(ssd_a_log.astype(jnp.float32))
    xh = xs.reshape(bsz, L, SSD_HEADS, SSD_HEAD_DIM)
    y = ssd_scan(xh, dt, a,
                 bm.reshape(bsz, L, SSD_GROUPS, SSD_STATE),
                 cm.reshape(bsz, L, SSD_GROUPS, SSD_STATE))
    y = y + xh.astype(jnp.float32) * ssd_d[:, None]
    y_b = gated_group_rmsnorm(y.reshape(bsz, L, SSD_WIDTH), ssd_z, ssd_norm)
    mixed = jnp.concatenate([y_a, y_b], axis=-1).astype(h.dtype)
    return h + mixed @ w_out


def stick_breaking_block(q_blk, k_ctx, v_ctx, q0):
    tq = q_blk.shape[1]
    s_len = k_ctx.shape[1]
    z = jnp.einsum('bthd,bshd->bhts', q_blk.astype(jnp.float32),
                   k_ctx.astype(jnp.float32)) * (SB_HEAD_DIM ** -0.5)
    before = jnp.arange(s_len)[None, :] < (q0 + jnp.arange(tq))[:, None]
    log_keep = jnp.where(before, jax.nn.log_sigmoid(-z), 0.0)
    csum = jnp.cumsum(log_keep, axis=-1)
    weights = jnp.where(before, jnp.exp(jax.nn.log_sigmoid(z) + csum[..., -1:] - csum), 0.0)
    return jnp.einsum('bhts,bshd->bthd', weights, v_ctx.astype(jnp.float32))


def stick_breaking_layer(h, norm_w, w_in, w_out):
    bsz, L, _ = h.shape
    u = rmsnorm(h, norm_w)
    q, k, v, g = jnp.split(u @ w_in, 4, axis=-1)
    q = q.reshape(bsz, L, SB_HEADS, SB_HEAD_DIM)
    k = k.reshape(bsz, L, SB_HEADS, SB_HEAD_DIM)
    v = v.reshape(bsz, L, SB_HEADS, SB_HEAD_DIM)
    bounds = [0] + list(range(N_META, L, SB_BLOCK)) + [L]
    outs = [stick_breaking_block(q[:, s:e], k[:, :e], v[:, :e], s)
            for s, e in zip(bounds[:-1], bounds[1:])]
    o = jnp.concatenate(outs, axis=1).reshape(bsz, L, SB_WIDTH)
    o = (o * jax.nn.silu(g.astype(jnp.float32))).astype(h.dtype)
    return h + o @ w_out


def setup_inputs(seed: int = 0) -> dict:
    key = jax.random.key(seed)
    ks = jax.random.split(key, 24)
    f32 = jnp.float32

    def nrm(k, shape, fan_in):
        return jax.random.normal(k, shape, f32) * (fan_in ** -0.5)

    def gain(k, shape):
        return 1.0 + 0.05 * jax.random.normal(k, shape, f32)

    def bias(k, shape, s=0.05):
        return s * jax.random.normal(k, shape, f32)

    x = jax.random.normal(ks[0], (BATCH, SEQ, D_MODEL), f32)
    meta = jax.random.normal(ks[1], (N_META, D_MODEL), f32)
    even_norm = gain(ks[2], (N_EVEN, D_MODEL))
    even_w_in = nrm(ks[3], (N_EVEN, D_MODEL, EVEN_IN), D_MODEL)
    lru_conv_w = nrm(ks[4], (N_EVEN, CONV_W, LRU_WIDTH), CONV_W)
    lru_conv_b = bias(ks[5], (N_EVEN, LRU_WIDTH))
    lru_w_a = nrm(ks[6], (N_EVEN, LRU_BLOCKS, LRU_BLOCK, LRU_BLOCK), LRU_BLOCK)
    lru_b_a = bias(ks[7], (N_EVEN, LRU_WIDTH), 0.1)
    lru_w_x = nrm(ks[8], (N_EVEN, LRU_BLOCKS, LRU_BLOCK, LRU_BLOCK), LRU_BLOCK)
    lru_b_x = bias(ks[9], (N_EVEN, LRU_WIDTH), 0.1)
    a_c = jax.random.uniform(ks[10], (N_EVEN, LRU_WIDTH), f32, minval=0.9, maxval=0.999)
    a0 = a_c ** (1.0 / RG_LRU_C)
    lru_lambda = jnp.log(a0) - jnp.log1p(-a0)
    ssd_conv_w = nrm(ks[11], (N_EVEN, CONV_W, SSD_CONV_DIM), CONV_W)
    ssd_conv_b = bias(ks[12], (N_EVEN, SSD_CONV_DIM))
    dt0 = jnp.exp(jax.random.uniform(ks[13], (N_EVEN, SSD_HEADS), f32,
                                     minval=float(np.log(1e-3)), maxval=float(np.log(1e-1))))
    ssd_dt_bias = dt0 + jnp.log(-jnp.expm1(-dt0))
    ssd_a_log = jnp.log(jax.random.uniform(ks[14], (N_EVEN, SSD_HEADS), f32, minval=1.0, maxval=16.0))
    ssd_d = gain(ks[15], (N_EVEN, SSD_HEADS))
    ssd_norm = gain(ks[16], (N_EVEN, SSD_WIDTH))
    even_w_out = nrm(ks[17], (N_EVEN, EVEN_MIX, D_MODEL), EVEN_MIX)
    odd_norm = gain(ks[18], (N_ODD, D_MODEL))
    odd_w_in = nrm(ks[19], (N_ODD, D_MODEL, ODD_IN), D_MODEL)
    odd_w_out = nrm(ks[20], (N_ODD, SB_WIDTH, D_MODEL), SB_WIDTH)
    final_norm = gain(ks[21], (D_MODEL,))
    return {'x': x, 'meta': meta, 'even_norm': even_norm, 'even_w_in': even_w_in,
            'lru_conv_w': lru_conv_w, 'lru_conv_b': lru_conv_b,
            'lru_w_a': lru_w_a, 'lru_b_a': lru_b_a, 'lru_w_x': lru_w_x, 'lru_b_x': lru_b_x,
            'lru_lambda': lru_lambda, 'ssd_conv_w': ssd_conv_w, 'ssd_conv_b': ssd_conv_b,
            'ssd_dt_bias': ssd_dt_bias, 'ssd_a_log': ssd_a_log, 'ssd_d': ssd_d,
            'ssd_norm': ssd_norm, 'even_w_out': even_w_out, 'odd_norm': odd_norm,
            'odd_w_in': odd_w_in, 'odd_w_out': odd_w_out, 'final_norm': final_norm}


def reference(x, meta, even_norm, even_w_in, lru_conv_w, lru_conv_b, lru_w_a, lru_b_a,
              lru_w_x, lru_b_x, lru_lambda, ssd_conv_w, ssd_conv_b, ssd_dt_bias,
              ssd_a_log, ssd_d, ssd_norm, even_w_out, odd_norm, odd_w_in, odd_w_out,
              final_norm):
    bsz = x.shape[0]
    meta_b = jnp.broadcast_to(meta[None].astype(x.dtype), (bsz, N_META, D_MODEL))
    h = jnp.concatenate([meta_b, x], axis=1)
    for layer in range(DEPTH):
        j = layer // 2
        if layer % 2 == 0:
            h = rglru_ssd_layer(h, even_norm[j], even_w_in[j], lru_conv_w[j], lru_conv_b[j],
                                lru_w_a[j], lru_b_a[j], lru_w_x[j], lru_b_x[j], lru_lambda[j],
                                ssd_conv_w[j], ssd_conv_b[j], ssd_dt_bias[j], ssd_a_log[j],
                                ssd_d[j], ssd_norm[j], even_w_out[j])
        else:
            h = stick_breaking_layer(h, odd_norm[j], odd_w_in[j], odd_w_out[j])
    return rmsnorm(h, final_norm)[:, N_META:].astype(x.dtype)
```

```python
import numpy as np
from contextlib import ExitStack
import concourse.bass as bass
import concourse.mybir as mybir
from concourse.bass_utils import run_bass_kernel_spmd


F32 = mybir.dt.float32
BF16 = mybir.dt.bfloat16
AF = mybir.ActivationFunctionType
ALU = mybir.AluOpType
AX = mybir.AxisListType

ENGS = ("pe", "act", "dve", "pool", "sp")


class Buf:
    __slots__ = ("name", "w", "rs", "dsem", "dcnt")

    def __init__(self, name):
        self.name = name
        self.w = None
        self.rs = []
        self.dsem = None
        self.dcnt = 0


class K:
    def __init__(self, nc, es):
        self.nc = nc
        self.es = es
        self.prog = {e: [] for e in ENGS}
        self.sem = {e: es.enter_context(nc.semaphore("s_" + e)) for e in ENGS}
        self.cnt = {e: 0 for e in ENGS}
        self.waited = {e: {} for e in ENGS}
        self.pending = {e: [] for e in ENGS}
        self.nsem = 5
        self.ninstr = 0
        self.scopes = [es]

    def push(self):
        self.scopes.append(ExitStack())

    def pop(self):
        self.scopes.pop().close()

    def barrier(self, dma_bufs=()):
        for e in ENGS:
            if self.pending[e]:
                raise RuntimeError("barrier with pending unsignalled ops on " + e)
        waits = {}
        for e in ENGS:
            if e != "pool" and self.cnt[e] > 0:
                self._need("pool", (e, self.cnt[e], self.sem[e]), waits)
        for b in dma_bufs:
            self._need("pool", b.w, waits)
            for t in b.rs:
                self._need("pool", t, waits)
        self.cnt["pool"] += 1
        tok = ("pool", self.cnt["pool"], self.sem["pool"])
        self.prog["pool"].append((list(waits.values()), lambda e: e.engine_nop(), (self.sem["pool"], 1)))
        for e in ENGS:
            if e != "pool":
                self.wait_tok(e, tok)

    def sb(self, name, shape, dt):
        return self.scopes[-1].enter_context(self.nc.sbuf_tensor(name, list(shape), dt))

    def ps(self, name, shape, dt=F32):
        return self.scopes[-1].enter_context(self.nc.psum_tensor(name, list(shape), dt))

    def dsem_of(self, buf):
        if buf.dsem is None:
            buf.dsem = self.es.enter_context(self.nc.semaphore("d_" + buf.name))
            self.nsem += 1
        return buf.dsem

    def _need(self, eng, tok, waits):
        if tok is None:
            return
        key, val, semh = tok
        if key == eng and False:
            return
        cur = self.waited[eng].get(key, 0)
        if val > cur:
            self.waited[eng][key] = val
            waits[key] = (semh, val)

    def _deps(self, eng, r, w):
        waits = {}
        for b in r:
            if b.w is not None and b.w[0] == "PENDING":
                raise RuntimeError("read of buffer %s with unsignalled writer" % b.name)
            self._need(eng, b.w, waits)
        for b in w:
            if b.w is not None and b.w[0] == "PENDING":
                if b.w[1] != eng:
                    raise RuntimeError("write of buffer %s with unsignalled writer" % b.name)
            else:
                self._need(eng, b.w, waits)
            for t in b.rs:
                if t[0] == "PENDING":
                    if t[1] != eng:
                        raise RuntimeError("WAR on buffer %s with unsignalled reader" % b.name)
                else:
                    self._need(eng, t, waits)
        return list(waits.values())

    def op(self, eng, fn, r=(), w=(), sig=True):
        waits = self._deps(eng, r, w)
        if sig:
            self.cnt[eng] += 1
            tok = (eng, self.cnt[eng], self.sem[eng])
            semh = self.sem[eng]
            for (b, kind) in self.pending[eng]:
                if kind == "r":
                    b.rs = [t for t in b.rs if not (t[0] == "PENDING" and t[1] == eng)]
                    b.rs.append(tok)
                else:
                    b.w = tok
                    b.rs = [t for t in b.rs if not (t[0] == "PENDING" and t[1] == eng)]
            self.pending[eng] = []
            for b in r:
                b.rs.append(tok)
            for b in w:
                b.w = tok
                b.rs = []
            self.prog[eng].append((waits, fn, (semh, 1)))
        else:
            ptok = ("PENDING", eng)
            for b in r:
                b.rs.append(ptok)
                self.pending[eng].append((b, "r"))
            for b in w:
                b.w = ptok
                b.rs = []
                self.pending[eng].append((b, "w"))
            self.prog[eng].append((waits, fn, None))
        self.ninstr += 1

    def dma(self, eng, out_ap, in_ap, r=(), w=(), sbuf=None, **kw):
        waits = self._deps(eng, r, w)
        semh = self.dsem_of(sbuf)
        sbuf.dcnt += 1
        tok = ("d_" + sbuf.name, 16 * sbuf.dcnt, semh)
        for b in r:
            b.rs.append(tok)
        for b in w:
            b.w = tok
            b.rs = []
        self.prog[eng].append((waits, lambda e: e.dma_start(out=out_ap, in_=in_ap, **kw), (semh, 16)))
        self.ninstr += 1
        return tok

    def wait_tok(self, eng, tok):
        waits = {}
        self._need(eng, tok, waits)
        for (semh, val) in waits.values():
            self.prog[eng].append(([(semh, val)], None, None))

    def final_wait(self, eng, bufs):
        waits = {}
        for b in bufs:
            self._need(eng, b.w, waits)
            for t in b.rs:
                self._need(eng, t, waits)
        if waits:
            self.prog[eng].append((list(waits.values()), None, None))

    def emit(self):
        nc = self.nc
        engmap = {"pe": "tensor", "act": "scalar", "dve": "vector", "pool": "gpsimd", "sp": "sync"}
        with nc.Block() as block:
            for e in ENGS:
                prog = self.prog[e]

                def body(engine, prog=prog):
                    for waits, fn, inc in prog:
                        for (semh, val) in waits:
                            engine.wait_ge(semh, val)
                        if fn is not None:
                            ins = fn(engine)
                            if inc is not None:
                                ins.then_inc(inc[0], inc[1])
                getattr(block, engmap[e])(body)
        self.prog = {e: [] for e in ENGS}


D_MODEL = 1024
EVEN_IN = 4624


class T:
    def __init__(self, k, name, shape, dt, space="sb"):
        self.t = k.sb("t_" + name, shape, dt) if space == "sb" else k.ps("t_" + name, shape, dt)
        self.b = Buf(name)
        self.name = name

    def __getitem__(self, idx):
        return self.t[idx]


class View:
    def __init__(self, ap, b):
        self.ap = ap
        self.b = b

    def __getitem__(self, idx):
        return self.ap[idx]


def _b(xs):
    return [getattr(x, "b", x) for x in xs]


class Ctx:
    pass


def setup_common(k, nc):
    c = Ctx()
    c.k = k
    c.nc = nc
    c.rr = 0
    return c


def OP(c, eng, fn, r=(), w=(), sig=True):
    c.k.op(eng, fn, r=_b(r), w=_b(w), sig=sig)


def ACT(c, out, in_, func, r, w, **kw):
    OP(c, "act", lambda e: e.activation(out=out, in_=in_, func=func, **kw), r, w)


def TT(c, eng, out, in0, in1, op, r, w):
    OP(c, eng, lambda e: e.tensor_tensor(out=out, in0=in0, in1=in1, op=op), r, w)


def TS(c, eng, out, in0, s1, s2, op0, op1, r, w):
    if s2 is None:
        OP(c, eng, lambda e: e.tensor_scalar(out=out, in0=in0, scalar1=s1, scalar2=None, op0=op0), r, w)
    else:
        OP(c, eng, lambda e: e.tensor_scalar(out=out, in0=in0, scalar1=s1, scalar2=s2, op0=op0, op1=op1), r, w)


def STT(c, eng, out, in0, scalar, in1, op0, op1, r, w):
    OP(c, eng, lambda e: e.scalar_tensor_tensor(out=out, in0=in0, scalar=scalar, in1=in1, op0=op0, op1=op1), r, w)


def CP(c, eng, out, in_, r, w):
    if eng == "act":
        ACT(c, out, in_, AF.Copy, r, w)
    else:
        OP(c, eng, lambda e: e.tensor_copy(out=out, in_=in_), r, w)


def MM(c, out, lhsT, rhs, start, stop, r, w, sig):
    OP(c, "pe", lambda e: e.matmul(out, lhsT=lhsT, rhs=rhs, start=start, stop=stop), r, w, sig=sig)


def TR(c, out, in_, ident, r, w, sig):
    OP(c, "pe", lambda e: e.transpose(out=out, in_=in_, identity=ident), r, w, sig=sig)


def load_cast_weight(c, dst, dst_slices, dram_rows, width, scale_aps, st, engs=("act", "dve")):
    k = c.k
    for i, (da, ra) in enumerate(zip(dst_slices, dram_rows)):
        s = st[c.rr % len(st)]
        eng = engs[c.rr % len(engs)]
        c.rr += 1
        k.dma("sp", s.t[:, 0:width], ra, w=[s.b], sbuf=s.b)
        sc = scale_aps[i]
        if sc is None:
            CP(c, eng, da, s.t[:, 0:width], [s], [dst])
        else:
            if eng == "act":
                OP(c, "act", lambda e, da=da, s=s, sc=sc: e.activation(out=da, in_=s.t[:, 0:width], func=AF.Copy, scale=sc), [s, c.vecF], [dst])
            else:
                TS(c, eng, da, s.t[:, 0:width], sc, None, ALU.mult, None, [s, c.vecF], [dst])


def setup_layer0(c, D):
    k = c.k
    c.w_in = T(k, "w_in", [128, 8, EVEN_IN], BF16)
    c.w_out = T(k, "w_out", [128, 16, 1024], BF16)
    c.wa = T(k, "wa", [128, 4, 2, 256], BF16)
    c.wx = T(k, "wx", [128, 4, 2, 256], BF16)
    c.vecF = T(k, "vecF", [128, 140], F32)
    c.vecT = T(k, "vecT", [128, 48], F32)
    k.dma("sp", c.vecF[:], D["vecF"][:, :], w=[c.vecF.b], sbuf=c.vecF.b)
    k.dma("sp", c.vecT[:], D["vecT"][0:1, :].partition_broadcast(128), w=[c.vecT.b], sbuf=c.vecT.b)
    st = c.stage
    for (c0, wd) in ((0, 1024), (1024, 1024), (2048, 1024), (3072, 1024), (4096, 528)):
        dsts, rows, scs = [], [], []
        for kc in range(8):
            dsts.append(c.w_in[:, kc, c0:c0 + wd])
            rows.append(D["w_in"][kc * 128:(kc + 1) * 128, c0:c0 + wd])
            scs.append(c.vecF[:, kc:kc + 1])
        load_cast_weight(c, c.w_in, dsts, rows, wd, scs, st)
    dsts, rows, scs = [], [], []
    for kc in range(16):
        dsts.append(c.w_out[:, kc, :])
        rows.append(D["w_out"][kc * 128:(kc + 1) * 128, :])
        scs.append(None if kc < 8 else c.vecF[:, 8 + kc - 8:8 + kc - 8 + 1])
    load_cast_weight(c, c.w_out, dsts, rows, 1024, scs, st)
    for (wt, nm) in ((c.wa, "wa"), (c.wx, "wx")):
        dsts, rows, scs = [], [], []
        for g in range(4):
            for kc in range(2):
                dsts.append(wt[:, g, kc, :])
                rows.append(D[nm][g, kc * 128:(kc + 1) * 128, :])
                scs.append(None)
        load_cast_weight(c, wt, dsts, rows, 256, scs, st)

    c.L1 = T(k, "L1", [128, 128], F32)
    c.L2 = T(k, "L2", [128, 128], F32)
    c.L4 = T(k, "L4", [128, 2, 128], F32)
    c.mle = T(k, "mle", [128, 64], F32)
    OP(c, "pool", lambda e: e.memset(c.L1[:], 0.0), [], [c.L1])
    OP(c, "pool", lambda e: e.memset(c.L2[:], 0.0), [], [c.L2])
    OP(c, "pool", lambda e: e.memset(c.L4[:], 0.0), [], [c.L4])
    OP(c, "pool", lambda e: e.memset(c.mle[:], 1.0), [], [c.mle])
    for h in range(2):
        ps = slice(h * 64, (h + 1) * 64)
        OP(c, "pool", lambda e, ps=ps: e.memset(c.L1[ps, ps], 1.0), [], [c.L1])
        OP(c, "pool", lambda e, ps=ps: e.memset(c.L2[ps, ps], 1.0), [], [c.L2])
        OP(c, "pool", lambda e, ps=ps, h=h: e.memset(c.L4[ps, h, :], 1.0), [], [c.L4])
        OP(c, "pool", lambda e, ps=ps: e.affine_select(out=c.L1[ps, ps], in_=c.L1[ps, ps], pattern=[[1, 64]], compare_op=ALU.is_ge, fill=0.0, base=0, channel_multiplier=-1), [c.L1], [c.L1])
        OP(c, "pool", lambda e, ps=ps: e.affine_select(out=c.L2[ps, ps], in_=c.L2[ps, ps], pattern=[[-1, 64]], compare_op=ALU.is_gt, fill=0.0, base=0, channel_multiplier=1), [c.L2], [c.L2])
        OP(c, "pool", lambda e, ps=ps: e.affine_select(out=c.mle[ps, :], in_=c.mle[ps, :], pattern=[[1, 64]], compare_op=ALU.is_ge, fill=0.0, base=0, channel_multiplier=-1), [c.mle], [c.mle])
    c.pv = T(k, "pv", [128, 64], F32)
    ACT(c, c.pv[:, 0:8], c.vecF[:, 72:80], AF.Exp, [c.vecF], [c.pv], scale=-1.0)
    ACT(c, c.pv[:, 0:8], c.pv[:, 0:8], AF.Ln, [c.pv], [c.pv], bias=1.0)
    TS(c, "dve", c.pv[:, 8:16], c.pv[:, 0:8], -16.0, None, ALU.mult, None, [c.pv], [c.pv])
    TS(c, "dve", c.pv[:, 0:8], c.pv[:, 0:8], -8.0, None, ALU.mult, None, [c.pv], [c.pv])
    TS(c, "dve", c.pv[:, 16:32], c.vecF[:, 56:72], -1.0, None, ALU.mult, None, [c.vecF], [c.pv])
    ACT(c, c.pv[:, 32:48], c.vecT[:, 16:32], AF.Exp, [c.vecT], [c.pv])
    TS(c, "dve", c.pv[:, 32:48], c.pv[:, 32:48], -1.0, None, ALU.mult, None, [c.pv], [c.pv])


def alloc_layer0_work(c):
    k = c.k
    c.xt = [T(k, "xt%d" % i, [128, 1024], F32) for i in range(2)]
    c.st1 = [T(k, "st1_%d" % i, [128, 8], F32) for i in range(2)]
    c.W = [T(k, "W%d" % i, [128, 1024], F32) for i in range(6)]
    c.H = [T(k, "H%d" % i, [128, (256 if i == 1 else (512 if i == 5 else 1024))], BF16) for i in range(6)]
    c.projF = [T(k, "projF%d" % i, [128, 20, 131], F32) for i in range(2)]
    c.sg = [T(k, "sg%d" % i, [128, 1024], BF16) for i in range(2)]
    c.gz = [T(k, "gz%d" % i, [128, 1024], BF16) for i in range(2)]
    c.sgt = T(k, "sgt", [128, 512], F32)
    c.dts = [T(k, "dts%d" % i, [128, 32], F32) for i in range(2)]
    c.ubP = T(k, "ubP", [128, 1024], BF16)
    c.uTP = T(k, "uTP", [128, 1024], BF16)
    c.xbc = T(k, "xbc", [128, 12, 128], F32)
    c.S = T(k, "S", [128, 1024], F32)
    c.Sbf = T(k, "Sbf", [128, 1024], BF16)
    c.hst = T(k, "hst", [128, 8], F32)
    c.sm = T(k, "sm", [128, 128], F32)
    c.cbm = T(k, "cbm", [128, 2, 64], F32)
    c.pT = T(k, "pT", [128, 8, 128], BF16, "ps")
    c.pG = T(k, "pG", [128, 512], F32, "ps")
    c.pT2 = View(c.pG[:, 256:384].bitcast(BF16).rearrange("p (c t) -> p c t", c=2), c.pG.b)
    c.pB = [T(k, "pB%d" % i, [128, 512], F32, "ps") for i in range(6)]
    c.stage = [c.W[0], c.W[1], c.W[2], c.W[3]]
    c.pTP = View(c.pB[0][:, :].bitcast(BF16).rearrange("p (c t) -> p c t", c=8), c.pB[0].b)


def seq_reset0(c, par):
    OP(c, "pool", lambda e: e.memset(c.projF[par][:, :, 0:3], 0.0), [], [c.projF[par]])
    OP(c, "pool", lambda e: e.memset(c.S[:], 0.0), [], [c.S])
    OP(c, "pool", lambda e: e.memset(c.Sbf[:], 0.0), [], [c.Sbf])
    OP(c, "pool", lambda e: e.memset(c.hst[:], 0.0), [], [c.hst])


def act_sigmoid_from(c, out, in_, rin, wout, neg_bias=None):
    ACT(c, out, in_, AF.Exp, rin, wout, scale=-1.0)
    ACT(c, out, out, AF.Ln, wout, wout, bias=1.0)
    ACT(c, out, out, AF.Exp, wout, wout, scale=-1.0)


def layer0_P(c, src_ap, nt, par, prev):
    k = c.k
    xt = c.xt[par]
    st1 = c.st1[par]
    P = slice(0, nt)
    identb = c.identb
    projF = c.projF[par]
    k.dma("sp", xt[P, :], src_ap, w=[xt.b], sbuf=xt.b)
    ACT(c, c.ubP[P, :], xt[P, :], AF.Square, [xt], [c.ubP, st1], accum_out=st1[P, 0:1])
    ACT(c, st1[P, 1:2], st1[P, 0:1], AF.Ln, [st1], [st1], scale=1.0 / D_MODEL, bias=1e-6)
    ACT(c, st1[P, 1:2], st1[P, 1:2], AF.Exp, [st1], [st1], scale=-0.5)
    ub = c.ubP
    TS(c, "dve", ub[P, :], xt[P, :], st1[P, 1:2], None, ALU.mult, None, [xt, st1], [ub])
    yield
    for kc in range(8):
        TR(c, c.pTP[:, kc, P], ub[P, kc * 128:(kc + 1) * 128], identb[P, P], [ub, identb], [c.pTP], sig=(kc == 7))
    uT = c.uTP
    uTv = uT[:].rearrange("p (c t) -> p c t", c=8)
    CP(c, "act", uTv[:, :, P], c.pTP[:, :, P], [c.pTP], [uT])
    yield
    if prev is not None:
        pp, pnt = prev
        CP(c, "pool", projF[:, :, 0:3], c.projF[pp][:, :, pnt:pnt + 3], [c.projF[pp]], [projF])
    pz = (c.pB[0], c.pB[1])
    gz = c.gz[par]
    dts = c.dts[par]
    for kc in range(8):
        MM(c, c.pB[1][P, 0:16], uTv[:, kc, P], c.w_in[:, kc, 4608:4624], kc == 0, kc == 7, [uT, c.w_in], [c.pB[1]], sig=(kc == 7))
    TT(c, "dve", dts[P, 0:16], c.pB[1][P, 0:16], c.vecT[P, 0:16], ALU.add, [c.pB[1], c.vecT], [dts])
    yield
    ACT(c, dts[P, 0:16], dts[P, 0:16], AF.Exp, [dts], [dts])
    ACT(c, dts[P, 0:16], dts[P, 0:16], AF.Ln, [dts], [dts], bias=1.0)
    TT(c, "dve", dts[P, 16:32], dts[P, 0:16], c.pv[P, 32:48], ALU.mult, [dts, c.pv], [dts])
    yield
    for hf in range(2):
        for kc in range(8):
            MM(c, pz[hf][P, :], uTv[:, kc, P], c.w_in[:, kc, 2048 + hf * 512:2048 + (hf + 1) * 512], kc == 0, kc == 7, [uT, c.w_in], [pz[hf]], sig=(kc == 7))
        yield
    for hf in range(2):
        hs = slice(hf * 512, (hf + 1) * 512)
        act_sigmoid_from(c, c.sgt[P, :], pz[hf][P, :], [pz[hf]], [c.sgt])
        yield
        TT(c, "dve", gz[P, hs], c.sgt[P, :], pz[hf][P, :], ALU.mult, [c.sgt, pz[hf]], [gz])
        yield
    sg = c.sg[par]
    sgv = sg[:].rearrange("p (c t) -> p c t", c=8)
    groups = []
    for g4 in range(2):
        groups.append(("x", [g4 * 4 + i for i in range(4)], 0))
    for g4 in range(2):
        groups.append(("g", [g4 * 4 + i for i in range(4)], 1024))
    for g4 in range(3):
        groups.append(("b", [g4 * 4 + i for i in range(4)], 3072))
    for gi, (kind, ocs, colbase) in enumerate(groups):
        pb = c.pB[gi % 2]
        pbv = pb[:].rearrange("p (c t) -> p c t", c=4)
        for i, oc in enumerate(ocs):
            for kc in range(8):
                MM(c, pbv[:, i, P], c.w_in[:, kc, colbase + oc * 128:colbase + (oc + 1) * 128], uTv[:, kc, P], kc == 0, kc == 7, [uT, c.w_in], [pb], sig=(kc == 7))
            yield
        if kind == "x":
            CP(c, "act", projF[:, ocs[0]:ocs[0] + 4, 3:3 + nt], pbv[:, :, P], [pb], [projF])
        elif kind == "b":
            CP(c, "act", projF[:, 8 + ocs[0]:8 + ocs[0] + 4, 3:3 + nt], pbv[:, :, P], [pb], [projF])
        else:
            o = sgv[:, ocs[0]:ocs[0] + 4, P]
            sc = c.sgt[:].rearrange("p (c t) -> p c t", c=4)[:, :, P]
            act_sigmoid_from(c, sc, pbv[:, :, P], [pb], [c.sgt])
            yield
            TT(c, "dve", o, sc, pbv[:, :, P], ALU.mult, [c.sgt, pb], [sg])
        yield


def layer0_M(c, dst_ap, nt, par, dst_buf):
    k = c.k
    xt = c.xt[par]
    st1 = c.st1[par]
    W = c.W
    H = c.H
    chunks = [(0, nt)] if nt <= 64 else [(0, 64), (64, 128)]
    cw = chunks[0][1]
    nch = len(chunks)
    identb = c.identb
    P = slice(0, nt)
    projF = c.projF[par]
    sg = c.sg[par]
    sgv = sg[:].rearrange("p (c t) -> p c t", c=8)
    gz = c.gz[par]
    dts = c.dts[par]
    sm = c.sm

    lx = W[3]
    lxv = lx[:].rearrange("p (c t) -> p c t", c=8)
    for ch in range(8):
        o = lxv[:, ch, P]
        TS(c, "dve", o, projF[:, ch, 0:nt], c.vecF[:, 16 + ch * 4:16 + ch * 4 + 1], c.vecF[:, 48 + ch:48 + ch + 1], ALU.mult, ALU.add, [projF, c.vecF], [lx])
        for tp in range(1, 4):
            STT(c, "dve", o, projF[:, ch, tp:tp + nt], c.vecF[:, 16 + ch * 4 + tp:16 + ch * 4 + tp + 1], o, ALU.mult, ALU.add, [projF, c.vecF, lx], [lx])
        yield
    yield
    lxb = H[2]
    lxbv = lxb[:].rearrange("p (c t) -> p c t", c=8)
    CP(c, "act", lxbv[:, :, P], lxv[:, :, P], [lx], [lxb])
    ea_ = W[4]
    ex_ = W[5]
    eav = ea_[:].rearrange("p (c t) -> p c t", c=8)
    exv = ex_[:].rearrange("p (c t) -> p c t", c=8)
    for (wt, pbs, ev, boff, et) in ((c.wa, (c.pB[2], c.pB[3]), eav, 16, ea_), (c.wx, (c.pB[4], c.pB[5]), exv, 24, ex_)):
        for oc in range(8):
            g = oc // 2
            pb = pbs[oc // 4]
            pbv = pb[:].rearrange("p (c t) -> p c t", c=4)
            for kc in range(2):
                MM(c, pbv[:, oc % 4, P], wt[:, g, kc, (oc % 2) * 128:(oc % 2 + 1) * 128], lxbv[:, 2 * g + kc, P], kc == 0, kc == 1, [lxb, wt], [pb], sig=(oc % 4 == 3 and kc == 1))
    yield
    xbc = c.xbc
    for ch in range(12):
        o = xbc[:, ch, P]
        TS(c, "dve", o, projF[:, 8 + ch, 0:nt], c.vecF[:, 80 + ch * 4:80 + ch * 4 + 1], c.vecF[:, 128 + ch:128 + ch + 1], ALU.mult, ALU.add, [projF, c.vecF], [xbc])
        for tp in range(1, 4):
            STT(c, "dve", o, projF[:, 8 + ch, tp:tp + nt], c.vecF[:, 80 + ch * 4 + tp:80 + ch * 4 + tp + 1], o, ALU.mult, ALU.add, [projF, c.vecF, xbc], [xbc])
        yield
    for (wt, pbs, ev, boff, et) in ((c.wa, (c.pB[2], c.pB[3]), eav, 16, ea_), (c.wx, (c.pB[4], c.pB[5]), exv, 24, ex_)):
        for oc in range(8):
            pb = pbs[oc // 4]
            pbv = pb[:].rearrange("p (c t) -> p c t", c=4)
            ACT(c, ev[:, oc, P], pbv[:, oc % 4, P], AF.Exp, [pb, c.pv], [et], scale=-1.0, bias=c.pv[:, boff + oc:boff + oc + 1])
            if oc % 4 == 3:
                yield
        ACT(c, ev[:, :, P], ev[:, :, P], AF.Ln, [et], [et], bias=1.0)
        ACT(c, ev[:, :, P], ev[:, :, P], AF.Exp, [et], [et], scale=-1.0)
    yield
    e0 = W[0][:].rearrange("p (c t) -> p c t", c=8)
    e1 = W[1][:].rearrange("p (c t) -> p c t", c=8)
    act_sigmoid_from(c, e0[:, :, P], xbc[:, 0:8, P], [xbc], [W[0]])
    act_sigmoid_from(c, e1[:, 0:4, P], xbc[:, 8:12, P], [xbc], [W[1]])
    yield
    xsT = H[3][:].rearrange("p (c t) -> p c t", c=8)
    bcT = H[5][:].rearrange("p (c t) -> p c t", c=4)
    TT(c, "dve", xsT[:, :, P], e0[:, :, P], xbc[:, 0:8, P], ALU.mult, [W[0], xbc], [H[3]])
    TT(c, "dve", bcT[:, 0:4, P], e1[:, 0:4, P], xbc[:, 8:12, P], ALU.mult, [W[1], xbc], [H[5]])
    yield
    a_ = W[2]
    av = a_[:].rearrange("p (c t) -> p c t", c=8)
    a2_ = W[1]
    a2v = a2_[:].rearrange("p (c t) -> p c t", c=8)
    for ch in range(8):
        ACT(c, av[:, ch, P], eav[:, ch, P], AF.Exp, [ea_, c.pv], [a_], scale=c.pv[:, ch:ch + 1])
        ACT(c, a2v[:, ch, P], eav[:, ch, P], AF.Exp, [ea_, c.pv], [a2_], scale=c.pv[:, 8 + ch:8 + ch + 1])
        if ch % 4 == 3:
            yield
    for ch in range(8):
        TR(c, c.pT[P, ch, :], xsT[:, ch, P], identb[:, :], [H[3], identb], [c.pT], sig=(ch == 7))
    for ch in range(2):
        TR(c, c.pT2[P, ch, :], bcT[:, ch, P], identb[:, :], [H[5], identb], [c.pT2], sig=(ch == 1))
    yield
    TS(c, "dve", a2v[:, :, P], a2v[:, :, P], -1.0, 1.0, ALU.mult, ALU.add, [a2_], [a2_])
    ACT(c, a2v[:, :, P], a2v[:, :, P], AF.Ln, [a2_], [a2_])
    ACT(c, a2v[:, :, P], a2v[:, :, P], AF.Exp, [a2_], [a2_], scale=0.5)
    yield
    Xps = c.pT[:].rearrange("p c t -> p (c t)")
    Xdt = H[0]
    TT(c, "dve", Xdt[P, :].rearrange("p (h d) -> p h d", h=16), Xps[P, :].rearrange("p (h d) -> p h d", h=16),
       dts[P, 0:16].unsqueeze(2).to_broadcast([nt, 16, 64]), ALU.mult, [c.pT, dts], [Xdt])
    skip = W[0]
    TT(c, "dve", skip[P, :].rearrange("p (h d) -> p h d", h=16), Xps[P, :].rearrange("p (h d) -> p h d", h=16),
       c.vecT[P, 32:48].unsqueeze(2).to_broadcast([nt, 16, 64]), ALU.mult, [c.pT, c.vecT], [skip])
    yield
    Btok = H[1]
    CP(c, "act", Btok[P, 0:256], c.pT2[P, :, :].rearrange("p c t -> p (c t)"), [c.pT2], [Btok])
    MM(c, c.pG[P, 16:32], c.L1[P, P], dts[P, 16:32], True, True, [c.L1, dts], [c.pG], sig=False)
    MM(c, c.pG[P, 32:48], c.L2[P, P], dts[P, 16:32], True, True, [c.L2, dts], [c.pG], sig=False)
    for ci in range(nch):
        MM(c, c.pG[:, 48 + 16 * ci:64 + 16 * ci], c.L4[P, ci, :], dts[P, 16:32], True, True, [c.L4, dts], [c.pG], sig=(ci == nch - 1))
    ACT(c, sm[P, 32:48], c.pG[P, 16:32], AF.Exp, [c.pG], [sm])
    ACT(c, sm[P, 48:64], c.pG[P, 32:48], AF.Exp, [c.pG], [sm])
    ACT(c, sm[:, 64:64 + 16 * nch], c.pG[:, 48:48 + 16 * nch], AF.Exp, [c.pG], [sm])
    yield
    TT(c, "dve", exv[:, :, P], exv[:, :, P], a2v[:, :, P], ALU.mult, [ex_, a2_], [ex_])
    TT(c, "dve", exv[:, :, P], exv[:, :, P], lxv[:, :, P], ALU.mult, [ex_, lx], [ex_])
    for ch in range(8):
        OP(c, "dve", lambda e, ch=ch: e.tensor_tensor_scan(out=eav[:, ch, P], data0=av[:, ch, P], data1=exv[:, ch, P], initial=c.hst[:, ch:ch + 1], op0=ALU.mult, op1=ALU.add), [a_, ex_, c.hst], [ea_])
        if ch % 4 == 3:
            yield
    CP(c, "dve", c.hst[:, :], eav[:, :, nt - 1], [ea_], [c.hst])
    mixA = H[4][:].rearrange("p (c t) -> p c t", c=8)
    mixB = H[2][:].rearrange("p (c t) -> p c t", c=8)
    TT(c, "pool", mixA[:, :, P], eav[:, :, P], sgv[:, :, P], ALU.mult, [ea_, sg], [H[4]])
    yield
    R1 = W[1]
    R1v = R1[:, 0:16 * cw].rearrange("p (h l) -> p h l", h=16)
    TT(c, "dve", R1v[P, :, :], dts[P, 16:32].unsqueeze(2).to_broadcast([nt, 16, cw]),
       c.mle[P, 0:cw].unsqueeze(1).to_broadcast([nt, 16, cw]), ALU.mult, [dts, c.mle], [R1])
    for hf in range(2):
        pb = c.pB[2 + hf]
        MM(c, pb[P, 0:8 * cw], c.L2[P, P], R1[P, hf * 8 * cw:(hf + 1) * 8 * cw], True, True, [c.L2, R1], [pb], sig=True)
    dec = W[2]
    decv = dec[:, 0:16 * cw].rearrange("p (h l) -> p h l", h=16)
    for hf in range(2):
        pb = c.pB[2 + hf]
        ACT(c, dec[P, hf * 8 * cw:(hf + 1) * 8 * cw], pb[P, 0:8 * cw], AF.Exp, [pb], [dec])
    yield
    cbps = c.pG[:, 128:256].rearrange("p (g l) -> p g l", g=2)
    for ci, (p0, p1) in enumerate(chunks):
        for g in range(2):
            MM(c, cbps[p0:p1, g, 0:cw], bcT[:, g, p0:p1], bcT[:, 2 + g, p0:p1], True, True, [H[5]], [c.pG], sig=(ci == nch - 1 and g == 1))
    TT(c, "dve", c.cbm[P, :, 0:cw], cbps[P, :, 0:cw], c.mle[P, 0:cw].unsqueeze(1).to_broadcast([nt, 2, cw]), ALU.mult, [c.pG, c.mle], [c.cbm])
    yield
    MT = H[3]
    MTv = MT[:, 0:16 * cw].rearrange("p (h l) -> p h l", h=16)
    for g in range(2):
        TT(c, "dve", MTv[P, g * 8:(g + 1) * 8, :], decv[P, g * 8:(g + 1) * 8, :],
           c.cbm[P, g:g + 1, 0:cw].to_broadcast([nt, 8, cw]), ALU.mult, [dec, c.cbm], [MT])
    yield
    Xd = H[2]
    TT(c, "pool", Xd[P, :].rearrange("p (h d) -> p h d", h=16), Xdt[P, :].rearrange("p (h d) -> p h d", h=16),
       sm[P, 48:64].unsqueeze(2).to_broadcast([nt, 16, 64]), ALU.mult, [Xdt, sm], [H[2]])
    for ci, (p0, p1) in enumerate(chunks):
        for h in range(16):
            pb = c.pB[4 + h // 8]
            MM(c, pb[p0:p1, (h % 8) * 64:(h % 8 + 1) * 64], MTv[p0:p1, h, :], Xdt[p0:p1, h * 64:(h + 1) * 64], True, True, [MT, Xdt], [pb],
               sig=(h % 8 == 7))
        yield
    y = W[1]
    for ci, (p0, p1) in enumerate(chunks):
        PC = slice(p0, p1)
        ncw = p1 - p0
        for g in range(2):
            pb = c.pB[2 + g]
            MM(c, pb[p0:p1, :], bcT[:, 2 + g, p0:p1], c.Sbf[:, g * 512:(g + 1) * 512], True, True, [H[5], c.Sbf], [pb], sig=True)
        yield
        for g in range(2):
            gs = slice(g * 512, (g + 1) * 512)
            TT(c, "dve", y[PC, gs].rearrange("p (h d) -> p h d", h=8), c.pB[2 + g][PC, :].rearrange("p (h d) -> p h d", h=8),
               sm[PC, 32 + 8 * g:40 + 8 * g].unsqueeze(2).to_broadcast([ncw, 8, 64]), ALU.mult, [c.pB[2 + g], sm], [y])
        yield
        for g in range(2):
            pb = c.pB[2 + g]
            MM(c, pb[:, :], Btok[p0:p1, g * 128:(g + 1) * 128], Xd[p0:p1, g * 512:(g + 1) * 512], True, True, [Btok, H[2]], [pb], sig=True)
        TT(c, "pool", c.S[:].rearrange("p (h d) -> p h d", h=16), c.S[:].rearrange("p (h d) -> p h d", h=16),
           sm[:, 64 + 16 * ci:80 + 16 * ci].unsqueeze(2).to_broadcast([128, 16, 64]), ALU.mult, [c.S, sm], [c.S])
        yield
        for g in range(2):
            gs = slice(g * 512, (g + 1) * 512)
            TT(c, "dve", c.S[:, gs], c.S[:, gs], c.pB[2 + g][:, :], ALU.add, [c.S, c.pB[2 + g]], [c.S])
        CP(c, "act", c.Sbf[:], c.S[:], [c.S], [c.Sbf])
        yield
    for g in range(2):
        gs = slice(g * 512, (g + 1) * 512)
        TT(c, "dve", y[P, gs], y[P, gs], c.pB[4 + g][P, :], ALU.add, [y, c.pB[4 + g]], [y])
    yield
    yield
    TT(c, "dve", y[P, :], y[P, :], skip[P, :], ALU.add, [y, skip], [y])
    TT(c, "dve", y[P, :], y[P, :], gz[P, :], ALU.mult, [y, gz], [y])
    for g in range(2):
        gs = slice(g * 512, (g + 1) * 512)
        ACT(c, W[4][P, gs], y[P, gs], AF.Square, [y], [W[4], st1], accum_out=st1[P, 2 + g:3 + g])
    ACT(c, st1[P, 4:6], st1[P, 2:4], AF.Ln, [st1], [st1], scale=1.0 / 512, bias=1e-6)
    ACT(c, st1[P, 4:6], st1[P, 4:6], AF.Exp, [st1], [st1], scale=-0.5)
    yield
    yb = H[0]
    for g in range(2):
        gs = slice(g * 512, (g + 1) * 512)
        TS(c, "dve", yb[P, gs], y[P, gs], st1[P, 4 + g:5 + g], None, ALU.mult, None, [y, st1], [yb])
    for ch in range(8):
        TR(c, c.pT[:, ch, P], yb[P, ch * 128:(ch + 1) * 128], identb[P, P], [yb, identb], [c.pT], sig=(ch == 7))
    CP(c, "act", mixB[:, :, P], c.pT[:, :, P], [c.pT], [H[2]])
    yield
    for hf in range(2):
        pb = c.pB[2 + hf]
        for kc in range(16):
            lhs = mixA[:, kc, P] if kc < 8 else mixB[:, kc - 8, P]
            MM(c, pb[P, :], lhs, c.w_out[:, kc, hf * 512:(hf + 1) * 512], kc == 0, kc == 15, [H[4], H[2], c.w_out], [pb], sig=(kc == 15))
    yield
    for hf in range(2):
        hs = slice(hf * 512, (hf + 1) * 512)
        TT(c, "dve", xt[P, hs], xt[P, hs], c.pB[2 + hf][P, :], ALU.add, [xt, c.pB[2 + hf]], [xt])
    k.dma("sp", dst_ap, xt[P, :], r=[xt.b], w=[dst_buf], sbuf=xt.b)


def interleave(gm, gp, ratio=2):
    am, ap = gm is not None, gp is not None
    while am or ap:
        if am:
            for _ in range(ratio):
                try:
                    next(gm)
                except StopIteration:
                    am = False
                    break
        if ap:
            try:
                next(gp)
            except StopIteration:
                ap = False


def layer0_seq(c, tiles, par0, dst_buf):
    par = par0
    seq_reset0(c, par)
    n = len(tiles)
    interleave(None, layer0_P(c, tiles[0][0], tiles[0][2], par, None))
    for j in range(n):
        gp = layer0_P(c, tiles[j + 1][0], tiles[j + 1][2], par ^ 1, (par, tiles[j][2])) if j + 1 < n else None
        gm = layer0_M(c, tiles[j][1], tiles[j][2], par, dst_buf)
        interleave(gm, gp)
        par ^= 1
    return par


LTOT = 2064
BIG = 30000.0


def setup_layer1(c, D, L):
    k = c.k
    c.w_in1 = T(k, "w_in1", [128, 8, 4096], BF16)
    c.w_out1 = T(k, "w_out1", [128, 8, 1024], BF16)
    c.vecF = T(k, "vec1F", [128, 8], F32)
    c.fn = T(k, "fn", [128, 1024], F32)
    k.dma("sp", c.vecF[:], D["vec1F"][:, :], w=[c.vecF.b], sbuf=c.vecF.b)
    k.dma("sp", c.fn[:], D["fnorm"][0:1, :].partition_broadcast(128), w=[c.fn.b], sbuf=c.fn.b)
    st = c.stage
    dsts, rows, scs = [], [], []
    for kc in range(8):
        for hf in range(8):
            dsts.append(c.w_in1[:, kc, hf * 512:(hf + 1) * 512])
            rows.append(D["w_in1"][kc * 128:(kc + 1) * 128, hf * 512:(hf + 1) * 512])
            scs.append(c.vecF[:, kc:kc + 1])
    load_cast_weight(c, c.w_in1, dsts, rows, 512, scs, st)
    dsts, rows, scs = [], [], []
    for kc in range(8):
        for hf in range(2):
            dsts.append(c.w_out1[:, kc, hf * 512:(hf + 1) * 512])
            rows.append(D["w_out1"][kc * 128:(kc + 1) * 128, hf * 512:(hf + 1) * 512])
            scs.append(None)
    load_cast_weight(c, c.w_out1, dsts, rows, 512, scs, st)
    c.negm = T(k, "negm", [128, 4, 128], BF16)
    c.tri2 = T(k, "tri2", [128, 128], BF16)
    c.zrow = T(k, "zrow", [1, 256], BF16)
    c.tri = T(k, "tri", [128, 128], BF16)
    c.ones = T(k, "ones", [128, 2], BF16)
    OP(c, "pool", lambda e: e.memset(c.negm[:], 0.0), [], [c.negm])
    OP(c, "pool", lambda e: e.memset(c.tri2[:], 1.0), [], [c.tri2])
    OP(c, "pool", lambda e: e.memset(c.zrow[:], 0.0), [], [c.zrow])
    OP(c, "pool", lambda e: e.memset(c.tri[:], 1.0), [], [c.tri])
    OP(c, "pool", lambda e: e.memset(c.ones[:], 1.0), [], [c.ones])
    for i in range(4):
        OP(c, "pool", lambda e, i=i: e.affine_select(out=c.negm[:, i, :], in_=c.negm[:, i, :], pattern=[[1, 128]], compare_op=ALU.is_gt, fill=-BIG, base=0, channel_multiplier=-1), [c.negm], [c.negm])
    OP(c, "pool", lambda e: e.affine_select(out=c.tri[:], in_=c.tri[:], pattern=[[-1, 128]], compare_op=ALU.is_ge, fill=0.0, base=0, channel_multiplier=1), [c.tri], [c.tri])
    OP(c, "pool", lambda e: e.affine_select(out=c.tri2[:], in_=c.tri2[:], pattern=[[1, 128]], compare_op=ALU.is_gt, fill=0.0, base=0, channel_multiplier=-1), [c.tri2], [c.tri2])
    nkb = 1 + (L - 16) // 128
    c.KT = T(k, "KT", [128, 8, L], BF16)
    c.V = T(k, "V", [128, nkb, 1024], BF16)
    c.KTb = [Buf("KTb%d" % i) for i in range(nkb)]
    c.Vb = [Buf("Vb%d" % i) for i in range(nkb)]


def alloc_layer1_work(c):
    k = c.k
    c.ht = [T(k, "ht%d" % i, [128, 1024], F32) for i in range(2)]
    c.st2 = [T(k, "st2_%d" % i, [128, 8], F32) for i in range(2)]
    c.ub1 = T(k, "ub1", [128, 1024], BF16)
    c.uT1 = T(k, "uT1", [128, 1024], BF16)
    c.QTs = [T(k, "QTs%d" % i, [128, 8, 2, 128], BF16) for i in range(2)]
    for i in range(2):
        OP(c, "pool", lambda e, i=i: e.memset(c.QTs[i][:], 0.0), [], [c.QTs[i]])
    c.sgt1 = T(k, "sgt1", [128, 512], F32)
    c.sgz = [T(k, "sgz%d" % i, [128, 1024], BF16) for i in range(2)]
    c.E = [T(k, "E%d" % i, [128, 512], F32) for i in range(4)]
    c.SP = [T(k, "SP%d" % i, [128, 512], BF16) for i in range(4)]
    c.X = [T(k, "X%d" % i, [128, 512], F32) for i in range(2)]
    c.Wt = [T(k, "Wt%d" % i, [128, 512], BF16) for i in range(2)]
    c.ob = [T(k, "ob%d" % i, [128, 1024], BF16) for i in range(2)]
    c.oT = T(k, "oT", [128, 1024], BF16)
    c.B = [T(k, "B%d" % i, [128, 512], F32, "ps") for i in range(8)]
    c.pT1 = View(c.B[6][:, :].bitcast(BF16).rearrange("p (c t) -> p c t", c=8), c.B[6].b)
    c.pTo = View(c.B[0][:, :].bitcast(BF16).rearrange("p (c t) -> p c t", c=8), c.B[0].b)


def l1_proj_gen(c, src_ap, nt, j, par, src_buf):
    k = c.k
    ht = c.ht[par]
    st2 = c.st2[par]
    P = slice(0, nt)
    identb = c.identb
    pos0 = 0 if j == 0 else 16 + (j - 1) * 128
    B = c.B
    k.dma("sp", ht[P, :], src_ap, r=[src_buf], w=[ht.b], sbuf=ht.b)
    ub = c.ub1
    ACT(c, ub[P, :], ht[P, :], AF.Square, [ht], [ub, st2], accum_out=st2[P, 0:1])
    ACT(c, st2[P, 1:2], st2[P, 0:1], AF.Ln, [st2], [st2], scale=1.0 / 1024, bias=1e-6)
    ACT(c, st2[P, 1:2], st2[P, 1:2], AF.Exp, [st2], [st2], scale=-0.5)
    TS(c, "dve", ub[P, :], ht[P, :], st2[P, 1:2], None, ALU.mult, None, [ht, st2], [ub])
    yield
    for kc in range(8):
        TR(c, c.pT1[:, kc, P], ub[P, kc * 128:(kc + 1) * 128], identb[P, P], [ub, identb], [c.pT1], sig=(kc == 7))
    uTv = c.uT1[:].rearrange("p (c t) -> p c t", c=8)
    CP(c, "dve", uTv[:, :, P], c.pT1[:, :, P], [c.pT1], [c.uT1])
    yield
    w = c.w_in1
    for g4 in range(2):
        pb = B[7 - g4]
        pbv = pb[:].rearrange("p (c t) -> p c t", c=4)
        for i in range(4):
            oc = g4 * 4 + i
            for kc in range(8):
                MM(c, pbv[:, i, P], w[:, kc, 1024 + oc * 128:1024 + (oc + 1) * 128], uTv[:, kc, P], kc == 0, kc == 7, [c.uT1, w], [pb], sig=(kc == 7))
            yield
        CP(c, "dve", c.KT[:, g4 * 4:(g4 + 1) * 4, pos0:pos0 + nt], pbv[:, :, P], [pb], [c.KTb[j]])
        yield
    for hf in range(2):
        pb = B[7 - hf]
        for kc in range(8):
            MM(c, pb[P, :], uTv[:, kc, P], w[:, kc, 2048 + hf * 512:2048 + (hf + 1) * 512], kc == 0, kc == 7, [c.uT1, w], [pb], sig=(kc == 7))
        yield
        CP(c, "dve", c.V[P, j, hf * 512:(hf + 1) * 512], pb[P, :], [pb], [c.Vb[j]])
        yield
    if j == 0:
        return
    QTs = c.QTs[par]
    for g4 in range(2):
        pb = B[7 - g4]
        pbv = pb[:].rearrange("p (c t) -> p c t", c=4)
        for i in range(4):
            oc = g4 * 4 + i
            for kc in range(8):
                MM(c, pbv[:, i, P], w[:, kc, oc * 128:(oc + 1) * 128], uTv[:, kc, P], kc == 0, kc == 7, [c.uT1, w], [pb], sig=(kc == 7))
            yield
        for hf_ in range(2):
            hp = slice(hf_ * 64, (hf_ + 1) * 64)
            TS(c, "dve", QTs[hp, g4 * 4:(g4 + 1) * 4, hf_, P], pbv[hp, :, P], 0.125, None, ALU.mult, None, [pb], [QTs])
        yield
    sgz = c.sgz[par]
    for hf in range(2):
        pb = B[7 - hf]
        hs = slice(hf * 512, (hf + 1) * 512)
        for kc in range(8):
            MM(c, pb[P, :], uTv[:, kc, P], w[:, kc, 3072 + hf * 512:3072 + (hf + 1) * 512], kc == 0, kc == 7, [c.uT1, w], [pb], sig=(kc == 7))
        yield
        ACT(c, c.sgt1[P, :], pb[P, :], AF.Exp, [pb], [c.sgt1], scale=-1.0)
        TS(c, "dve", c.sgt1[P, :], c.sgt1[P, :], 1.0, None, ALU.add, None, [c.sgt1], [c.sgt1])
        OP(c, "dve", lambda e: e.reciprocal(out=c.sgt1[P, :], in_=c.sgt1[P, :]), [c.sgt1], [c.sgt1])
        TT(c, "dve", sgz[P, hs], c.sgt1[P, :], pb[P, :], ALU.mult, [c.sgt1, pb], [sgz])
        yield


def run_gen(g, n=None):
    if g is None:
        return False
    try:
        if n is None:
            while True:
                next(g)
        for _ in range(n):
            next(g)
    except StopIteration:
        return False
    return True


def layer1_attn(c, nt, j, par, gen_next):
    k = c.k
    ht = c.ht[par]
    st2 = c.st2[par]
    P = slice(0, nt)
    identb = c.identb
    B = c.B
    QTs_ = c.QTs[par]
    sgz_ = c.sgz[par]

    units = []
    for pr in range(2):
        for kb in range(j, -1, -1):
            for q in range(2):
                units.append((kb, 2 * pr + q, q))
    nu = len(units)

    def kinfo(kb):
        if kb == 0:
            return 16, 0
        return 128, 16 + (kb - 1) * 128

    def views(u):
        kb, hg, q = units[u]
        ks, kp = kinfo(kb)
        return kb, hg, q, ks, kp

    def stageA(u):
        kb, hg, q, ks, kp = views(u)
        diag = (kb == j)
        z = B[u % 2]
        zv = z[:].rearrange("p (i t) -> p i t", i=4)
        first = True
        if diag:
            MM(c, z[0:ks, :], identb[0:ks, 0:ks], c.negm[0:ks, :, :].rearrange("p i t -> p (i t)"), True, False, [identb, c.negm], [z], sig=False)
            first = False
        for i2 in range(2):
            ch = 2 * hg + i2
            MM(c, z[0:ks, 2 * i2 * 128:(2 * i2 + 2) * 128], c.KT[:, ch, kp:kp + ks], QTs_[:, ch, :, :].rearrange("p a t -> p (a t)"), first, True, [c.KTb[kb], QTs_], [z], sig=(i2 == 1))
        E = c.E[u % 4]
        SP = c.SP[u % 4]
        Ev = E[:].rearrange("p (i t) -> p i t", i=4)
        SPv = SP[:].rearrange("p (i t) -> p i t", i=4)
        ACT(c, Ev[0:ks, :, P], zv[0:ks, :, P], AF.Exp, [z], [E])
        ACT(c, SPv[0:ks, :, P], Ev[0:ks, :, P], AF.Ln, [E], [SP], bias=1.0)

    def stageB(u):
        kb, hg, q, ks, kp = views(u)
        tb = B[2 + q]
        tv = tb[:].rearrange("p (i t) -> p i t", i=4)
        SP = c.SP[u % 4]
        SPv = SP[:].rearrange("p (i t) -> p i t", i=4)
        MM(c, tb[0:ks, :], c.tri[0:ks, 0:ks], SP[0:ks, :], kb == j, False, [c.tri, SP], [tb], sig=True)
        X = c.X[u % 2]
        Xv = X[:].rearrange("p (i t) -> p i t", i=4)
        ACT(c, Xv[0:ks, :, P], tv[0:ks, :, P], AF.Exp, [tb], [X], scale=-1.0)

    def stageC(u):
        kb, hg, q, ks, kp = views(u)
        tb = B[2 + q]
        tv = tb[:].rearrange("p (i t) -> p i t", i=4)
        SP = c.SP[u % 4]
        SPv = SP[:].rearrange("p (i t) -> p i t", i=4)
        if kb > 0:
            MM(c, tb[0:ks, :], c.tri2[0:ks, 0:ks], SP[0:ks, :], False, kb == 1, [c.tri2, SP], [tb], sig=True)
        E = c.E[u % 4]
        X = c.X[u % 2]
        Wt = c.Wt[u % 2]
        Ev = E[:].rearrange("p (i t) -> p i t", i=4)
        Xv = X[:].rearrange("p (i t) -> p i t", i=4)
        Wv = Wt[:].rearrange("p (i t) -> p i t", i=4)
        TT(c, "dve", Wv[0:ks, :, P], Ev[0:ks, :, P], Xv[0:ks, :, P], ALU.mult, [E, X], [Wt])

    def stageD(u):
        kb, hg, q, ks, kp = views(u)
        ob_ = B[4 + q]
        Wt = c.Wt[u % 2]
        Wv = Wt[:].rearrange("p (i t) -> p i t", i=4)
        if kb == j:
            MM(c, ob_[P, 0:256], c.zrow[0:1, P], c.zrow[0:1, 0:256], True, False, [c.zrow], [ob_], sig=False)
        for i in range(4):
            hd = 4 * hg + i
            MM(c, ob_[P, i * 64:(i + 1) * 64], Wv[0:ks, i, P], c.V[0:ks, kb, hd * 64:(hd + 1) * 64], False, kb == 0, [Wt, c.Vb[kb]], [ob_], sig=(i == 3))
        if kb == 0:
            hsl = slice(hg * 256, (hg + 1) * 256)
            TT(c, "dve", c.ob[par][P, hsl], ob_[P, 0:256], sgz_[P, hsl], ALU.mult, [ob_, sgz_], [c.ob[par]])

    per = max(1, -(-52 // max(1, nu - 2)))
    for step in range(nu + 3):
        if step < nu:
            stageA(step)
        if 0 <= step - 1 < nu:
            stageB(step - 1)
        if 0 <= step - 2 < nu:
            stageC(step - 2)
        if 0 <= step - 3 < nu:
            stageD(step - 3)
        run_gen(gen_next, per)
    run_gen(gen_next, None)


def l1_tail_gen(c, dst_ap, nt, par):
    k = c.k
    ht = c.ht[par]
    st2 = c.st2[par]
    ob = c.ob[par]
    P = slice(0, nt)
    identb = c.identb
    B = c.B
    for kc in range(8):
        TR(c, c.pT1[:, kc, P], ob[P, kc * 128:(kc + 1) * 128], identb[P, P], [ob, identb], [c.pT1], sig=(kc == 7))
    oTv = c.oT[:].rearrange("p (c t) -> p c t", c=8)
    CP(c, "dve", oTv[:, :, P], c.pT1[:, :, P], [c.pT1], [c.oT])
    yield
    for hf in range(2):
        pb = B[6 + hf]
        for kc in range(8):
            MM(c, pb[P, :], oTv[:, kc, P], c.w_out1[:, kc, hf * 512:(hf + 1) * 512], kc == 0, kc == 7, [c.oT, c.w_out1], [pb], sig=(kc == 7))
        yield
    for hf in range(2):
        hs = slice(hf * 512, (hf + 1) * 512)
        TT(c, "dve", ht[P, hs], ht[P, hs], B[6 + hf][P, :], ALU.add, [ht, B[6 + hf]], [ht])
    yield
    ACT(c, c.oT[P, :], ht[P, :], AF.Square, [ht], [c.oT, st2], accum_out=st2[P, 2:3])
    ACT(c, st2[P, 3:4], st2[P, 2:3], AF.Ln, [st2], [st2], scale=1.0 / 1024, bias=1e-6)
    ACT(c, st2[P, 3:4], st2[P, 3:4], AF.Exp, [st2], [st2], scale=-0.5)
    yield
    STT(c, "dve", ht[P, :], ht[P, :], st2[P, 3:4], c.fn[P, :], ALU.mult, ALU.mult, [ht, st2, c.fn], [ht])
    k.dma("sp", dst_ap, ht[P, :], r=[ht.b], sbuf=ht.b)
    yield


def chain_gens(*gens):
    for g in gens:
        if g is not None:
            yield from g


def layer1_seq(c, h1s, outs, nfull, par0, src_buf):
    par = par0
    run_gen(l1_proj_gen(c, h1s[0], 16, 0, par, src_buf), None)
    par ^= 1
    run_gen(l1_proj_gen(c, h1s[1], 128, 1, par, src_buf), None)
    for j in range(1, nfull + 1):
        gt = l1_tail_gen(c, outs[j - 1], 128, par ^ 1) if j >= 2 else None
        gp = l1_proj_gen(c, h1s[j + 1], 128, j + 1, par ^ 1, src_buf) if j + 1 <= nfull else None
        layer1_attn(c, 128, j, par, chain_gens(gt, gp))
        par ^= 1
    run_gen(l1_tail_gen(c, outs[nfull], 128, par ^ 1), None)
    return par

NSEQ = 4
NFULL = 16
LTOT_ = 16 + NFULL * 128


def common_setup(c, sw=2312):
    k = c.k
    if sw:
        c.stage = [T(k, "stage%d" % i, [128, sw], F32) for i in range(2)]
    ident = T(k, "ident", [128, 128], F32)
    c.identb = T(k, "identb", [128, 128], BF16)
    OP(c, "pool", lambda e: e.memset(ident[:], 1.0), [], [ident])
    OP(c, "pool", lambda e: e.affine_select(out=ident[:], in_=ident[:], pattern=[[-1, 128]], compare_op=ALU.is_equal, fill=0.0, base=0, channel_multiplier=1), [ident], [ident])
    CP(c, "dve", c.identb[:], ident[:], [ident], [c.identb])


def build_l0(nseq, nfull):
    nc = bass.Bass("TRN2", target_bir_lowering=False)
    L = 16 + nfull * 128
    D = {}
    x = nc.dram_tensor("x", [nseq, nfull * 128, 1024], F32, kind="ExternalInput").ap()
    meta = nc.dram_tensor("meta", [16, 1024], F32, kind="ExternalInput").ap()
    D["w_in"] = nc.dram_tensor("w_in", [1024, 4624], F32, kind="ExternalInput").ap()
    D["w_out"] = nc.dram_tensor("w_out", [2048, 1024], F32, kind="ExternalInput").ap()
    D["wa"] = nc.dram_tensor("wa", [4, 256, 256], F32, kind="ExternalInput").ap()
    D["wx"] = nc.dram_tensor("wx", [4, 256, 256], F32, kind="ExternalInput").ap()
    D["vecF"] = nc.dram_tensor("vecF", [128, 140], F32, kind="ExternalInput").ap()
    D["vecT"] = nc.dram_tensor("vecT", [1, 48], F32, kind="ExternalInput").ap()
    h1 = nc.dram_tensor("h1", [nseq, L, 1024], F32, kind="ExternalOutput").ap()
    with ExitStack() as es:
        k = K(nc, es)
        c = setup_common(k, nc)
        common_setup(c, 0)
        alloc_layer0_work(c)
        setup_layer0(c, D)
        hb = Buf("h1dram")
        par = 0
        for s in range(nseq):
            tiles = [(meta[:, :], h1[s, 0:16, :], 16)]
            for j in range(nfull):
                tiles.append((x[s, j * 128:(j + 1) * 128, :], h1[s, 16 + j * 128:16 + (j + 1) * 128, :], 128))
            par = layer0_seq(c, tiles, par, hb)
        k.final_wait("sp", [c.xt[0].b, c.xt[1].b])
        k.emit()
    return nc


def build_l1(nseq, nfull):
    nc = bass.Bass("TRN2", target_bir_lowering=False)
    L = 16 + nfull * 128
    D = {}
    h1 = nc.dram_tensor("h1", [nseq, L, 1024], F32, kind="ExternalInput").ap()
    D["w_in1"] = nc.dram_tensor("w_in1", [1024, 4096], F32, kind="ExternalInput").ap()
    D["w_out1"] = nc.dram_tensor("w_out1", [1024, 1024], F32, kind="ExternalInput").ap()
    D["vec1F"] = nc.dram_tensor("vec1F", [128, 8], F32, kind="ExternalInput").ap()
    D["fnorm"] = nc.dram_tensor("fnorm", [1, 1024], F32, kind="ExternalInput").ap()
    out = nc.dram_tensor("out", [nseq, nfull * 128, 1024], F32, kind="ExternalOutput").ap()
    with ExitStack() as es:
        k = K(nc, es)
        c = setup_common(k, nc)
        common_setup(c, 0)
        alloc_layer1_work(c)
        c.stage = list(c.E)
        setup_layer1(c, D, L)
        hb = Buf("h1dram")
        par = 0
        for s in range(nseq):
            h1s = [h1[s, 0:16, :]] + [h1[s, 16 + (j - 1) * 128:16 + j * 128, :] for j in range(1, nfull + 1)]
            outs = [None] + [out[s, (j - 1) * 128:j * 128, :] for j in range(1, nfull + 1)]
            par = layer1_seq(c, h1s, outs, nfull, par, hb)
        k.final_wait("sp", [c.ht[0].b, c.ht[1].b])
        k.emit()
    return nc


def build_fused(nseq, nfull):
    nc = bass.Bass("TRN2", target_bir_lowering=False)
    L = 16 + nfull * 128
    D = {}
    x = nc.dram_tensor("x", [nseq, nfull * 128, 1024], F32, kind="ExternalInput").ap()
    meta = nc.dram_tensor("meta", [16, 1024], F32, kind="ExternalInput").ap()
    D["w_in"] = nc.dram_tensor("w_in", [1024, 4624], F32, kind="ExternalInput").ap()
    D["w_out"] = nc.dram_tensor("w_out", [2048, 1024], F32, kind="ExternalInput").ap()
    D["wa"] = nc.dram_tensor("wa", [4, 256, 256], F32, kind="ExternalInput").ap()
    D["wx"] = nc.dram_tensor("wx", [4, 256, 256], F32, kind="ExternalInput").ap()
    D["vecF"] = nc.dram_tensor("vecF", [128, 140], F32, kind="ExternalInput").ap()
    D["vecT"] = nc.dram_tensor("vecT", [1, 48], F32, kind="ExternalInput").ap()
    D["w_in1"] = nc.dram_tensor("w_in1", [1024, 4096], F32, kind="ExternalInput").ap()
    D["w_out1"] = nc.dram_tensor("w_out1", [1024, 1024], F32, kind="ExternalInput").ap()
    D["vec1F"] = nc.dram_tensor("vec1F", [128, 8], F32, kind="ExternalInput").ap()
    D["fnorm"] = nc.dram_tensor("fnorm", [1, 1024], F32, kind="ExternalInput").ap()
    out = nc.dram_tensor("out", [nseq, nfull * 128, 1024], F32, kind="ExternalOutput").ap()
    h1 = nc.dram_tensor("h1s", [nseq, L, 1024], F32, kind="Internal").ap()
    with ExitStack() as es:
        k = K(nc, es)
        c = setup_common(k, nc)
        common_setup(c, 0)
        hb = Buf("h1dram")
        k.push()
        alloc_layer0_work(c)
        setup_layer0(c, D)
        par = 0
        for s in range(nseq):
            tiles = [(meta[:, :], h1[s, 0:16, :], 16)]
            for j in range(nfull):
                tiles.append((x[s, j * 128:(j + 1) * 128, :], h1[s, 16 + j * 128:16 + (j + 1) * 128, :], 128))
            par = layer0_seq(c, tiles, par, hb)
        k.barrier([c.xt[0].b, c.xt[1].b, c.vecF.b, c.vecT.b] + [t.b for t in c.stage])
        k.emit()
        k.pop()
        hb = Buf("h1dram2")
        k.push()
        alloc_layer1_work(c)
        c.stage = list(c.E)
        setup_layer1(c, D, L)
        par = 0
        for s in range(nseq):
            h1s = [h1[s, 0:16, :]] + [h1[s, 16 + (j - 1) * 128:16 + j * 128, :] for j in range(1, nfull + 1)]
            outs = [None] + [out[s, (j - 1) * 128:j * 128, :] for j in range(1, nfull + 1)]
            par = layer1_seq(c, h1s, outs, nfull, par, hb)
        k.final_wait("sp", [c.ht[0].b, c.ht[1].b])
        k.emit()
        k.pop()
    return nc


def pack_vecs(inp):
    f = lambda v: np.ascontiguousarray(np.asarray(v).reshape(-1, 128).T)
    vecF = np.zeros((128, 140), np.float32)
    vecF[:, 0:8] = f(inp["even_norm"][0])
    vecF[:, 8:16] = f(inp["ssd_norm"][0])
    vecF[:, 16:48] = np.asarray(inp["lru_conv_w"][0]).reshape(4, 8, 128).transpose(2, 1, 0).reshape(128, 32)
    vecF[:, 48:56] = f(inp["lru_conv_b"][0])
    vecF[:, 56:64] = f(inp["lru_b_a"][0])
    vecF[:, 64:72] = f(inp["lru_b_x"][0])
    vecF[:, 72:80] = f(inp["lru_lambda"][0])
    vecF[:, 80:128] = np.asarray(inp["ssd_conv_w"][0]).reshape(4, 12, 128).transpose(2, 1, 0).reshape(128, 48)
    vecF[:, 128:140] = f(inp["ssd_conv_b"][0])
    vecT = np.concatenate([np.asarray(inp["ssd_dt_bias"][0]), np.asarray(inp["ssd_a_log"][0]), np.asarray(inp["ssd_d"][0])])[None, :].astype(np.float32)
    return vecF, vecT


_NC_CACHE = {}


def kernel(**inp):
    inp = {k_: np.asarray(v, dtype=np.float32) for k_, v in inp.items()}
    x = inp["x"]
    ncores = 8
    vecF, vecT = pack_vecs(inp)
    vec1F = np.ascontiguousarray(inp["odd_norm"][0].reshape(8, 128).T)
    if "f" not in _NC_CACHE:
        _NC_CACHE["f"] = build_fused(NSEQ, NFULL)
    xs = np.split(np.ascontiguousarray(x), ncores, axis=0)
    maps = [{"x": xs[i], "meta": inp["meta"], "w_in": inp["even_w_in"][0], "w_out": inp["even_w_out"][0],
             "wa": inp["lru_w_a"][0], "wx": inp["lru_w_x"][0], "vecF": vecF, "vecT": vecT,
             "w_in1": inp["odd_w_in"][0], "w_out1": inp["odd_w_out"][0], "vec1F": vec1F,
             "fnorm": inp["final_norm"][None, :]} for i in range(ncores)]
    r = run_bass_kernel_spmd(_NC_CACHE["f"], maps, core_ids=list(range(ncores)))
    return np.concatenate([r.results[i]["out"] for i in range(ncores)], axis=0).astype(np.float32)
```

```python
import numpy as np
from contextlib import ExitStack
import concourse.bass as bass
import concourse.mybir as mybir
from concourse.bass_utils import run_bass_kernel_spmd


F32 = mybir.dt.float32
BF16 = mybir.dt.bfloat16
AF = mybir.ActivationFunctionType
ALU = mybir.AluOpType
AX = mybir.AxisListType

ENGS = ("pe", "act", "dve", "pool", "sp")


class Buf:
    __slots__ = ("name", "w", "rs", "dsem", "dcnt")

    def __init__(self, name):
        self.name = name
        self.w = None
        self.rs = []
        self.dsem = None
        self.dcnt = 0


class K:
    def __init__(self, nc, es):
        self.nc = nc
        self.es = es
        self.prog = {e: [] for e in ENGS}
        self.sem = {e: es.enter_context(nc.semaphore("s_" + e)) for e in ENGS}
        self.cnt = {e: 0 for e in ENGS}
        self.waited = {e: {} for e in ENGS}
        self.pending = {e: [] for e in ENGS}
        self.nsem = 5
        self.ninstr = 0
        self.scopes = [es]

    def push(self):
        self.scopes.append(ExitStack())

    def pop(self):
        self.scopes.pop().close()

    def barrier(self, dma_bufs=()):
        for e in ENGS:
            if self.pending[e]:
                raise RuntimeError("barrier with pending unsignalled ops on " + e)
        waits = {}
        for e in ENGS:
            if e != "pool" and self.cnt[e] > 0:
                self._need("pool", (e, self.cnt[e], self.sem[e]), waits)
        for b in dma_bufs:
            self._need("pool", b.w, waits)
            for t in b.rs:
                self._need("pool", t, waits)
        self.cnt["pool"] += 1
        tok = ("pool", self.cnt["pool"], self.sem["pool"])
        self.prog["pool"].append((list(waits.values()), lambda e: e.engine_nop(), (self.sem["pool"], 1)))
        for e in ENGS:
            if e != "pool":
                self.wait_tok(e, tok)

    def sb(self, name, shape, dt):
        return self.scopes[-1].enter_context(self.nc.sbuf_tensor(name, list(shape), dt))

    def ps(self, name, shape, dt=F32):
        return self.scopes[-1].enter_context(self.nc.psum_tensor(name, list(shape), dt))

    def dsem_of(self, buf):
        if buf.dsem is None:
            buf.dsem = self.es.enter_context(self.nc.semaphore("d_" + buf.name))
            self.nsem += 1
        return buf.dsem

    def _need(self, eng, tok, waits):
        if tok is None:
            return
        key, val, semh = tok
        if key == eng and False:
            return
        cur = self.waited[eng].get(key, 0)
        if val > cur:
            self.waited[eng][key] = val
            waits[key] = (semh, val)

    def _deps(self, eng, r, w):
        waits = {}
        for b in r:
            if b.w is not None and b.w[0] == "PENDING":
                raise RuntimeError("read of buffer %s with unsignalled writer" % b.name)
            self._need(eng, b.w, waits)
        for b in w:
            if b.w is not None and b.w[0] == "PENDING":
                if b.w[1] != eng:
                    raise RuntimeError("write of buffer %s with unsignalled writer" % b.name)
            else:
                self._need(eng, b.w, waits)
            for t in b.rs:
                if t[0] == "PENDING":
                    if t[1] != eng:
                        raise RuntimeError("WAR on buffer %s with unsignalled reader" % b.name)
                else:
                    self._need(eng, t, waits)
        return list(waits.values())

    def op(self, eng, fn, r=(), w=(), sig=True):
        waits = self._deps(eng, r, w)
        if sig:
            self.cnt[eng] += 1
            tok = (eng, self.cnt[eng], self.sem[eng])
            semh = self.sem[eng]
            for (b, kind) in self.pending[eng]:
                if kind == "r":
                    b.rs = [t for t in b.rs if not (t[0] == "PENDING" and t[1] == eng)]
                    b.rs.append(tok)
                else:
                    b.w = tok
                    b.rs = [t for t in b.rs if not (t[0] == "PENDING" and t[1] == eng)]
            self.pending[eng] = []
            for b in r:
                b.rs.append(tok)
            for b in w:
                b.w = tok
                b.rs = []
            self.prog[eng].append((waits, fn, (semh, 1)))
        else:
            ptok = ("PENDING", eng)
            for b in r:
                b.rs.append(ptok)
                self.pending[eng].append((b, "r"))
            for b in w:
                b.w = ptok
                b.rs = []
                self.pending[eng].append((b, "w"))
            self.prog[eng].append((waits, fn, None))
        self.ninstr += 1

    def dma(self, eng, out_ap, in_ap, r=(), w=(), sbuf=None, **kw):
        waits = self._deps(eng, r, w)
        semh = self.dsem_of(sbuf)
        sbuf.dcnt += 1
        tok = ("d_" + sbuf.name, 16 * sbuf.dcnt, semh)
        for b in r:
            b.rs.append(tok)
        for b in w:
            b.w = tok
            b.rs = []
        self.prog[eng].append((waits, lambda e: e.dma_start(out=out_ap, in_=in_ap, **kw), (semh, 16)))
        self.ninstr += 1
        return tok

    def wait_tok(self, eng, tok):
        waits = {}
        self._need(eng, tok, waits)
        for (semh, val) in waits.values():
            self.prog[eng].append(([(semh, val)], None, None))

    def final_wait(self, eng, bufs):
        waits = {}
        for b in bufs:
            self._need(eng, b.w, waits)
            for t in b.rs:
                self._need(eng, t, waits)
        if waits:
            self.prog[eng].append((list(waits.values()), None, None))

    def emit(self):
        nc = self.nc
        engmap = {"pe": "tensor", "act": "scalar", "dve": "vector", "pool": "gpsimd", "sp": "sync"}
        with nc.Block() as block:
            for e in ENGS:
                prog = self.prog[e]

                def body(engine, prog=prog):
                    for waits, fn, inc in prog:
                        for (semh, val) in waits:
                            engine.wait_ge(semh, val)
                        if fn is not None:
                            ins = fn(engine)
                            if inc is not None:
                                ins.then_inc(inc[0], inc[1])
                getattr(block, engmap[e])(body)
        self.prog = {e: [] for e in ENGS}


D_MODEL = 1024
EVEN_IN = 4624


class T:
    def __init__(self, k, name, shape, dt, space="sb"):
        self.t = k.sb("t_" + name, shape, dt) if space == "sb" else k.ps("t_" + name, shape, dt)
        self.b = Buf(name)
        self.name = name

    def __getitem__(self, idx):
        return self.t[idx]


class View:
    def __init__(self, ap, b):
        self.ap = ap
        self.b = b

    def __getitem__(self, idx):
        return self.ap[idx]


def _b(xs):
    return [getattr(x, "b", x) for x in xs]


class Ctx:
    pass


def setup_common(k, nc):
    c = Ctx()
    c.k = k
    c.nc = nc
    c.rr = 0
    return c


def OP(c, eng, fn, r=(), w=(), sig=True):
    c.k.op(eng, fn, r=_b(r), w=_b(w), sig=sig)


def ACT(c, out, in_, func, r, w, **kw):
    OP(c, "act", lambda e: e.activation(out=out, in_=in_, func=func, **kw), r, w)


def TT(c, eng, out, in0, in1, op, r, w):
    OP(c, eng, lambda e: e.tensor_tensor(out=out, in0=in0, in1=in1, op=op), r, w)


def TS(c, eng, out, in0, s1, s2, op0, op1, r, w):
    if s2 is None:
        OP(c, eng, lambda e: e.tensor_scalar(out=out, in0=in0, scalar1=s1, scalar2=None, op0=op0), r, w)
    else:
        OP(c, eng, lambda e: e.tensor_scalar(out=out, in0=in0, scalar1=s1, scalar2=s2, op0=op0, op1=op1), r, w)


def STT(c, eng, out, in0, scalar, in1, op0, op1, r, w):
    OP(c, eng, lambda e: e.scalar_tensor_tensor(out=out, in0=in0, scalar=scalar, in1=in1, op0=op0, op1=op1), r, w)


def CP(c, eng, out, in_, r, w):
    if eng == "act":
        ACT(c, out, in_, AF.Copy, r, w)
    else:
        OP(c, eng, lambda e: e.tensor_copy(out=out, in_=in_), r, w)


def MM(c, out, lhsT, rhs, start, stop, r, w, sig):
    OP(c, "pe", lambda e: e.matmul(out, lhsT=lhsT, rhs=rhs, start=start, stop=stop), r, w, sig=sig)


def TR(c, out, in_, ident, r, w, sig):
    OP(c, "pe", lambda e: e.transpose(out=out, in_=in_, identity=ident), r, w, sig=sig)


def load_cast_weight(c, dst, dst_slices, dram_rows, width, scale_aps, st, engs=("act", "dve")):
    k = c.k
    for i, (da, ra) in enumerate(zip(dst_slices, dram_rows)):
        s = st[c.rr % len(st)]
        eng = engs[c.rr % len(engs)]
        c.rr += 1
        k.dma("sp", s.t[:, 0:width], ra, w=[s.b], sbuf=s.b)
        sc = scale_aps[i]
        if sc is None:
            CP(c, eng, da, s.t[:, 0:width], [s], [dst])
        else:
            if eng == "act":
                OP(c, "act", lambda e, da=da, s=s, sc=sc: e.activation(out=da, in_=s.t[:, 0:width], func=AF.Copy, scale=sc), [s, c.vecF], [dst])
            else:
                TS(c, eng, da, s.t[:, 0:width], sc, None, ALU.mult, None, [s, c.vecF], [dst])


def setup_layer0(c, D):
    k = c.k
    c.w_in = T(k, "w_in", [128, 8, EVEN_IN], BF16)
    c.w_out = T(k, "w_out", [128, 16, 1024], BF16)
    c.wa = T(k, "wa", [128, 4, 2, 256], BF16)
    c.wx = T(k, "wx", [128, 4, 2, 256], BF16)
    c.vecF = T(k, "vecF", [128, 140], F32)
    c.vecT = T(k, "vecT", [128, 48], F32)
    k.dma("sp", c.vecF[:], D["vecF"][:, :], w=[c.vecF.b], sbuf=c.vecF.b)
    k.dma("sp", c.vecT[:], D["vecT"][0:1, :].partition_broadcast(128), w=[c.vecT.b], sbuf=c.vecT.b)
    st = c.stage
    for (c0, wd) in ((0, 1024), (1024, 1024), (2048, 1024), (3072, 1024), (4096, 528)):
        dsts, rows, scs = [], [], []
        for kc in range(8):
            dsts.append(c.w_in[:, kc, c0:c0 + wd])
            rows.append(D["w_in"][kc * 128:(kc + 1) * 128, c0:c0 + wd])
            scs.append(c.vecF[:, kc:kc + 1])
        load_cast_weight(c, c.w_in, dsts, rows, wd, scs, st)
    dsts, rows, scs = [], [], []
    for kc in range(16):
        dsts.append(c.w_out[:, kc, :])
        rows.append(D["w_out"][kc * 128:(kc + 1) * 128, :])
        scs.append(None if kc < 8 else c.vecF[:, 8 + kc - 8:8 + kc - 8 + 1])
    load_cast_weight(c, c.w_out, dsts, rows, 1024, scs, st)
    for (wt, nm) in ((c.wa, "wa"), (c.wx, "wx")):
        dsts, rows, scs = [], [], []
        for g in range(4):
            for kc in range(2):
                dsts.append(wt[:, g, kc, :])
                rows.append(D[nm][g, kc * 128:(kc + 1) * 128, :])
                scs.append(None)
        load_cast_weight(c, wt, dsts, rows, 256, scs, st)

    c.L1 = T(k, "L1", [128, 128], F32)
    c.L2 = T(k, "L2", [128, 128], F32)
    c.L4 = T(k, "L4", [128, 2, 128], F32)
    c.mle = T(k, "mle", [128, 64], F32)
    OP(c, "pool", lambda e: e.memset(c.L1[:], 0.0), [], [c.L1])
    OP(c, "pool", lambda e: e.memset(c.L2[:], 0.0), [], [c.L2])
    OP(c, "pool", lambda e: e.memset(c.L4[:], 0.0), [], [c.L4])
    OP(c, "pool", lambda e: e.memset(c.mle[:], 1.0), [], [c.mle])
    for h in range(2):
        ps = slice(h * 64, (h + 1) * 64)
        OP(c, "pool", lambda e, ps=ps: e.memset(c.L1[ps, ps], 1.0), [], [c.L1])
        OP(c, "pool", lambda e, ps=ps: e.memset(c.L2[ps, ps], 1.0), [], [c.L2])
        OP(c, "pool", lambda e, ps=ps, h=h: e.memset(c.L4[ps, h, :], 1.0), [], [c.L4])
        OP(c, "pool", lambda e, ps=ps: e.affine_select(out=c.L1[ps, ps], in_=c.L1[ps, ps], pattern=[[1, 64]], compare_op=ALU.is_ge, fill=0.0, base=0, channel_multiplier=-1), [c.L1], [c.L1])
        OP(c, "pool", lambda e, ps=ps: e.affine_select(out=c.L2[ps, ps], in_=c.L2[ps, ps], pattern=[[-1, 64]], compare_op=ALU.is_gt, fill=0.0, base=0, channel_multiplier=1), [c.L2], [c.L2])
        OP(c, "pool", lambda e, ps=ps: e.affine_select(out=c.mle[ps, :], in_=c.mle[ps, :], pattern=[[1, 64]], compare_op=ALU.is_ge, fill=0.0, base=0, channel_multiplier=-1), [c.mle], [c.mle])
    c.pv = T(k, "pv", [128, 64], F32)
    ACT(c, c.pv[:, 0:8], c.vecF[:, 72:80], AF.Exp, [c.vecF], [c.pv], scale=-1.0)
    ACT(c, c.pv[:, 0:8], c.pv[:, 0:8], AF.Ln, [c.pv], [c.pv], bias=1.0)
    TS(c, "dve", c.pv[:, 8:16], c.pv[:, 0:8], -16.0, None, ALU.mult, None, [c.pv], [c.pv])
    TS(c, "dve", c.pv[:, 0:8], c.pv[:, 0:8], -8.0, None, ALU.mult, None, [c.pv], [c.pv])
    TS(c, "dve", c.pv[:, 16:32], c.vecF[:, 56:72], -1.0, None, ALU.mult, None, [c.vecF], [c.pv])
    ACT(c, c.pv[:, 32:48], c.vecT[:, 16:32], AF.Exp, [c.vecT], [c.pv])
    TS(c, "dve", c.pv[:, 32:48], c.pv[:, 32:48], -1.0, None, ALU.mult, None, [c.pv], [c.pv])


def alloc_layer0_work(c):
    k = c.k
    c.xt = [T(k, "xt%d" % i, [128, 1024], F32) for i in range(2)]
    c.st1 = [T(k, "st1_%d" % i, [128, 8], F32) for i in range(2)]
    c.W = [T(k, "W%d" % i, [128, 1024], F32) for i in range(6)]
    c.H = [T(k, "H%d" % i, [128, (256 if i == 1 else (512 if i == 5 else 1024))], BF16) for i in range(6)]
    c.projF = [T(k, "projF%d" % i, [128, 20, 131], F32) for i in range(2)]
    c.sg = [T(k, "sg%d" % i, [128, 1024], BF16) for i in range(2)]
    c.gz = [T(k, "gz%d" % i, [128, 1024], BF16) for i in range(2)]
    c.dts = [T(k, "dts%d" % i, [128, 32], F32) for i in range(2)]
    c.ubP = T(k, "ubP", [128, 1024], BF16)
    c.sgt = View(c.ubP.t[:, :].bitcast(F32), c.ubP.b)
    c.uTP = T(k, "uTP", [128, 1024], BF16)
    c.xbc = T(k, "xbc", [128, 12, 128], F32)
    c.S = T(k, "S", [128, 1024], F32)
    c.Sbf = T(k, "Sbf", [128, 1024], BF16)
    c.hst = T(k, "hst", [128, 8], F32)
    c.S_meta = T(k, "S_meta", [128, 1024], F32)
    c.hst_meta = T(k, "hst_meta", [128, 8], F32)
    c.hist_meta = T(k, "hist_meta", [128, 20, 3], F32)
    c.sm = T(k, "sm", [128, 96], F32)
    c.cbm = T(k, "cbm", [128, 2, 64], F32)
    c.pT = T(k, "pT", [128, 8, 128], BF16, "ps")
    c.pG = T(k, "pG", [128, 512], F32, "ps")
    c.pT2 = View(c.pG[:, 256:384].bitcast(BF16).rearrange("p (c t) -> p c t", c=2), c.pG.b)
    c.pB = [T(k, "pB%d" % i, [128, 512], F32, "ps") for i in range(6)]
    c.stage = [c.W[0], c.W[1], c.W[2], c.W[3]]
    c.pTP = View(c.pB[0][:, :].bitcast(BF16).rearrange("p (c t) -> p c t", c=8), c.pB[0].b)


def seq_reset0(c, par):
    OP(c, "pool", lambda e: e.memset(c.projF[par][:, :, 0:3], 0.0), [], [c.projF[par]])
    OP(c, "pool", lambda e: e.memset(c.S[:], 0.0), [], [c.S])
    OP(c, "pool", lambda e: e.memset(c.Sbf[:], 0.0), [], [c.Sbf])
    OP(c, "pool", lambda e: e.memset(c.hst[:], 0.0), [], [c.hst])


def act_sigmoid_from(c, out, in_, rin, wout, neg_bias=None):
    ACT(c, out, in_, AF.Exp, rin, wout, scale=-1.0)
    ACT(c, out, out, AF.Ln, wout, wout, bias=1.0)
    ACT(c, out, out, AF.Exp, wout, wout, scale=-1.0)


def layer0_P(c, src_ap, nt, par, prev):
    k = c.k
    xt = c.xt[par]
    st1 = c.st1[par]
    P = slice(0, nt)
    identb = c.identb
    projF = c.projF[par]
    k.dma("sp", xt[P, :], src_ap, w=[xt.b], sbuf=xt.b)
    ACT(c, c.ubP[P, :], xt[P, :], AF.Square, [xt], [c.ubP, st1], accum_out=st1[P, 0:1])
    ACT(c, st1[P, 1:2], st1[P, 0:1], AF.Ln, [st1], [st1], scale=1.0 / D_MODEL, bias=1e-6)
    ACT(c, st1[P, 1:2], st1[P, 1:2], AF.Exp, [st1], [st1], scale=-0.5)
    ub = c.ubP
    TS(c, "dve", ub[P, :], xt[P, :], st1[P, 1:2], None, ALU.mult, None, [xt, st1], [ub])
    yield
    for kc in range(8):
        TR(c, c.pTP[:, kc, P], ub[P, kc * 128:(kc + 1) * 128], identb[P, P], [ub, identb], [c.pTP], sig=(kc == 7))
    uT = c.uTP
    uTv = uT[:].rearrange("p (c t) -> p c t", c=8)
    CP(c, "act", uTv[:, :, P], c.pTP[:, :, P], [c.pTP], [uT])
    yield
    if prev == "meta":
        CP(c, "pool", projF[:, :, 0:3], c.hist_meta[:, :, :], [c.hist_meta], [projF])
    elif prev is not None:
        pp, pnt = prev
        CP(c, "pool", projF[:, :, 0:3], c.projF[pp][:, :, pnt:pnt + 3], [c.projF[pp]], [projF])
    pz = (c.pB[0], c.pB[1])
    gz = c.gz[par]
    dts = c.dts[par]
    for kc in range(8):
        MM(c, c.pB[1][P, 0:16], uTv[:, kc, P], c.w_in[:, kc, 4608:4624], kc == 0, kc == 7, [uT, c.w_in], [c.pB[1]], sig=(kc == 7))
    TT(c, "dve", dts[P, 0:16], c.pB[1][P, 0:16], c.vecT[P, 0:16], ALU.add, [c.pB[1], c.vecT], [dts])
    yield
    ACT(c, dts[P, 0:16], dts[P, 0:16], AF.Exp, [dts], [dts])
    ACT(c, dts[P, 0:16], dts[P, 0:16], AF.Ln, [dts], [dts], bias=1.0)
    TT(c, "dve", dts[P, 16:32], dts[P, 0:16], c.pv[P, 32:48], ALU.mult, [dts, c.pv], [dts])
    yield
    for hf in range(2):
        for kc in range(8):
            MM(c, pz[hf][P, :], uTv[:, kc, P], c.w_in[:, kc, 2048 + hf * 512:2048 + (hf + 1) * 512], kc == 0, kc == 7, [uT, c.w_in], [pz[hf]], sig=(kc == 7))
        yield
    for hf in range(2):
        hs = slice(hf * 512, (hf + 1) * 512)
        act_sigmoid_from(c, c.sgt[P, :], pz[hf][P, :], [pz[hf]], [c.sgt])
        yield
        TT(c, "dve", gz[P, hs], c.sgt[P, :], pz[hf][P, :], ALU.mult, [c.sgt, pz[hf]], [gz])
        yield
    sg = c.sg[par]
    sgv = sg[:].rearrange("p (c t) -> p c t", c=8)
    groups = []
    for g4 in range(2):
        groups.append(("x", [g4 * 4 + i for i in range(4)], 0))
    for g4 in range(2):
        groups.append(("g", [g4 * 4 + i for i in range(4)], 1024))
    for g4 in range(3):
        groups.append(("b", [g4 * 4 + i for i in range(4)], 3072))
    for gi, (kind, ocs, colbase) in enumerate(groups):
        pb = c.pB[gi % 2]
        pbv = pb[:].rearrange("p (c t) -> p c t", c=4)
        for i, oc in enumerate(ocs):
            for kc in range(8):
                MM(c, pbv[:, i, P], c.w_in[:, kc, colbase + oc * 128:colbase + (oc + 1) * 128], uTv[:, kc, P], kc == 0, kc == 7, [uT, c.w_in], [pb], sig=(kc == 7))
            yield
        if kind == "x":
            CP(c, "act", projF[:, ocs[0]:ocs[0] + 4, 3:3 + nt], pbv[:, :, P], [pb], [projF])
        elif kind == "b":
            CP(c, "act", projF[:, 8 + ocs[0]:8 + ocs[0] + 4, 3:3 + nt], pbv[:, :, P], [pb], [projF])
        else:
            o = sgv[:, ocs[0]:ocs[0] + 4, P]
            sc = c.sgt[:].rearrange("p (c t) -> p c t", c=4)[:, :, P]
            act_sigmoid_from(c, sc, pbv[:, :, P], [pb], [c.sgt])
            yield
            TT(c, "dve", o, sc, pbv[:, :, P], ALU.mult, [c.sgt, pb], [sg])
        yield


def layer0_M(c, dst_ap, nt, par, dst_buf):
    k = c.k
    xt = c.xt[par]
    st1 = c.st1[par]
    W = c.W
    H = c.H
    chunks = [(0, nt)] if nt <= 64 else [(0, 64), (64, 128)]
    cw = chunks[0][1]
    nch = len(chunks)
    identb = c.identb
    P = slice(0, nt)
    projF = c.projF[par]
    sg = c.sg[par]
    sgv = sg[:].rearrange("p (c t) -> p c t", c=8)
    gz = c.gz[par]
    dts = c.dts[par]
    sm = c.sm

    lx = W[3]
    lxv = lx[:].rearrange("p (c t) -> p c t", c=8)
    for ch in range(8):
        o = lxv[:, ch, P]
        TS(c, "dve", o, projF[:, ch, 0:nt], c.vecF[:, 16 + ch * 4:16 + ch * 4 + 1], c.vecF[:, 48 + ch:48 + ch + 1], ALU.mult, ALU.add, [projF, c.vecF], [lx])
        for tp in range(1, 4):
            STT(c, "dve", o, projF[:, ch, tp:tp + nt], c.vecF[:, 16 + ch * 4 + tp:16 + ch * 4 + tp + 1], o, ALU.mult, ALU.add, [projF, c.vecF, lx], [lx])
        yield
    yield
    lxb = H[2]
    lxbv = lxb[:].rearrange("p (c t) -> p c t", c=8)
    CP(c, "act", lxbv[:, :, P], lxv[:, :, P], [lx], [lxb])
    ea_ = W[4]
    ex_ = W[5]
    eav = ea_[:].rearrange("p (c t) -> p c t", c=8)
    exv = ex_[:].rearrange("p (c t) -> p c t", c=8)
    for (wt, pbs, ev, boff, et) in ((c.wa, (c.pB[2], c.pB[3]), eav, 16, ea_), (c.wx, (c.pB[4], c.pB[5]), exv, 24, ex_)):
        for oc in range(8):
            g = oc // 2
            pb = pbs[oc // 4]
            pbv = pb[:].rearrange("p (c t) -> p c t", c=4)
            for kc in range(2):
                MM(c, pbv[:, oc % 4, P], wt[:, g, kc, (oc % 2) * 128:(oc % 2 + 1) * 128], lxbv[:, 2 * g + kc, P], kc == 0, kc == 1, [lxb, wt], [pb], sig=(oc % 4 == 3 and kc == 1))
    yield
    xbc = c.xbc
    for ch in range(12):
        o = xbc[:, ch, P]
        TS(c, "dve", o, projF[:, 8 + ch, 0:nt], c.vecF[:, 80 + ch * 4:80 + ch * 4 + 1], c.vecF[:, 128 + ch:128 + ch + 1], ALU.mult, ALU.add, [projF, c.vecF], [xbc])
        for tp in range(1, 4):
            STT(c, "dve", o, projF[:, 8 + ch, tp:tp + nt], c.vecF[:, 80 + ch * 4 + tp:80 + ch * 4 + tp + 1], o, ALU.mult, ALU.add, [projF, c.vecF, xbc], [xbc])
        yield
    for (wt, pbs, ev, boff, et) in ((c.wa, (c.pB[2], c.pB[3]), eav, 16, ea_), (c.wx, (c.pB[4], c.pB[5]), exv, 24, ex_)):
        for oc in range(8):
            pb = pbs[oc // 4]
            pbv = pb[:].rearrange("p (c t) -> p c t", c=4)
            ACT(c, ev[:, oc, P], pbv[:, oc % 4, P], AF.Exp, [pb, c.pv], [et], scale=-1.0, bias=c.pv[:, boff + oc:boff + oc + 1])
            if oc % 4 == 3:
                yield
        ACT(c, ev[:, :, P], ev[:, :, P], AF.Ln, [et], [et], bias=1.0)
        ACT(c, ev[:, :, P], ev[:, :, P], AF.Exp, [et], [et], scale=-1.0)
    yield
    e0 = W[0][:].rearrange("p (c t) -> p c t", c=8)
    e1 = W[1][:].rearrange("p (c t) -> p c t", c=8)
    act_sigmoid_from(c, e0[:, :, P], xbc[:, 0:8, P], [xbc], [W[0]])
    act_sigmoid_from(c, e1[:, 0:4, P], xbc[:, 8:12, P], [xbc], [W[1]])
    yield
    xsT = H[3][:].rearrange("p (c t) -> p c t", c=8)
    bcT = H[5][:].rearrange("p (c t) -> p c t", c=4)
    TT(c, "dve", xsT[:, :, P], e0[:, :, P], xbc[:, 0:8, P], ALU.mult, [W[0], xbc], [H[3]])
    TT(c, "dve", bcT[:, 0:4, P], e1[:, 0:4, P], xbc[:, 8:12, P], ALU.mult, [W[1], xbc], [H[5]])
    yield
    a_ = W[2]
    av = a_[:].rearrange("p (c t) -> p c t", c=8)
    a2_ = W[1]
    a2v = a2_[:].rearrange("p (c t) -> p c t", c=8)
    for ch in range(8):
        ACT(c, av[:, ch, P], eav[:, ch, P], AF.Exp, [ea_, c.pv], [a_], scale=c.pv[:, ch:ch + 1])
        ACT(c, a2v[:, ch, P], eav[:, ch, P], AF.Exp, [ea_, c.pv], [a2_], scale=c.pv[:, 8 + ch:8 + ch + 1])
        if ch % 4 == 3:
            yield
    for ch in range(8):
        TR(c, c.pT[P, ch, :], xsT[:, ch, P], identb[:, :], [H[3], identb], [c.pT], sig=(ch == 7))
    for ch in range(2):
        TR(c, c.pT2[P, ch, :], bcT[:, ch, P], identb[:, :], [H[5], identb], [c.pT2], sig=(ch == 1))
    yield
    TS(c, "dve", a2v[:, :, P], a2v[:, :, P], -1.0, 1.0, ALU.mult, ALU.add, [a2_], [a2_])
    ACT(c, a2v[:, :, P], a2v[:, :, P], AF.Ln, [a2_], [a2_])
    ACT(c, a2v[:, :, P], a2v[:, :, P], AF.Exp, [a2_], [a2_], scale=0.5)
    yield
    Xps = c.pT[:].rearrange("p c t -> p (c t)")
    Xdt = H[0]
    TT(c, "dve", Xdt[P, :].rearrange("p (h d) -> p h d", h=16), Xps[P, :].rearrange("p (h d) -> p h d", h=16),
       dts[P, 0:16].unsqueeze(2).to_broadcast([nt, 16, 64]), ALU.mult, [c.pT, dts], [Xdt])
    skip = W[0]
    TT(c, "dve", skip[P, :].rearrange("p (h d) -> p h d", h=16), Xps[P, :].rearrange("p (h d) -> p h d", h=16),
       c.vecT[P, 32:48].unsqueeze(2).to_broadcast([nt, 16, 64]), ALU.mult, [c.pT, c.vecT], [skip])
    yield
    Btok = H[1]
    CP(c, "act", Btok[P, 0:256], c.pT2[P, :, :].rearrange("p c t -> p (c t)"), [c.pT2], [Btok])
    MM(c, c.pG[P, 16:32], c.L1[P, P], dts[P, 16:32], True, True, [c.L1, dts], [c.pG], sig=False)
    MM(c, c.pG[P, 32:48], c.L2[P, P], dts[P, 16:32], True, True, [c.L2, dts], [c.pG], sig=False)
    for ci in range(nch):
        MM(c, c.pG[:, 48 + 16 * ci:64 + 16 * ci], c.L4[P, ci, :], dts[P, 16:32], True, True, [c.L4, dts], [c.pG], sig=(ci == nch - 1))
    ACT(c, sm[P, 32:48], c.pG[P, 16:32], AF.Exp, [c.pG], [sm])
    ACT(c, sm[P, 48:64], c.pG[P, 32:48], AF.Exp, [c.pG], [sm])
    ACT(c, sm[:, 64:64 + 16 * nch], c.pG[:, 48:48 + 16 * nch], AF.Exp, [c.pG], [sm])
    yield
    TT(c, "dve", exv[:, :, P], exv[:, :, P], a2v[:, :, P], ALU.mult, [ex_, a2_], [ex_])
    TT(c, "dve", exv[:, :, P], exv[:, :, P], lxv[:, :, P], ALU.mult, [ex_, lx], [ex_])
    for ch in range(8):
        OP(c, "dve", lambda e, ch=ch: e.tensor_tensor_scan(out=eav[:, ch, P], data0=av[:, ch, P], data1=exv[:, ch, P], initial=c.hst[:, ch:ch + 1], op0=ALU.mult, op1=ALU.add), [a_, ex_, c.hst], [ea_])
        if ch % 4 == 3:
            yield
    CP(c, "dve", c.hst[:, :], eav[:, :, nt - 1], [ea_], [c.hst])
    mixA = H[4][:].rearrange("p (c t) -> p c t", c=8)
    mixB = H[2][:].rearrange("p (c t) -> p c t", c=8)
    TT(c, "pool", mixA[:, :, P], eav[:, :, P], sgv[:, :, P], ALU.mult, [ea_, sg], [H[4]])
    yield
    R1 = W[1]
    R1v = R1[:, 0:16 * cw].rearrange("p (h l) -> p h l", h=16)
    TT(c, "dve", R1v[P, :, :], dts[P, 16:32].unsqueeze(2).to_broadcast([nt, 16, cw]),
       c.mle[P, 0:cw].unsqueeze(1).to_broadcast([nt, 16, cw]), ALU.mult, [dts, c.mle], [R1])
    for hf in range(2):
        pb = c.pB[2 + hf]
        MM(c, pb[P, 0:8 * cw], c.L2[P, P], R1[P, hf * 8 * cw:(hf + 1) * 8 * cw], True, True, [c.L2, R1], [pb], sig=True)
    dec = W[2]
    decv = dec[:, 0:16 * cw].rearrange("p (h l) -> p h l", h=16)
    for hf in range(2):
        pb = c.pB[2 + hf]
        ACT(c, dec[P, hf * 8 * cw:(hf + 1) * 8 * cw], pb[P, 0:8 * cw], AF.Exp, [pb], [dec])
    yield
    cbps = c.pG[:, 128:256].rearrange("p (g l) -> p g l", g=2)
    for ci, (p0, p1) in enumerate(chunks):
        for g in range(2):
            MM(c, cbps[p0:p1, g, 0:cw], bcT[:, g, p0:p1], bcT[:, 2 + g, p0:p1], True, True, [H[5]], [c.pG], sig=(ci == nch - 1 and g == 1))
    TT(c, "dve", c.cbm[P, :, 0:cw], cbps[P, :, 0:cw], c.mle[P, 0:cw].unsqueeze(1).to_broadcast([nt, 2, cw]), ALU.mult, [c.pG, c.mle], [c.cbm])
    yield
    MT = H[3]
    MTv = MT[:, 0:16 * cw].rearrange("p (h l) -> p h l", h=16)
    for g in range(2):
        TT(c, "dve", MTv[P, g * 8:(g + 1) * 8, :], decv[P, g * 8:(g + 1) * 8, :],
           c.cbm[P, g:g + 1, 0:cw].to_broadcast([nt, 8, cw]), ALU.mult, [dec, c.cbm], [MT])
    yield
    Xd = H[2]
    TT(c, "pool", Xd[P, :].rearrange("p (h d) -> p h d", h=16), Xdt[P, :].rearrange("p (h d) -> p h d", h=16),
       sm[P, 48:64].unsqueeze(2).to_broadcast([nt, 16, 64]), ALU.mult, [Xdt, sm], [H[2]])
    for ci, (p0, p1) in enumerate(chunks):
        for h in range(16):
            pb = c.pB[4 + h // 8]
            MM(c, pb[p0:p1, (h % 8) * 64:(h % 8 + 1) * 64], MTv[p0:p1, h, :], Xdt[p0:p1, h * 64:(h + 1) * 64], True, True, [MT, Xdt], [pb],
               sig=(h % 8 == 7))
        yield
    y = W[1]
    for ci, (p0, p1) in enumerate(chunks):
        PC = slice(p0, p1)
        ncw = p1 - p0
        for g in range(2):
            pb = c.pB[2 + g]
            MM(c, pb[p0:p1, :], bcT[:, 2 + g, p0:p1], c.Sbf[:, g * 512:(g + 1) * 512], True, True, [H[5], c.Sbf], [pb], sig=True)
        yield
        for g in range(2):
            gs = slice(g * 512, (g + 1) * 512)
            TT(c, "dve", y[PC, gs].rearrange("p (h d) -> p h d", h=8), c.pB[2 + g][PC, :].rearrange("p (h d) -> p h d", h=8),
               sm[PC, 32 + 8 * g:40 + 8 * g].unsqueeze(2).to_broadcast([ncw, 8, 64]), ALU.mult, [c.pB[2 + g], sm], [y])
        yield
        for g in range(2):
            pb = c.pB[2 + g]
            MM(c, pb[:, :], Btok[p0:p1, g * 128:(g + 1) * 128], Xd[p0:p1, g * 512:(g + 1) * 512], True, True, [Btok, H[2]], [pb], sig=True)
        TT(c, "pool", c.S[:].rearrange("p (h d) -> p h d", h=16), c.S[:].rearrange("p (h d) -> p h d", h=16),
           sm[:, 64 + 16 * ci:80 + 16 * ci].unsqueeze(2).to_broadcast([128, 16, 64]), ALU.mult, [c.S, sm], [c.S])
        yield
        for g in range(2):
            gs = slice(g * 512, (g + 1) * 512)
            TT(c, "dve", c.S[:, gs], c.S[:, gs], c.pB[2 + g][:, :], ALU.add, [c.S, c.pB[2 + g]], [c.S])
        CP(c, "act", c.Sbf[:], c.S[:], [c.S], [c.Sbf])
        yield
    for g in range(2):
        gs = slice(g * 512, (g + 1) * 512)
        TT(c, "dve", y[P, gs], y[P, gs], c.pB[4 + g][P, :], ALU.add, [y, c.pB[4 + g]], [y])
    yield
    yield
    TT(c, "dve", y[P, :], y[P, :], skip[P, :], ALU.add, [y, skip], [y])
    TT(c, "dve", y[P, :], y[P, :], gz[P, :], ALU.mult, [y, gz], [y])
    for g in range(2):
        gs = slice(g * 512, (g + 1) * 512)
        ACT(c, W[4][P, gs], y[P, gs], AF.Square, [y], [W[4], st1], accum_out=st1[P, 2 + g:3 + g])
    ACT(c, st1[P, 4:6], st1[P, 2:4], AF.Ln, [st1], [st1], scale=1.0 / 512, bias=1e-6)
    ACT(c, st1[P, 4:6], st1[P, 4:6], AF.Exp, [st1], [st1], scale=-0.5)
    yield
    yb = H[0]
    for g in range(2):
        gs = slice(g * 512, (g + 1) * 512)
        TS(c, "dve", yb[P, gs], y[P, gs], st1[P, 4 + g:5 + g], None, ALU.mult, None, [y, st1], [yb])
    for ch in range(8):
        TR(c, c.pT[:, ch, P], yb[P, ch * 128:(ch + 1) * 128], identb[P, P], [yb, identb], [c.pT], sig=(ch == 7))
    CP(c, "act", mixB[:, :, P], c.pT[:, :, P], [c.pT], [H[2]])
    yield
    for hf in range(2):
        pb = c.pB[2 + hf]
        for kc in range(16):
            lhs = mixA[:, kc, P] if kc < 8 else mixB[:, kc - 8, P]
            MM(c, pb[P, :], lhs, c.w_out[:, kc, hf * 512:(hf + 1) * 512], kc == 0, kc == 15, [H[4], H[2], c.w_out], [pb], sig=(kc == 15))
    yield
    for hf in range(2):
        hs = slice(hf * 512, (hf + 1) * 512)
        TT(c, "dve", xt[P, hs], xt[P, hs], c.pB[2 + hf][P, :], ALU.add, [xt, c.pB[2 + hf]], [xt])
    k.dma("sp", dst_ap, xt[P, :], r=[xt.b], w=[dst_buf], sbuf=xt.b)


def interleave(gm, gp, ratio=1):
    am, ap = gm is not None, gp is not None
    while am or ap:
        if am:
            for _ in range(ratio):
                try:
                    next(gm)
                except StopIteration:
                    am = False
                    break
        if ap:
            try:
                next(gp)
            except StopIteration:
                ap = False


def layer0_seq(c, tiles, par0, dst_buf, first_seq=True):
    par = par0
    n = len(tiles)
    if first_seq:
        seq_reset0(c, par)
        interleave(None, layer0_P(c, tiles[0][0], tiles[0][2], par, None))
        start = 0
    else:
        CP(c, "pool", c.S[:], c.S_meta[:], [c.S_meta], [c.S])
        CP(c, "act", c.Sbf[:], c.S_meta[:], [c.S_meta], [c.Sbf])
        CP(c, "pool", c.hst[:], c.hst_meta[:], [c.hst_meta], [c.hst])
        interleave(None, layer0_P(c, tiles[1][0], tiles[1][2], par, "meta"))
        start = 1
    for j in range(start, n):
        gp = layer0_P(c, tiles[j + 1][0], tiles[j + 1][2], par ^ 1, (par, tiles[j][2])) if j + 1 < n else None
        gm = layer0_M(c, tiles[j][1], tiles[j][2], par, dst_buf)
        interleave(gm, gp)
        if first_seq and j == 0:
            CP(c, "pool", c.S_meta[:], c.S[:], [c.S], [c.S_meta])
            CP(c, "pool", c.hst_meta[:], c.hst[:], [c.hst], [c.hst_meta])
            CP(c, "pool", c.hist_meta[:, :, :], c.projF[par][:, :, tiles[0][2]:tiles[0][2] + 3], [c.projF[par]], [c.hist_meta])
        par ^= 1
    return par


LTOT = 2064
BIG = 30000.0


def setup_layer1(c, D, L):
    k = c.k
    c.w_in1 = T(k, "w_in1", [128, 8, 4096], BF16)
    c.w_out1 = T(k, "w_out1", [128, 8, 1024], BF16)
    c.vecF = T(k, "vec1F", [128, 8], F32)
    c.fn = T(k, "fn", [128, 1024], F32)
    k.dma("sp", c.vecF[:], D["vec1F"][:, :], w=[c.vecF.b], sbuf=c.vecF.b)
    k.dma("sp", c.fn[:], D["fnorm"][0:1, :].partition_broadcast(128), w=[c.fn.b], sbuf=c.fn.b)
    st = c.stage
    dsts, rows, scs = [], [], []
    for kc in range(8):
        for hf in range(8):
            dsts.append(c.w_in1[:, kc, hf * 512:(hf + 1) * 512])
            rows.append(D["w_in1"][kc * 128:(kc + 1) * 128, hf * 512:(hf + 1) * 512])
            scs.append(c.vecF[:, kc:kc + 1])
    load_cast_weight(c, c.w_in1, dsts, rows, 512, scs, st)
    dsts, rows, scs = [], [], []
    for kc in range(8):
        for hf in range(2):
            dsts.append(c.w_out1[:, kc, hf * 512:(hf + 1) * 512])
            rows.append(D["w_out1"][kc * 128:(kc + 1) * 128, hf * 512:(hf + 1) * 512])
            scs.append(None)
    load_cast_weight(c, c.w_out1, dsts, rows, 512, scs, st)
    c.negm = T(k, "negm", [128, 4, 128], BF16)
    c.tri2 = T(k, "tri2", [128, 128], BF16)
    c.zrow = T(k, "zrow", [1, 256], BF16)
    c.tri = T(k, "tri", [128, 128], BF16)
    c.ones = T(k, "ones", [128, 2], BF16)
    OP(c, "pool", lambda e: e.memset(c.negm[:], 0.0), [], [c.negm])
    OP(c, "pool", lambda e: e.memset(c.tri2[:], 1.0), [], [c.tri2])
    OP(c, "pool", lambda e: e.memset(c.zrow[:], 0.0), [], [c.zrow])
    OP(c, "pool", lambda e: e.memset(c.tri[:], 1.0), [], [c.tri])
    OP(c, "pool", lambda e: e.memset(c.ones[:], 1.0), [], [c.ones])
    for i in range(4):
        OP(c, "pool", lambda e, i=i: e.affine_select(out=c.negm[:, i, :], in_=c.negm[:, i, :], pattern=[[1, 128]], compare_op=ALU.is_gt, fill=-BIG, base=0, channel_multiplier=-1), [c.negm], [c.negm])
    OP(c, "pool", lambda e: e.affine_select(out=c.tri[:], in_=c.tri[:], pattern=[[-1, 128]], compare_op=ALU.is_ge, fill=0.0, base=0, channel_multiplier=1), [c.tri], [c.tri])
    OP(c, "pool", lambda e: e.affine_select(out=c.tri2[:], in_=c.tri2[:], pattern=[[1, 128]], compare_op=ALU.is_gt, fill=0.0, base=0, channel_multiplier=-1), [c.tri2], [c.tri2])
    nkb = 1 + (L - 16) // 128
    c.KT = T(k, "KT", [128, 8, L], BF16)
    c.V = T(k, "V", [128, nkb, 1024], BF16)
    c.KTb = [Buf("KTb%d" % i) for i in range(nkb)]
    c.Vb = [Buf("Vb%d" % i) for i in range(nkb)]


def alloc_layer1_work(c):
    k = c.k
    c.ht = [T(k, "ht%d" % i, [128, 1024], F32) for i in range(2)]
    c.st2 = [T(k, "st2_%d" % i, [128, 8], F32) for i in range(2)]
    c.ub1 = T(k, "ub1", [128, 1024], BF16)
    c.uT1 = T(k, "uT1", [128, 1024], BF16)
    c.QTs = [T(k, "QTs%d" % i, [128, 8, 2, 128], BF16) for i in range(2)]
    for i in range(2):
        OP(c, "pool", lambda e, i=i: e.memset(c.QTs[i][:], 0.0), [], [c.QTs[i]])
    c.sgt1 = T(k, "sgt1", [128, 512], F32)
    c.sgz = [T(k, "sgz%d" % i, [128, 1024], BF16) for i in range(2)]
    c.E = [T(k, "E%d" % i, [128, 512], F32) for i in range(4)]
    c.SP = [T(k, "SP%d" % i, [128, 512], BF16) for i in range(4)]
    c.X = [T(k, "X%d" % i, [128, 512], F32) for i in range(2)]
    c.Wt = [T(k, "Wt%d" % i, [128, 512], BF16) for i in range(2)]
    c.ob = [T(k, "ob%d" % i, [128, 1024], BF16) for i in range(2)]
    c.oT = T(k, "oT", [128, 1024], BF16)
    c.B = [T(k, "B%d" % i, [128, 512], F32, "ps") for i in range(8)]
    c.pT1 = View(c.B[6][:, :].bitcast(BF16).rearrange("p (c t) -> p c t", c=8), c.B[6].b)
    c.pTo = View(c.B[0][:, :].bitcast(BF16).rearrange("p (c t) -> p c t", c=8), c.B[0].b)


def l1_proj_gen(c, src_ap, nt, j, par, src_buf):
    k = c.k
    ht = c.ht[par]
    st2 = c.st2[par]
    P = slice(0, nt)
    identb = c.identb
    pos0 = 0 if j == 0 else 16 + (j - 1) * 128
    B = c.B
    k.dma("sp", ht[P, :], src_ap, r=[src_buf], w=[ht.b], sbuf=ht.b)
    ub = c.ub1
    ACT(c, ub[P, :], ht[P, :], AF.Square, [ht], [ub, st2], accum_out=st2[P, 0:1])
    ACT(c, st2[P, 1:2], st2[P, 0:1], AF.Ln, [st2], [st2], scale=1.0 / 1024, bias=1e-6)
    ACT(c, st2[P, 1:2], st2[P, 1:2], AF.Exp, [st2], [st2], scale=-0.5)
    TS(c, "dve", ub[P, :], ht[P, :], st2[P, 1:2], None, ALU.mult, None, [ht, st2], [ub])
    yield
    for kc in range(8):
        TR(c, c.pT1[:, kc, P], ub[P, kc * 128:(kc + 1) * 128], identb[P, P], [ub, identb], [c.pT1], sig=(kc == 7))
    uTv = c.uT1[:].rearrange("p (c t) -> p c t", c=8)
    CP(c, "dve", uTv[:, :, P], c.pT1[:, :, P], [c.pT1], [c.uT1])
    yield
    w = c.w_in1
    for g4 in range(2):
        pb = B[7 - g4]
        pbv = pb[:].rearrange("p (c t) -> p c t", c=4)
        for i in range(4):
            oc = g4 * 4 + i
            for kc in range(8):
                MM(c, pbv[:, i, P], w[:, kc, 1024 + oc * 128:1024 + (oc + 1) * 128], uTv[:, kc, P], kc == 0, kc == 7, [c.uT1, w], [pb], sig=(kc == 7))
            yield
        CP(c, "dve", c.KT[:, g4 * 4:(g4 + 1) * 4, pos0:pos0 + nt], pbv[:, :, P], [pb], [c.KTb[j]])
        yield
    for hf in range(2):
        pb = B[7 - hf]
        for kc in range(8):
            MM(c, pb[P, :], uTv[:, kc, P], w[:, kc, 2048 + hf * 512:2048 + (hf + 1) * 512], kc == 0, kc == 7, [c.uT1, w], [pb], sig=(kc == 7))
        yield
        CP(c, "dve", c.V[P, j, hf * 512:(hf + 1) * 512], pb[P, :], [pb], [c.Vb[j]])
        yield
    if j == 0:
        return
    QTs = c.QTs[par]
    for g4 in range(2):
        pb = B[7 - g4]
        pbv = pb[:].rearrange("p (c t) -> p c t", c=4)
        for i in range(4):
            oc = g4 * 4 + i
            for kc in range(8):
                MM(c, pbv[:, i, P], w[:, kc, oc * 128:(oc + 1) * 128], uTv[:, kc, P], kc == 0, kc == 7, [c.uT1, w], [pb], sig=(kc == 7))
            yield
        for hf_ in range(2):
            hp = slice(hf_ * 64, (hf_ + 1) * 64)
            TS(c, "dve", QTs[hp, g4 * 4:(g4 + 1) * 4, hf_, P], pbv[hp, :, P], 0.125, None, ALU.mult, None, [pb], [QTs])
        yield
    sgz = c.sgz[par]
    for hf in range(2):
        pb = B[7 - hf]
        hs = slice(hf * 512, (hf + 1) * 512)
        for kc in range(8):
            MM(c, pb[P, :], uTv[:, kc, P], w[:, kc, 3072 + hf * 512:3072 + (hf + 1) * 512], kc == 0, kc == 7, [c.uT1, w], [pb], sig=(kc == 7))
        yield
        ACT(c, c.sgt1[P, :], pb[P, :], AF.Exp, [pb], [c.sgt1], scale=-1.0)
        TS(c, "dve", c.sgt1[P, :], c.sgt1[P, :], 1.0, None, ALU.add, None, [c.sgt1], [c.sgt1])
        OP(c, "dve", lambda e: e.reciprocal(out=c.sgt1[P, :], in_=c.sgt1[P, :]), [c.sgt1], [c.sgt1])
        TT(c, "dve", sgz[P, hs], c.sgt1[P, :], pb[P, :], ALU.mult, [c.sgt1, pb], [sgz])
        yield


def run_gen(g, n=None):
    if g is None:
        return False
    try:
        if n is None:
            while True:
                next(g)
        for _ in range(n):
            next(g)
    except StopIteration:
        return False
    return True


def layer1_attn(c, nt, j, par, gen_next):
    k = c.k
    ht = c.ht[par]
    st2 = c.st2[par]
    P = slice(0, nt)
    identb = c.identb
    B = c.B
    QTs_ = c.QTs[par]
    sgz_ = c.sgz[par]

    units = []
    for pr in range(2):
        for kb in range(j, -1, -1):
            for q in range(2):
                units.append((kb, 2 * pr + q, q))
    nu = len(units)

    def kinfo(kb):
        if kb == 0:
            return 16, 0
        return 128, 16 + (kb - 1) * 128

    def views(u):
        kb, hg, q = units[u]
        ks, kp = kinfo(kb)
        return kb, hg, q, ks, kp

    def stageA(u):
        kb, hg, q, ks, kp = views(u)
        diag = (kb == j)
        z = B[u % 2]
        zv = z[:].rearrange("p (i t) -> p i t", i=4)
        first = True
        if diag:
            MM(c, z[0:ks, :], identb[0:ks, 0:ks], c.negm[0:ks, :, :].rearrange("p i t -> p (i t)"), True, False, [identb, c.negm], [z], sig=False)
            first = False
        for i2 in range(2):
            ch = 2 * hg + i2
            MM(c, z[0:ks, 2 * i2 * 128:(2 * i2 + 2) * 128], c.KT[:, ch, kp:kp + ks], QTs_[:, ch, :, :].rearrange("p a t -> p (a t)"), first, True, [c.KTb[kb], QTs_], [z], sig=(i2 == 1))
        E = c.E[u % 4]
        SP = c.SP[u % 4]
        Ev = E[:].rearrange("p (i t) -> p i t", i=4)
        SPv = SP[:].rearrange("p (i t) -> p i t", i=4)
        ACT(c, Ev[0:ks, :, P], zv[0:ks, :, P], AF.Exp, [z], [E])
        ACT(c, SPv[0:ks, :, P], Ev[0:ks, :, P], AF.Ln, [E], [SP], bias=1.0)

    def stageB(u):
        kb, hg, q, ks, kp = views(u)
        tb = B[2 + q]
        tv = tb[:].rearrange("p (i t) -> p i t", i=4)
        SP = c.SP[u % 4]
        SPv = SP[:].rearrange("p (i t) -> p i t", i=4)
        MM(c, tb[0:ks, :], c.tri[0:ks, 0:ks], SP[0:ks, :], kb == j, False, [c.tri, SP], [tb], sig=True)
        X = c.X[u % 2]
        Xv = X[:].rearrange("p (i t) -> p i t", i=4)
        ACT(c, Xv[0:ks, :, P], tv[0:ks, :, P], AF.Exp, [tb], [X], scale=-1.0)

    def stageC(u):
        kb, hg, q, ks, kp = views(u)
        tb = B[2 + q]
        tv = tb[:].rearrange("p (i t) -> p i t", i=4)
        SP = c.SP[u % 4]
        SPv = SP[:].rearrange("p (i t) -> p i t", i=4)
        if kb > 0:
            MM(c, tb[0:ks, :], c.tri2[0:ks, 0:ks], SP[0:ks, :], False, kb == 1, [c.tri2, SP], [tb], sig=True)
        E = c.E[u % 4]
        X = c.X[u % 2]
        Wt = c.Wt[u % 2]
        Ev = E[:].rearrange("p (i t) -> p i t", i=4)
        Xv = X[:].rearrange("p (i t) -> p i t", i=4)
        Wv = Wt[:].rearrange("p (i t) -> p i t", i=4)
        TT(c, "dve", Wv[0:ks, :, P], Ev[0:ks, :, P], Xv[0:ks, :, P], ALU.mult, [E, X], [Wt])

    def stageD(u):
        kb, hg, q, ks, kp = views(u)
        ob_ = B[4 + q]
        Wt = c.Wt[u % 2]
        Wv = Wt[:].rearrange("p (i t) -> p i t", i=4)
        if kb == j:
            MM(c, ob_[P, 0:256], c.zrow[0:1, P], c.zrow[0:1, 0:256], True, False, [c.zrow], [ob_], sig=False)
        for i in range(4):
            hd = 4 * hg + i
            MM(c, ob_[P, i * 64:(i + 1) * 64], Wv[0:ks, i, P], c.V[0:ks, kb, hd * 64:(hd + 1) * 64], False, kb == 0, [Wt, c.Vb[kb]], [ob_], sig=(i == 3))
        if kb == 0:
            hsl = slice(hg * 256, (hg + 1) * 256)
            TT(c, "dve", c.ob[par][P, hsl], ob_[P, 0:256], sgz_[P, hsl], ALU.mult, [ob_, sgz_], [c.ob[par]])

    per = max(1, -(-52 // max(1, nu - 2)))
    for step in range(nu + 3):
        if step < nu:
            stageA(step)
        if 0 <= step - 1 < nu:
            stageB(step - 1)
        if 0 <= step - 2 < nu:
            stageC(step - 2)
        if 0 <= step - 3 < nu:
            stageD(step - 3)
        run_gen(gen_next, per)
    run_gen(gen_next, None)


def l1_tail_gen(c, dst_ap, nt, par):
    k = c.k
    ht = c.ht[par]
    st2 = c.st2[par]
    ob = c.ob[par]
    P = slice(0, nt)
    identb = c.identb
    B = c.B
    for kc in range(8):
        TR(c, c.pT1[:, kc, P], ob[P, kc * 128:(kc + 1) * 128], identb[P, P], [ob, identb], [c.pT1], sig=(kc == 7))
    oTv = c.oT[:].rearrange("p (c t) -> p c t", c=8)
    CP(c, "dve", oTv[:, :, P], c.pT1[:, :, P], [c.pT1], [c.oT])
    yield
    for hf in range(2):
        pb = B[6 + hf]
        for kc in range(8):
            MM(c, pb[P, :], oTv[:, kc, P], c.w_out1[:, kc, hf * 512:(hf + 1) * 512], kc == 0, kc == 7, [c.oT, c.w_out1], [pb], sig=(kc == 7))
        yield
    for hf in range(2):
        hs = slice(hf * 512, (hf + 1) * 512)
        TT(c, "dve", ht[P, hs], ht[P, hs], B[6 + hf][P, :], ALU.add, [ht, B[6 + hf]], [ht])
    yield
    ACT(c, c.oT[P, :], ht[P, :], AF.Square, [ht], [c.oT, st2], accum_out=st2[P, 2:3])
    ACT(c, st2[P, 3:4], st2[P, 2:3], AF.Ln, [st2], [st2], scale=1.0 / 1024, bias=1e-6)
    ACT(c, st2[P, 3:4], st2[P, 3:4], AF.Exp, [st2], [st2], scale=-0.5)
    yield
    STT(c, "dve", ht[P, :], ht[P, :], st2[P, 3:4], c.fn[P, :], ALU.mult, ALU.mult, [ht, st2, c.fn], [ht])
    k.dma("sp", dst_ap, ht[P, :], r=[ht.b], sbuf=ht.b)
    yield


def chain_gens(*gens):
    for g in gens:
        if g is not None:
            yield from g


def layer1_seq(c, h1s, outs, nfull, par0, src_buf, first_seq=True):
    par = par0
    if first_seq:
        run_gen(l1_proj_gen(c, h1s[0], 16, 0, par, src_buf), None)
        par ^= 1
    run_gen(l1_proj_gen(c, h1s[1], 128, 1, par, src_buf), None)
    for j in range(1, nfull + 1):
        gt = l1_tail_gen(c, outs[j - 1], 128, par ^ 1) if j >= 2 else None
        gp = l1_proj_gen(c, h1s[j + 1], 128, j + 1, par ^ 1, src_buf) if j + 1 <= nfull else None
        layer1_attn(c, 128, j, par, chain_gens(gt, gp))
        par ^= 1
    run_gen(l1_tail_gen(c, outs[nfull], 128, par ^ 1), None)
    return par

NSEQ = 4
NFULL = 16
LTOT_ = 16 + NFULL * 128


def common_setup(c, sw=2312):
    k = c.k
    if sw:
        c.stage = [T(k, "stage%d" % i, [128, sw], F32) for i in range(2)]
    ident = T(k, "ident", [128, 128], F32)
    c.identb = T(k, "identb", [128, 128], BF16)
    OP(c, "pool", lambda e: e.memset(ident[:], 1.0), [], [ident])
    OP(c, "pool", lambda e: e.affine_select(out=ident[:], in_=ident[:], pattern=[[-1, 128]], compare_op=ALU.is_equal, fill=0.0, base=0, channel_multiplier=1), [ident], [ident])
    CP(c, "dve", c.identb[:], ident[:], [ident], [c.identb])


def build_l0(nseq, nfull):
    nc = bass.Bass("TRN2", target_bir_lowering=False)
    L = 16 + nfull * 128
    D = {}
    x = nc.dram_tensor("x", [nseq, nfull * 128, 1024], F32, kind="ExternalInput").ap()
    meta = nc.dram_tensor("meta", [16, 1024], F32, kind="ExternalInput").ap()
    D["w_in"] = nc.dram_tensor("w_in", [1024, 4624], F32, kind="ExternalInput").ap()
    D["w_out"] = nc.dram_tensor("w_out", [2048, 1024], F32, kind="ExternalInput").ap()
    D["wa"] = nc.dram_tensor("wa", [4, 256, 256], F32, kind="ExternalInput").ap()
    D["wx"] = nc.dram_tensor("wx", [4, 256, 256], F32, kind="ExternalInput").ap()
    D["vecF"] = nc.dram_tensor("vecF", [128, 140], F32, kind="ExternalInput").ap()
    D["vecT"] = nc.dram_tensor("vecT", [1, 48], F32, kind="ExternalInput").ap()
    h1 = nc.dram_tensor("h1", [nseq, L, 1024], F32, kind="ExternalOutput").ap()
    with ExitStack() as es:
        k = K(nc, es)
        c = setup_common(k, nc)
        common_setup(c, 0)
        alloc_layer0_work(c)
        setup_layer0(c, D)
        hb = Buf("h1dram")
        par = 0
        for s in range(nseq):
            tiles = [(meta[:, :], h1[s, 0:16, :], 16)]
            for j in range(nfull):
                tiles.append((x[s, j * 128:(j + 1) * 128, :], h1[s, 16 + j * 128:16 + (j + 1) * 128, :], 128))
            par = layer0_seq(c, tiles, par, hb, first_seq=(s == 0))
        k.final_wait("sp", [c.xt[0].b, c.xt[1].b])
        k.emit()
    return nc


def build_l1(nseq, nfull):
    nc = bass.Bass("TRN2", target_bir_lowering=False)
    L = 16 + nfull * 128
    D = {}
    h1 = nc.dram_tensor("h1", [nseq, L, 1024], F32, kind="ExternalInput").ap()
    D["w_in1"] = nc.dram_tensor("w_in1", [1024, 4096], F32, kind="ExternalInput").ap()
    D["w_out1"] = nc.dram_tensor("w_out1", [1024, 1024], F32, kind="ExternalInput").ap()
    D["vec1F"] = nc.dram_tensor("vec1F", [128, 8], F32, kind="ExternalInput").ap()
    D["fnorm"] = nc.dram_tensor("fnorm", [1, 1024], F32, kind="ExternalInput").ap()
    out = nc.dram_tensor("out", [nseq, nfull * 128, 1024], F32, kind="ExternalOutput").ap()
    with ExitStack() as es:
        k = K(nc, es)
        c = setup_common(k, nc)
        common_setup(c, 0)
        alloc_layer1_work(c)
        c.stage = list(c.E)
        setup_layer1(c, D, L)
        hb = Buf("h1dram")
        par = 0
        for s in range(nseq):
            h1s = [h1[s, 0:16, :]] + [h1[s, 16 + (j - 1) * 128:16 + j * 128, :] for j in range(1, nfull + 1)]
            outs = [None] + [out[s, (j - 1) * 128:j * 128, :] for j in range(1, nfull + 1)]
            par = layer1_seq(c, h1s, outs, nfull, par, hb, first_seq=(s == 0))
        k.final_wait("sp", [c.ht[0].b, c.ht[1].b])
        k.emit()
    return nc


def build_fused(nseq, nfull):
    nc = bass.Bass("TRN2", target_bir_lowering=False)
    L = 16 + nfull * 128
    D = {}
    x = nc.dram_tensor("x", [nseq, nfull * 128, 1024], F32, kind="ExternalInput").ap()
    meta = nc.dram_tensor("meta", [16, 1024], F32, kind="ExternalInput").ap()
    D["w_in"] = nc.dram_tensor("w_in", [1024, 4624], F32, kind="ExternalInput").ap()
    D["w_out"] = nc.dram_tensor("w_out", [2048, 1024], F32, kind="ExternalInput").ap()
    D["wa"] = nc.dram_tensor("wa", [4, 256, 256], F32, kind="ExternalInput").ap()
    D["wx"] = nc.dram_tensor("wx", [4, 256, 256], F32, kind="ExternalInput").ap()
    D["vecF"] = nc.dram_tensor("vecF", [128, 140], F32, kind="ExternalInput").ap()
    D["vecT"] = nc.dram_tensor("vecT", [1, 48], F32, kind="ExternalInput").ap()
    D["w_in1"] = nc.dram_tensor("w_in1", [1024, 4096], F32, kind="ExternalInput").ap()
    D["w_out1"] = nc.dram_tensor("w_out1", [1024, 1024], F32, kind="ExternalInput").ap()
    D["vec1F"] = nc.dram_tensor("vec1F", [128, 8], F32, kind="ExternalInput").ap()
    D["fnorm"] = nc.dram_tensor("fnorm", [1, 1024], F32, kind="ExternalInput").ap()
    out = nc.dram_tensor("out", [nseq, nfull * 128, 1024], F32, kind="ExternalOutput").ap()
    h1 = nc.dram_tensor("h1s", [nseq, L, 1024], F32, kind="Internal").ap()
    with ExitStack() as es:
        k = K(nc, es)
        c = setup_common(k, nc)
        common_setup(c, 0)
        hb = Buf("h1dram")
        k.push()
        alloc_layer0_work(c)
        setup_layer0(c, D)
        par = 0
        for s in range(nseq):
            tiles = [(meta[:, :], h1[s, 0:16, :], 16)]
            for j in range(nfull):
                tiles.append((x[s, j * 128:(j + 1) * 128, :], h1[s, 16 + j * 128:16 + (j + 1) * 128, :], 128))
            par = layer0_seq(c, tiles, par, hb, first_seq=(s == 0))
        k.barrier([c.xt[0].b, c.xt[1].b, c.vecF.b, c.vecT.b] + [t.b for t in c.stage])
        k.emit()
        k.pop()
        hb = Buf("h1dram2")
        k.push()
        alloc_layer1_work(c)
        c.stage = list(c.E)
        setup_layer1(c, D, L)
        par = 0
        for s in range(nseq):
            h1s = [h1[s, 0:16, :]] + [h1[s, 16 + (j - 1) * 128:16 + j * 128, :] for j in range(1, nfull + 1)]
            outs = [None] + [out[s, (j - 1) * 128:j * 128, :] for j in range(1, nfull + 1)]
            par = layer1_seq(c, h1s, outs, nfull, par, hb, first_seq=(s == 0))
        k.final_wait("sp", [c.ht[0].b, c.ht[1].b])
        k.emit()
        k.pop()
    return nc


def pack_vecs(inp):
    f = lambda v: np.ascontiguousarray(np.asarray(v).reshape(-1, 128).T)
    vecF = np.zeros((128, 140), np.float32)
    vecF[:, 0:8] = f(inp["even_norm"][0])
    vecF[:, 8:16] = f(inp["ssd_norm"][0])
    vecF[:, 16:48] = np.asarray(inp["lru_conv_w"][0]).reshape(4, 8, 128).transpose(2, 1, 0).reshape(128, 32)
    vecF[:, 48:56] = f(inp["lru_conv_b"][0])
    vecF[:, 56:64] = f(inp["lru_b_a"][0])
    vecF[:, 64:72] = f(inp["lru_b_x"][0])
    vecF[:, 72:80] = f(inp["lru_lambda"][0])
    vecF[:, 80:128] = np.asarray(inp["ssd_conv_w"][0]).reshape(4, 12, 128).transpose(2, 1, 0).reshape(128, 48)
    vecF[:, 128:140] = f(inp["ssd_conv_b"][0])
    vecT = np.concatenate([np.asarray(inp["ssd_dt_bias"][0]), np.asarray(inp["ssd_a_log"][0]), np.asarray(inp["ssd_d"][0])])[None, :].astype(np.float32)
    return vecF, vecT


_NC_CACHE = {}


def kernel(**inp):
    inp = {k_: np.asarray(v, dtype=np.float32) for k_, v in inp.items()}
    x = inp["x"]
    ncores = 8
    vecF, vecT = pack_vecs(inp)
    vec1F = np.ascontiguousarray(inp["odd_norm"][0].reshape(8, 128).T)
    if "f" not in _NC_CACHE:
        _NC_CACHE["f"] = build_fused(NSEQ, NFULL)
    xs = np.split(np.ascontiguousarray(x), ncores, axis=0)
    maps = [{"x": xs[i], "meta": inp["meta"], "w_in": inp["even_w_in"][0], "w_out": inp["even_w_out"][0],
             "wa": inp["lru_w_a"][0], "wx": inp["lru_w_x"][0], "vecF": vecF, "vecT": vecT,
             "w_in1": inp["odd_w_in"][0], "w_out1": inp["odd_w_out"][0], "vec1F": vec1F,
             "fnorm": inp["final_norm"][None, :]} for i in range(ncores)]
    r = run_bass_kernel_spmd(_NC_CACHE["f"], maps, core_ids=list(range(ncores)))
    return np.concatenate([r.results[i]["out"] for i in range(ncores)], axis=0).astype(np.float32)
```

```python
import numpy as np
from contextlib import ExitStack
import concourse.bass as bass
import concourse.mybir as mybir
from concourse.bass_utils import run_bass_kernel_spmd


F32 = mybir.dt.float32
BF16 = mybir.dt.bfloat16
AF = mybir.ActivationFunctionType
ALU = mybir.AluOpType
AX = mybir.AxisListType

ENGS = ("pe", "act", "dve", "pool", "sp")


class Buf:
    __slots__ = ("name", "w", "rs", "dsem", "dcnt")

    def __init__(self, name):
        self.name = name
        self.w = None
        self.rs = []
        self.dsem = None
        self.dcnt = 0


class K:
    def __init__(self, nc, es):
        self.nc = nc
        self.es = es
        self.prog = {e: [] for e in ENGS}
        self.sem = {e: es.enter_context(nc.semaphore("s_" + e)) for e in ENGS}
        self.cnt = {e: 0 for e in ENGS}
        self.waited = {e: {} for e in ENGS}
        self.pending = {e: [] for e in ENGS}
        self.nsem = 5
        self.ninstr = 0
        self.scopes = [es]

    def push(self):
        self.scopes.append(ExitStack())

    def pop(self):
        self.scopes.pop().close()

    def barrier(self, dma_bufs=()):
        for e in ENGS:
            if self.pending[e]:
                raise RuntimeError("barrier with pending unsignalled ops on " + e)
        waits = {}
        for e in ENGS:
            if e != "pool" and self.cnt[e] > 0:
                self._need("pool", (e, self.cnt[e], self.sem[e]), waits)
        for b in dma_bufs:
            self._need("pool", b.w, waits)
            for t in b.rs:
                self._need("pool", t, waits)
        self.cnt["pool"] += 1
        tok = ("pool", self.cnt["pool"], self.sem["pool"])
        self.prog["pool"].append((list(waits.values()), lambda e: e.engine_nop(), (self.sem["pool"], 1)))
        for e in ENGS:
            if e != "pool":
                self.wait_tok(e, tok)

    def sb(self, name, shape, dt):
        return self.scopes[-1].enter_context(self.nc.sbuf_tensor(name, list(shape), dt))

    def ps(self, name, shape, dt=F32):
        return self.scopes[-1].enter_context(self.nc.psum_tensor(name, list(shape), dt))

    def dsem_of(self, buf):
        if buf.dsem is None:
            buf.dsem = self.es.enter_context(self.nc.semaphore("d_" + buf.name))
            self.nsem += 1
        return buf.dsem

    def _need(self, eng, tok, waits):
        if tok is None:
            return
        key, val, semh = tok
        if key == eng and False:
            return
        cur = self.waited[eng].get(key, 0)
        if val > cur:
            self.waited[eng][key] = val
            waits[key] = (semh, val)

    def _deps(self, eng, r, w):
        waits = {}
        for b in r:
            if b.w is not None and b.w[0] == "PENDING":
                raise RuntimeError("read of buffer %s with unsignalled writer" % b.name)
            self._need(eng, b.w, waits)
        for b in w:
            if b.w is not None and b.w[0] == "PENDING":
                if b.w[1] != eng:
                    raise RuntimeError("write of buffer %s with unsignalled writer" % b.name)
            else:
                self._need(eng, b.w, waits)
            for t in b.rs:
                if t[0] == "PENDING":
                    if t[1] != eng:
                        raise RuntimeError("WAR on buffer %s with unsignalled reader" % b.name)
                else:
                    self._need(eng, t, waits)
        return list(waits.values())

    def op(self, eng, fn, r=(), w=(), sig=True):
        waits = self._deps(eng, r, w)
        if sig:
            self.cnt[eng] += 1
            tok = (eng, self.cnt[eng], self.sem[eng])
            semh = self.sem[eng]
            for (b, kind) in self.pending[eng]:
                if kind == "r":
                    b.rs = [t for t in b.rs if not (t[0] == "PENDING" and t[1] == eng)]
                    b.rs.append(tok)
                else:
                    b.w = tok
                    b.rs = [t for t in b.rs if not (t[0] == "PENDING" and t[1] == eng)]
            self.pending[eng] = []
            for b in r:
                b.rs.append(tok)
            for b in w:
                b.w = tok
                b.rs = []
            self.prog[eng].append((waits, fn, (semh, 1)))
        else:
            ptok = ("PENDING", eng)
            for b in r:
                b.rs.append(ptok)
                self.pending[eng].append((b, "r"))
            for b in w:
                b.w = ptok
                b.rs = []
                self.pending[eng].append((b, "w"))
            self.prog[eng].append((waits, fn, None))
        self.ninstr += 1

    def dma(self, eng, out_ap, in_ap, r=(), w=(), sbuf=None, **kw):
        waits = self._deps(eng, r, w)
        semh = self.dsem_of(sbuf)
        sbuf.dcnt += 1
        tok = ("d_" + sbuf.name, 16 * sbuf.dcnt, semh)
        for b in r:
            b.rs.append(tok)
        for b in w:
            b.w = tok
            b.rs = []
        self.prog[eng].append((waits, lambda e: e.dma_start(out=out_ap, in_=in_ap, **kw), (semh, 16)))
        self.ninstr += 1
        return tok

    def wait_tok(self, eng, tok):
        waits = {}
        self._need(eng, tok, waits)
        for (semh, val) in waits.values():
            self.prog[eng].append(([(semh, val)], None, None))

    def final_wait(self, eng, bufs):
        waits = {}
        for b in bufs:
            self._need(eng, b.w, waits)
            for t in b.rs:
                self._need(eng, t, waits)
        if waits:
            self.prog[eng].append((list(waits.values()), None, None))

    def emit(self):
        nc = self.nc
        engmap = {"pe": "tensor", "act": "scalar", "dve": "vector", "pool": "gpsimd", "sp": "sync"}
        with nc.Block() as block:
            for e in ENGS:
                prog = self.prog[e]

                def body(engine, prog=prog):
                    for waits, fn, inc in prog:
                        for (semh, val) in waits:
                            engine.wait_ge(semh, val)
                        if fn is not None:
                            ins = fn(engine)
                            if inc is not None:
                                ins.then_inc(inc[0], inc[1])
                getattr(block, engmap[e])(body)
        self.prog = {e: [] for e in ENGS}


D_MODEL = 1024
EVEN_IN = 4624


class T:
    def __init__(self, k, name, shape, dt, space="sb"):
        self.t = k.sb("t_" + name, shape, dt) if space == "sb" else k.ps("t_" + name, shape, dt)
        self.b = Buf(name)
        self.name = name

    def __getitem__(self, idx):
        return self.t[idx]


class View:
    def __init__(self, ap, b):
        self.ap = ap
        self.b = b

    def __getitem__(self, idx):
        return self.ap[idx]


def _b(xs):
    return [getattr(x, "b", x) for x in xs]


class Ctx:
    pass


def setup_common(k, nc):
    c = Ctx()
    c.k = k
    c.nc = nc
    c.rr = 0
    return c


def OP(c, eng, fn, r=(), w=(), sig=True):
    c.k.op(eng, fn, r=_b(r), w=_b(w), sig=sig)


def ACT(c, out, in_, func, r, w, **kw):
    OP(c, "act", lambda e: e.activation(out=out, in_=in_, func=func, **kw), r, w)


def TT(c, eng, out, in0, in1, op, r, w):
    OP(c, eng, lambda e: e.tensor_tensor(out=out, in0=in0, in1=in1, op=op), r, w)


def TS(c, eng, out, in0, s1, s2, op0, op1, r, w):
    if s2 is None:
        OP(c, eng, lambda e: e.tensor_scalar(out=out, in0=in0, scalar1=s1, scalar2=None, op0=op0), r, w)
    else:
        OP(c, eng, lambda e: e.tensor_scalar(out=out, in0=in0, scalar1=s1, scalar2=s2, op0=op0, op1=op1), r, w)


def STT(c, eng, out, in0, scalar, in1, op0, op1, r, w):
    OP(c, eng, lambda e: e.scalar_tensor_tensor(out=out, in0=in0, scalar=scalar, in1=in1, op0=op0, op1=op1), r, w)


def CP(c, eng, out, in_, r, w):
    if eng == "act":
        ACT(c, out, in_, AF.Copy, r, w)
    else:
        OP(c, eng, lambda e: e.tensor_copy(out=out, in_=in_), r, w)


def MM(c, out, lhsT, rhs, start, stop, r, w, sig, skip=False):
    if skip:
        OP(c, "pe", lambda e: e.matmul(out, lhsT=lhsT, rhs=rhs, start=start, stop=stop, skip_group_check=True), r, w, sig=sig)
    else:
        OP(c, "pe", lambda e: e.matmul(out, lhsT=lhsT, rhs=rhs, start=start, stop=stop), r, w, sig=sig)


def TR(c, out, in_, ident, r, w, sig):
    OP(c, "pe", lambda e: e.transpose(out=out, in_=in_, identity=ident), r, w, sig=sig)


def load_cast_weight(c, dst, dst_slices, dram_rows, width, scale_aps, st, engs=("act", "dve")):
    k = c.k
    for i, (da, ra) in enumerate(zip(dst_slices, dram_rows)):
        s = st[c.rr % len(st)]
        eng = engs[c.rr % len(engs)]
        c.rr += 1
        k.dma("sp", s.t[:, 0:width], ra, w=[s.b], sbuf=s.b)
        sc = scale_aps[i]
        if sc is None:
            CP(c, eng, da, s.t[:, 0:width], [s], [dst])
        else:
            if eng == "act":
                OP(c, "act", lambda e, da=da, s=s, sc=sc: e.activation(out=da, in_=s.t[:, 0:width], func=AF.Copy, scale=sc), [s, c.vecF], [dst])
            else:
                TS(c, eng, da, s.t[:, 0:width], sc, None, ALU.mult, None, [s, c.vecF], [dst])


def setup_layer0(c, D):
    k = c.k
    c.w_in = T(k, "w_in", [128, 8, EVEN_IN], BF16)
    c.w_out = T(k, "w_out", [128, 16, 1024], BF16)
    c.wa = T(k, "wa", [128, 4, 2, 256], BF16)
    c.wx = T(k, "wx", [128, 4, 2, 256], BF16)
    c.vecF = T(k, "vecF", [128, 140], F32)
    c.vecT = T(k, "vecT", [128, 48], F32)
    k.dma("sp", c.vecF[:], D["vecF"][:, :], w=[c.vecF.b], sbuf=c.vecF.b)
    k.dma("sp", c.vecT[:], D["vecT"][0:1, :].partition_broadcast(128), w=[c.vecT.b], sbuf=c.vecT.b)
    st = c.stage
    for (c0, wd) in ((0, 1024), (1024, 1024), (2048, 1024), (3072, 1024), (4096, 528)):
        dsts, rows, scs = [], [], []
        for kc in range(8):
            dsts.append(c.w_in[:, kc, c0:c0 + wd])
            rows.append(D["w_in"][kc * 128:(kc + 1) * 128, c0:c0 + wd])
            scs.append(c.vecF[:, kc:kc + 1])
        load_cast_weight(c, c.w_in, dsts, rows, wd, scs, st)
    dsts, rows, scs = [], [], []
    for kc in range(16):
        dsts.append(c.w_out[:, kc, :])
        rows.append(D["w_out"][kc * 128:(kc + 1) * 128, :])
        scs.append(None if kc < 8 else c.vecF[:, 8 + kc - 8:8 + kc - 8 + 1])
    load_cast_weight(c, c.w_out, dsts, rows, 1024, scs, st)
    for (wt, nm) in ((c.wa, "wa"), (c.wx, "wx")):
        dsts, rows, scs = [], [], []
        for g in range(4):
            for kc in range(2):
                dsts.append(wt[:, g, kc, :])
                rows.append(D[nm][g, kc * 128:(kc + 1) * 128, :])
                scs.append(None)
        load_cast_weight(c, wt, dsts, rows, 256, scs, st)

    c.L1 = T(k, "L1", [128, 128], F32)
    c.L2 = T(k, "L2", [128, 128], F32)
    c.L4 = T(k, "L4", [128, 2, 128], F32)
    c.mle = T(k, "mle", [128, 64], F32)
    OP(c, "pool", lambda e: e.memset(c.L1[:], 0.0), [], [c.L1])
    OP(c, "pool", lambda e: e.memset(c.L2[:], 0.0), [], [c.L2])
    OP(c, "pool", lambda e: e.memset(c.L4[:], 0.0), [], [c.L4])
    OP(c, "pool", lambda e: e.memset(c.mle[:], 1.0), [], [c.mle])
    for h in range(2):
        ps = slice(h * 64, (h + 1) * 64)
        OP(c, "pool", lambda e, ps=ps: e.memset(c.L1[ps, ps], 1.0), [], [c.L1])
        OP(c, "pool", lambda e, ps=ps: e.memset(c.L2[ps, ps], 1.0), [], [c.L2])
        OP(c, "pool", lambda e, ps=ps, h=h: e.memset(c.L4[ps, h, :], 1.0), [], [c.L4])
        OP(c, "pool", lambda e, ps=ps: e.affine_select(out=c.L1[ps, ps], in_=c.L1[ps, ps], pattern=[[1, 64]], compare_op=ALU.is_ge, fill=0.0, base=0, channel_multiplier=-1), [c.L1], [c.L1])
        OP(c, "pool", lambda e, ps=ps: e.affine_select(out=c.L2[ps, ps], in_=c.L2[ps, ps], pattern=[[-1, 64]], compare_op=ALU.is_gt, fill=0.0, base=0, channel_multiplier=1), [c.L2], [c.L2])
        OP(c, "pool", lambda e, ps=ps: e.affine_select(out=c.mle[ps, :], in_=c.mle[ps, :], pattern=[[1, 64]], compare_op=ALU.is_ge, fill=0.0, base=0, channel_multiplier=-1), [c.mle], [c.mle])
    c.pv = T(k, "pv", [128, 64], F32)
    ACT(c, c.pv[:, 0:8], c.vecF[:, 72:80], AF.Exp, [c.vecF], [c.pv], scale=-1.0)
    ACT(c, c.pv[:, 0:8], c.pv[:, 0:8], AF.Ln, [c.pv], [c.pv], bias=1.0)
    TS(c, "dve", c.pv[:, 8:16], c.pv[:, 0:8], -16.0, None, ALU.mult, None, [c.pv], [c.pv])
    TS(c, "dve", c.pv[:, 0:8], c.pv[:, 0:8], -8.0, None, ALU.mult, None, [c.pv], [c.pv])
    TS(c, "dve", c.pv[:, 16:32], c.vecF[:, 56:72], -1.0, None, ALU.mult, None, [c.vecF], [c.pv])
    ACT(c, c.pv[:, 32:48], c.vecT[:, 16:32], AF.Exp, [c.vecT], [c.pv])
    TS(c, "dve", c.pv[:, 32:48], c.pv[:, 32:48], -1.0, None, ALU.mult, None, [c.pv], [c.pv])


def alloc_layer0_work(c):
    k = c.k
    c.xt = [T(k, "xt%d" % i, [128, 1024], F32) for i in range(2)]
    c.st1 = [T(k, "st1_%d" % i, [128, 8], F32) for i in range(2)]
    c.W = [T(k, "W%d" % i, [128, 1024], F32) for i in range(6)]
    c.H = [T(k, "H%d" % i, [128, (256 if i == 1 else (512 if i == 5 else 1024))], BF16) for i in range(6)]
    c.projF = [T(k, "projF%d" % i, [128, 20, 131], F32) for i in range(2)]
    c.sg = [T(k, "sg%d" % i, [128, 1024], BF16) for i in range(2)]
    c.gz = [T(k, "gz%d" % i, [128, 1024], BF16) for i in range(2)]
    c.dts = [T(k, "dts%d" % i, [128, 32], F32) for i in range(2)]
    c.ubP = T(k, "ubP", [128, 1024], BF16)
    c.sgt = View(c.ubP.t[:, :].bitcast(F32), c.ubP.b)
    c.uTP = T(k, "uTP", [128, 1024], BF16)
    c.xbc = T(k, "xbc", [128, 12, 128], F32)
    c.S = T(k, "S", [128, 1024], F32)
    c.Sbf = T(k, "Sbf", [128, 1024], BF16)
    c.hst = T(k, "hst", [128, 8], F32)
    c.S_meta = T(k, "S_meta", [128, 1024], F32)
    c.hst_meta = T(k, "hst_meta", [128, 8], F32)
    c.hist_meta = T(k, "hist_meta", [128, 20, 3], F32)
    c.sm = T(k, "sm", [128, 96], F32)
    c.cbm = T(k, "cbm", [128, 2, 64], F32)
    c.pT = T(k, "pT", [128, 8, 128], BF16, "ps")
    c.pG = T(k, "pG", [128, 512], F32, "ps")
    c.pT2 = View(c.pG[:, 256:384].bitcast(BF16).rearrange("p (c t) -> p c t", c=2), c.pG.b)
    c.pB = [T(k, "pB%d" % i, [128, 512], F32, "ps") for i in range(6)]
    c.stage = [c.W[0], c.W[1], c.W[2], c.W[3]]
    c.pTP = View(c.pB[0][:, :].bitcast(BF16).rearrange("p (c t) -> p c t", c=8), c.pB[0].b)


def seq_reset0(c, par):
    OP(c, "pool", lambda e: e.memset(c.projF[par][:, :, 0:3], 0.0), [], [c.projF[par]])
    OP(c, "pool", lambda e: e.memset(c.S[:], 0.0), [], [c.S])
    OP(c, "pool", lambda e: e.memset(c.Sbf[:], 0.0), [], [c.Sbf])
    OP(c, "pool", lambda e: e.memset(c.hst[:], 0.0), [], [c.hst])


def act_sigmoid_from(c, out, in_, rin, wout, neg_bias=None):
    ACT(c, out, in_, AF.Exp, rin, wout, scale=-1.0)
    ACT(c, out, out, AF.Ln, wout, wout, bias=1.0)
    ACT(c, out, out, AF.Exp, wout, wout, scale=-1.0)


def layer0_P(c, src_ap, nt, par, prev):
    k = c.k
    xt = c.xt[par]
    st1 = c.st1[par]
    P = slice(0, nt)
    identb = c.identb
    projF = c.projF[par]
    k.dma("sp", xt[P, :], src_ap, w=[xt.b], sbuf=xt.b)
    ACT(c, c.ubP[P, :], xt[P, :], AF.Square, [xt], [c.ubP, st1], accum_out=st1[P, 0:1])
    ACT(c, st1[P, 1:2], st1[P, 0:1], AF.Ln, [st1], [st1], scale=1.0 / D_MODEL, bias=1e-6)
    ACT(c, st1[P, 1:2], st1[P, 1:2], AF.Exp, [st1], [st1], scale=-0.5)
    ub = c.ubP
    TS(c, "dve", ub[P, :], xt[P, :], st1[P, 1:2], None, ALU.mult, None, [xt, st1], [ub])
    yield
    for kc in range(8):
        TR(c, c.pTP[:, kc, P], ub[P, kc * 128:(kc + 1) * 128], identb[P, P], [ub, identb], [c.pTP], sig=(kc == 7))
    uT = c.uTP
    uTv = uT[:].rearrange("p (c t) -> p c t", c=8)
    CP(c, "act", uTv[:, :, P], c.pTP[:, :, P], [c.pTP], [uT])
    yield
    if prev == "meta":
        CP(c, "pool", projF[:, :, 0:3], c.hist_meta[:, :, :], [c.hist_meta], [projF])
    elif prev is not None:
        pp, pnt = prev
        CP(c, "pool", projF[:, :, 0:3], c.projF[pp][:, :, pnt:pnt + 3], [c.projF[pp]], [projF])
    pz = (c.pB[0], c.pB[1])
    gz = c.gz[par]
    dts = c.dts[par]
    for kc in range(8):
        MM(c, c.pB[1][P, 0:16], uTv[:, kc, P], c.w_in[:, kc, 4608:4624], kc == 0, kc == 7, [uT, c.w_in], [c.pB[1]], sig=(kc == 7))
    TT(c, "dve", dts[P, 0:16], c.pB[1][P, 0:16], c.vecT[P, 0:16], ALU.add, [c.pB[1], c.vecT], [dts])
    yield
    ACT(c, dts[P, 0:16], dts[P, 0:16], AF.Exp, [dts], [dts])
    ACT(c, dts[P, 0:16], dts[P, 0:16], AF.Ln, [dts], [dts], bias=1.0)
    TT(c, "dve", dts[P, 16:32], dts[P, 0:16], c.pv[P, 32:48], ALU.mult, [dts, c.pv], [dts])
    yield
    for hf in range(2):
        for kc in range(8):
            MM(c, pz[hf][P, :], uTv[:, kc, P], c.w_in[:, kc, 2048 + hf * 512:2048 + (hf + 1) * 512], kc == 0, kc == 7, [uT, c.w_in], [pz[hf]], sig=(kc == 7))
        yield
    for hf in range(2):
        hs = slice(hf * 512, (hf + 1) * 512)
        act_sigmoid_from(c, c.sgt[P, :], pz[hf][P, :], [pz[hf]], [c.sgt])
        yield
        TT(c, "dve", gz[P, hs], c.sgt[P, :], pz[hf][P, :], ALU.mult, [c.sgt, pz[hf]], [gz])
        yield
    sg = c.sg[par]
    sgv = sg[:].rearrange("p (c t) -> p c t", c=8)
    groups = []
    for g4 in range(2):
        groups.append(("x", [g4 * 4 + i for i in range(4)], 0))
    for g4 in range(2):
        groups.append(("g", [g4 * 4 + i for i in range(4)], 1024))
    for g4 in range(3):
        groups.append(("b", [g4 * 4 + i for i in range(4)], 3072))
    for gi, (kind, ocs, colbase) in enumerate(groups):
        pb = c.pB[gi % 2]
        pbv = pb[:].rearrange("p (c t) -> p c t", c=4)
        for i, oc in enumerate(ocs):
            for kc in range(8):
                MM(c, pbv[:, i, P], c.w_in[:, kc, colbase + oc * 128:colbase + (oc + 1) * 128], uTv[:, kc, P], kc == 0, kc == 7, [uT, c.w_in], [pb], sig=(kc == 7))
            yield
        if kind == "x":
            CP(c, "act", projF[:, ocs[0]:ocs[0] + 4, 3:3 + nt], pbv[:, :, P], [pb], [projF])
        elif kind == "b":
            CP(c, "act", projF[:, 8 + ocs[0]:8 + ocs[0] + 4, 3:3 + nt], pbv[:, :, P], [pb], [projF])
        else:
            o = sgv[:, ocs[0]:ocs[0] + 4, P]
            sc = c.sgt[:].rearrange("p (c t) -> p c t", c=4)[:, :, P]
            act_sigmoid_from(c, sc, pbv[:, :, P], [pb], [c.sgt])
            yield
            TT(c, "dve", o, sc, pbv[:, :, P], ALU.mult, [c.sgt, pb], [sg])
        yield


def layer0_M(c, dst_ap, nt, par, dst_buf):
    k = c.k
    xt = c.xt[par]
    st1 = c.st1[par]
    W = c.W
    H = c.H
    chunks = [(0, nt)] if nt <= 64 else [(0, 64), (64, 128)]
    cw = chunks[0][1]
    nch = len(chunks)
    identb = c.identb
    P = slice(0, nt)
    projF = c.projF[par]
    sg = c.sg[par]
    sgv = sg[:].rearrange("p (c t) -> p c t", c=8)
    gz = c.gz[par]
    dts = c.dts[par]
    sm = c.sm

    lx = W[3]
    lxv = lx[:].rearrange("p (c t) -> p c t", c=8)
    for ch in range(8):
        o = lxv[:, ch, P]
        TS(c, "dve", o, projF[:, ch, 0:nt], c.vecF[:, 16 + ch * 4:16 + ch * 4 + 1], c.vecF[:, 48 + ch:48 + ch + 1], ALU.mult, ALU.add, [projF, c.vecF], [lx])
        for tp in range(1, 4):
            STT(c, "dve", o, projF[:, ch, tp:tp + nt], c.vecF[:, 16 + ch * 4 + tp:16 + ch * 4 + tp + 1], o, ALU.mult, ALU.add, [projF, c.vecF, lx], [lx])
        yield
    yield
    lxb = H[2]
    lxbv = lxb[:].rearrange("p (c t) -> p c t", c=8)
    CP(c, "act", lxbv[:, :, P], lxv[:, :, P], [lx], [lxb])
    ea_ = W[4]
    ex_ = W[5]
    eav = ea_[:].rearrange("p (c t) -> p c t", c=8)
    exv = ex_[:].rearrange("p (c t) -> p c t", c=8)
    for (wt, pbs, ev, boff, et) in ((c.wa, (c.pB[2], c.pB[3]), eav, 16, ea_), (c.wx, (c.pB[4], c.pB[5]), exv, 24, ex_)):
        for oc in range(8):
            g = oc // 2
            pb = pbs[oc // 4]
            pbv = pb[:].rearrange("p (c t) -> p c t", c=4)
            for kc in range(2):
                MM(c, pbv[:, oc % 4, P], wt[:, g, kc, (oc % 2) * 128:(oc % 2 + 1) * 128], lxbv[:, 2 * g + kc, P], kc == 0, kc == 1, [lxb, wt], [pb], sig=(oc % 4 == 3 and kc == 1))
    yield
    xbc = c.xbc
    for ch in range(12):
        o = xbc[:, ch, P]
        TS(c, "dve", o, projF[:, 8 + ch, 0:nt], c.vecF[:, 80 + ch * 4:80 + ch * 4 + 1], c.vecF[:, 128 + ch:128 + ch + 1], ALU.mult, ALU.add, [projF, c.vecF], [xbc])
        for tp in range(1, 4):
            STT(c, "dve", o, projF[:, 8 + ch, tp:tp + nt], c.vecF[:, 80 + ch * 4 + tp:80 + ch * 4 + tp + 1], o, ALU.mult, ALU.add, [projF, c.vecF, xbc], [xbc])
        yield
    for (wt, pbs, ev, boff, et) in ((c.wa, (c.pB[2], c.pB[3]), eav, 16, ea_), (c.wx, (c.pB[4], c.pB[5]), exv, 24, ex_)):
        for oc in range(8):
            pb = pbs[oc // 4]
            pbv = pb[:].rearrange("p (c t) -> p c t", c=4)
            ACT(c, ev[:, oc, P], pbv[:, oc % 4, P], AF.Exp, [pb, c.pv], [et], scale=-1.0, bias=c.pv[:, boff + oc:boff + oc + 1])
            if oc % 4 == 3:
                yield
        ACT(c, ev[:, :, P], ev[:, :, P], AF.Ln, [et], [et], bias=1.0)
        ACT(c, ev[:, :, P], ev[:, :, P], AF.Exp, [et], [et], scale=-1.0)
    yield
    e0 = W[0][:].rearrange("p (c t) -> p c t", c=8)
    e1 = W[1][:].rearrange("p (c t) -> p c t", c=8)
    act_sigmoid_from(c, e0[:, :, P], xbc[:, 0:8, P], [xbc], [W[0]])
    act_sigmoid_from(c, e1[:, 0:4, P], xbc[:, 8:12, P], [xbc], [W[1]])
    yield
    xsT = H[3][:].rearrange("p (c t) -> p c t", c=8)
    bcT = H[5][:].rearrange("p (c t) -> p c t", c=4)
    TT(c, "dve", xsT[:, :, P], e0[:, :, P], xbc[:, 0:8, P], ALU.mult, [W[0], xbc], [H[3]])
    TT(c, "dve", bcT[:, 0:4, P], e1[:, 0:4, P], xbc[:, 8:12, P], ALU.mult, [W[1], xbc], [H[5]])
    yield
    a_ = W[2]
    av = a_[:].rearrange("p (c t) -> p c t", c=8)
    a2_ = W[1]
    a2v = a2_[:].rearrange("p (c t) -> p c t", c=8)
    for ch in range(8):
        ACT(c, av[:, ch, P], eav[:, ch, P], AF.Exp, [ea_, c.pv], [a_], scale=c.pv[:, ch:ch + 1])
        ACT(c, a2v[:, ch, P], eav[:, ch, P], AF.Exp, [ea_, c.pv], [a2_], scale=c.pv[:, 8 + ch:8 + ch + 1])
        if ch % 4 == 3:
            yield
    for ch in range(8):
        TR(c, c.pT[P, ch, :], xsT[:, ch, P], identb[:, :], [H[3], identb], [c.pT], sig=(ch == 7))
    for ch in range(2):
        TR(c, c.pT2[P, ch, :], bcT[:, ch, P], identb[:, :], [H[5], identb], [c.pT2], sig=(ch == 1))
    yield
    TS(c, "dve", a2v[:, :, P], a2v[:, :, P], -1.0, 1.0, ALU.mult, ALU.add, [a2_], [a2_])
    ACT(c, a2v[:, :, P], a2v[:, :, P], AF.Ln, [a2_], [a2_])
    ACT(c, a2v[:, :, P], a2v[:, :, P], AF.Exp, [a2_], [a2_], scale=0.5)
    yield
    Xps = c.pT[:].rearrange("p c t -> p (c t)")
    Xdt = H[0]
    TT(c, "dve", Xdt[P, :].rearrange("p (h d) -> p h d", h=16), Xps[P, :].rearrange("p (h d) -> p h d", h=16),
       dts[P, 0:16].unsqueeze(2).to_broadcast([nt, 16, 64]), ALU.mult, [c.pT, dts], [Xdt])
    skip = W[0]
    TT(c, "dve", skip[P, :].rearrange("p (h d) -> p h d", h=16), Xps[P, :].rearrange("p (h d) -> p h d", h=16),
       c.vecT[P, 32:48].unsqueeze(2).to_broadcast([nt, 16, 64]), ALU.mult, [c.pT, c.vecT], [skip])
    yield
    Btok = H[1]
    CP(c, "act", Btok[P, 0:256], c.pT2[P, :, :].rearrange("p c t -> p (c t)"), [c.pT2], [Btok])
    MM(c, c.pG[P, 16:32], c.L1[P, P], dts[P, 16:32], True, True, [c.L1, dts], [c.pG], sig=False)
    MM(c, c.pG[P, 32:48], c.L2[P, P], dts[P, 16:32], True, True, [c.L2, dts], [c.pG], sig=False)
    for ci in range(nch):
        MM(c, c.pG[:, 48 + 16 * ci:64 + 16 * ci], c.L4[P, ci, :], dts[P, 16:32], True, True, [c.L4, dts], [c.pG], sig=(ci == nch - 1))
    ACT(c, sm[P, 32:48], c.pG[P, 16:32], AF.Exp, [c.pG], [sm])
    ACT(c, sm[P, 48:64], c.pG[P, 32:48], AF.Exp, [c.pG], [sm])
    ACT(c, sm[:, 64:64 + 16 * nch], c.pG[:, 48:48 + 16 * nch], AF.Exp, [c.pG], [sm])
    yield
    TT(c, "dve", exv[:, :, P], exv[:, :, P], a2v[:, :, P], ALU.mult, [ex_, a2_], [ex_])
    TT(c, "dve", exv[:, :, P], exv[:, :, P], lxv[:, :, P], ALU.mult, [ex_, lx], [ex_])
    for ch in range(8):
        OP(c, "dve", lambda e, ch=ch: e.tensor_tensor_scan(out=eav[:, ch, P], data0=av[:, ch, P], data1=exv[:, ch, P], initial=c.hst[:, ch:ch + 1], op0=ALU.mult, op1=ALU.add), [a_, ex_, c.hst], [ea_])
        if ch % 4 == 3:
            yield
    CP(c, "dve", c.hst[:, :], eav[:, :, nt - 1], [ea_], [c.hst])
    mixA = H[4][:].rearrange("p (c t) -> p c t", c=8)
    mixB = H[2][:].rearrange("p (c t) -> p c t", c=8)
    TT(c, "pool", mixA[:, :, P], eav[:, :, P], sgv[:, :, P], ALU.mult, [ea_, sg], [H[4]])
    yield
    R1 = W[1]
    R1v = R1[:, 0:16 * cw].rearrange("p (h l) -> p h l", h=16)
    TT(c, "dve", R1v[P, :, :], dts[P, 16:32].unsqueeze(2).to_broadcast([nt, 16, cw]),
       c.mle[P, 0:cw].unsqueeze(1).to_broadcast([nt, 16, cw]), ALU.mult, [dts, c.mle], [R1])
    for hf in range(2):
        pb = c.pB[2 + hf]
        MM(c, pb[P, 0:8 * cw], c.L2[P, P], R1[P, hf * 8 * cw:(hf + 1) * 8 * cw], True, True, [c.L2, R1], [pb], sig=True)
    dec = W[2]
    decv = dec[:, 0:16 * cw].rearrange("p (h l) -> p h l", h=16)
    for hf in range(2):
        pb = c.pB[2 + hf]
        ACT(c, dec[P, hf * 8 * cw:(hf + 1) * 8 * cw], pb[P, 0:8 * cw], AF.Exp, [pb], [dec])
    yield
    cbps = c.pG[:, 128:256].rearrange("p (g l) -> p g l", g=2)
    for ci, (p0, p1) in enumerate(chunks):
        for g in range(2):
            MM(c, cbps[p0:p1, g, 0:cw], bcT[:, g, p0:p1], bcT[:, 2 + g, p0:p1], True, True, [H[5]], [c.pG], sig=(ci == nch - 1 and g == 1))
    TT(c, "dve", c.cbm[P, :, 0:cw], cbps[P, :, 0:cw], c.mle[P, 0:cw].unsqueeze(1).to_broadcast([nt, 2, cw]), ALU.mult, [c.pG, c.mle], [c.cbm])
    yield
    MT = H[3]
    MTv = MT[:, 0:16 * cw].rearrange("p (h l) -> p h l", h=16)
    for g in range(2):
        TT(c, "dve", MTv[P, g * 8:(g + 1) * 8, :], decv[P, g * 8:(g + 1) * 8, :],
           c.cbm[P, g:g + 1, 0:cw].to_broadcast([nt, 8, cw]), ALU.mult, [dec, c.cbm], [MT])
    yield
    Xd = H[2]
    TT(c, "pool", Xd[P, :].rearrange("p (h d) -> p h d", h=16), Xdt[P, :].rearrange("p (h d) -> p h d", h=16),
       sm[P, 48:64].unsqueeze(2).to_broadcast([nt, 16, 64]), ALU.mult, [Xdt, sm], [H[2]])
    for ci, (p0, p1) in enumerate(chunks):
        for h in range(16):
            pb = c.pB[4 + h // 8]
            MM(c, pb[p0:p1, (h % 8) * 64:(h % 8 + 1) * 64], MTv[p0:p1, h, :], Xdt[p0:p1, h * 64:(h + 1) * 64], True, True, [MT, Xdt], [pb],
               sig=(h % 8 == 7))
        yield
    y = W[1]
    for ci, (p0, p1) in enumerate(chunks):
        PC = slice(p0, p1)
        ncw = p1 - p0
        for g in range(2):
            pb = c.pB[2 + g]
            MM(c, pb[p0:p1, :], bcT[:, 2 + g, p0:p1], c.Sbf[:, g * 512:(g + 1) * 512], True, True, [H[5], c.Sbf], [pb], sig=True)
        yield
        for g in range(2):
            gs = slice(g * 512, (g + 1) * 512)
            TT(c, "dve", y[PC, gs].rearrange("p (h d) -> p h d", h=8), c.pB[2 + g][PC, :].rearrange("p (h d) -> p h d", h=8),
               sm[PC, 32 + 8 * g:40 + 8 * g].unsqueeze(2).to_broadcast([ncw, 8, 64]), ALU.mult, [c.pB[2 + g], sm], [y])
        yield
        for g in range(2):
            pb = c.pB[2 + g]
            MM(c, pb[:, :], Btok[p0:p1, g * 128:(g + 1) * 128], Xd[p0:p1, g * 512:(g + 1) * 512], True, True, [Btok, H[2]], [pb], sig=True)
        TT(c, "pool", c.S[:].rearrange("p (h d) -> p h d", h=16), c.S[:].rearrange("p (h d) -> p h d", h=16),
           sm[:, 64 + 16 * ci:80 + 16 * ci].unsqueeze(2).to_broadcast([128, 16, 64]), ALU.mult, [c.S, sm], [c.S])
        yield
        for g in range(2):
            gs = slice(g * 512, (g + 1) * 512)
            TT(c, "dve", c.S[:, gs], c.S[:, gs], c.pB[2 + g][:, :], ALU.add, [c.S, c.pB[2 + g]], [c.S])
        CP(c, "act", c.Sbf[:], c.S[:], [c.S], [c.Sbf])
        yield
    for g in range(2):
        gs = slice(g * 512, (g + 1) * 512)
        TT(c, "dve", y[P, gs], y[P, gs], c.pB[4 + g][P, :], ALU.add, [y, c.pB[4 + g]], [y])
    yield
    yield
    TT(c, "dve", y[P, :], y[P, :], skip[P, :], ALU.add, [y, skip], [y])
    TT(c, "dve", y[P, :], y[P, :], gz[P, :], ALU.mult, [y, gz], [y])
    for g in range(2):
        gs = slice(g * 512, (g + 1) * 512)
        ACT(c, W[4][P, gs], y[P, gs], AF.Square, [y], [W[4], st1], accum_out=st1[P, 2 + g:3 + g])
    ACT(c, st1[P, 4:6], st1[P, 2:4], AF.Ln, [st1], [st1], scale=1.0 / 512, bias=1e-6)
    ACT(c, st1[P, 4:6], st1[P, 4:6], AF.Exp, [st1], [st1], scale=-0.5)
    yield
    yb = H[0]
    for g in range(2):
        gs = slice(g * 512, (g + 1) * 512)
        TS(c, "dve", yb[P, gs], y[P, gs], st1[P, 4 + g:5 + g], None, ALU.mult, None, [y, st1], [yb])
    for ch in range(8):
        TR(c, c.pT[:, ch, P], yb[P, ch * 128:(ch + 1) * 128], identb[P, P], [yb, identb], [c.pT], sig=(ch == 7))
    CP(c, "act", mixB[:, :, P], c.pT[:, :, P], [c.pT], [H[2]])
    yield
    for hf in range(2):
        pb = c.pB[2 + hf]
        for kc in range(16):
            lhs = mixA[:, kc, P] if kc < 8 else mixB[:, kc - 8, P]
            MM(c, pb[P, :], lhs, c.w_out[:, kc, hf * 512:(hf + 1) * 512], kc == 0, kc == 15, [H[4], H[2], c.w_out], [pb], sig=(kc == 15))
    yield
    for hf in range(2):
        hs = slice(hf * 512, (hf + 1) * 512)
        TT(c, "dve", xt[P, hs], xt[P, hs], c.pB[2 + hf][P, :], ALU.add, [xt, c.pB[2 + hf]], [xt])
    k.dma("sp", dst_ap, xt[P, :], r=[xt.b], w=[dst_buf], sbuf=xt.b)


def interleave(gm, gp, ratio=1):
    am, ap = gm is not None, gp is not None
    while am or ap:
        if am:
            for _ in range(ratio):
                try:
                    next(gm)
                except StopIteration:
                    am = False
                    break
        if ap:
            try:
                next(gp)
            except StopIteration:
                ap = False


def layer0_seq(c, tiles, par0, dst_buf, first_seq=True):
    par = par0
    n = len(tiles)
    if first_seq:
        seq_reset0(c, par)
        interleave(None, layer0_P(c, tiles[0][0], tiles[0][2], par, None))
        start = 0
    else:
        CP(c, "pool", c.S[:], c.S_meta[:], [c.S_meta], [c.S])
        CP(c, "act", c.Sbf[:], c.S_meta[:], [c.S_meta], [c.Sbf])
        CP(c, "pool", c.hst[:], c.hst_meta[:], [c.hst_meta], [c.hst])
        interleave(None, layer0_P(c, tiles[1][0], tiles[1][2], par, "meta"))
        start = 1
    for j in range(start, n):
        gp = layer0_P(c, tiles[j + 1][0], tiles[j + 1][2], par ^ 1, (par, tiles[j][2])) if j + 1 < n else None
        gm = layer0_M(c, tiles[j][1], tiles[j][2], par, dst_buf)
        interleave(gm, gp)
        if first_seq and j == 0:
            CP(c, "pool", c.S_meta[:], c.S[:], [c.S], [c.S_meta])
            CP(c, "pool", c.hst_meta[:], c.hst[:], [c.hst], [c.hst_meta])
            CP(c, "pool", c.hist_meta[:, :, :], c.projF[par][:, :, tiles[0][2]:tiles[0][2] + 3], [c.projF[par]], [c.hist_meta])
        par ^= 1
    return par


LTOT = 2064
BIG = 30000.0


def setup_layer1(c, D, L):
    k = c.k
    c.w_in1 = T(k, "w_in1", [128, 8, 4096], BF16)
    c.w_out1 = T(k, "w_out1", [128, 8, 1024], BF16)
    c.vecF = T(k, "vec1F", [128, 8], F32)
    c.fn = T(k, "fn", [128, 1024], F32)
    k.dma("sp", c.vecF[:], D["vec1F"][:, :], w=[c.vecF.b], sbuf=c.vecF.b)
    k.dma("sp", c.fn[:], D["fnorm"][0:1, :].partition_broadcast(128), w=[c.fn.b], sbuf=c.fn.b)
    st = c.stage
    dsts, rows, scs = [], [], []
    for kc in range(8):
        for hf in range(8):
            dsts.append(c.w_in1[:, kc, hf * 512:(hf + 1) * 512])
            rows.append(D["w_in1"][kc * 128:(kc + 1) * 128, hf * 512:(hf + 1) * 512])
            scs.append(c.vecF[:, kc:kc + 1])
    load_cast_weight(c, c.w_in1, dsts, rows, 512, scs, st)
    dsts, rows, scs = [], [], []
    for kc in range(8):
        for hf in range(2):
            dsts.append(c.w_out1[:, kc, hf * 512:(hf + 1) * 512])
            rows.append(D["w_out1"][kc * 128:(kc + 1) * 128, hf * 512:(hf + 1) * 512])
            scs.append(None)
    load_cast_weight(c, c.w_out1, dsts, rows, 512, scs, st)
    c.negm = T(k, "negm", [128, 4, 128], BF16)
    c.tri2 = T(k, "tri2", [128, 128], BF16)
    c.zrow = T(k, "zrow", [1, 256], BF16)
    c.tri = T(k, "tri", [128, 128], BF16)
    c.ones = T(k, "ones", [128, 2], BF16)
    OP(c, "pool", lambda e: e.memset(c.negm[:], 0.0), [], [c.negm])
    OP(c, "pool", lambda e: e.memset(c.tri2[:], 1.0), [], [c.tri2])
    OP(c, "pool", lambda e: e.memset(c.zrow[:], 0.0), [], [c.zrow])
    OP(c, "pool", lambda e: e.memset(c.tri[:], 1.0), [], [c.tri])
    OP(c, "pool", lambda e: e.memset(c.ones[:], 1.0), [], [c.ones])
    for i in range(4):
        OP(c, "pool", lambda e, i=i: e.affine_select(out=c.negm[:, i, :], in_=c.negm[:, i, :], pattern=[[1, 128]], compare_op=ALU.is_gt, fill=-BIG, base=0, channel_multiplier=-1), [c.negm], [c.negm])
    OP(c, "pool", lambda e: e.affine_select(out=c.tri[:], in_=c.tri[:], pattern=[[-1, 128]], compare_op=ALU.is_ge, fill=0.0, base=0, channel_multiplier=1), [c.tri], [c.tri])
    OP(c, "pool", lambda e: e.affine_select(out=c.tri2[:], in_=c.tri2[:], pattern=[[1, 128]], compare_op=ALU.is_gt, fill=0.0, base=0, channel_multiplier=-1), [c.tri2], [c.tri2])
    nkb = 1 + (L - 16) // 128
    c.KT = T(k, "KT", [128, 8, L], BF16)
    c.V = T(k, "V", [128, nkb, 1024], BF16)
    c.KTb = [Buf("KTb%d" % i) for i in range(nkb)]
    c.Vb = [Buf("Vb%d" % i) for i in range(nkb)]


def alloc_layer1_work(c):
    k = c.k
    c.ht = [T(k, "ht%d" % i, [128, 1024], F32) for i in range(2)]
    c.st2 = [T(k, "st2_%d" % i, [128, 8], F32) for i in range(2)]
    c.ub1 = T(k, "ub1", [128, 1024], BF16)
    c.uT1 = T(k, "uT1", [128, 1024], BF16)
    c.QTs = [T(k, "QTs%d" % i, [128, 8, 2, 128], BF16) for i in range(2)]
    for i in range(2):
        OP(c, "pool", lambda e, i=i: e.memset(c.QTs[i][:], 0.0), [], [c.QTs[i]])
    c.sgt1 = T(k, "sgt1", [128, 512], F32)
    c.sgz = [T(k, "sgz%d" % i, [128, 1024], BF16) for i in range(2)]
    c.E = [T(k, "E%d" % i, [128, 512], F32) for i in range(4)]
    c.SP = [T(k, "SP%d" % i, [128, 512], BF16) for i in range(4)]
    c.X = [T(k, "X%d" % i, [128, 512], F32) for i in range(2)]
    c.Wt = [T(k, "Wt%d" % i, [128, 512], BF16) for i in range(2)]
    c.ob = [T(k, "ob%d" % i, [128, 1024], BF16) for i in range(2)]
    c.oT = T(k, "oT", [128, 1024], BF16)
    c.B = [T(k, "B%d" % i, [128, 512], F32, "ps") for i in range(8)]
    c.pT1 = View(c.B[6][:, :].bitcast(BF16).rearrange("p (c t) -> p c t", c=8), c.B[6].b)
    c.pTo = View(c.B[0][:, :].bitcast(BF16).rearrange("p (c t) -> p c t", c=8), c.B[0].b)


def l1_proj_gen(c, src_ap, nt, j, par, src_buf):
    k = c.k
    ht = c.ht[par]
    st2 = c.st2[par]
    P = slice(0, nt)
    identb = c.identb
    pos0 = 0 if j == 0 else 16 + (j - 1) * 128
    B = c.B
    k.dma("sp", ht[P, :], src_ap, r=[src_buf], w=[ht.b], sbuf=ht.b)
    ub = c.ub1
    ACT(c, ub[P, :], ht[P, :], AF.Square, [ht], [ub, st2], accum_out=st2[P, 0:1])
    ACT(c, st2[P, 1:2], st2[P, 0:1], AF.Ln, [st2], [st2], scale=1.0 / 1024, bias=1e-6)
    ACT(c, st2[P, 1:2], st2[P, 1:2], AF.Exp, [st2], [st2], scale=-0.5)
    TS(c, "dve", ub[P, :], ht[P, :], st2[P, 1:2], None, ALU.mult, None, [ht, st2], [ub])
    yield
    for kc in range(8):
        TR(c, c.pT1[:, kc, P], ub[P, kc * 128:(kc + 1) * 128], identb[P, P], [ub, identb], [c.pT1], sig=(kc == 7))
    uTv = c.uT1[:].rearrange("p (c t) -> p c t", c=8)
    CP(c, "dve", uTv[:, :, P], c.pT1[:, :, P], [c.pT1], [c.uT1])
    yield
    w = c.w_in1
    for g4 in range(2):
        pb = B[7 - g4]
        pbv = pb[:].rearrange("p (c t) -> p c t", c=4)
        for i in range(4):
            oc = g4 * 4 + i
            for kc in range(8):
                MM(c, pbv[:, i, P], w[:, kc, 1024 + oc * 128:1024 + (oc + 1) * 128], uTv[:, kc, P], kc == 0, kc == 7, [c.uT1, w], [pb], sig=(kc == 7))
            yield
        CP(c, "dve", c.KT[:, g4 * 4:(g4 + 1) * 4, pos0:pos0 + nt], pbv[:, :, P], [pb], [c.KTb[j]])
        yield
    for hf in range(2):
        pb = B[7 - hf]
        for kc in range(8):
            MM(c, pb[P, :], uTv[:, kc, P], w[:, kc, 2048 + hf * 512:2048 + (hf + 1) * 512], kc == 0, kc == 7, [c.uT1, w], [pb], sig=(kc == 7))
        yield
        CP(c, "dve", c.V[P, j, hf * 512:(hf + 1) * 512], pb[P, :], [pb], [c.Vb[j]])
        yield
    if j == 0:
        return
    QTs = c.QTs[par]
    for g4 in range(2):
        pb = B[7 - g4]
        pbv = pb[:].rearrange("p (c t) -> p c t", c=4)
        for i in range(4):
            oc = g4 * 4 + i
            for kc in range(8):
                MM(c, pbv[:, i, P], w[:, kc, oc * 128:(oc + 1) * 128], uTv[:, kc, P], kc == 0, kc == 7, [c.uT1, w], [pb], sig=(kc == 7))
            yield
        for hf_ in range(2):
            hp = slice(hf_ * 64, (hf_ + 1) * 64)
            TS(c, "dve", QTs[hp, g4 * 4:(g4 + 1) * 4, hf_, P], pbv[hp, :, P], 0.125, None, ALU.mult, None, [pb], [QTs])
        yield
    sgz = c.sgz[par]
    for hf in range(2):
        pb = B[7 - hf]
        hs = slice(hf * 512, (hf + 1) * 512)
        for kc in range(8):
            MM(c, pb[P, :], uTv[:, kc, P], w[:, kc, 3072 + hf * 512:3072 + (hf + 1) * 512], kc == 0, kc == 7, [c.uT1, w], [pb], sig=(kc == 7))
        yield
        ACT(c, c.sgt1[P, :], pb[P, :], AF.Exp, [pb], [c.sgt1], scale=-1.0)
        TS(c, "dve", c.sgt1[P, :], c.sgt1[P, :], 1.0, None, ALU.add, None, [c.sgt1], [c.sgt1])
        OP(c, "dve", lambda e: e.reciprocal(out=c.sgt1[P, :], in_=c.sgt1[P, :]), [c.sgt1], [c.sgt1])
        TT(c, "dve", sgz[P, hs], c.sgt1[P, :], pb[P, :], ALU.mult, [c.sgt1, pb], [sgz])
        yield


def run_gen(g, n=None):
    if g is None:
        return False
    try:
        if n is None:
            while True:
                next(g)
        for _ in range(n):
            next(g)
    except StopIteration:
        return False
    return True


def layer1_attn(c, nt, j, par, gen_next):
    k = c.k
    ht = c.ht[par]
    st2 = c.st2[par]
    P = slice(0, nt)
    identb = c.identb
    B = c.B
    QTs_ = c.QTs[par]
    sgz_ = c.sgz[par]

    units = []
    for pr in range(2):
        for kb in range(j, -1, -1):
            for q in range(2):
                units.append((kb, 2 * pr + q, q))
    nu = len(units)

    def kinfo(kb):
        if kb == 0:
            return 16, 0
        return 128, 16 + (kb - 1) * 128

    def views(u):
        kb, hg, q = units[u]
        ks, kp = kinfo(kb)
        return kb, hg, q, ks, kp

    def stageA(u):
        kb, hg, q, ks, kp = views(u)
        diag = (kb == j)
        z = B[u % 2]
        zv = z[:].rearrange("p (i t) -> p i t", i=4)
        first = True
        if diag:
            MM(c, z[0:ks, :], identb[0:ks, 0:ks], c.negm[0:ks, :, :].rearrange("p i t -> p (i t)"), True, False, [identb, c.negm], [z], sig=False)
            first = False
        for i2 in range(2):
            ch = 2 * hg + i2
            MM(c, z[0:ks, 2 * i2 * 128:(2 * i2 + 2) * 128], c.KT[:, ch, kp:kp + ks], QTs_[:, ch, :, :].rearrange("p a t -> p (a t)"), first, (i2 == 1) or first, [c.KTb[kb], QTs_], [z], sig=(i2 == 1))
        E = c.E[u % 4]
        SP = c.SP[u % 4]
        Ev = E[:].rearrange("p (i t) -> p i t", i=4)
        SPv = SP[:].rearrange("p (i t) -> p i t", i=4)
        ACT(c, Ev[0:ks, :, P], zv[0:ks, :, P], AF.Exp, [z], [E])
        ACT(c, SPv[0:ks, :, P], Ev[0:ks, :, P], AF.Ln, [E], [SP], bias=1.0)

    def stageB(u):
        kb, hg, q, ks, kp = views(u)
        tb = B[2 + q]
        tv = tb[:].rearrange("p (i t) -> p i t", i=4)
        SP = c.SP[u % 4]
        SPv = SP[:].rearrange("p (i t) -> p i t", i=4)
        MM(c, tb[0:ks, :], c.tri[0:ks, 0:ks], SP[0:ks, :], kb == j, False, [c.tri, SP], [tb], sig=True, skip=True)
        X = c.X[u % 2]
        Xv = X[:].rearrange("p (i t) -> p i t", i=4)
        ACT(c, Xv[0:ks, :, P], tv[0:ks, :, P], AF.Exp, [tb], [X], scale=-1.0)

    def stageC(u):
        kb, hg, q, ks, kp = views(u)
        tb = B[2 + q]
        tv = tb[:].rearrange("p (i t) -> p i t", i=4)
        SP = c.SP[u % 4]
        SPv = SP[:].rearrange("p (i t) -> p i t", i=4)
        if kb > 0:
            MM(c, tb[0:ks, :], c.tri2[0:ks, 0:ks], SP[0:ks, :], False, False, [c.tri2, SP], [tb], sig=True, skip=True)
        E = c.E[u % 4]
        X = c.X[u % 2]
        Wt = c.Wt[u % 2]
        Ev = E[:].rearrange("p (i t) -> p i t", i=4)
        Xv = X[:].rearrange("p (i t) -> p i t", i=4)
        Wv = Wt[:].rearrange("p (i t) -> p i t", i=4)
        TT(c, "dve", Wv[0:ks, :, P], Ev[0:ks, :, P], Xv[0:ks, :, P], ALU.mult, [E, X], [Wt])

    def stageD(u):
        kb, hg, q, ks, kp = views(u)
        ob_ = B[4 + q]
        Wt = c.Wt[u % 2]
        Wv = Wt[:].rearrange("p (i t) -> p i t", i=4)
        if kb == j:
            MM(c, ob_[P, 0:256], c.zrow[0:1, P], c.zrow[0:1, 0:256], True, False, [c.zrow], [ob_], sig=False)
        for i in range(4):
            hd = 4 * hg + i
            MM(c, ob_[P, i * 64:(i + 1) * 64], Wv[0:ks, i, P], c.V[0:ks, kb, hd * 64:(hd + 1) * 64], False, (kb == 0 and i == 3), [Wt, c.Vb[kb]], [ob_], sig=(i == 3))
        if kb == 0:
            hsl = slice(hg * 256, (hg + 1) * 256)
            TT(c, "dve", c.ob[par][P, hsl], ob_[P, 0:256], sgz_[P, hsl], ALU.mult, [ob_, sgz_], [c.ob[par]])

    per = max(1, -(-52 // max(1, nu - 2)))
    for step in range(nu + 3):
        if step < nu:
            stageA(step)
        if 0 <= step - 1 < nu:
            stageB(step - 1)
        if 0 <= step - 2 < nu:
            stageC(step - 2)
        if 0 <= step - 3 < nu:
            stageD(step - 3)
        run_gen(gen_next, per)
    run_gen(gen_next, None)


def l1_tail_gen(c, dst_ap, nt, par):
    k = c.k
    ht = c.ht[par]
    st2 = c.st2[par]
    ob = c.ob[par]
    P = slice(0, nt)
    identb = c.identb
    B = c.B
    for kc in range(8):
        TR(c, c.pT1[:, kc, P], ob[P, kc * 128:(kc + 1) * 128], identb[P, P], [ob, identb], [c.pT1], sig=(kc == 7))
    oTv = c.oT[:].rearrange("p (c t) -> p c t", c=8)
    CP(c, "dve", oTv[:, :, P], c.pT1[:, :, P], [c.pT1], [c.oT])
    yield
    for hf in range(2):
        pb = B[6 + hf]
        for kc in range(8):
            MM(c, pb[P, :], oTv[:, kc, P], c.w_out1[:, kc, hf * 512:(hf + 1) * 512], kc == 0, kc == 7, [c.oT, c.w_out1], [pb], sig=(kc == 7))
        yield
    for hf in range(2):
        hs = slice(hf * 512, (hf + 1) * 512)
        TT(c, "dve", ht[P, hs], ht[P, hs], B[6 + hf][P, :], ALU.add, [ht, B[6 + hf]], [ht])
    yield
    ACT(c, c.oT[P, :], ht[P, :], AF.Square, [ht], [c.oT, st2], accum_out=st2[P, 2:3])
    ACT(c, st2[P, 3:4], st2[P, 2:3], AF.Ln, [st2], [st2], scale=1.0 / 1024, bias=1e-6)
    ACT(c, st2[P, 3:4], st2[P, 3:4], AF.Exp, [st2], [st2], scale=-0.5)
    yield
    STT(c, "dve", ht[P, :], ht[P, :], st2[P, 3:4], c.fn[P, :], ALU.mult, ALU.mult, [ht, st2, c.fn], [ht])
    k.dma("sp", dst_ap, ht[P, :], r=[ht.b], sbuf=ht.b)
    yield


def chain_gens(*gens):
    for g in gens:
        if g is not None:
            yield from g


def layer1_seq(c, h1s, outs, nfull, par0, src_buf, first_seq=True):
    par = par0
    if first_seq:
        run_gen(l1_proj_gen(c, h1s[0], 16, 0, par, src_buf), None)
        par ^= 1
    run_gen(l1_proj_gen(c, h1s[1], 128, 1, par, src_buf), None)
    for j in range(1, nfull + 1):
        gt = l1_tail_gen(c, outs[j - 1], 128, par ^ 1) if j >= 2 else None
        gp = l1_proj_gen(c, h1s[j + 1], 128, j + 1, par ^ 1, src_buf) if j + 1 <= nfull else None
        layer1_attn(c, 128, j, par, chain_gens(gt, gp))
        par ^= 1
    run_gen(l1_tail_gen(c, outs[nfull], 128, par ^ 1), None)
    return par

NSEQ = 4
NFULL = 16
LTOT_ = 16 + NFULL * 128


def common_setup(c, sw=2312):
    k = c.k
    if sw:
        c.stage = [T(k, "stage%d" % i, [128, sw], F32) for i in range(2)]
    ident = T(k, "ident", [128, 128], F32)
    c.identb = T(k, "identb", [128, 128], BF16)
    OP(c, "pool", lambda e: e.memset(ident[:], 1.0), [], [ident])
    OP(c, "pool", lambda e: e.affine_select(out=ident[:], in_=ident[:], pattern=[[-1, 128]], compare_op=ALU.is_equal, fill=0.0, base=0, channel_multiplier=1), [ident], [ident])
    CP(c, "dve", c.identb[:], ident[:], [ident], [c.identb])


def build_l0(nseq, nfull):
    nc = bass.Bass("TRN2", target_bir_lowering=False)
    L = 16 + nfull * 128
    D = {}
    x = nc.dram_tensor("x", [nseq, nfull * 128, 1024], F32, kind="ExternalInput").ap()
    meta = nc.dram_tensor("meta", [16, 1024], F32, kind="ExternalInput").ap()
    D["w_in"] = nc.dram_tensor("w_in", [1024, 4624], F32, kind="ExternalInput").ap()
    D["w_out"] = nc.dram_tensor("w_out", [2048, 1024], F32, kind="ExternalInput").ap()
    D["wa"] = nc.dram_tensor("wa", [4, 256, 256], F32, kind="ExternalInput").ap()
    D["wx"] = nc.dram_tensor("wx", [4, 256, 256], F32, kind="ExternalInput").ap()
    D["vecF"] = nc.dram_tensor("vecF", [128, 140], F32, kind="ExternalInput").ap()
    D["vecT"] = nc.dram_tensor("vecT", [1, 48], F32, kind="ExternalInput").ap()
    h1 = nc.dram_tensor("h1", [nseq, L, 1024], F32, kind="ExternalOutput").ap()
    with ExitStack() as es:
        k = K(nc, es)
        c = setup_common(k, nc)
        common_setup(c, 0)
        alloc_layer0_work(c)
        setup_layer0(c, D)
        hb = Buf("h1dram")
        par = 0
        for s in range(nseq):
            tiles = [(meta[:, :], h1[s, 0:16, :], 16)]
            for j in range(nfull):
                tiles.append((x[s, j * 128:(j + 1) * 128, :], h1[s, 16 + j * 128:16 + (j + 1) * 128, :], 128))
            par = layer0_seq(c, tiles, par, hb, first_seq=(s == 0))
        k.final_wait("sp", [c.xt[0].b, c.xt[1].b])
        k.emit()
    return nc


def build_l1(nseq, nfull):
    nc = bass.Bass("TRN2", target_bir_lowering=False)
    L = 16 + nfull * 128
    D = {}
    h1 = nc.dram_tensor("h1", [nseq, L, 1024], F32, kind="ExternalInput").ap()
    D["w_in1"] = nc.dram_tensor("w_in1", [1024, 4096], F32, kind="ExternalInput").ap()
    D["w_out1"] = nc.dram_tensor("w_out1", [1024, 1024], F32, kind="ExternalInput").ap()
    D["vec1F"] = nc.dram_tensor("vec1F", [128, 8], F32, kind="ExternalInput").ap()
    D["fnorm"] = nc.dram_tensor("fnorm", [1, 1024], F32, kind="ExternalInput").ap()
    out = nc.dram_tensor("out", [nseq, nfull * 128, 1024], F32, kind="ExternalOutput").ap()
    with ExitStack() as es:
        k = K(nc, es)
        c = setup_common(k, nc)
        common_setup(c, 0)
        alloc_layer1_work(c)
        c.stage = list(c.E)
        setup_layer1(c, D, L)
        hb = Buf("h1dram")
        par = 0
        for s in range(nseq):
            h1s = [h1[s, 0:16, :]] + [h1[s, 16 + (j - 1) * 128:16 + j * 128, :] for j in range(1, nfull + 1)]
            outs = [None] + [out[s, (j - 1) * 128:j * 128, :] for j in range(1, nfull + 1)]
            par = layer1_seq(c, h1s, outs, nfull, par, hb, first_seq=(s == 0))
        k.final_wait("sp", [c.ht[0].b, c.ht[1].b])
        k.emit()
    return nc


def build_fused(nseq, nfull):
    nc = bass.Bass("TRN2", target_bir_lowering=False)
    L = 16 + nfull * 128
    D = {}
    x = nc.dram_tensor("x", [nseq, nfull * 128, 1024], F32, kind="ExternalInput").ap()
    meta = nc.dram_tensor("meta", [16, 1024], F32, kind="ExternalInput").ap()
    D["w_in"] = nc.dram_tensor("w_in", [1024, 4624], F32, kind="ExternalInput").ap()
    D["w_out"] = nc.dram_tensor("w_out", [2048, 1024], F32, kind="ExternalInput").ap()
    D["wa"] = nc.dram_tensor("wa", [4, 256, 256], F32, kind="ExternalInput").ap()
    D["wx"] = nc.dram_tensor("wx", [4, 256, 256], F32, kind="ExternalInput").ap()
    D["vecF"] = nc.dram_tensor("vecF", [128, 140], F32, kind="ExternalInput").ap()
    D["vecT"] = nc.dram_tensor("vecT", [1, 48], F32, kind="ExternalInput").ap()
    D["w_in1"] = nc.dram_tensor("w_in1", [1024, 4096], F32, kind="ExternalInput").ap()
    D["w_out1"] = nc.dram_tensor("w_out1", [1024, 1024], F32, kind="ExternalInput").ap()
    D["vec1F"] = nc.dram_tensor("vec1F", [128, 8], F32, kind="ExternalInput").ap()
    D["fnorm"] = nc.dram_tensor("fnorm", [1, 1024], F32, kind="ExternalInput").ap()
    out = nc.dram_tensor("out", [nseq, nfull * 128, 1024], F32, kind="ExternalOutput").ap()
    h1 = nc.dram_tensor("h1s", [nseq, L, 1024], F32, kind="Internal").ap()
    with ExitStack() as es:
        k = K(nc, es)
        c = setup_common(k, nc)
        common_setup(c, 0)
        hb = Buf("h1dram")
        k.push()
        alloc_layer0_work(c)
        setup_layer0(c, D)
        par = 0
        for s in range(nseq):
            tiles = [(meta[:, :], h1[s, 0:16, :], 16)]
            for j in range(nfull):
                tiles.append((x[s, j * 128:(j + 1) * 128, :], h1[s, 16 + j * 128:16 + (j + 1) * 128, :], 128))
            par = layer0_seq(c, tiles, par, hb, first_seq=(s == 0))
        k.barrier([c.xt[0].b, c.xt[1].b, c.vecF.b, c.vecT.b] + [t.b for t in c.stage])
        k.emit()
        k.pop()
        hb = Buf("h1dram2")
        k.push()
        alloc_layer1_work(c)
        c.stage = list(c.E)
        setup_layer1(c, D, L)
        par = 0
        for s in range(nseq):
            h1s = [h1[s, 0:16, :]] + [h1[s, 16 + (j - 1) * 128:16 + j * 128, :] for j in range(1, nfull + 1)]
            outs = [None] + [out[s, (j - 1) * 128:j * 128, :] for j in range(1, nfull + 1)]
            par = layer1_seq(c, h1s, outs, nfull, par, hb, first_seq=(s == 0))
        k.final_wait("sp", [c.ht[0].b, c.ht[1].b])
        k.emit()
        k.pop()
    return nc


def pack_vecs(inp):
    f = lambda v: np.ascontiguousarray(np.asarray(v).reshape(-1, 128).T)
    vecF = np.zeros((128, 140), np.float32)
    vecF[:, 0:8] = f(inp["even_norm"][0])
    vecF[:, 8:16] = f(inp["ssd_norm"][0])
    vecF[:, 16:48] = np.asarray(inp["lru_conv_w"][0]).reshape(4, 8, 128).transpose(2, 1, 0).reshape(128, 32)
    vecF[:, 48:56] = f(inp["lru_conv_b"][0])
    vecF[:, 56:64] = f(inp["lru_b_a"][0])
    vecF[:, 64:72] = f(inp["lru_b_x"][0])
    vecF[:, 72:80] = f(inp["lru_lambda"][0])
    vecF[:, 80:128] = np.asarray(inp["ssd_conv_w"][0]).reshape(4, 12, 128).transpose(2, 1, 0).reshape(128, 48)
    vecF[:, 128:140] = f(inp["ssd_conv_b"][0])
    vecT = np.concatenate([np.asarray(inp["ssd_dt_bias"][0]), np.asarray(inp["ssd_a_log"][0]), np.asarray(inp["ssd_d"][0])])[None, :].astype(np.float32)
    return vecF, vecT


_NC_CACHE = {}


def kernel(**inp):
    inp = {k_: np.asarray(v, dtype=np.float32) for k_, v in inp.items()}
    x = inp["x"]
    ncores = 8
    vecF, vecT = pack_vecs(inp)
    vec1F = np.ascontiguousarray(inp["odd_norm"][0].reshape(8, 128).T)
    if "f" not in _NC_CACHE:
        _NC_CACHE["f"] = build_fused(NSEQ, NFULL)
    xs = np.split(np.ascontiguousarray(x), ncores, axis=0)
    maps = [{"x": xs[i], "meta": inp["meta"], "w_in": inp["even_w_in"][0], "w_out": inp["even_w_out"][0],
             "wa": inp["lru_w_a"][0], "wx": inp["lru_w_x"][0], "vecF": vecF, "vecT": vecT,
             "w_in1": inp["odd_w_in"][0], "w_out1": inp["odd_w_out"][0], "vec1F": vec1F,
             "fnorm": inp["final_norm"][None, :]} for i in range(ncores)]
    r = run_bass_kernel_spmd(_NC_CACHE["f"], maps, core_ids=list(range(ncores)))
    return np.concatenate([r.results[i]["out"] for i in range(ncores)], axis=0).astype(np.float32)
```

```python
import numpy as np
from contextlib import ExitStack
import concourse.bass as bass
import concourse.mybir as mybir
from concourse.bass_utils import run_bass_kernel_spmd


F32 = mybir.dt.float32
BF16 = mybir.dt.bfloat16
AF = mybir.ActivationFunctionType
ALU = mybir.AluOpType
AX = mybir.AxisListType

ENGS = ("pe", "act", "dve", "pool", "sp")


class Buf:
    __slots__ = ("name", "w", "rs", "dsem", "dcnt")

    def __init__(self, name):
        self.name = name
        self.w = None
        self.rs = []
        self.dsem = None
        self.dcnt = 0


class K:
    def __init__(self, nc, es):
        self.nc = nc
        self.es = es
        self.prog = {e: [] for e in ENGS}
        self.sem = {e: es.enter_context(nc.semaphore("s_" + e)) for e in ENGS}
        self.cnt = {e: 0 for e in ENGS}
        self.waited = {e: {} for e in ENGS}
        self.pending = {e: [] for e in ENGS}
        self.nsem = 5
        self.ninstr = 0
        self.scopes = [es]

    def push(self):
        self.scopes.append(ExitStack())

    def pop(self):
        self.scopes.pop().close()

    def barrier(self, dma_bufs=()):
        for e in ENGS:
            if self.pending[e]:
                raise RuntimeError("barrier with pending unsignalled ops on " + e)
        waits = {}
        for e in ENGS:
            if e != "pool" and self.cnt[e] > 0:
                self._need("pool", (e, self.cnt[e], self.sem[e]), waits)
        for b in dma_bufs:
            self._need("pool", b.w, waits)
            for t in b.rs:
                self._need("pool", t, waits)
        self.cnt["pool"] += 1
        tok = ("pool", self.cnt["pool"], self.sem["pool"])
        self.prog["pool"].append((list(waits.values()), lambda e: e.engine_nop(), (self.sem["pool"], 1)))
        for e in ENGS:
            if e != "pool":
                self.wait_tok(e, tok)

    def sb(self, name, shape, dt):
        return self.scopes[-1].enter_context(self.nc.sbuf_tensor(name, list(shape), dt))

    def ps(self, name, shape, dt=F32):
        return self.scopes[-1].enter_context(self.nc.psum_tensor(name, list(shape), dt))

    def dsem_of(self, buf):
        if buf.dsem is None:
            buf.dsem = self.es.enter_context(self.nc.semaphore("d_" + buf.name))
            self.nsem += 1
        return buf.dsem

    def _need(self, eng, tok, waits):
        if tok is None:
            return
        key, val, semh = tok
        if key == eng and False:
            return
        cur = self.waited[eng].get(key, 0)
        if val > cur:
            self.waited[eng][key] = val
            waits[key] = (semh, val)

    def _deps(self, eng, r, w):
        waits = {}
        for b in r:
            if b.w is not None and b.w[0] == "PENDING":
                raise RuntimeError("read of buffer %s with unsignalled writer" % b.name)
            self._need(eng, b.w, waits)
        for b in w:
            if b.w is not None and b.w[0] == "PENDING":
                if b.w[1] != eng:
                    raise RuntimeError("write of buffer %s with unsignalled writer" % b.name)
            else:
                self._need(eng, b.w, waits)
            for t in b.rs:
                if t[0] == "PENDING":
                    if t[1] != eng:
                        raise RuntimeError("WAR on buffer %s with unsignalled reader" % b.name)
                else:
                    self._need(eng, t, waits)
        return list(waits.values())

    def op(self, eng, fn, r=(), w=(), sig=True):
        waits = self._deps(eng, r, w)
        if sig:
            self.cnt[eng] += 1
            tok = (eng, self.cnt[eng], self.sem[eng])
            semh = self.sem[eng]
            for (b, kind) in self.pending[eng]:
                if kind == "r":
                    b.rs = [t for t in b.rs if not (t[0] == "PENDING" and t[1] == eng)]
                    b.rs.append(tok)
                else:
                    b.w = tok
                    b.rs = [t for t in b.rs if not (t[0] == "PENDING" and t[1] == eng)]
            self.pending[eng] = []
            for b in r:
                b.rs.append(tok)
            for b in w:
                b.w = tok
                b.rs = []
            self.prog[eng].append((waits, fn, (semh, 1)))
        else:
            ptok = ("PENDING", eng)
            for b in r:
                b.rs.append(ptok)
                self.pending[eng].append((b, "r"))
            for b in w:
                b.w = ptok
                b.rs = []
                self.pending[eng].append((b, "w"))
            self.prog[eng].append((waits, fn, None))
        self.ninstr += 1

    def dma(self, eng, out_ap, in_ap, r=(), w=(), sbuf=None, **kw):
        waits = self._deps(eng, r, w)
        semh = self.dsem_of(sbuf)
        sbuf.dcnt += 1
        tok = ("d_" + sbuf.name, 16 * sbuf.dcnt, semh)
        for b in r:
            b.rs.append(tok)
        for b in w:
            b.w = tok
            b.rs = []
        self.prog[eng].append((waits, lambda e: e.dma_start(out=out_ap, in_=in_ap, **kw), (semh, 16)))
        self.ninstr += 1
        return tok

    def wait_tok(self, eng, tok):
        waits = {}
        self._need(eng, tok, waits)
        for (semh, val) in waits.values():
            self.prog[eng].append(([(semh, val)], None, None))

    def final_wait(self, eng, bufs):
        waits = {}
        for b in bufs:
            self._need(eng, b.w, waits)
            for t in b.rs:
                self._need(eng, t, waits)
        if waits:
            self.prog[eng].append((list(waits.values()), None, None))

    def emit(self):
        nc = self.nc
        engmap = {"pe": "tensor", "act": "scalar", "dve": "vector", "pool": "gpsimd", "sp": "sync"}
        with nc.Block() as block:
            for e in ENGS:
                prog = self.prog[e]

                def body(engine, prog=prog):
                    for waits, fn, inc in prog:
                        for (semh, val) in waits:
                            engine.wait_ge(semh, val)
                        if fn is not None:
                            ins = fn(engine)
                            if inc is not None:
                                ins.then_inc(inc[0], inc[1])
                getattr(block, engmap[e])(body)
        self.prog = {e: [] for e in ENGS}


D_MODEL = 1024
EVEN_IN = 4624


class T:
    def __init__(self, k, name, shape, dt, space="sb"):
        self.t = k.sb("t_" + name, shape, dt) if space == "sb" else k.ps("t_" + name, shape, dt)
        self.b = Buf(name)
        self.name = name

    def __getitem__(self, idx):
        return self.t[idx]


class View:
    def __init__(self, ap, b):
        self.ap = ap
        self.b = b

    def __getitem__(self, idx):
        return self.ap[idx]


def _b(xs):
    return [getattr(x, "b", x) for x in xs]


class Ctx:
    pass


def setup_common(k, nc):
    c = Ctx()
    c.k = k
    c.nc = nc
    c.rr = 0
    return c


def OP(c, eng, fn, r=(), w=(), sig=True):
    c.k.op(eng, fn, r=_b(r), w=_b(w), sig=sig)


def ACT(c, out, in_, func, r, w, **kw):
    OP(c, "act", lambda e: e.activation(out=out, in_=in_, func=func, **kw), r, w)


def TT(c, eng, out, in0, in1, op, r, w):
    OP(c, eng, lambda e: e.tensor_tensor(out=out, in0=in0, in1=in1, op=op), r, w)


def TS(c, eng, out, in0, s1, s2, op0, op1, r, w):
    if s2 is None:
        OP(c, eng, lambda e: e.tensor_scalar(out=out, in0=in0, scalar1=s1, scalar2=None, op0=op0), r, w)
    else:
        OP(c, eng, lambda e: e.tensor_scalar(out=out, in0=in0, scalar1=s1, scalar2=s2, op0=op0, op1=op1), r, w)


def STT(c, eng, out, in0, scalar, in1, op0, op1, r, w):
    OP(c, eng, lambda e: e.scalar_tensor_tensor(out=out, in0=in0, scalar=scalar, in1=in1, op0=op0, op1=op1), r, w)


def CP(c, eng, out, in_, r, w):
    if eng == "act":
        ACT(c, out, in_, AF.Copy, r, w)
    else:
        OP(c, eng, lambda e: e.tensor_copy(out=out, in_=in_), r, w)


def MM(c, out, lhsT, rhs, start, stop, r, w, sig, skip=False):
    if skip:
        OP(c, "pe", lambda e: e.matmul(out, lhsT=lhsT, rhs=rhs, start=start, stop=stop, skip_group_check=True), r, w, sig=sig)
    else:
        OP(c, "pe", lambda e: e.matmul(out, lhsT=lhsT, rhs=rhs, start=start, stop=stop), r, w, sig=sig)


def TR(c, out, in_, ident, r, w, sig):
    OP(c, "pe", lambda e: e.transpose(out=out, in_=in_, identity=ident), r, w, sig=sig)


def load_cast_weight(c, dst, dst_slices, dram_rows, width, scale_aps, st, engs=("act", "dve")):
    k = c.k
    for i, (da, ra) in enumerate(zip(dst_slices, dram_rows)):
        s = st[c.rr % len(st)]
        eng = engs[c.rr % len(engs)]
        c.rr += 1
        k.dma("sp", s.t[:, 0:width], ra, w=[s.b], sbuf=s.b)
        sc = scale_aps[i]
        if sc is None:
            CP(c, eng, da, s.t[:, 0:width], [s], [dst])
        else:
            if eng == "act":
                OP(c, "act", lambda e, da=da, s=s, sc=sc: e.activation(out=da, in_=s.t[:, 0:width], func=AF.Copy, scale=sc), [s, c.vecF], [dst])
            else:
                TS(c, eng, da, s.t[:, 0:width], sc, None, ALU.mult, None, [s, c.vecF], [dst])


def setup_layer0(c, D):
    k = c.k
    c.w_in = T(k, "w_in", [128, 8, EVEN_IN], BF16)
    c.w_out = T(k, "w_out", [128, 16, 1024], BF16)
    c.wa = T(k, "wa", [128, 4, 2, 256], BF16)
    c.wx = T(k, "wx", [128, 4, 2, 256], BF16)
    c.vecF = T(k, "vecF", [128, 140], F32)
    c.vecT = T(k, "vecT", [128, 48], F32)
    k.dma("sp", c.vecF[:], D["vecF"][:, :], w=[c.vecF.b], sbuf=c.vecF.b)
    k.dma("sp", c.vecT[:], D["vecT"][0:1, :].partition_broadcast(128), w=[c.vecT.b], sbuf=c.vecT.b)
    st = c.stage
    for (c0, wd) in ((0, 1024), (1024, 1024), (2048, 1024), (3072, 1024), (4096, 528)):
        dsts, rows, scs = [], [], []
        for kc in range(8):
            dsts.append(c.w_in[:, kc, c0:c0 + wd])
            rows.append(D["w_in"][kc * 128:(kc + 1) * 128, c0:c0 + wd])
            scs.append(c.vecF[:, kc:kc + 1])
        load_cast_weight(c, c.w_in, dsts, rows, wd, scs, st)
    dsts, rows, scs = [], [], []
    for kc in range(16):
        dsts.append(c.w_out[:, kc, :])
        rows.append(D["w_out"][kc * 128:(kc + 1) * 128, :])
        scs.append(None if kc < 8 else c.vecF[:, 8 + kc - 8:8 + kc - 8 + 1])
    load_cast_weight(c, c.w_out, dsts, rows, 1024, scs, st)
    for (wt, nm) in ((c.wa, "wa"), (c.wx, "wx")):
        dsts, rows, scs = [], [], []
        for g in range(4):
            for kc in range(2):
                dsts.append(wt[:, g, kc, :])
                rows.append(D[nm][g, kc * 128:(kc + 1) * 128, :])
                scs.append(None)
        load_cast_weight(c, wt, dsts, rows, 256, scs, st)

    c.L1 = T(k, "L1", [128, 128], F32)
    c.L2 = T(k, "L2", [128, 128], F32)
    c.L4 = T(k, "L4", [128, 2, 128], F32)
    c.mle = T(k, "mle", [128, 64], F32)
    OP(c, "pool", lambda e: e.memset(c.L1[:], 0.0), [], [c.L1])
    OP(c, "pool", lambda e: e.memset(c.L2[:], 0.0), [], [c.L2])
    OP(c, "pool", lambda e: e.memset(c.L4[:], 0.0), [], [c.L4])
    OP(c, "pool", lambda e: e.memset(c.mle[:], 1.0), [], [c.mle])
    for h in range(2):
        ps = slice(h * 64, (h + 1) * 64)
        OP(c, "pool", lambda e, ps=ps: e.memset(c.L1[ps, ps], 1.0), [], [c.L1])
        OP(c, "pool", lambda e, ps=ps: e.memset(c.L2[ps, ps], 1.0), [], [c.L2])
        OP(c, "pool", lambda e, ps=ps, h=h: e.memset(c.L4[ps, h, :], 1.0), [], [c.L4])
        OP(c, "pool", lambda e, ps=ps: e.affine_select(out=c.L1[ps, ps], in_=c.L1[ps, ps], pattern=[[1, 64]], compare_op=ALU.is_ge, fill=0.0, base=0, channel_multiplier=-1), [c.L1], [c.L1])
        OP(c, "pool", lambda e, ps=ps: e.affine_select(out=c.L2[ps, ps], in_=c.L2[ps, ps], pattern=[[-1, 64]], compare_op=ALU.is_gt, fill=0.0, base=0, channel_multiplier=1), [c.L2], [c.L2])
        OP(c, "pool", lambda e, ps=ps: e.affine_select(out=c.mle[ps, :], in_=c.mle[ps, :], pattern=[[1, 64]], compare_op=ALU.is_ge, fill=0.0, base=0, channel_multiplier=-1), [c.mle], [c.mle])
    c.pv = T(k, "pv", [128, 64], F32)
    ACT(c, c.pv[:, 0:8], c.vecF[:, 72:80], AF.Exp, [c.vecF], [c.pv], scale=-1.0)
    ACT(c, c.pv[:, 0:8], c.pv[:, 0:8], AF.Ln, [c.pv], [c.pv], bias=1.0)
    TS(c, "dve", c.pv[:, 8:16], c.pv[:, 0:8], -16.0, None, ALU.mult, None, [c.pv], [c.pv])
    TS(c, "dve", c.pv[:, 0:8], c.pv[:, 0:8], -8.0, None, ALU.mult, None, [c.pv], [c.pv])
    TS(c, "dve", c.pv[:, 16:32], c.vecF[:, 56:72], -1.0, None, ALU.mult, None, [c.vecF], [c.pv])
    ACT(c, c.pv[:, 32:48], c.vecT[:, 16:32], AF.Exp, [c.vecT], [c.pv])
    TS(c, "dve", c.pv[:, 32:48], c.pv[:, 32:48], -1.0, None, ALU.mult, None, [c.pv], [c.pv])


def alloc_layer0_work(c):
    k = c.k
    c.xt = [T(k, "xt%d" % i, [128, 1024], F32) for i in range(2)]
    c.st1 = [T(k, "st1_%d" % i, [128, 8], F32) for i in range(2)]
    c.W = [T(k, "W%d" % i, [128, 1024], F32) for i in range(6)]
    c.H = [T(k, "H%d" % i, [128, (256 if i == 1 else (512 if i == 5 else 1024))], BF16) for i in range(6)]
    c.projF = [T(k, "projF%d" % i, [128, 20, 131], F32) for i in range(2)]
    c.sg = [T(k, "sg%d" % i, [128, 1024], BF16) for i in range(2)]
    c.gz = [T(k, "gz%d" % i, [128, 1024], BF16) for i in range(2)]
    c.dts = [T(k, "dts%d" % i, [128, 32], F32) for i in range(2)]
    c.ubP = T(k, "ubP", [128, 1024], BF16)
    c.sgt = View(c.ubP.t[:, :].bitcast(F32), c.ubP.b)
    c.uTP = T(k, "uTP", [128, 1024], BF16)
    c.xbc = T(k, "xbc", [128, 12, 128], F32)
    c.lxcb = [Buf("lxc%d" % i) for i in range(8)]
    c.xbccb = [Buf("xbcc%d" % i) for i in range(12)]
    c.S = T(k, "S", [128, 1024], F32)
    c.Sbf = T(k, "Sbf", [128, 1024], BF16)
    c.hst = T(k, "hst", [128, 8], F32)
    c.S_meta = T(k, "S_meta", [128, 1024], F32)
    c.hst_meta = T(k, "hst_meta", [128, 8], F32)
    c.hist_meta = T(k, "hist_meta", [128, 20, 3], F32)
    c.sm = T(k, "sm", [128, 96], F32)
    c.cbm = T(k, "cbm", [128, 2, 64], F32)
    c.pT = T(k, "pT", [128, 8, 128], BF16, "ps")
    c.pG = T(k, "pG", [128, 512], F32, "ps")
    c.pT2 = View(c.pG[:, 256:384].bitcast(BF16).rearrange("p (c t) -> p c t", c=2), c.pG.b)
    c.pB = [T(k, "pB%d" % i, [128, 512], F32, "ps") for i in range(6)]
    c.stage = [c.W[0], c.W[1], c.W[2], c.W[3]]
    c.pTP = View(c.pB[0][:, :].bitcast(BF16).rearrange("p (c t) -> p c t", c=8), c.pB[0].b)


def seq_reset0(c, par):
    OP(c, "pool", lambda e: e.memset(c.projF[par][:, :, 0:3], 0.0), [], [c.projF[par]])
    OP(c, "pool", lambda e: e.memset(c.S[:], 0.0), [], [c.S])
    OP(c, "pool", lambda e: e.memset(c.Sbf[:], 0.0), [], [c.Sbf])
    OP(c, "pool", lambda e: e.memset(c.hst[:], 0.0), [], [c.hst])


def act_sigmoid_from(c, out, in_, rin, wout, neg_bias=None):
    ACT(c, out, in_, AF.Exp, rin, wout, scale=-1.0)
    ACT(c, out, out, AF.Ln, wout, wout, bias=1.0)
    ACT(c, out, out, AF.Exp, wout, wout, scale=-1.0)


def layer0_P(c, src_ap, nt, par, prev):
    k = c.k
    xt = c.xt[par]
    st1 = c.st1[par]
    P = slice(0, nt)
    identb = c.identb
    projF = c.projF[par]
    k.dma("sp", xt[P, :], src_ap, w=[xt.b], sbuf=xt.b)
    ACT(c, c.ubP[P, :], xt[P, :], AF.Square, [xt], [c.ubP, st1], accum_out=st1[P, 0:1])
    ACT(c, st1[P, 1:2], st1[P, 0:1], AF.Ln, [st1], [st1], scale=1.0 / D_MODEL, bias=1e-6)
    ACT(c, st1[P, 1:2], st1[P, 1:2], AF.Exp, [st1], [st1], scale=-0.5)
    ub = c.ubP
    TS(c, "dve", ub[P, :], xt[P, :], st1[P, 1:2], None, ALU.mult, None, [xt, st1], [ub])
    yield
    for kc in range(8):
        TR(c, c.pTP[:, kc, P], ub[P, kc * 128:(kc + 1) * 128], identb[P, P], [ub, identb], [c.pTP], sig=(kc == 7))
    uT = c.uTP
    uTv = uT[:].rearrange("p (c t) -> p c t", c=8)
    CP(c, "act", uTv[:, :, P], c.pTP[:, :, P], [c.pTP], [uT])
    yield
    if prev == "meta":
        CP(c, "pool", projF[:, :, 0:3], c.hist_meta[:, :, :], [c.hist_meta], [projF])
    elif prev is not None:
        pp, pnt = prev
        CP(c, "pool", projF[:, :, 0:3], c.projF[pp][:, :, pnt:pnt + 3], [c.projF[pp]], [projF])
    pz = (c.pB[0], c.pB[1])
    gz = c.gz[par]
    dts = c.dts[par]
    for kc in range(8):
        MM(c, c.pB[1][P, 0:16], uTv[:, kc, P], c.w_in[:, kc, 4608:4624], kc == 0, kc == 7, [uT, c.w_in], [c.pB[1]], sig=(kc == 7))
    TT(c, "dve", dts[P, 0:16], c.pB[1][P, 0:16], c.vecT[P, 0:16], ALU.add, [c.pB[1], c.vecT], [dts])
    yield
    ACT(c, dts[P, 0:16], dts[P, 0:16], AF.Exp, [dts], [dts])
    ACT(c, dts[P, 0:16], dts[P, 0:16], AF.Ln, [dts], [dts], bias=1.0)
    TT(c, "dve", dts[P, 16:32], dts[P, 0:16], c.pv[P, 32:48], ALU.mult, [dts, c.pv], [dts])
    yield
    for hf in range(2):
        for kc in range(8):
            MM(c, pz[hf][P, :], uTv[:, kc, P], c.w_in[:, kc, 2048 + hf * 512:2048 + (hf + 1) * 512], kc == 0, kc == 7, [uT, c.w_in], [pz[hf]], sig=(kc == 7))
        yield
    for hf in range(2):
        hs = slice(hf * 512, (hf + 1) * 512)
        act_sigmoid_from(c, c.sgt[P, :], pz[hf][P, :], [pz[hf]], [c.sgt])
        yield
        TT(c, "dve", gz[P, hs], c.sgt[P, :], pz[hf][P, :], ALU.mult, [c.sgt, pz[hf]], [gz])
        yield
    sg = c.sg[par]
    sgv = sg[:].rearrange("p (c t) -> p c t", c=8)
    groups = []
    for g4 in range(2):
        groups.append(("x", [g4 * 4 + i for i in range(4)], 0))
    for g4 in range(2):
        groups.append(("g", [g4 * 4 + i for i in range(4)], 1024))
    for g4 in range(3):
        groups.append(("b", [g4 * 4 + i for i in range(4)], 3072))
    for gi, (kind, ocs, colbase) in enumerate(groups):
        pb = c.pB[gi % 2]
        pbv = pb[:].rearrange("p (c t) -> p c t", c=4)
        for i, oc in enumerate(ocs):
            for kc in range(8):
                MM(c, pbv[:, i, P], c.w_in[:, kc, colbase + oc * 128:colbase + (oc + 1) * 128], uTv[:, kc, P], kc == 0, kc == 7, [uT, c.w_in], [pb], sig=(kc == 7))
            yield
        if kind == "x":
            CP(c, "act", projF[:, ocs[0]:ocs[0] + 4, 3:3 + nt], pbv[:, :, P], [pb], [projF])
        elif kind == "b":
            CP(c, "act", projF[:, 8 + ocs[0]:8 + ocs[0] + 4, 3:3 + nt], pbv[:, :, P], [pb], [projF])
        else:
            o = sgv[:, ocs[0]:ocs[0] + 4, P]
            sc = c.sgt[:].rearrange("p (c t) -> p c t", c=4)[:, :, P]
            act_sigmoid_from(c, sc, pbv[:, :, P], [pb], [c.sgt])
            yield
            TT(c, "dve", o, sc, pbv[:, :, P], ALU.mult, [c.sgt, pb], [sg])
        yield


def layer0_M(c, dst_ap, nt, par, dst_buf):
    k = c.k
    xt = c.xt[par]
    st1 = c.st1[par]
    W = c.W
    H = c.H
    chunks = [(0, nt)] if nt <= 64 else [(0, 64), (64, 128)]
    cw = chunks[0][1]
    nch = len(chunks)
    identb = c.identb
    P = slice(0, nt)
    projF = c.projF[par]
    sg = c.sg[par]
    sgv = sg[:].rearrange("p (c t) -> p c t", c=8)
    gz = c.gz[par]
    dts = c.dts[par]
    sm = c.sm

    lx = W[3]
    lxv = lx[:].rearrange("p (c t) -> p c t", c=8)
    lcb = c.lxcb
    ne = 0
    for tp in range(4):
        for ch in range(8):
            o = lxv[:, ch, P]
            if tp == 0:
                TS(c, "dve", o, projF[:, ch, 0:nt], c.vecF[:, 16 + ch * 4:16 + ch * 4 + 1], c.vecF[:, 48 + ch:48 + ch + 1], ALU.mult, ALU.add, [projF, c.vecF], ([lx, lcb[ch]] if ch == 0 else [lcb[ch]]))
            else:
                STT(c, "dve", o, projF[:, ch, tp:tp + nt], c.vecF[:, 16 + ch * 4 + tp:16 + ch * 4 + tp + 1], o, ALU.mult, ALU.add, [projF, c.vecF, lcb[ch]], [lcb[ch]])
            ne += 1
            if ne % 4 == 0:
                yield
    yield
    lxb = H[2]
    lxbv = lxb[:].rearrange("p (c t) -> p c t", c=8)
    CP(c, "act", lxbv[:, :, P], lxv[:, :, P], [lx] + c.lxcb, [lxb])
    ea_ = W[4]
    ex_ = W[5]
    eav = ea_[:].rearrange("p (c t) -> p c t", c=8)
    exv = ex_[:].rearrange("p (c t) -> p c t", c=8)
    for (wt, pbs, ev, boff, et) in ((c.wa, (c.pB[2], c.pB[3]), eav, 16, ea_), (c.wx, (c.pB[4], c.pB[5]), exv, 24, ex_)):
        for oc in range(8):
            g = oc // 2
            pb = pbs[oc // 4]
            pbv = pb[:].rearrange("p (c t) -> p c t", c=4)
            for kc in range(2):
                MM(c, pbv[:, oc % 4, P], wt[:, g, kc, (oc % 2) * 128:(oc % 2 + 1) * 128], lxbv[:, 2 * g + kc, P], kc == 0, kc == 1, [lxb, wt], [pb], sig=(oc % 4 == 3 and kc == 1))
    yield
    xbc = c.xbc
    xcb = c.xbccb
    ne = 0
    for tp in range(4):
        for ch in range(12):
            o = xbc[:, ch, P]
            if tp == 0:
                TS(c, "dve", o, projF[:, 8 + ch, 0:nt], c.vecF[:, 80 + ch * 4:80 + ch * 4 + 1], c.vecF[:, 128 + ch:128 + ch + 1], ALU.mult, ALU.add, [projF, c.vecF], ([xbc, xcb[ch]] if ch == 0 else [xcb[ch]]))
            else:
                STT(c, "dve", o, projF[:, 8 + ch, tp:tp + nt], c.vecF[:, 80 + ch * 4 + tp:80 + ch * 4 + tp + 1], o, ALU.mult, ALU.add, [projF, c.vecF, xcb[ch]], [xcb[ch]])
            ne += 1
            if ne % 4 == 0:
                yield
    for (wt, pbs, ev, boff, et) in ((c.wa, (c.pB[2], c.pB[3]), eav, 16, ea_), (c.wx, (c.pB[4], c.pB[5]), exv, 24, ex_)):
        for oc in range(8):
            pb = pbs[oc // 4]
            pbv = pb[:].rearrange("p (c t) -> p c t", c=4)
            ACT(c, ev[:, oc, P], pbv[:, oc % 4, P], AF.Exp, [pb, c.pv], [et], scale=-1.0, bias=c.pv[:, boff + oc:boff + oc + 1])
            if oc % 4 == 3:
                yield
        ACT(c, ev[:, :, P], ev[:, :, P], AF.Ln, [et], [et], bias=1.0)
        ACT(c, ev[:, :, P], ev[:, :, P], AF.Exp, [et], [et], scale=-1.0)
    yield
    e0 = W[0][:].rearrange("p (c t) -> p c t", c=8)
    e1 = W[1][:].rearrange("p (c t) -> p c t", c=8)
    act_sigmoid_from(c, e0[:, :, P], xbc[:, 0:8, P], [xbc] + c.xbccb, [W[0]])
    act_sigmoid_from(c, e1[:, 0:4, P], xbc[:, 8:12, P], [xbc] + c.xbccb, [W[1]])
    yield
    xsT = H[3][:].rearrange("p (c t) -> p c t", c=8)
    bcT = H[5][:].rearrange("p (c t) -> p c t", c=4)
    TT(c, "dve", xsT[:, :, P], e0[:, :, P], xbc[:, 0:8, P], ALU.mult, [W[0], xbc] + c.xbccb, [H[3]])
    TT(c, "dve", bcT[:, 0:4, P], e1[:, 0:4, P], xbc[:, 8:12, P], ALU.mult, [W[1], xbc] + c.xbccb, [H[5]])
    yield
    a_ = W[2]
    av = a_[:].rearrange("p (c t) -> p c t", c=8)
    a2_ = W[1]
    a2v = a2_[:].rearrange("p (c t) -> p c t", c=8)
    for ch in range(8):
        ACT(c, av[:, ch, P], eav[:, ch, P], AF.Exp, [ea_, c.pv], [a_], scale=c.pv[:, ch:ch + 1])
        ACT(c, a2v[:, ch, P], eav[:, ch, P], AF.Exp, [ea_, c.pv], [a2_], scale=c.pv[:, 8 + ch:8 + ch + 1])
        if ch % 4 == 3:
            yield
    for ch in range(8):
        TR(c, c.pT[P, ch, :], xsT[:, ch, P], identb[:, :], [H[3], identb], [c.pT], sig=(ch == 7))
    for ch in range(2):
        TR(c, c.pT2[P, ch, :], bcT[:, ch, P], identb[:, :], [H[5], identb], [c.pT2], sig=(ch == 1))
    yield
    TS(c, "dve", a2v[:, :, P], a2v[:, :, P], -1.0, 1.0, ALU.mult, ALU.add, [a2_], [a2_])
    ACT(c, a2v[:, :, P], a2v[:, :, P], AF.Ln, [a2_], [a2_])
    ACT(c, a2v[:, :, P], a2v[:, :, P], AF.Exp, [a2_], [a2_], scale=0.5)
    yield
    Xps = c.pT[:].rearrange("p c t -> p (c t)")
    Xdt = H[0]
    TT(c, "dve", Xdt[P, :].rearrange("p (h d) -> p h d", h=16), Xps[P, :].rearrange("p (h d) -> p h d", h=16),
       dts[P, 0:16].unsqueeze(2).to_broadcast([nt, 16, 64]), ALU.mult, [c.pT, dts], [Xdt])
    skip = W[0]
    TT(c, "dve", skip[P, :].rearrange("p (h d) -> p h d", h=16), Xps[P, :].rearrange("p (h d) -> p h d", h=16),
       c.vecT[P, 32:48].unsqueeze(2).to_broadcast([nt, 16, 64]), ALU.mult, [c.pT, c.vecT], [skip])
    yield
    Btok = H[1]
    CP(c, "act", Btok[P, 0:256], c.pT2[P, :, :].rearrange("p c t -> p (c t)"), [c.pT2], [Btok])
    MM(c, c.pG[P, 16:32], c.L1[P, P], dts[P, 16:32], True, True, [c.L1, dts], [c.pG], sig=False)
    MM(c, c.pG[P, 32:48], c.L2[P, P], dts[P, 16:32], True, True, [c.L2, dts], [c.pG], sig=False)
    for ci in range(nch):
        MM(c, c.pG[:, 48 + 16 * ci:64 + 16 * ci], c.L4[P, ci, :], dts[P, 16:32], True, True, [c.L4, dts], [c.pG], sig=(ci == nch - 1))
    ACT(c, sm[P, 32:48], c.pG[P, 16:32], AF.Exp, [c.pG], [sm])
    ACT(c, sm[P, 48:64], c.pG[P, 32:48], AF.Exp, [c.pG], [sm])
    ACT(c, sm[:, 64:64 + 16 * nch], c.pG[:, 48:48 + 16 * nch], AF.Exp, [c.pG], [sm])
    yield
    TT(c, "dve", exv[:, :, P], exv[:, :, P], a2v[:, :, P], ALU.mult, [ex_, a2_], [ex_])
    TT(c, "dve", exv[:, :, P], exv[:, :, P], lxv[:, :, P], ALU.mult, [ex_, lx] + c.lxcb, [ex_])
    for ch in range(8):
        OP(c, "dve", lambda e, ch=ch: e.tensor_tensor_scan(out=eav[:, ch, P], data0=av[:, ch, P], data1=exv[:, ch, P], initial=c.hst[:, ch:ch + 1], op0=ALU.mult, op1=ALU.add), [a_, ex_, c.hst], [ea_])
        if ch % 4 == 3:
            yield
    CP(c, "dve", c.hst[:, :], eav[:, :, nt - 1], [ea_], [c.hst])
    mixA = H[4][:].rearrange("p (c t) -> p c t", c=8)
    mixB = H[2][:].rearrange("p (c t) -> p c t", c=8)
    TT(c, "pool", mixA[:, :, P], eav[:, :, P], sgv[:, :, P], ALU.mult, [ea_, sg], [H[4]])
    yield
    R1 = W[1]
    R1v = R1[:, 0:16 * cw].rearrange("p (h l) -> p h l", h=16)
    TT(c, "dve", R1v[P, :, :], dts[P, 16:32].unsqueeze(2).to_broadcast([nt, 16, cw]),
       c.mle[P, 0:cw].unsqueeze(1).to_broadcast([nt, 16, cw]), ALU.mult, [dts, c.mle], [R1])
    for hf in range(2):
        pb = c.pB[2 + hf]
        MM(c, pb[P, 0:8 * cw], c.L2[P, P], R1[P, hf * 8 * cw:(hf + 1) * 8 * cw], True, True, [c.L2, R1], [pb], sig=True)
    dec = W[2]
    decv = dec[:, 0:16 * cw].rearrange("p (h l) -> p h l", h=16)
    for hf in range(2):
        pb = c.pB[2 + hf]
        ACT(c, dec[P, hf * 8 * cw:(hf + 1) * 8 * cw], pb[P, 0:8 * cw], AF.Exp, [pb], [dec])
    yield
    cbps = c.pG[:, 128:256].rearrange("p (g l) -> p g l", g=2)
    for ci, (p0, p1) in enumerate(chunks):
        for g in range(2):
            MM(c, cbps[p0:p1, g, 0:cw], bcT[:, g, p0:p1], bcT[:, 2 + g, p0:p1], True, True, [H[5]], [c.pG], sig=(ci == nch - 1 and g == 1))
    TT(c, "dve", c.cbm[P, :, 0:cw], cbps[P, :, 0:cw], c.mle[P, 0:cw].unsqueeze(1).to_broadcast([nt, 2, cw]), ALU.mult, [c.pG, c.mle], [c.cbm])
    yield
    MT = H[3]
    MTv = MT[:, 0:16 * cw].rearrange("p (h l) -> p h l", h=16)
    for g in range(2):
        TT(c, "dve", MTv[P, g * 8:(g + 1) * 8, :], decv[P, g * 8:(g + 1) * 8, :],
           c.cbm[P, g:g + 1, 0:cw].to_broadcast([nt, 8, cw]), ALU.mult, [dec, c.cbm], [MT])
    yield
    Xd = H[2]
    TT(c, "pool", Xd[P, :].rearrange("p (h d) -> p h d", h=16), Xdt[P, :].rearrange("p (h d) -> p h d", h=16),
       sm[P, 48:64].unsqueeze(2).to_broadcast([nt, 16, 64]), ALU.mult, [Xdt, sm], [H[2]])
    for ci, (p0, p1) in enumerate(chunks):
        for h in range(16):
            pb = c.pB[4 + h // 8]
            MM(c, pb[p0:p1, (h % 8) * 64:(h % 8 + 1) * 64], MTv[p0:p1, h, :], Xdt[p0:p1, h * 64:(h + 1) * 64], True, True, [MT, Xdt], [pb],
               sig=(h % 8 == 7))
        yield
    y = W[1]
    for ci, (p0, p1) in enumerate(chunks):
        PC = slice(p0, p1)
        ncw = p1 - p0
        for g in range(2):
            pb = c.pB[2 + g]
            MM(c, pb[p0:p1, :], bcT[:, 2 + g, p0:p1], c.Sbf[:, g * 512:(g + 1) * 512], True, True, [H[5], c.Sbf], [pb], sig=True)
        yield
        for g in range(2):
            gs = slice(g * 512, (g + 1) * 512)
            TT(c, "dve", y[PC, gs].rearrange("p (h d) -> p h d", h=8), c.pB[2 + g][PC, :].rearrange("p (h d) -> p h d", h=8),
               sm[PC, 32 + 8 * g:40 + 8 * g].unsqueeze(2).to_broadcast([ncw, 8, 64]), ALU.mult, [c.pB[2 + g], sm], [y])
        yield
        for g in range(2):
            pb = c.pB[2 + g]
            MM(c, pb[:, :], Btok[p0:p1, g * 128:(g + 1) * 128], Xd[p0:p1, g * 512:(g + 1) * 512], True, True, [Btok, H[2]], [pb], sig=True)
        TT(c, "pool", c.S[:].rearrange("p (h d) -> p h d", h=16), c.S[:].rearrange("p (h d) -> p h d", h=16),
           sm[:, 64 + 16 * ci:80 + 16 * ci].unsqueeze(2).to_broadcast([128, 16, 64]), ALU.mult, [c.S, sm], [c.S])
        yield
        for g in range(2):
            gs = slice(g * 512, (g + 1) * 512)
            TT(c, "dve", c.S[:, gs], c.S[:, gs], c.pB[2 + g][:, :], ALU.add, [c.S, c.pB[2 + g]], [c.S])
        CP(c, "act", c.Sbf[:], c.S[:], [c.S], [c.Sbf])
        yield
    for g in range(2):
        gs = slice(g * 512, (g + 1) * 512)
        TT(c, "dve", y[P, gs], y[P, gs], c.pB[4 + g][P, :], ALU.add, [y, c.pB[4 + g]], [y])
    yield
    yield
    TT(c, "dve", y[P, :], y[P, :], skip[P, :], ALU.add, [y, skip], [y])
    TT(c, "dve", y[P, :], y[P, :], gz[P, :], ALU.mult, [y, gz], [y])
    for g in range(2):
        gs = slice(g * 512, (g + 1) * 512)
        ACT(c, W[4][P, gs], y[P, gs], AF.Square, [y], [W[4], st1], accum_out=st1[P, 2 + g:3 + g])
    ACT(c, st1[P, 4:6], st1[P, 2:4], AF.Ln, [st1], [st1], scale=1.0 / 512, bias=1e-6)
    ACT(c, st1[P, 4:6], st1[P, 4:6], AF.Exp, [st1], [st1], scale=-0.5)
    yield
    yb = H[0]
    for g in range(2):
        gs = slice(g * 512, (g + 1) * 512)
        TS(c, "dve", yb[P, gs], y[P, gs], st1[P, 4 + g:5 + g], None, ALU.mult, None, [y, st1], [yb])
    for ch in range(8):
        TR(c, c.pT[:, ch, P], yb[P, ch * 128:(ch + 1) * 128], identb[P, P], [yb, identb], [c.pT], sig=(ch == 7))
    CP(c, "act", mixB[:, :, P], c.pT[:, :, P], [c.pT], [H[2]])
    yield
    for hf in range(2):
        pb = c.pB[2 + hf]
        for kc in range(16):
            lhs = mixA[:, kc, P] if kc < 8 else mixB[:, kc - 8, P]
            MM(c, pb[P, :], lhs, c.w_out[:, kc, hf * 512:(hf + 1) * 512], kc == 0, kc == 15, [H[4], H[2], c.w_out], [pb], sig=(kc == 15))
    yield
    for hf in range(2):
        hs = slice(hf * 512, (hf + 1) * 512)
        TT(c, "dve", xt[P, hs], xt[P, hs], c.pB[2 + hf][P, :], ALU.add, [xt, c.pB[2 + hf]], [xt])
    k.dma("sp", dst_ap, xt[P, :], r=[xt.b], w=[dst_buf], sbuf=xt.b)


def interleave(gm, gp, ratio=1):
    am, ap = gm is not None, gp is not None
    while am or ap:
        if am:
            for _ in range(ratio):
                try:
                    next(gm)
                except StopIteration:
                    am = False
                    break
        if ap:
            try:
                next(gp)
            except StopIteration:
                ap = False


def layer0_seq(c, tiles, par0, dst_buf, first_seq=True):
    par = par0
    n = len(tiles)
    if first_seq:
        seq_reset0(c, par)
        interleave(None, layer0_P(c, tiles[0][0], tiles[0][2], par, None))
        start = 0
    else:
        CP(c, "pool", c.S[:], c.S_meta[:], [c.S_meta], [c.S])
        CP(c, "act", c.Sbf[:], c.S_meta[:], [c.S_meta], [c.Sbf])
        CP(c, "pool", c.hst[:], c.hst_meta[:], [c.hst_meta], [c.hst])
        interleave(None, layer0_P(c, tiles[1][0], tiles[1][2], par, "meta"))
        start = 1
    for j in range(start, n):
        gp = layer0_P(c, tiles[j + 1][0], tiles[j + 1][2], par ^ 1, (par, tiles[j][2])) if j + 1 < n else None
        gm = layer0_M(c, tiles[j][1], tiles[j][2], par, dst_buf)
        interleave(gm, gp)
        if first_seq and j == 0:
            CP(c, "pool", c.S_meta[:], c.S[:], [c.S], [c.S_meta])
            CP(c, "pool", c.hst_meta[:], c.hst[:], [c.hst], [c.hst_meta])
            CP(c, "pool", c.hist_meta[:, :, :], c.projF[par][:, :, tiles[0][2]:tiles[0][2] + 3], [c.projF[par]], [c.hist_meta])
        par ^= 1
    return par


LTOT = 2064
BIG = 30000.0


def setup_layer1(c, D, L):
    k = c.k
    c.w_in1 = T(k, "w_in1", [128, 8, 4096], BF16)
    c.w_out1 = T(k, "w_out1", [128, 8, 1024], BF16)
    c.vecF = T(k, "vec1F", [128, 8], F32)
    c.fn = T(k, "fn", [128, 1024], F32)
    k.dma("sp", c.vecF[:], D["vec1F"][:, :], w=[c.vecF.b], sbuf=c.vecF.b)
    k.dma("sp", c.fn[:], D["fnorm"][0:1, :].partition_broadcast(128), w=[c.fn.b], sbuf=c.fn.b)
    st = c.stage
    dsts, rows, scs = [], [], []
    for kc in range(8):
        for hf in range(8):
            dsts.append(c.w_in1[:, kc, hf * 512:(hf + 1) * 512])
            rows.append(D["w_in1"][kc * 128:(kc + 1) * 128, hf * 512:(hf + 1) * 512])
            scs.append(c.vecF[:, kc:kc + 1])
    load_cast_weight(c, c.w_in1, dsts, rows, 512, scs, st)
    dsts, rows, scs = [], [], []
    for kc in range(8):
        for hf in range(2):
            dsts.append(c.w_out1[:, kc, hf * 512:(hf + 1) * 512])
            rows.append(D["w_out1"][kc * 128:(kc + 1) * 128, hf * 512:(hf + 1) * 512])
            scs.append(None)
    load_cast_weight(c, c.w_out1, dsts, rows, 512, scs, st)
    c.negm = T(k, "negm", [128, 4, 128], BF16)
    c.tri2 = T(k, "tri2", [128, 128], BF16)
    c.zrow = T(k, "zrow", [1, 256], BF16)
    c.tri = T(k, "tri", [128, 128], BF16)
    c.ones = T(k, "ones", [128, 2], BF16)
    OP(c, "pool", lambda e: e.memset(c.negm[:], 0.0), [], [c.negm])
    OP(c, "pool", lambda e: e.memset(c.tri2[:], 1.0), [], [c.tri2])
    OP(c, "pool", lambda e: e.memset(c.zrow[:], 0.0), [], [c.zrow])
    OP(c, "pool", lambda e: e.memset(c.tri[:], 1.0), [], [c.tri])
    OP(c, "pool", lambda e: e.memset(c.ones[:], 1.0), [], [c.ones])
    for i in range(4):
        OP(c, "pool", lambda e, i=i: e.affine_select(out=c.negm[:, i, :], in_=c.negm[:, i, :], pattern=[[1, 128]], compare_op=ALU.is_gt, fill=-BIG, base=0, channel_multiplier=-1), [c.negm], [c.negm])
    OP(c, "pool", lambda e: e.affine_select(out=c.tri[:], in_=c.tri[:], pattern=[[-1, 128]], compare_op=ALU.is_ge, fill=0.0, base=0, channel_multiplier=1), [c.tri], [c.tri])
    OP(c, "pool", lambda e: e.affine_select(out=c.tri2[:], in_=c.tri2[:], pattern=[[1, 128]], compare_op=ALU.is_gt, fill=0.0, base=0, channel_multiplier=-1), [c.tri2], [c.tri2])
    nkb = 1 + (L - 16) // 128
    c.KT = T(k, "KT", [128, 8, L], BF16)
    c.V = T(k, "V", [128, nkb, 1024], BF16)
    c.KTb = [Buf("KTb%d" % i) for i in range(nkb)]
    c.Vb = [Buf("Vb%d" % i) for i in range(nkb)]


def alloc_layer1_work(c):
    k = c.k
    c.ht = [T(k, "ht%d" % i, [128, 1024], F32) for i in range(2)]
    c.st2 = [T(k, "st2_%d" % i, [128, 8], F32) for i in range(2)]
    c.ub1 = T(k, "ub1", [128, 1024], BF16)
    c.uT1 = T(k, "uT1", [128, 1024], BF16)
    c.QTs = [T(k, "QTs%d" % i, [128, 8, 2, 128], BF16) for i in range(2)]
    for i in range(2):
        OP(c, "pool", lambda e, i=i: e.memset(c.QTs[i][:], 0.0), [], [c.QTs[i]])
    c.sgt1 = T(k, "sgt1", [128, 512], F32)
    c.sgz = [T(k, "sgz%d" % i, [128, 1024], BF16) for i in range(2)]
    c.E = [T(k, "E%d" % i, [128, 512], F32) for i in range(4)]
    c.SP = [T(k, "SP%d" % i, [128, 512], BF16) for i in range(4)]
    c.X = [T(k, "X%d" % i, [128, 512], F32) for i in range(2)]
    c.Wt = [T(k, "Wt%d" % i, [128, 512], BF16) for i in range(2)]
    c.ob = [T(k, "ob%d" % i, [128, 1024], BF16) for i in range(2)]
    c.oT = T(k, "oT", [128, 1024], BF16)
    c.B = [T(k, "B%d" % i, [128, 512], F32, "ps") for i in range(8)]
    c.pT1 = View(c.B[6][:, :].bitcast(BF16).rearrange("p (c t) -> p c t", c=8), c.B[6].b)
    c.pTo = View(c.B[0][:, :].bitcast(BF16).rearrange("p (c t) -> p c t", c=8), c.B[0].b)


def l1_proj_gen(c, src_ap, nt, j, par, src_buf):
    k = c.k
    ht = c.ht[par]
    st2 = c.st2[par]
    P = slice(0, nt)
    identb = c.identb
    pos0 = 0 if j == 0 else 16 + (j - 1) * 128
    B = c.B
    k.dma("sp", ht[P, :], src_ap, r=[src_buf], w=[ht.b], sbuf=ht.b)
    ub = c.ub1
    ACT(c, ub[P, :], ht[P, :], AF.Square, [ht], [ub, st2], accum_out=st2[P, 0:1])
    ACT(c, st2[P, 1:2], st2[P, 0:1], AF.Ln, [st2], [st2], scale=1.0 / 1024, bias=1e-6)
    ACT(c, st2[P, 1:2], st2[P, 1:2], AF.Exp, [st2], [st2], scale=-0.5)
    TS(c, "dve", ub[P, :], ht[P, :], st2[P, 1:2], None, ALU.mult, None, [ht, st2], [ub])
    yield
    for kc in range(8):
        TR(c, c.pT1[:, kc, P], ub[P, kc * 128:(kc + 1) * 128], identb[P, P], [ub, identb], [c.pT1], sig=(kc == 7))
    uTv = c.uT1[:].rearrange("p (c t) -> p c t", c=8)
    CP(c, "dve", uTv[:, :, P], c.pT1[:, :, P], [c.pT1], [c.uT1])
    yield
    w = c.w_in1
    for g4 in range(2):
        pb = B[7 - g4]
        pbv = pb[:].rearrange("p (c t) -> p c t", c=4)
        for i in range(4):
            oc = g4 * 4 + i
            for kc in range(8):
                MM(c, pbv[:, i, P], w[:, kc, 1024 + oc * 128:1024 + (oc + 1) * 128], uTv[:, kc, P], kc == 0, kc == 7, [c.uT1, w], [pb], sig=(kc == 7))
            yield
        CP(c, "dve", c.KT[:, g4 * 4:(g4 + 1) * 4, pos0:pos0 + nt], pbv[:, :, P], [pb], [c.KTb[j]])
        yield
    for hf in range(2):
        pb = B[7 - hf]
        for kc in range(8):
            MM(c, pb[P, :], uTv[:, kc, P], w[:, kc, 2048 + hf * 512:2048 + (hf + 1) * 512], kc == 0, kc == 7, [c.uT1, w], [pb], sig=(kc == 7))
        yield
        CP(c, "dve", c.V[P, j, hf * 512:(hf + 1) * 512], pb[P, :], [pb], [c.Vb[j]])
        yield
    if j == 0:
        return
    QTs = c.QTs[par]
    for g4 in range(2):
        pb = B[7 - g4]
        pbv = pb[:].rearrange("p (c t) -> p c t", c=4)
        for i in range(4):
            oc = g4 * 4 + i
            for kc in range(8):
                MM(c, pbv[:, i, P], w[:, kc, oc * 128:(oc + 1) * 128], uTv[:, kc, P], kc == 0, kc == 7, [c.uT1, w], [pb], sig=(kc == 7))
            yield
        for hf_ in range(2):
            hp = slice(hf_ * 64, (hf_ + 1) * 64)
            TS(c, "dve", QTs[hp, g4 * 4:(g4 + 1) * 4, hf_, P], pbv[hp, :, P], 0.125, None, ALU.mult, None, [pb], [QTs])
        yield
    sgz = c.sgz[par]
    for hf in range(2):
        pb = B[7 - hf]
        hs = slice(hf * 512, (hf + 1) * 512)
        for kc in range(8):
            MM(c, pb[P, :], uTv[:, kc, P], w[:, kc, 3072 + hf * 512:3072 + (hf + 1) * 512], kc == 0, kc == 7, [c.uT1, w], [pb], sig=(kc == 7))
        yield
        ACT(c, c.sgt1[P, :], pb[P, :], AF.Exp, [pb], [c.sgt1], scale=-1.0)
        TS(c, "dve", c.sgt1[P, :], c.sgt1[P, :], 1.0, None, ALU.add, None, [c.sgt1], [c.sgt1])
        OP(c, "dve", lambda e: e.reciprocal(out=c.sgt1[P, :], in_=c.sgt1[P, :]), [c.sgt1], [c.sgt1])
        TT(c, "dve", sgz[P, hs], c.sgt1[P, :], pb[P, :], ALU.mult, [c.sgt1, pb], [sgz])
        yield


def run_gen(g, n=None):
    if g is None:
        return False
    try:
        if n is None:
            while True:
                next(g)
        for _ in range(n):
            next(g)
    except StopIteration:
        return False
    return True


def layer1_attn(c, nt, j, par, gen_next):
    k = c.k
    ht = c.ht[par]
    st2 = c.st2[par]
    P = slice(0, nt)
    identb = c.identb
    B = c.B
    QTs_ = c.QTs[par]
    sgz_ = c.sgz[par]

    units = []
    for pr in range(2):
        for kb in range(j, -1, -1):
            for q in range(2):
                units.append((kb, 2 * pr + q, q))
    nu = len(units)

    def kinfo(kb):
        if kb == 0:
            return 16, 0
        return 128, 16 + (kb - 1) * 128

    def views(u):
        kb, hg, q = units[u]
        ks, kp = kinfo(kb)
        return kb, hg, q, ks, kp

    def stageA(u):
        kb, hg, q, ks, kp = views(u)
        diag = (kb == j)
        z = B[u % 2]
        zv = z[:].rearrange("p (i t) -> p i t", i=4)
        first = True
        if diag:
            MM(c, z[0:ks, :], identb[0:ks, 0:ks], c.negm[0:ks, :, :].rearrange("p i t -> p (i t)"), True, False, [identb, c.negm], [z], sig=False)
            first = False
        for i2 in range(2):
            ch = 2 * hg + i2
            MM(c, z[0:ks, 2 * i2 * 128:(2 * i2 + 2) * 128], c.KT[:, ch, kp:kp + ks], QTs_[:, ch, :, :].rearrange("p a t -> p (a t)"), first, (i2 == 1) or first, [c.KTb[kb], QTs_], [z], sig=(i2 == 1))
        E = c.E[u % 4]
        SP = c.SP[u % 4]
        Ev = E[:].rearrange("p (i t) -> p i t", i=4)
        SPv = SP[:].rearrange("p (i t) -> p i t", i=4)
        ACT(c, Ev[0:ks, :, P], zv[0:ks, :, P], AF.Exp, [z], [E])
        ACT(c, SPv[0:ks, :, P], Ev[0:ks, :, P], AF.Ln, [E], [SP], bias=1.0)

    def stageB(u):
        kb, hg, q, ks, kp = views(u)
        tb = B[2 + q]
        tv = tb[:].rearrange("p (i t) -> p i t", i=4)
        SP = c.SP[u % 4]
        SPv = SP[:].rearrange("p (i t) -> p i t", i=4)
        MM(c, tb[0:ks, :], c.tri[0:ks, 0:ks], SP[0:ks, :], kb == j, False, [c.tri, SP], [tb], sig=True, skip=True)
        X = c.X[u % 2]
        Xv = X[:].rearrange("p (i t) -> p i t", i=4)
        ACT(c, Xv[0:ks, :, P], tv[0:ks, :, P], AF.Exp, [tb], [X], scale=-1.0)

    def stageC(u):
        kb, hg, q, ks, kp = views(u)
        tb = B[2 + q]
        tv = tb[:].rearrange("p (i t) -> p i t", i=4)
        SP = c.SP[u % 4]
        SPv = SP[:].rearrange("p (i t) -> p i t", i=4)
        if kb > 0:
            MM(c, tb[0:ks, :], c.tri2[0:ks, 0:ks], SP[0:ks, :], False, False, [c.tri2, SP], [tb], sig=True, skip=True)
        E = c.E[u % 4]
        X = c.X[u % 2]
        Wt = c.Wt[u % 2]
        Ev = E[:].rearrange("p (i t) -> p i t", i=4)
        Xv = X[:].rearrange("p (i t) -> p i t", i=4)
        Wv = Wt[:].rearrange("p (i t) -> p i t", i=4)
        TT(c, "dve", Wv[0:ks, :, P], Ev[0:ks, :, P], Xv[0:ks, :, P], ALU.mult, [E, X], [Wt])

    def stageD(u):
        kb, hg, q, ks, kp = views(u)
        ob_ = B[4 + q]
        Wt = c.Wt[u % 2]
        Wv = Wt[:].rearrange("p (i t) -> p i t", i=4)
        if kb == j:
            MM(c, ob_[P, 0:256], c.zrow[0:1, P], c.zrow[0:1, 0:256], True, False, [c.zrow], [ob_], sig=False)
        for i in range(4):
            hd = 4 * hg + i
            MM(c, ob_[P, i * 64:(i + 1) * 64], Wv[0:ks, i, P], c.V[0:ks, kb, hd * 64:(hd + 1) * 64], False, (kb == 0 and i == 3), [Wt, c.Vb[kb]], [ob_], sig=(i == 3))
        if kb == 0:
            hsl = slice(hg * 256, (hg + 1) * 256)
            TT(c, "dve", c.ob[par][P, hsl], ob_[P, 0:256], sgz_[P, hsl], ALU.mult, [ob_, sgz_], [c.ob[par]])

    per = max(1, -(-52 // max(1, nu - 2)))
    for step in range(nu + 3):
        if step < nu:
            stageA(step)
        if 0 <= step - 1 < nu:
            stageB(step - 1)
        if 0 <= step - 2 < nu:
            stageC(step - 2)
        if 0 <= step - 3 < nu:
            stageD(step - 3)
        run_gen(gen_next, per)
    run_gen(gen_next, None)


def l1_tail_gen(c, dst_ap, nt, par):
    k = c.k
    ht = c.ht[par]
    st2 = c.st2[par]
    ob = c.ob[par]
    P = slice(0, nt)
    identb = c.identb
    B = c.B
    for kc in range(8):
        TR(c, c.pT1[:, kc, P], ob[P, kc * 128:(kc + 1) * 128], identb[P, P], [ob, identb], [c.pT1], sig=(kc == 7))
    oTv = c.oT[:].rearrange("p (c t) -> p c t", c=8)
    CP(c, "dve", oTv[:, :, P], c.pT1[:, :, P], [c.pT1], [c.oT])
    yield
    for hf in range(2):
        pb = B[6 + hf]
        for kc in range(8):
            MM(c, pb[P, :], oTv[:, kc, P], c.w_out1[:, kc, hf * 512:(hf + 1) * 512], kc == 0, kc == 7, [c.oT, c.w_out1], [pb], sig=(kc == 7))
        yield
    for hf in range(2):
        hs = slice(hf * 512, (hf + 1) * 512)
        TT(c, "dve", ht[P, hs], ht[P, hs], B[6 + hf][P, :], ALU.add, [ht, B[6 + hf]], [ht])
    yield
    ACT(c, c.oT[P, :], ht[P, :], AF.Square, [ht], [c.oT, st2], accum_out=st2[P, 2:3])
    ACT(c, st2[P, 3:4], st2[P, 2:3], AF.Ln, [st2], [st2], scale=1.0 / 1024, bias=1e-6)
    ACT(c, st2[P, 3:4], st2[P, 3:4], AF.Exp, [st2], [st2], scale=-0.5)
    yield
    STT(c, "dve", ht[P, :], ht[P, :], st2[P, 3:4], c.fn[P, :], ALU.mult, ALU.mult, [ht, st2, c.fn], [ht])
    k.dma("sp", dst_ap, ht[P, :], r=[ht.b], sbuf=ht.b)
    yield


def chain_gens(*gens):
    for g in gens:
        if g is not None:
            yield from g


def layer1_seq(c, h1s, outs, nfull, par0, src_buf, first_seq=True):
    par = par0
    if first_seq:
        run_gen(l1_proj_gen(c, h1s[0], 16, 0, par, src_buf), None)
        par ^= 1
    run_gen(l1_proj_gen(c, h1s[1], 128, 1, par, src_buf), None)
    for j in range(1, nfull + 1):
        gt = l1_tail_gen(c, outs[j - 1], 128, par ^ 1) if j >= 2 else None
        gp = l1_proj_gen(c, h1s[j + 1], 128, j + 1, par ^ 1, src_buf) if j + 1 <= nfull else None
        layer1_attn(c, 128, j, par, chain_gens(gt, gp))
        par ^= 1
    run_gen(l1_tail_gen(c, outs[nfull], 128, par ^ 1), None)
    return par

NSEQ = 4
NFULL = 16
LTOT_ = 16 + NFULL * 128


def common_setup(c, sw=2312):
    k = c.k
    if sw:
        c.stage = [T(k, "stage%d" % i, [128, sw], F32) for i in range(2)]
    ident = T(k, "ident", [128, 128], F32)
    c.identb = T(k, "identb", [128, 128], BF16)
    OP(c, "pool", lambda e: e.memset(ident[:], 1.0), [], [ident])
    OP(c, "pool", lambda e: e.affine_select(out=ident[:], in_=ident[:], pattern=[[-1, 128]], compare_op=ALU.is_equal, fill=0.0, base=0, channel_multiplier=1), [ident], [ident])
    CP(c, "dve", c.identb[:], ident[:], [ident], [c.identb])


def build_l0(nseq, nfull):
    nc = bass.Bass("TRN2", target_bir_lowering=False)
    L = 16 + nfull * 128
    D = {}
    x = nc.dram_tensor("x", [nseq, nfull * 128, 1024], F32, kind="ExternalInput").ap()
    meta = nc.dram_tensor("meta", [16, 1024], F32, kind="ExternalInput").ap()
    D["w_in"] = nc.dram_tensor("w_in", [1024, 4624], F32, kind="ExternalInput").ap()
    D["w_out"] = nc.dram_tensor("w_out", [2048, 1024], F32, kind="ExternalInput").ap()
    D["wa"] = nc.dram_tensor("wa", [4, 256, 256], F32, kind="ExternalInput").ap()
    D["wx"] = nc.dram_tensor("wx", [4, 256, 256], F32, kind="ExternalInput").ap()
    D["vecF"] = nc.dram_tensor("vecF", [128, 140], F32, kind="ExternalInput").ap()
    D["vecT"] = nc.dram_tensor("vecT", [1, 48], F32, kind="ExternalInput").ap()
    h1 = nc.dram_tensor("h1", [nseq, L, 1024], F32, kind="ExternalOutput").ap()
    with ExitStack() as es:
        k = K(nc, es)
        c = setup_common(k, nc)
        common_setup(c, 0)
        alloc_layer0_work(c)
        setup_layer0(c, D)
        hb = Buf("h1dram")
        par = 0
        for s in range(nseq):
            tiles = [(meta[:, :], h1[s, 0:16, :], 16)]
            for j in range(nfull):
                tiles.append((x[s, j * 128:(j + 1) * 128, :], h1[s, 16 + j * 128:16 + (j + 1) * 128, :], 128))
            par = layer0_seq(c, tiles, par, hb, first_seq=(s == 0))
        k.final_wait("sp", [c.xt[0].b, c.xt[1].b])
        k.emit()
    return nc


def build_l1(nseq, nfull):
    nc = bass.Bass("TRN2", target_bir_lowering=False)
    L = 16 + nfull * 128
    D = {}
    h1 = nc.dram_tensor("h1", [nseq, L, 1024], F32, kind="ExternalInput").ap()
    D["w_in1"] = nc.dram_tensor("w_in1", [1024, 4096], F32, kind="ExternalInput").ap()
    D["w_out1"] = nc.dram_tensor("w_out1", [1024, 1024], F32, kind="ExternalInput").ap()
    D["vec1F"] = nc.dram_tensor("vec1F", [128, 8], F32, kind="ExternalInput").ap()
    D["fnorm"] = nc.dram_tensor("fnorm", [1, 1024], F32, kind="ExternalInput").ap()
    out = nc.dram_tensor("out", [nseq, nfull * 128, 1024], F32, kind="ExternalOutput").ap()
    with ExitStack() as es:
        k = K(nc, es)
        c = setup_common(k, nc)
        common_setup(c, 0)
        alloc_layer1_work(c)
        c.stage = list(c.E)
        setup_layer1(c, D, L)
        hb = Buf("h1dram")
        par = 0
        for s in range(nseq):
            h1s = [h1[s, 0:16, :]] + [h1[s, 16 + (j - 1) * 128:16 + j * 128, :] for j in range(1, nfull + 1)]
            outs = [None] + [out[s, (j - 1) * 128:j * 128, :] for j in range(1, nfull + 1)]
            par = layer1_seq(c, h1s, outs, nfull, par, hb, first_seq=(s == 0))
        k.final_wait("sp", [c.ht[0].b, c.ht[1].b])
        k.emit()
    return nc


def build_fused(nseq, nfull):
    nc = bass.Bass("TRN2", target_bir_lowering=False)
    L = 16 + nfull * 128
    D = {}
    x = nc.dram_tensor("x", [nseq, nfull * 128, 1024], F32, kind="ExternalInput").ap()
    meta = nc.dram_tensor("meta", [16, 1024], F32, kind="ExternalInput").ap()
    D["w_in"] = nc.dram_tensor("w_in", [1024, 4624], F32, kind="ExternalInput").ap()
    D["w_out"] = nc.dram_tensor("w_out", [2048, 1024], F32, kind="ExternalInput").ap()
    D["wa"] = nc.dram_tensor("wa", [4, 256, 256], F32, kind="ExternalInput").ap()
    D["wx"] = nc.dram_tensor("wx", [4, 256, 256], F32, kind="ExternalInput").ap()
    D["vecF"] = nc.dram_tensor("vecF", [128, 140], F32, kind="ExternalInput").ap()
    D["vecT"] = nc.dram_tensor("vecT", [1, 48], F32, kind="ExternalInput").ap()
    D["w_in1"] = nc.dram_tensor("w_in1", [1024, 4096], F32, kind="ExternalInput").ap()
    D["w_out1"] = nc.dram_tensor("w_out1", [1024, 1024], F32, kind="ExternalInput").ap()
    D["vec1F"] = nc.dram_tensor("vec1F", [128, 8], F32, kind="ExternalInput").ap()
    D["fnorm"] = nc.dram_tensor("fnorm", [1, 1024], F32, kind="ExternalInput").ap()
    out = nc.dram_tensor("out", [nseq, nfull * 128, 1024], F32, kind="ExternalOutput").ap()
    h1 = nc.dram_tensor("h1s", [nseq, L, 1024], F32, kind="Internal").ap()
    with ExitStack() as es:
        k = K(nc, es)
        c = setup_common(k, nc)
        common_setup(c, 0)
        hb = Buf("h1dram")
        k.push()
        alloc_layer0_work(c)
        setup_layer0(c, D)
        par = 0
        for s in range(nseq):
            tiles = [(meta[:, :], h1[s, 0:16, :], 16)]
            for j in range(nfull):
                tiles.append((x[s, j * 128:(j + 1) * 128, :], h1[s, 16 + j * 128:16 + (j + 1) * 128, :], 128))
            par = layer0_seq(c, tiles, par, hb, first_seq=(s == 0))
        k.barrier([c.xt[0].b, c.xt[1].b, c.vecF.b, c.vecT.b] + [t.b for t in c.stage])
        k.emit()
        k.pop()
        hb = Buf("h1dram2")
        k.push()
        alloc_layer1_work(c)
        c.stage = list(c.E)
        setup_layer1(c, D, L)
        par = 0
        for s in range(nseq):
            h1s = [h1[s, 0:16, :]] + [h1[s, 16 + (j - 1) * 128:16 + j * 128, :] for j in range(1, nfull + 1)]
            outs = [None] + [out[s, (j - 1) * 128:j * 128, :] for j in range(1, nfull + 1)]
            par = layer1_seq(c, h1s, outs, nfull, par, hb, first_seq=(s == 0))
        k.final_wait("sp", [c.ht[0].b, c.ht[1].b])
        k.emit()
        k.pop()
    return nc


def pack_vecs(inp):
    f = lambda v: np.ascontiguousarray(np.asarray(v).reshape(-1, 128).T)
    vecF = np.zeros((128, 140), np.float32)
    vecF[:, 0:8] = f(inp["even_norm"][0])
    vecF[:, 8:16] = f(inp["ssd_norm"][0])
    vecF[:, 16:48] = np.asarray(inp["lru_conv_w"][0]).reshape(4, 8, 128).transpose(2, 1, 0).reshape(128, 32)
    vecF[:, 48:56] = f(inp["lru_conv_b"][0])
    vecF[:, 56:64] = f(inp["lru_b_a"][0])
    vecF[:, 64:72] = f(inp["lru_b_x"][0])
    vecF[:, 72:80] = f(inp["lru_lambda"][0])
    vecF[:, 80:128] = np.asarray(inp["ssd_conv_w"][0]).reshape(4, 12, 128).transpose(2, 1, 0).reshape(128, 48)
    vecF[:, 128:140] = f(inp["ssd_conv_b"][0])
    vecT = np.concatenate([np.asarray(inp["ssd_dt_bias"][0]), np.asarray(inp["ssd_a_log"][0]), np.asarray(inp["ssd_d"][0])])[None, :].astype(np.float32)
    return vecF, vecT


_NC_CACHE = {}


def kernel(**inp):
    inp = {k_: np.asarray(v, dtype=np.float32) for k_, v in inp.items()}
    x = inp["x"]
    ncores = 8
    vecF, vecT = pack_vecs(inp)
    vec1F = np.ascontiguousarray(inp["odd_norm"][0].reshape(8, 128).T)
    if "f" not in _NC_CACHE:
        _NC_CACHE["f"] = build_fused(NSEQ, NFULL)
    xs = np.split(np.ascontiguousarray(x), ncores, axis=0)
    maps = [{"x": xs[i], "meta": inp["meta"], "w_in": inp["even_w_in"][0], "w_out": inp["even_w_out"][0],
             "wa": inp["lru_w_a"][0], "wx": inp["lru_w_x"][0], "vecF": vecF, "vecT": vecT,
             "w_in1": inp["odd_w_in"][0], "w_out1": inp["odd_w_out"][0], "vec1F": vec1F,
             "fnorm": inp["final_norm"][None, :]} for i in range(ncores)]
    r = run_bass_kernel_spmd(_NC_CACHE["f"], maps, core_ids=list(range(ncores)))
    return np.concatenate([r.results[i]["out"] for i in range(ncores)], axis=0).astype(np.float32)
```

```python
import numpy as np
from contextlib import ExitStack
import concourse.bass as bass
import concourse.mybir as mybir
from concourse.bass_utils import run_bass_kernel_spmd


F32 = mybir.dt.float32
BF16 = mybir.dt.bfloat16
AF = mybir.ActivationFunctionType
ALU = mybir.AluOpType
AX = mybir.AxisListType

ENGS = ("pe", "act", "dve", "pool", "sp")


class Buf:
    __slots__ = ("name", "w", "rs", "dsem", "dcnt")

    def __init__(self, name):
        self.name = name
        self.w = None
        self.rs = []
        self.dsem = None
        self.dcnt = 0


class K:
    def __init__(self, nc, es):
        self.nc = nc
        self.es = es
        self.prog = {e: [] for e in ENGS}
        self.sem = {e: es.enter_context(nc.semaphore("s_" + e)) for e in ENGS}
        self.cnt = {e: 0 for e in ENGS}
        self.waited = {e: {} for e in ENGS}
        self.pending = {e: [] for e in ENGS}
        self.nsem = 5
        self.ninstr = 0
        self.scopes = [es]

    def push(self):
        self.scopes.append(ExitStack())

    def pop(self):
        self.scopes.pop().close()

    def barrier(self, dma_bufs=()):
        for e in ENGS:
            if self.pending[e]:
                raise RuntimeError("barrier with pending unsignalled ops on " + e)
        waits = {}
        for e in ENGS:
            if e != "pool" and self.cnt[e] > 0:
                self._need("pool", (e, self.cnt[e], self.sem[e]), waits)
        for b in dma_bufs:
            self._need("pool", b.w, waits)
            for t in b.rs:
                self._need("pool", t, waits)
        self.cnt["pool"] += 1
        tok = ("pool", self.cnt["pool"], self.sem["pool"])
        self.prog["pool"].append((list(waits.values()), lambda e: e.engine_nop(), (self.sem["pool"], 1)))
        for e in ENGS:
            if e != "pool":
                self.wait_tok(e, tok)

    def sb(self, name, shape, dt):
        return self.scopes[-1].enter_context(self.nc.sbuf_tensor(name, list(shape), dt))

    def ps(self, name, shape, dt=F32):
        return self.scopes[-1].enter_context(self.nc.psum_tensor(name, list(shape), dt))

    def dsem_of(self, buf):
        if buf.dsem is None:
            buf.dsem = self.es.enter_context(self.nc.semaphore("d_" + buf.name))
            self.nsem += 1
        return buf.dsem

    def _need(self, eng, tok, waits):
        if tok is None:
            return
        key, val, semh = tok
        if key == eng and False:
            return
        cur = self.waited[eng].get(key, 0)
        if val > cur:
            self.waited[eng][key] = val
            waits[key] = (semh, val)

    def _deps(self, eng, r, w):
        waits = {}
        for b in r:
            if b.w is not None and b.w[0] == "PENDING":
                raise RuntimeError("read of buffer %s with unsignalled writer" % b.name)
            self._need(eng, b.w, waits)
        for b in w:
            if b.w is not None and b.w[0] == "PENDING":
                if b.w[1] != eng:
                    raise RuntimeError("write of buffer %s with unsignalled writer" % b.name)
            else:
                self._need(eng, b.w, waits)
            for t in b.rs:
                if t[0] == "PENDING":
                    if t[1] != eng:
                        raise RuntimeError("WAR on buffer %s with unsignalled reader" % b.name)
                else:
                    self._need(eng, t, waits)
        return list(waits.values())

    def op(self, eng, fn, r=(), w=(), sig=True):
        waits = self._deps(eng, r, w)
        if sig:
            self.cnt[eng] += 1
            tok = (eng, self.cnt[eng], self.sem[eng])
            semh = self.sem[eng]
            for (b, kind) in self.pending[eng]:
                if kind == "r":
                    b.rs = [t for t in b.rs if not (t[0] == "PENDING" and t[1] == eng)]
                    b.rs.append(tok)
                else:
                    b.w = tok
                    b.rs = [t for t in b.rs if not (t[0] == "PENDING" and t[1] == eng)]
            self.pending[eng] = []
            for b in r:
                b.rs.append(tok)
            for b in w:
                b.w = tok
                b.rs = []
            self.prog[eng].append((waits, fn, (semh, 1)))
        else:
            ptok = ("PENDING", eng)
            for b in r:
                b.rs.append(ptok)
                self.pending[eng].append((b, "r"))
            for b in w:
                b.w = ptok
                b.rs = []
                self.pending[eng].append((b, "w"))
            self.prog[eng].append((waits, fn, None))
        self.ninstr += 1

    def dma(self, eng, out_ap, in_ap, r=(), w=(), sbuf=None, **kw):
        waits = self._deps(eng, r, w)
        semh = self.dsem_of(sbuf)
        sbuf.dcnt += 1
        tok = ("d_" + sbuf.name, 16 * sbuf.dcnt, semh)
        for b in r:
            b.rs.append(tok)
        for b in w:
            b.w = tok
            b.rs = []
        self.prog[eng].append((waits, lambda e: e.dma_start(out=out_ap, in_=in_ap, **kw), (semh, 16)))
        self.ninstr += 1
        return tok

    def wait_tok(self, eng, tok):
        waits = {}
        self._need(eng, tok, waits)
        for (semh, val) in waits.values():
            self.prog[eng].append(([(semh, val)], None, None))

    def final_wait(self, eng, bufs):
        waits = {}
        for b in bufs:
            self._need(eng, b.w, waits)
            for t in b.rs:
                self._need(eng, t, waits)
        if waits:
            self.prog[eng].append((list(waits.values()), None, None))

    def emit(self):
        nc = self.nc
        engmap = {"pe": "tensor", "act": "scalar", "dve": "vector", "pool": "gpsimd", "sp": "sync"}
        with nc.Block() as block:
            for e in ENGS:
                prog = self.prog[e]

                def body(engine, prog=prog):
                    for waits, fn, inc in prog:
                        for (semh, val) in waits:
                            engine.wait_ge(semh, val)
                        if fn is not None:
                            ins = fn(engine)
                            if inc is not None:
                                ins.then_inc(inc[0], inc[1])
                getattr(block, engmap[e])(body)
        self.prog = {e: [] for e in ENGS}


D_MODEL = 1024
EVEN_IN = 4624


class T:
    def __init__(self, k, name, shape, dt, space="sb"):
        self.t = k.sb("t_" + name, shape, dt) if space == "sb" else k.ps("t_" + name, shape, dt)
        self.b = Buf(name)
        self.name = name

    def __getitem__(self, idx):
        return self.t[idx]


class View:
    def __init__(self, ap, b):
        self.ap = ap
        self.b = b

    def __getitem__(self, idx):
        return self.ap[idx]


def _b(xs):
    return [getattr(x, "b", x) for x in xs]


class Ctx:
    pass


def setup_common(k, nc):
    c = Ctx()
    c.k = k
    c.nc = nc
    c.rr = 0
    return c


def OP(c, eng, fn, r=(), w=(), sig=True):
    c.k.op(eng, fn, r=_b(r), w=_b(w), sig=sig)


def ACT(c, out, in_, func, r, w, **kw):
    OP(c, "act", lambda e: e.activation(out=out, in_=in_, func=func, **kw), r, w)


def TT(c, eng, out, in0, in1, op, r, w):
    OP(c, eng, lambda e: e.tensor_tensor(out=out, in0=in0, in1=in1, op=op), r, w)


def TS(c, eng, out, in0, s1, s2, op0, op1, r, w):
    if s2 is None:
        OP(c, eng, lambda e: e.tensor_scalar(out=out, in0=in0, scalar1=s1, scalar2=None, op0=op0), r, w)
    else:
        OP(c, eng, lambda e: e.tensor_scalar(out=out, in0=in0, scalar1=s1, scalar2=s2, op0=op0, op1=op1), r, w)


def STT(c, eng, out, in0, scalar, in1, op0, op1, r, w):
    OP(c, eng, lambda e: e.scalar_tensor_tensor(out=out, in0=in0, scalar=scalar, in1=in1, op0=op0, op1=op1), r, w)


def CP(c, eng, out, in_, r, w):
    if eng == "act":
        ACT(c, out, in_, AF.Copy, r, w)
    else:
        OP(c, eng, lambda e: e.tensor_copy(out=out, in_=in_), r, w)


def MM(c, out, lhsT, rhs, start, stop, r, w, sig, skip=False):
    if skip:
        OP(c, "pe", lambda e: e.matmul(out, lhsT=lhsT, rhs=rhs, start=start, stop=stop, skip_group_check=True), r, w, sig=sig)
    else:
        OP(c, "pe", lambda e: e.matmul(out, lhsT=lhsT, rhs=rhs, start=start, stop=stop), r, w, sig=sig)


def TR(c, out, in_, ident, r, w, sig):
    OP(c, "pe", lambda e: e.transpose(out=out, in_=in_, identity=ident), r, w, sig=sig)


def load_cast_weight(c, dst, dst_slices, dram_rows, width, scale_aps, st, engs=("act", "dve")):
    k = c.k
    for i, (da, ra) in enumerate(zip(dst_slices, dram_rows)):
        s = st[c.rr % len(st)]
        eng = engs[c.rr % len(engs)]
        c.rr += 1
        k.dma("sp", s.t[:, 0:width], ra, w=[s.b], sbuf=s.b)
        sc = scale_aps[i]
        if sc is None:
            CP(c, eng, da, s.t[:, 0:width], [s], [dst])
        else:
            if eng == "act":
                OP(c, "act", lambda e, da=da, s=s, sc=sc: e.activation(out=da, in_=s.t[:, 0:width], func=AF.Copy, scale=sc), [s, c.vecF], [dst])
            else:
                TS(c, eng, da, s.t[:, 0:width], sc, None, ALU.mult, None, [s, c.vecF], [dst])


def setup_layer0(c, D):
    k = c.k
    c.w_in = T(k, "w_in", [128, 8, EVEN_IN], BF16)
    c.w_out = T(k, "w_out", [128, 16, 1024], BF16)
    c.wa = T(k, "wa", [128, 4, 2, 256], BF16)
    c.wx = T(k, "wx", [128, 4, 2, 256], BF16)
    c.vecF = T(k, "vecF", [128, 140], F32)
    c.vecT = T(k, "vecT", [128, 48], F32)
    k.dma("sp", c.vecF[:], D["vecF"][:, :], w=[c.vecF.b], sbuf=c.vecF.b)
    k.dma("sp", c.vecT[:], D["vecT"][0:1, :].partition_broadcast(128), w=[c.vecT.b], sbuf=c.vecT.b)
    st = c.stage
    for (c0, wd) in ((0, 1024), (1024, 1024), (2048, 1024), (3072, 1024), (4096, 528)):
        dsts, rows, scs = [], [], []
        for kc in range(8):
            dsts.append(c.w_in[:, kc, c0:c0 + wd])
            rows.append(D["w_in"][kc * 128:(kc + 1) * 128, c0:c0 + wd])
            scs.append(c.vecF[:, kc:kc + 1])
        load_cast_weight(c, c.w_in, dsts, rows, wd, scs, st)
    dsts, rows, scs = [], [], []
    for kc in range(16):
        dsts.append(c.w_out[:, kc, :])
        rows.append(D["w_out"][kc * 128:(kc + 1) * 128, :])
        scs.append(None if kc < 8 else c.vecF[:, 8 + kc - 8:8 + kc - 8 + 1])
    load_cast_weight(c, c.w_out, dsts, rows, 1024, scs, st)
    for (wt, nm) in ((c.wa, "wa"), (c.wx, "wx")):
        dsts, rows, scs = [], [], []
        for g in range(4):
            for kc in range(2):
                dsts.append(wt[:, g, kc, :])
                rows.append(D[nm][g, kc * 128:(kc + 1) * 128, :])
                scs.append(None)
        load_cast_weight(c, wt, dsts, rows, 256, scs, st)

    c.L1 = T(k, "L1", [128, 128], F32)
    c.L2 = T(k, "L2", [128, 128], F32)
    c.L4 = T(k, "L4", [128, 2, 128], F32)
    c.mle = T(k, "mle", [128, 64], F32)
    OP(c, "pool", lambda e: e.memset(c.L1[:], 0.0), [], [c.L1])
    OP(c, "pool", lambda e: e.memset(c.L2[:], 0.0), [], [c.L2])
    OP(c, "pool", lambda e: e.memset(c.L4[:], 0.0), [], [c.L4])
    OP(c, "pool", lambda e: e.memset(c.mle[:], 1.0), [], [c.mle])
    for h in range(2):
        ps = slice(h * 64, (h + 1) * 64)
        OP(c, "pool", lambda e, ps=ps: e.memset(c.L1[ps, ps], 1.0), [], [c.L1])
        OP(c, "pool", lambda e, ps=ps: e.memset(c.L2[ps, ps], 1.0), [], [c.L2])
        OP(c, "pool", lambda e, ps=ps, h=h: e.memset(c.L4[ps, h, :], 1.0), [], [c.L4])
        OP(c, "pool", lambda e, ps=ps: e.affine_select(out=c.L1[ps, ps], in_=c.L1[ps, ps], pattern=[[1, 64]], compare_op=ALU.is_ge, fill=0.0, base=0, channel_multiplier=-1), [c.L1], [c.L1])
        OP(c, "pool", lambda e, ps=ps: e.affine_select(out=c.L2[ps, ps], in_=c.L2[ps, ps], pattern=[[-1, 64]], compare_op=ALU.is_gt, fill=0.0, base=0, channel_multiplier=1), [c.L2], [c.L2])
        OP(c, "pool", lambda e, ps=ps: e.affine_select(out=c.mle[ps, :], in_=c.mle[ps, :], pattern=[[1, 64]], compare_op=ALU.is_ge, fill=0.0, base=0, channel_multiplier=-1), [c.mle], [c.mle])
    c.pv = T(k, "pv", [128, 64], F32)
    ACT(c, c.pv[:, 0:8], c.vecF[:, 72:80], AF.Exp, [c.vecF], [c.pv], scale=-1.0)
    ACT(c, c.pv[:, 0:8], c.pv[:, 0:8], AF.Ln, [c.pv], [c.pv], bias=1.0)
    TS(c, "dve", c.pv[:, 8:16], c.pv[:, 0:8], -16.0, None, ALU.mult, None, [c.pv], [c.pv])
    TS(c, "dve", c.pv[:, 0:8], c.pv[:, 0:8], -8.0, None, ALU.mult, None, [c.pv], [c.pv])
    TS(c, "dve", c.pv[:, 16:32], c.vecF[:, 56:72], -1.0, None, ALU.mult, None, [c.vecF], [c.pv])
    ACT(c, c.pv[:, 32:48], c.vecT[:, 16:32], AF.Exp, [c.vecT], [c.pv])
    TS(c, "dve", c.pv[:, 32:48], c.pv[:, 32:48], -1.0, None, ALU.mult, None, [c.pv], [c.pv])


def alloc_layer0_work(c):
    k = c.k
    c.xt = [T(k, "xt%d" % i, [128, 1024], F32) for i in range(2)]
    c.st1 = [T(k, "st1_%d" % i, [128, 8], F32) for i in range(2)]
    c.W = [T(k, "W%d" % i, [128, 1024], F32) for i in range(6)]
    c.H = [T(k, "H%d" % i, [128, (256 if i == 1 else (512 if i == 5 else 1024))], BF16) for i in range(6)]
    c.projF = [T(k, "projF%d" % i, [128, 20, 131], F32) for i in range(2)]
    c.sg = [T(k, "sg%d" % i, [128, 1024], BF16) for i in range(2)]
    c.gz = [T(k, "gz%d" % i, [128, 1024], BF16) for i in range(2)]
    c.dts = [T(k, "dts%d" % i, [128, 32], F32) for i in range(2)]
    c.ubP = T(k, "ubP", [128, 1024], BF16)
    c.sgt = View(c.ubP.t[:, :].bitcast(F32), c.ubP.b)
    c.uTP = T(k, "uTP", [128, 1024], BF16)
    c.xbc = T(k, "xbc", [128, 12, 128], F32)
    c.lxcb = [Buf("lxc%d" % i) for i in range(8)]
    c.xbccb = [Buf("xbcc%d" % i) for i in range(12)]
    c.S = T(k, "S", [128, 1024], F32)
    c.Sbf = T(k, "Sbf", [128, 1024], BF16)
    c.hst = T(k, "hst", [128, 8], F32)
    c.S_meta = T(k, "S_meta", [128, 1024], F32)
    c.hst_meta = T(k, "hst_meta", [128, 8], F32)
    c.hist_meta = T(k, "hist_meta", [128, 20, 3], F32)
    c.sm = T(k, "sm", [128, 96], F32)
    c.cbm = T(k, "cbm", [128, 2, 64], F32)
    c.pT = T(k, "pT", [128, 8, 128], BF16, "ps")
    c.pG = T(k, "pG", [128, 512], F32, "ps")
    c.pT2 = View(c.pG[:, 256:384].bitcast(BF16).rearrange("p (c t) -> p c t", c=2), c.pG.b)
    c.pB = [T(k, "pB%d" % i, [128, 512], F32, "ps") for i in range(6)]
    c.stage = [c.W[0], c.W[1], c.W[2], c.W[3]]
    c.pTP = View(c.pB[0][:, :].bitcast(BF16).rearrange("p (c t) -> p c t", c=8), c.pB[0].b)


def seq_reset0(c, par):
    OP(c, "pool", lambda e: e.memset(c.projF[par][:, :, 0:3], 0.0), [], [c.projF[par]])
    OP(c, "pool", lambda e: e.memset(c.S[:], 0.0), [], [c.S])
    OP(c, "pool", lambda e: e.memset(c.Sbf[:], 0.0), [], [c.Sbf])
    OP(c, "pool", lambda e: e.memset(c.hst[:], 0.0), [], [c.hst])


def act_sigmoid_from(c, out, in_, rin, wout, neg_bias=None):
    ACT(c, out, in_, AF.Exp, rin, wout, scale=-1.0)
    ACT(c, out, out, AF.Ln, wout, wout, bias=1.0)
    ACT(c, out, out, AF.Exp, wout, wout, scale=-1.0)


def layer0_P(c, src_ap, nt, par, prev):
    k = c.k
    xt = c.xt[par]
    st1 = c.st1[par]
    P = slice(0, nt)
    identb = c.identb
    projF = c.projF[par]
    k.dma("sp", xt[P, :], src_ap, w=[xt.b], sbuf=xt.b)
    ACT(c, c.ubP[P, :], xt[P, :], AF.Square, [xt], [c.ubP, st1], accum_out=st1[P, 0:1])
    ACT(c, st1[P, 1:2], st1[P, 0:1], AF.Ln, [st1], [st1], scale=1.0 / D_MODEL, bias=1e-6)
    ACT(c, st1[P, 1:2], st1[P, 1:2], AF.Exp, [st1], [st1], scale=-0.5)
    ub = c.ubP
    TS(c, "dve", ub[P, :], xt[P, :], st1[P, 1:2], None, ALU.mult, None, [xt, st1], [ub])
    yield
    for kc in range(8):
        TR(c, c.pTP[:, kc, P], ub[P, kc * 128:(kc + 1) * 128], identb[P, P], [ub, identb], [c.pTP], sig=(kc == 7))
    uT = c.uTP
    uTv = uT[:].rearrange("p (c t) -> p c t", c=8)
    CP(c, "act", uTv[:, :, P], c.pTP[:, :, P], [c.pTP], [uT])
    yield
    if prev == "meta":
        CP(c, "pool", projF[:, :, 0:3], c.hist_meta[:, :, :], [c.hist_meta], [projF])
    elif prev is not None:
        pp, pnt = prev
        CP(c, "pool", projF[:, :, 0:3], c.projF[pp][:, :, pnt:pnt + 3], [c.projF[pp]], [projF])
    pz = (c.pB[0], c.pB[1])
    gz = c.gz[par]
    dts = c.dts[par]
    for kc in range(8):
        MM(c, c.pB[1][P, 0:16], uTv[:, kc, P], c.w_in[:, kc, 4608:4624], kc == 0, kc == 7, [uT, c.w_in], [c.pB[1]], sig=(kc == 7))
    TT(c, "dve", dts[P, 0:16], c.pB[1][P, 0:16], c.vecT[P, 0:16], ALU.add, [c.pB[1], c.vecT], [dts])
    yield
    ACT(c, dts[P, 0:16], dts[P, 0:16], AF.Exp, [dts], [dts])
    ACT(c, dts[P, 0:16], dts[P, 0:16], AF.Ln, [dts], [dts], bias=1.0)
    TT(c, "dve", dts[P, 16:32], dts[P, 0:16], c.pv[P, 32:48], ALU.mult, [dts, c.pv], [dts])
    yield
    for hf in range(2):
        for kc in range(8):
            MM(c, pz[hf][P, :], uTv[:, kc, P], c.w_in[:, kc, 2048 + hf * 512:2048 + (hf + 1) * 512], kc == 0, kc == 7, [uT, c.w_in], [pz[hf]], sig=(kc == 7))
        yield
    for hf in range(2):
        hs = slice(hf * 512, (hf + 1) * 512)
        act_sigmoid_from(c, c.sgt[P, :], pz[hf][P, :], [pz[hf]], [c.sgt])
        yield
        TT(c, "dve", gz[P, hs], c.sgt[P, :], pz[hf][P, :], ALU.mult, [c.sgt, pz[hf]], [gz])
        yield
    sg = c.sg[par]
    sgv = sg[:].rearrange("p (c t) -> p c t", c=8)
    groups = []
    for g4 in range(2):
        groups.append(("x", [g4 * 4 + i for i in range(4)], 0))
    for g4 in range(2):
        groups.append(("g", [g4 * 4 + i for i in range(4)], 1024))
    for g4 in range(3):
        groups.append(("b", [g4 * 4 + i for i in range(4)], 3072))
    for gi, (kind, ocs, colbase) in enumerate(groups):
        pb = c.pB[gi % 2]
        pbv = pb[:].rearrange("p (c t) -> p c t", c=4)
        for i, oc in enumerate(ocs):
            for kc in range(8):
                MM(c, pbv[:, i, P], c.w_in[:, kc, colbase + oc * 128:colbase + (oc + 1) * 128], uTv[:, kc, P], kc == 0, kc == 7, [uT, c.w_in], [pb], sig=(kc == 7))
            yield
        if kind == "x":
            CP(c, "act", projF[:, ocs[0]:ocs[0] + 4, 3:3 + nt], pbv[:, :, P], [pb], [projF])
        elif kind == "b":
            CP(c, "act", projF[:, 8 + ocs[0]:8 + ocs[0] + 4, 3:3 + nt], pbv[:, :, P], [pb], [projF])
        else:
            o = sgv[:, ocs[0]:ocs[0] + 4, P]
            sc = c.sgt[:].rearrange("p (c t) -> p c t", c=4)[:, :, P]
            act_sigmoid_from(c, sc, pbv[:, :, P], [pb], [c.sgt])
            yield
            TT(c, "dve", o, sc, pbv[:, :, P], ALU.mult, [c.sgt, pb], [sg])
        yield


def layer0_M(c, dst_ap, nt, par, dst_buf):
    k = c.k
    xt = c.xt[par]
    st1 = c.st1[par]
    W = c.W
    H = c.H
    chunks = [(0, nt)] if nt <= 64 else [(0, 64), (64, 128)]
    cw = chunks[0][1]
    nch = len(chunks)
    identb = c.identb
    P = slice(0, nt)
    projF = c.projF[par]
    sg = c.sg[par]
    sgv = sg[:].rearrange("p (c t) -> p c t", c=8)
    gz = c.gz[par]
    dts = c.dts[par]
    sm = c.sm

    lx = W[3]
    lxv = lx[:].rearrange("p (c t) -> p c t", c=8)
    lcb = c.lxcb
    ne = 0
    for tp in range(4):
        for ch in range(8):
            o = lxv[:, ch, P]
            if tp == 0:
                TS(c, "dve", o, projF[:, ch, 0:nt], c.vecF[:, 16 + ch * 4:16 + ch * 4 + 1], c.vecF[:, 48 + ch:48 + ch + 1], ALU.mult, ALU.add, [projF, c.vecF], ([lx, lcb[ch]] if ch == 0 else [lcb[ch]]))
            else:
                STT(c, "dve", o, projF[:, ch, tp:tp + nt], c.vecF[:, 16 + ch * 4 + tp:16 + ch * 4 + tp + 1], o, ALU.mult, ALU.add, [projF, c.vecF, lcb[ch]], [lcb[ch]])
            ne += 1
            if ne % 4 == 0:
                yield
    yield
    lxb = H[2]
    lxbv = lxb[:].rearrange("p (c t) -> p c t", c=8)
    CP(c, "act", lxbv[:, :, P], lxv[:, :, P], [lx] + c.lxcb, [lxb])
    ea_ = W[4]
    ex_ = W[5]
    eav = ea_[:].rearrange("p (c t) -> p c t", c=8)
    exv = ex_[:].rearrange("p (c t) -> p c t", c=8)
    for (wt, pbs, ev, boff, et) in ((c.wa, (c.pB[2], c.pB[3]), eav, 16, ea_), (c.wx, (c.pB[4], c.pB[5]), exv, 24, ex_)):
        for oc in range(8):
            g = oc // 2
            pb = pbs[oc // 4]
            pbv = pb[:].rearrange("p (c t) -> p c t", c=4)
            for kc in range(2):
                MM(c, pbv[:, oc % 4, P], wt[:, g, kc, (oc % 2) * 128:(oc % 2 + 1) * 128], lxbv[:, 2 * g + kc, P], kc == 0, kc == 1, [lxb, wt], [pb], sig=(oc % 4 == 3 and kc == 1))
    yield
    xbc = c.xbc
    xcb = c.xbccb
    ne = 0
    for tp in range(4):
        for ch in range(12):
            o = xbc[:, ch, P]
            if tp == 0:
                TS(c, "dve", o, projF[:, 8 + ch, 0:nt], c.vecF[:, 80 + ch * 4:80 + ch * 4 + 1], c.vecF[:, 128 + ch:128 + ch + 1], ALU.mult, ALU.add, [projF, c.vecF], ([xbc, xcb[ch]] if ch == 0 else [xcb[ch]]))
            else:
                STT(c, "dve", o, projF[:, 8 + ch, tp:tp + nt], c.vecF[:, 80 + ch * 4 + tp:80 + ch * 4 + tp + 1], o, ALU.mult, ALU.add, [projF, c.vecF, xcb[ch]], [xcb[ch]])
            ne += 1
            if ne % 4 == 0:
                yield
    for (wt, pbs, ev, boff, et) in ((c.wa, (c.pB[2], c.pB[3]), eav, 16, ea_), (c.wx, (c.pB[4], c.pB[5]), exv, 24, ex_)):
        for oc in range(8):
            pb = pbs[oc // 4]
            pbv = pb[:].rearrange("p (c t) -> p c t", c=4)
            ACT(c, ev[:, oc, P], pbv[:, oc % 4, P], AF.Exp, [pb, c.pv], [et], scale=-1.0, bias=c.pv[:, boff + oc:boff + oc + 1])
            if oc % 4 == 3:
                yield
        ACT(c, ev[:, :, P], ev[:, :, P], AF.Ln, [et], [et], bias=1.0)
        ACT(c, ev[:, :, P], ev[:, :, P], AF.Exp, [et], [et], scale=-1.0)
    yield
    e0 = W[0][:].rearrange("p (c t) -> p c t", c=8)
    e1 = W[1][:].rearrange("p (c t) -> p c t", c=8)
    act_sigmoid_from(c, e0[:, :, P], xbc[:, 0:8, P], [xbc] + c.xbccb, [W[0]])
    act_sigmoid_from(c, e1[:, 0:4, P], xbc[:, 8:12, P], [xbc] + c.xbccb, [W[1]])
    yield
    xsT = H[3][:].rearrange("p (c t) -> p c t", c=8)
    bcT = H[5][:].rearrange("p (c t) -> p c t", c=4)
    TT(c, "dve", xsT[:, :, P], e0[:, :, P], xbc[:, 0:8, P], ALU.mult, [W[0], xbc] + c.xbccb, [H[3]])
    TT(c, "dve", bcT[:, 0:4, P], e1[:, 0:4, P], xbc[:, 8:12, P], ALU.mult, [W[1], xbc] + c.xbccb, [H[5]])
    yield
    a_ = W[2]
    av = a_[:].rearrange("p (c t) -> p c t", c=8)
    a2_ = W[1]
    a2v = a2_[:].rearrange("p (c t) -> p c t", c=8)
    for ch in range(8):
        ACT(c, av[:, ch, P], eav[:, ch, P], AF.Exp, [ea_, c.pv], [a_], scale=c.pv[:, ch:ch + 1])
        ACT(c, a2v[:, ch, P], eav[:, ch, P], AF.Exp, [ea_, c.pv], [a2_], scale=c.pv[:, 8 + ch:8 + ch + 1])
        if ch % 4 == 3:
            yield
    for ch in range(8):
        TR(c, c.pT[P, ch, :], xsT[:, ch, P], identb[:, :], [H[3], identb], [c.pT], sig=(ch == 7))
    for ch in range(2):
        TR(c, c.pT2[P, ch, :], bcT[:, ch, P], identb[:, :], [H[5], identb], [c.pT2], sig=(ch == 1))
    yield
    TS(c, "dve", a2v[:, :, P], a2v[:, :, P], -1.0, 1.0, ALU.mult, ALU.add, [a2_], [a2_])
    ACT(c, a2v[:, :, P], a2v[:, :, P], AF.Ln, [a2_], [a2_])
    ACT(c, a2v[:, :, P], a2v[:, :, P], AF.Exp, [a2_], [a2_], scale=0.5)
    yield
    Xps = c.pT[:].rearrange("p c t -> p (c t)")
    Xdt = H[0]
    TT(c, "dve", Xdt[P, :].rearrange("p (h d) -> p h d", h=16), Xps[P, :].rearrange("p (h d) -> p h d", h=16),
       dts[P, 0:16].unsqueeze(2).to_broadcast([nt, 16, 64]), ALU.mult, [c.pT, dts], [Xdt])
    skip = W[0]
    TT(c, "dve", skip[P, :].rearrange("p (h d) -> p h d", h=16), Xps[P, :].rearrange("p (h d) -> p h d", h=16),
       c.vecT[P, 32:48].unsqueeze(2).to_broadcast([nt, 16, 64]), ALU.mult, [c.pT, c.vecT], [skip])
    yield
    Btok = H[1]
    CP(c, "act", Btok[P, 0:256], c.pT2[P, :, :].rearrange("p c t -> p (c t)"), [c.pT2], [Btok])
    MM(c, c.pG[P, 16:32], c.L1[P, P], dts[P, 16:32], True, True, [c.L1, dts], [c.pG], sig=False)
    MM(c, c.pG[P, 32:48], c.L2[P, P], dts[P, 16:32], True, True, [c.L2, dts], [c.pG], sig=False)
    for ci in range(nch):
        MM(c, c.pG[:, 48 + 16 * ci:64 + 16 * ci], c.L4[P, ci, :], dts[P, 16:32], True, True, [c.L4, dts], [c.pG], sig=(ci == nch - 1))
    ACT(c, sm[P, 32:48], c.pG[P, 16:32], AF.Exp, [c.pG], [sm])
    ACT(c, sm[P, 48:64], c.pG[P, 32:48], AF.Exp, [c.pG], [sm])
    ACT(c, sm[:, 64:64 + 16 * nch], c.pG[:, 48:48 + 16 * nch], AF.Exp, [c.pG], [sm])
    yield
    TT(c, "dve", exv[:, :, P], exv[:, :, P], a2v[:, :, P], ALU.mult, [ex_, a2_], [ex_])
    TT(c, "dve", exv[:, :, P], exv[:, :, P], lxv[:, :, P], ALU.mult, [ex_, lx] + c.lxcb, [ex_])
    for ch in range(8):
        OP(c, "dve", lambda e, ch=ch: e.tensor_tensor_scan(out=eav[:, ch, P], data0=av[:, ch, P], data1=exv[:, ch, P], initial=c.hst[:, ch:ch + 1], op0=ALU.mult, op1=ALU.add), [a_, ex_, c.hst], [ea_])
        if ch % 4 == 3:
            yield
    CP(c, "dve", c.hst[:, :], eav[:, :, nt - 1], [ea_], [c.hst])
    mixA = H[4][:].rearrange("p (c t) -> p c t", c=8)
    mixB = H[2][:].rearrange("p (c t) -> p c t", c=8)
    TT(c, "pool", mixA[:, :, P], eav[:, :, P], sgv[:, :, P], ALU.mult, [ea_, sg], [H[4]])
    yield
    R1 = W[1]
    R1v = R1[:, 0:16 * cw].rearrange("p (h l) -> p h l", h=16)
    TT(c, "dve", R1v[P, :, :], dts[P, 16:32].unsqueeze(2).to_broadcast([nt, 16, cw]),
       c.mle[P, 0:cw].unsqueeze(1).to_broadcast([nt, 16, cw]), ALU.mult, [dts, c.mle], [R1])
    for hf in range(2):
        pb = c.pB[2 + hf]
        MM(c, pb[P, 0:8 * cw], c.L2[P, P], R1[P, hf * 8 * cw:(hf + 1) * 8 * cw], True, True, [c.L2, R1], [pb], sig=True)
    dec = W[2]
    decv = dec[:, 0:16 * cw].rearrange("p (h l) -> p h l", h=16)
    for hf in range(2):
        pb = c.pB[2 + hf]
        ACT(c, dec[P, hf * 8 * cw:(hf + 1) * 8 * cw], pb[P, 0:8 * cw], AF.Exp, [pb], [dec])
    yield
    cbps = c.pG[:, 128:256].rearrange("p (g l) -> p g l", g=2)
    for ci, (p0, p1) in enumerate(chunks):
        for g in range(2):
            MM(c, cbps[p0:p1, g, 0:cw], bcT[:, g, p0:p1], bcT[:, 2 + g, p0:p1], True, True, [H[5]], [c.pG], sig=(ci == nch - 1 and g == 1))
    TT(c, "dve", c.cbm[P, :, 0:cw], cbps[P, :, 0:cw], c.mle[P, 0:cw].unsqueeze(1).to_broadcast([nt, 2, cw]), ALU.mult, [c.pG, c.mle], [c.cbm])
    yield
    MT = H[3]
    MTv = MT[:, 0:16 * cw].rearrange("p (h l) -> p h l", h=16)
    for g in range(2):
        TT(c, "dve", MTv[P, g * 8:(g + 1) * 8, :], decv[P, g * 8:(g + 1) * 8, :],
           c.cbm[P, g:g + 1, 0:cw].to_broadcast([nt, 8, cw]), ALU.mult, [dec, c.cbm], [MT])
    yield
    Xd = H[2]
    TT(c, "pool", Xd[P, :].rearrange("p (h d) -> p h d", h=16), Xdt[P, :].rearrange("p (h d) -> p h d", h=16),
       sm[P, 48:64].unsqueeze(2).to_broadcast([nt, 16, 64]), ALU.mult, [Xdt, sm], [H[2]])
    for ci, (p0, p1) in enumerate(chunks):
        for h in range(16):
            pb = c.pB[4 + h // 8]
            MM(c, pb[p0:p1, (h % 8) * 64:(h % 8 + 1) * 64], MTv[p0:p1, h, :], Xdt[p0:p1, h * 64:(h + 1) * 64], True, True, [MT, Xdt], [pb],
               sig=(h % 8 == 7))
        yield
    y = W[1]
    for ci, (p0, p1) in enumerate(chunks):
        PC = slice(p0, p1)
        ncw = p1 - p0
        for g in range(2):
            pb = c.pB[2 + g]
            MM(c, pb[p0:p1, :], bcT[:, 2 + g, p0:p1], c.Sbf[:, g * 512:(g + 1) * 512], True, True, [H[5], c.Sbf], [pb], sig=True)
        yield
        for g in range(2):
            gs = slice(g * 512, (g + 1) * 512)
            TT(c, "dve", y[PC, gs].rearrange("p (h d) -> p h d", h=8), c.pB[2 + g][PC, :].rearrange("p (h d) -> p h d", h=8),
               sm[PC, 32 + 8 * g:40 + 8 * g].unsqueeze(2).to_broadcast([ncw, 8, 64]), ALU.mult, [c.pB[2 + g], sm], [y])
        yield
        for g in range(2):
            pb = c.pB[2 + g]
            MM(c, pb[:, :], Btok[p0:p1, g * 128:(g + 1) * 128], Xd[p0:p1, g * 512:(g + 1) * 512], True, True, [Btok, H[2]], [pb], sig=True)
        TT(c, "pool", c.S[:].rearrange("p (h d) -> p h d", h=16), c.S[:].rearrange("p (h d) -> p h d", h=16),
           sm[:, 64 + 16 * ci:80 + 16 * ci].unsqueeze(2).to_broadcast([128, 16, 64]), ALU.mult, [c.S, sm], [c.S])
        yield
        for g in range(2):
            gs = slice(g * 512, (g + 1) * 512)
            TT(c, "dve", c.S[:, gs], c.S[:, gs], c.pB[2 + g][:, :], ALU.add, [c.S, c.pB[2 + g]], [c.S])
        CP(c, "act", c.Sbf[:], c.S[:], [c.S], [c.Sbf])
        yield
    for g in range(2):
        gs = slice(g * 512, (g + 1) * 512)
        TT(c, "dve", y[P, gs], y[P, gs], c.pB[4 + g][P, :], ALU.add, [y, c.pB[4 + g]], [y])
    yield
    yield
    TT(c, "dve", y[P, :], y[P, :], skip[P, :], ALU.add, [y, skip], [y])
    TT(c, "dve", y[P, :], y[P, :], gz[P, :], ALU.mult, [y, gz], [y])
    for g in range(2):
        gs = slice(g * 512, (g + 1) * 512)
        ACT(c, W[4][P, gs], y[P, gs], AF.Square, [y], [W[4], st1], accum_out=st1[P, 2 + g:3 + g])
    ACT(c, st1[P, 4:6], st1[P, 2:4], AF.Ln, [st1], [st1], scale=1.0 / 512, bias=1e-6)
    ACT(c, st1[P, 4:6], st1[P, 4:6], AF.Exp, [st1], [st1], scale=-0.5)
    yield
    yb = H[0]
    for g in range(2):
        gs = slice(g * 512, (g + 1) * 512)
        TS(c, "dve", yb[P, gs], y[P, gs], st1[P, 4 + g:5 + g], None, ALU.mult, None, [y, st1], [yb])
    for ch in range(8):
        TR(c, c.pT[:, ch, P], yb[P, ch * 128:(ch + 1) * 128], identb[P, P], [yb, identb], [c.pT], sig=(ch == 7))
    CP(c, "act", mixB[:, :, P], c.pT[:, :, P], [c.pT], [H[2]])
    yield
    for hf in range(2):
        pb = c.pB[2 + hf]
        for kc in range(16):
            lhs = mixA[:, kc, P] if kc < 8 else mixB[:, kc - 8, P]
            MM(c, pb[P, :], lhs, c.w_out[:, kc, hf * 512:(hf + 1) * 512], kc == 0, kc == 15, [H[4], H[2], c.w_out], [pb], sig=(kc == 15))
    yield
    for hf in range(2):
        hs = slice(hf * 512, (hf + 1) * 512)
        TT(c, "dve", xt[P, hs], xt[P, hs], c.pB[2 + hf][P, :], ALU.add, [xt, c.pB[2 + hf]], [xt])
    k.dma("sp", dst_ap, xt[P, :], r=[xt.b], w=[dst_buf], sbuf=xt.b)


def interleave(gm, gp, ratio=1):
    am, ap = gm is not None, gp is not None
    while am or ap:
        if am:
            for _ in range(ratio):
                try:
                    next(gm)
                except StopIteration:
                    am = False
                    break
        if ap:
            try:
                next(gp)
            except StopIteration:
                ap = False


def layer0_seq(c, tiles, par0, dst_buf, first_seq=True):
    par = par0
    n = len(tiles)
    if first_seq:
        seq_reset0(c, par)
        interleave(None, layer0_P(c, tiles[0][0], tiles[0][2], par, None))
        start = 0
    else:
        CP(c, "pool", c.S[:], c.S_meta[:], [c.S_meta], [c.S])
        CP(c, "act", c.Sbf[:], c.S_meta[:], [c.S_meta], [c.Sbf])
        CP(c, "pool", c.hst[:], c.hst_meta[:], [c.hst_meta], [c.hst])
        interleave(None, layer0_P(c, tiles[1][0], tiles[1][2], par, "meta"))
        start = 1
    for j in range(start, n):
        gp = layer0_P(c, tiles[j + 1][0], tiles[j + 1][2], par ^ 1, (par, tiles[j][2])) if j + 1 < n else None
        gm = layer0_M(c, tiles[j][1], tiles[j][2], par, dst_buf)
        interleave(gm, gp)
        if first_seq and j == 0:
            CP(c, "pool", c.S_meta[:], c.S[:], [c.S], [c.S_meta])
            CP(c, "pool", c.hst_meta[:], c.hst[:], [c.hst], [c.hst_meta])
            CP(c, "pool", c.hist_meta[:, :, :], c.projF[par][:, :, tiles[0][2]:tiles[0][2] + 3], [c.projF[par]], [c.hist_meta])
        par ^= 1
    return par


LTOT = 2064
BIG = 30000.0


def setup_layer1(c, D, L):
    k = c.k
    c.w_in1 = T(k, "w_in1", [128, 8, 4096], BF16)
    c.w_out1 = T(k, "w_out1", [128, 8, 1024], BF16)
    c.vecF = T(k, "vec1F", [128, 8], F32)
    c.fn = T(k, "fn", [128, 1024], F32)
    k.dma("sp", c.vecF[:], D["vec1F"][:, :], w=[c.vecF.b], sbuf=c.vecF.b)
    k.dma("sp", c.fn[:], D["fnorm"][0:1, :].partition_broadcast(128), w=[c.fn.b], sbuf=c.fn.b)
    st = c.stage
    dsts, rows, scs = [], [], []
    for kc in range(8):
        for hf in range(8):
            dsts.append(c.w_in1[:, kc, hf * 512:(hf + 1) * 512])
            rows.append(D["w_in1"][kc * 128:(kc + 1) * 128, hf * 512:(hf + 1) * 512])
            scs.append(c.vecF[:, kc:kc + 1])
    load_cast_weight(c, c.w_in1, dsts, rows, 512, scs, st)
    dsts, rows, scs = [], [], []
    for kc in range(8):
        for hf in range(2):
            dsts.append(c.w_out1[:, kc, hf * 512:(hf + 1) * 512])
            rows.append(D["w_out1"][kc * 128:(kc + 1) * 128, hf * 512:(hf + 1) * 512])
            scs.append(None)
    load_cast_weight(c, c.w_out1, dsts, rows, 512, scs, st)
    c.negm = T(k, "negm", [128, 4, 128], BF16)
    c.tri2 = T(k, "tri2", [128, 128], BF16)
    c.zrow = T(k, "zrow", [1, 256], BF16)
    c.tri = T(k, "tri", [128, 128], BF16)
    c.ones = T(k, "ones", [128, 2], BF16)
    OP(c, "pool", lambda e: e.memset(c.negm[:], 0.0), [], [c.negm])
    OP(c, "pool", lambda e: e.memset(c.tri2[:], 1.0), [], [c.tri2])
    OP(c, "pool", lambda e: e.memset(c.zrow[:], 0.0), [], [c.zrow])
    OP(c, "pool", lambda e: e.memset(c.tri[:], 1.0), [], [c.tri])
    OP(c, "pool", lambda e: e.memset(c.ones[:], 1.0), [], [c.ones])
    for i in range(4):
        OP(c, "pool", lambda e, i=i: e.affine_select(out=c.negm[:, i, :], in_=c.negm[:, i, :], pattern=[[1, 128]], compare_op=ALU.is_gt, fill=-BIG, base=0, channel_multiplier=-1), [c.negm], [c.negm])
    OP(c, "pool", lambda e: e.affine_select(out=c.tri[:], in_=c.tri[:], pattern=[[-1, 128]], compare_op=ALU.is_ge, fill=0.0, base=0, channel_multiplier=1), [c.tri], [c.tri])
    OP(c, "pool", lambda e: e.affine_select(out=c.tri2[:], in_=c.tri2[:], pattern=[[1, 128]], compare_op=ALU.is_gt, fill=0.0, base=0, channel_multiplier=-1), [c.tri2], [c.tri2])
    nkb = 1 + (L - 16) // 128
    c.KT = T(k, "KT", [128, 8, L], BF16)
    c.V = T(k, "V", [128, nkb, 1024], BF16)
    c.KTb = [Buf("KTb%d" % i) for i in range(nkb)]
    c.Vb = [Buf("Vb%d" % i) for i in range(nkb)]


def alloc_layer1_work(c):
    k = c.k
    c.ht = [T(k, "ht%d" % i, [128, 1024], F32) for i in range(2)]
    c.st2 = [T(k, "st2_%d" % i, [128, 8], F32) for i in range(2)]
    c.ub1 = T(k, "ub1", [128, 1024], BF16)
    c.uT1 = T(k, "uT1", [128, 1024], BF16)
    c.QTs = [T(k, "QTs%d" % i, [128, 8, 2, 128], BF16) for i in range(2)]
    for i in range(2):
        OP(c, "pool", lambda e, i=i: e.memset(c.QTs[i][:], 0.0), [], [c.QTs[i]])
    c.sgt1 = T(k, "sgt1", [128, 512], F32)
    c.sgz = [T(k, "sgz%d" % i, [128, 1024], BF16) for i in range(2)]
    c.E = [T(k, "E%d" % i, [128, 512], F32) for i in range(4)]
    c.SP = [T(k, "SP%d" % i, [128, 512], BF16) for i in range(4)]
    c.X = [T(k, "X%d" % i, [128, 512], F32) for i in range(2)]
    c.Wt = [T(k, "Wt%d" % i, [128, 512], BF16) for i in range(2)]
    c.ob = [T(k, "ob%d" % i, [128, 1024], BF16) for i in range(2)]
    c.oT = T(k, "oT", [128, 1024], BF16)
    c.B = [T(k, "B%d" % i, [128, 512], F32, "ps") for i in range(8)]
    c.pT1 = View(c.B[6][:, :].bitcast(BF16).rearrange("p (c t) -> p c t", c=8), c.B[6].b)
    c.pTo = View(c.B[0][:, :].bitcast(BF16).rearrange("p (c t) -> p c t", c=8), c.B[0].b)


def l1_proj_gen(c, src_ap, nt, j, par, src_buf):
    k = c.k
    ht = c.ht[par]
    st2 = c.st2[par]
    P = slice(0, nt)
    identb = c.identb
    pos0 = 0 if j == 0 else 16 + (j - 1) * 128
    B = c.B
    k.dma("sp", ht[P, :], src_ap, r=[src_buf], w=[ht.b], sbuf=ht.b)
    ub = c.ub1
    ACT(c, ub[P, :], ht[P, :], AF.Square, [ht], [ub, st2], accum_out=st2[P, 0:1])
    ACT(c, st2[P, 1:2], st2[P, 0:1], AF.Ln, [st2], [st2], scale=1.0 / 1024, bias=1e-6)
    ACT(c, st2[P, 1:2], st2[P, 1:2], AF.Exp, [st2], [st2], scale=-0.5)
    TS(c, "dve", ub[P, :], ht[P, :], st2[P, 1:2], None, ALU.mult, None, [ht, st2], [ub])
    yield
    for kc in range(8):
        TR(c, c.pT1[:, kc, P], ub[P, kc * 128:(kc + 1) * 128], identb[P, P], [ub, identb], [c.pT1], sig=(kc == 7))
    uTv = c.uT1[:].rearrange("p (c t) -> p c t", c=8)
    CP(c, "dve", uTv[:, :, P], c.pT1[:, :, P], [c.pT1], [c.uT1])
    yield
    w = c.w_in1
    for g4 in range(2):
        pb = B[7 - g4]
        pbv = pb[:].rearrange("p (c t) -> p c t", c=4)
        for i in range(4):
            oc = g4 * 4 + i
            for kc in range(8):
                MM(c, pbv[:, i, P], w[:, kc, 1024 + oc * 128:1024 + (oc + 1) * 128], uTv[:, kc, P], kc == 0, kc == 7, [c.uT1, w], [pb], sig=(kc == 7))
            yield
        CP(c, "dve", c.KT[:, g4 * 4:(g4 + 1) * 4, pos0:pos0 + nt], pbv[:, :, P], [pb], [c.KTb[j]])
        yield
    for hf in range(2):
        pb = B[7 - hf]
        for kc in range(8):
            MM(c, pb[P, :], uTv[:, kc, P], w[:, kc, 2048 + hf * 512:2048 + (hf + 1) * 512], kc == 0, kc == 7, [c.uT1, w], [pb], sig=(kc == 7))
        yield
        CP(c, "dve", c.V[P, j, hf * 512:(hf + 1) * 512], pb[P, :], [pb], [c.Vb[j]])
        yield
    if j == 0:
        return
    QTs = c.QTs[par]
    for g4 in range(2):
        pb = B[7 - g4]
        pbv = pb[:].rearrange("p (c t) -> p c t", c=4)
        for i in range(4):
            oc = g4 * 4 + i
            for kc in range(8):
                MM(c, pbv[:, i, P], w[:, kc, oc * 128:(oc + 1) * 128], uTv[:, kc, P], kc == 0, kc == 7, [c.uT1, w], [pb], sig=(kc == 7))
            yield
        for hf_ in range(2):
            hp = slice(hf_ * 64, (hf_ + 1) * 64)
            TS(c, "dve", QTs[hp, g4 * 4:(g4 + 1) * 4, hf_, P], pbv[hp, :, P], 0.125, None, ALU.mult, None, [pb], [QTs])
        yield
    sgz = c.sgz[par]
    for hf in range(2):
        pb = B[7 - hf]
        hs = slice(hf * 512, (hf + 1) * 512)
        for kc in range(8):
            MM(c, pb[P, :], uTv[:, kc, P], w[:, kc, 3072 + hf * 512:3072 + (hf + 1) * 512], kc == 0, kc == 7, [c.uT1, w], [pb], sig=(kc == 7))
        yield
        ACT(c, c.sgt1[P, :], pb[P, :], AF.Exp, [pb], [c.sgt1], scale=-1.0)
        TS(c, "dve", c.sgt1[P, :], c.sgt1[P, :], 1.0, None, ALU.add, None, [c.sgt1], [c.sgt1])
        OP(c, "dve", lambda e: e.reciprocal(out=c.sgt1[P, :], in_=c.sgt1[P, :]), [c.sgt1], [c.sgt1])
        TT(c, "dve", sgz[P, hs], c.sgt1[P, :], pb[P, :], ALU.mult, [c.sgt1, pb], [sgz])
        yield


def run_gen(g, n=None):
    if g is None:
        return False
    try:
        if n is None:
            while True:
                next(g)
        for _ in range(n):
            next(g)
    except StopIteration:
        return False
    return True


def layer1_attn(c, nt, j, par, gen_next):
    k = c.k
    ht = c.ht[par]
    st2 = c.st2[par]
    P = slice(0, nt)
    identb = c.identb
    B = c.B
    QTs_ = c.QTs[par]
    sgz_ = c.sgz[par]

    units = []
    for pr in range(2):
        for kb in range(j, -1, -1):
            for q in range(2):
                units.append((kb, 2 * pr + q, q))
    nu = len(units)

    def kinfo(kb):
        if kb == 0:
            return 16, 0
        return 128, 16 + (kb - 1) * 128

    def views(u):
        kb, hg, q = units[u]
        ks, kp = kinfo(kb)
        return kb, hg, q, ks, kp

    def stageA(u):
        kb, hg, q, ks, kp = views(u)
        diag = (kb == j)
        z = B[u % 2]
        zv = z[:].rearrange("p (i t) -> p i t", i=4)
        first = True
        if diag:
            MM(c, z[0:ks, :], identb[0:ks, 0:ks], c.negm[0:ks, :, :].rearrange("p i t -> p (i t)"), True, False, [identb, c.negm], [z], sig=False)
            first = False
        for i2 in range(2):
            ch = 2 * hg + i2
            MM(c, z[0:ks, 2 * i2 * 128:(2 * i2 + 2) * 128], c.KT[:, ch, kp:kp + ks], QTs_[:, ch, :, :].rearrange("p a t -> p (a t)"), first, (i2 == 1) or first, [c.KTb[kb], QTs_], [z], sig=(i2 == 1))
        E = c.E[u % 4]
        SP = c.SP[u % 4]
        Ev = E[:].rearrange("p (i t) -> p i t", i=4)
        SPv = SP[:].rearrange("p (i t) -> p i t", i=4)
        ACT(c, Ev[0:ks, :, P], zv[0:ks, :, P], AF.Exp, [z], [E])

    def stageA2(u):
        kb, hg, q, ks, kp = views(u)
        E = c.E[u % 4]
        SP = c.SP[u % 4]
        Ev = E[:].rearrange("p (i t) -> p i t", i=4)
        SPv = SP[:].rearrange("p (i t) -> p i t", i=4)
        ACT(c, SPv[0:ks, :, P], Ev[0:ks, :, P], AF.Ln, [E], [SP], bias=1.0)

    def stageB(u):
        kb, hg, q, ks, kp = views(u)
        tb = B[2 + q]
        tv = tb[:].rearrange("p (i t) -> p i t", i=4)
        SP = c.SP[u % 4]
        SPv = SP[:].rearrange("p (i t) -> p i t", i=4)
        MM(c, tb[0:ks, :], c.tri[0:ks, 0:ks], SP[0:ks, :], kb == j, False, [c.tri, SP], [tb], sig=True, skip=True)
        X = c.X[u % 2]
        Xv = X[:].rearrange("p (i t) -> p i t", i=4)
        ACT(c, Xv[0:ks, :, P], tv[0:ks, :, P], AF.Exp, [tb], [X], scale=-1.0)

    def stageC(u):
        kb, hg, q, ks, kp = views(u)
        tb = B[2 + q]
        tv = tb[:].rearrange("p (i t) -> p i t", i=4)
        SP = c.SP[u % 4]
        SPv = SP[:].rearrange("p (i t) -> p i t", i=4)
        if kb > 0:
            MM(c, tb[0:ks, :], c.tri2[0:ks, 0:ks], SP[0:ks, :], False, False, [c.tri2, SP], [tb], sig=True, skip=True)
        E = c.E[u % 4]
        X = c.X[u % 2]
        Wt = c.Wt[u % 2]
        Ev = E[:].rearrange("p (i t) -> p i t", i=4)
        Xv = X[:].rearrange("p (i t) -> p i t", i=4)
        Wv = Wt[:].rearrange("p (i t) -> p i t", i=4)
        TT(c, "dve", Wv[0:ks, :, P], Ev[0:ks, :, P], Xv[0:ks, :, P], ALU.mult, [E, X], [Wt])

    def stageD(u):
        kb, hg, q, ks, kp = views(u)
        ob_ = B[4 + q]
        Wt = c.Wt[u % 2]
        Wv = Wt[:].rearrange("p (i t) -> p i t", i=4)
        if kb == j:
            MM(c, ob_[P, 0:256], c.zrow[0:1, P], c.zrow[0:1, 0:256], True, False, [c.zrow], [ob_], sig=False)
        for i in range(4):
            hd = 4 * hg + i
            MM(c, ob_[P, i * 64:(i + 1) * 64], Wv[0:ks, i, P], c.V[0:ks, kb, hd * 64:(hd + 1) * 64], False, (kb == 0 and i == 3), [Wt, c.Vb[kb]], [ob_], sig=(i == 3))
        if kb == 0:
            hsl = slice(hg * 256, (hg + 1) * 256)
            TT(c, "dve", c.ob[par][P, hsl], ob_[P, 0:256], sgz_[P, hsl], ALU.mult, [ob_, sgz_], [c.ob[par]])

    per = max(1, -(-52 // max(1, nu - 2)))
    for step in range(nu + 4):
        if step < nu:
            stageA(step)
        if 0 <= step - 2 < nu:
            stageB(step - 2)
        if step < nu:
            stageA2(step)
        if 0 <= step - 3 < nu:
            stageC(step - 3)
        if 0 <= step - 4 < nu:
            stageD(step - 4)
        run_gen(gen_next, per)
    run_gen(gen_next, None)


def l1_tail_gen(c, dst_ap, nt, par):
    k = c.k
    ht = c.ht[par]
    st2 = c.st2[par]
    ob = c.ob[par]
    P = slice(0, nt)
    identb = c.identb
    B = c.B
    for kc in range(8):
        TR(c, c.pT1[:, kc, P], ob[P, kc * 128:(kc + 1) * 128], identb[P, P], [ob, identb], [c.pT1], sig=(kc == 7))
    oTv = c.oT[:].rearrange("p (c t) -> p c t", c=8)
    CP(c, "dve", oTv[:, :, P], c.pT1[:, :, P], [c.pT1], [c.oT])
    yield
    for hf in range(2):
        pb = B[6 + hf]
        for kc in range(8):
            MM(c, pb[P, :], oTv[:, kc, P], c.w_out1[:, kc, hf * 512:(hf + 1) * 512], kc == 0, kc == 7, [c.oT, c.w_out1], [pb], sig=(kc == 7))
        yield
    for hf in range(2):
        hs = slice(hf * 512, (hf + 1) * 512)
        TT(c, "dve", ht[P, hs], ht[P, hs], B[6 + hf][P, :], ALU.add, [ht, B[6 + hf]], [ht])
    yield
    ACT(c, c.oT[P, :], ht[P, :], AF.Square, [ht], [c.oT, st2], accum_out=st2[P, 2:3])
    ACT(c, st2[P, 3:4], st2[P, 2:3], AF.Ln, [st2], [st2], scale=1.0 / 1024, bias=1e-6)
    ACT(c, st2[P, 3:4], st2[P, 3:4], AF.Exp, [st2], [st2], scale=-0.5)
    yield
    STT(c, "dve", ht[P, :], ht[P, :], st2[P, 3:4], c.fn[P, :], ALU.mult, ALU.mult, [ht, st2, c.fn], [ht])
    k.dma("sp", dst_ap, ht[P, :], r=[ht.b], sbuf=ht.b)
    yield


def chain_gens(*gens):
    for g in gens:
        if g is not None:
            yield from g


def layer1_seq(c, h1s, outs, nfull, par0, src_buf, first_seq=True):
    par = par0
    if first_seq:
        run_gen(l1_proj_gen(c, h1s[0], 16, 0, par, src_buf), None)
        par ^= 1
    run_gen(l1_proj_gen(c, h1s[1], 128, 1, par, src_buf), None)
    for j in range(1, nfull + 1):
        gt = l1_tail_gen(c, outs[j - 1], 128, par ^ 1) if j >= 2 else None
        gp = l1_proj_gen(c, h1s[j + 1], 128, j + 1, par ^ 1, src_buf) if j + 1 <= nfull else None
        layer1_attn(c, 128, j, par, chain_gens(gt, gp))
        par ^= 1
    run_gen(l1_tail_gen(c, outs[nfull], 128, par ^ 1), None)
    return par

NSEQ = 4
NFULL = 16
LTOT_ = 16 + NFULL * 128


def common_setup(c, sw=2312):
    k = c.k
    if sw:
        c.stage = [T(k, "stage%d" % i, [128, sw], F32) for i in range(2)]
    ident = T(k, "ident", [128, 128], F32)
    c.identb = T(k, "identb", [128, 128], BF16)
    OP(c, "pool", lambda e: e.memset(ident[:], 1.0), [], [ident])
    OP(c, "pool", lambda e: e.affine_select(out=ident[:], in_=ident[:], pattern=[[-1, 128]], compare_op=ALU.is_equal, fill=0.0, base=0, channel_multiplier=1), [ident], [ident])
    CP(c, "dve", c.identb[:], ident[:], [ident], [c.identb])


def build_l0(nseq, nfull):
    nc = bass.Bass("TRN2", target_bir_lowering=False)
    L = 16 + nfull * 128
    D = {}
    x = nc.dram_tensor("x", [nseq, nfull * 128, 1024], F32, kind="ExternalInput").ap()
    meta = nc.dram_tensor("meta", [16, 1024], F32, kind="ExternalInput").ap()
    D["w_in"] = nc.dram_tensor("w_in", [1024, 4624], F32, kind="ExternalInput").ap()
    D["w_out"] = nc.dram_tensor("w_out", [2048, 1024], F32, kind="ExternalInput").ap()
    D["wa"] = nc.dram_tensor("wa", [4, 256, 256], F32, kind="ExternalInput").ap()
    D["wx"] = nc.dram_tensor("wx", [4, 256, 256], F32, kind="ExternalInput").ap()
    D["vecF"] = nc.dram_tensor("vecF", [128, 140], F32, kind="ExternalInput").ap()
    D["vecT"] = nc.dram_tensor("vecT", [1, 48], F32, kind="ExternalInput").ap()
    h1 = nc.dram_tensor("h1", [nseq, L, 1024], F32, kind="ExternalOutput").ap()
    with ExitStack() as es:
        k = K(nc, es)
        c = setup_common(k, nc)
        common_setup(c, 0)
        alloc_layer0_work(c)
        setup_layer0(c, D)
        hb = Buf("h1dram")
        par = 0
        for s in range(nseq):
            tiles = [(meta[:, :], h1[s, 0:16, :], 16)]
            for j in range(nfull):
                tiles.append((x[s, j * 128:(j + 1) * 128, :], h1[s, 16 + j * 128:16 + (j + 1) * 128, :], 128))
            par = layer0_seq(c, tiles, par, hb, first_seq=(s == 0))
        k.final_wait("sp", [c.xt[0].b, c.xt[1].b])
        k.emit()
    return nc


def build_l1(nseq, nfull):
    nc = bass.Bass("TRN2", target_bir_lowering=False)
    L = 16 + nfull * 128
    D = {}
    h1 = nc.dram_tensor("h1", [nseq, L, 1024], F32, kind="ExternalInput").ap()
    D["w_in1"] = nc.dram_tensor("w_in1", [1024, 4096], F32, kind="ExternalInput").ap()
    D["w_out1"] = nc.dram_tensor("w_out1", [1024, 1024], F32, kind="ExternalInput").ap()
    D["vec1F"] = nc.dram_tensor("vec1F", [128, 8], F32, kind="ExternalInput").ap()
    D["fnorm"] = nc.dram_tensor("fnorm", [1, 1024], F32, kind="ExternalInput").ap()
    out = nc.dram_tensor("out", [nseq, nfull * 128, 1024], F32, kind="ExternalOutput").ap()
    with ExitStack() as es:
        k = K(nc, es)
        c = setup_common(k, nc)
        common_setup(c, 0)
        alloc_layer1_work(c)
        c.stage = list(c.E)
        setup_layer1(c, D, L)
        hb = Buf("h1dram")
        par = 0
        for s in range(nseq):
            h1s = [h1[s, 0:16, :]] + [h1[s, 16 + (j - 1) * 128:16 + j * 128, :] for j in range(1, nfull + 1)]
            outs = [None] + [out[s, (j - 1) * 128:j * 128, :] for j in range(1, nfull + 1)]
            par = layer1_seq(c, h1s, outs, nfull, par, hb, first_seq=(s == 0))
        k.final_wait("sp", [c.ht[0].b, c.ht[1].b])
        k.emit()
    return nc


def build_fused(nseq, nfull):
    nc = bass.Bass("TRN2", target_bir_lowering=False)
    L = 16 + nfull * 128
    D = {}
    x = nc.dram_tensor("x", [nseq, nfull * 128, 1024], F32, kind="ExternalInput").ap()
    meta = nc.dram_tensor("meta", [16, 1024], F32, kind="ExternalInput").ap()
    D["w_in"] = nc.dram_tensor("w_in", [1024, 4624], F32, kind="ExternalInput").ap()
    D["w_out"] = nc.dram_tensor("w_out", [2048, 1024], F32, kind="ExternalInput").ap()
    D["wa"] = nc.dram_tensor("wa", [4, 256, 256], F32, kind="ExternalInput").ap()
    D["wx"] = nc.dram_tensor("wx", [4, 256, 256], F32, kind="ExternalInput").ap()
    D["vecF"] = nc.dram_tensor("vecF", [128, 140], F32, kind="ExternalInput").ap()
    D["vecT"] = nc.dram_tensor("vecT", [1, 48], F32, kind="ExternalInput").ap()
    D["w_in1"] = nc.dram_tensor("w_in1", [1024, 4096], F32, kind="ExternalInput").ap()
    D["w_out1"] = nc.dram_tensor("w_out1", [1024, 1024], F32, kind="ExternalInput").ap()
    D["vec1F"] = nc.dram_tensor("vec1F", [128, 8], F32, kind="ExternalInput").ap()
    D["fnorm"] = nc.dram_tensor("fnorm", [1, 1024], F32, kind="ExternalInput").ap()
    out = nc.dram_tensor("out", [nseq, nfull * 128, 1024], F32, kind="ExternalOutput").ap()
    h1 = nc.dram_tensor("h1s", [nseq, L, 1024], F32, kind="Internal").ap()
    with ExitStack() as es:
        k = K(nc, es)
        c = setup_common(k, nc)
        common_setup(c, 0)
        hb = Buf("h1dram")
        k.push()
        alloc_layer0_work(c)
        setup_layer0(c, D)
        par = 0
        for s in range(nseq):
            tiles = [(meta[:, :], h1[s, 0:16, :], 16)]
            for j in range(nfull):
                tiles.append((x[s, j * 128:(j + 1) * 128, :], h1[s, 16 + j * 128:16 + (j + 1) * 128, :], 128))
            par = layer0_seq(c, tiles, par, hb, first_seq=(s == 0))
        k.barrier([c.xt[0].b, c.xt[1].b, c.vecF.b, c.vecT.b] + [t.b for t in c.stage])
        k.emit()
        k.pop()
        hb = Buf("h1dram2")
        k.push()
        alloc_layer1_work(c)
        c.stage = list(c.E)
        setup_layer1(c, D, L)
        par = 0
        for s in range(nseq):
            h1s = [h1[s, 0:16, :]] + [h1[s, 16 + (j - 1) * 128:16 + j * 128, :] for j in range(1, nfull + 1)]
            outs = [None] + [out[s, (j - 1) * 128:j * 128, :] for j in range(1, nfull + 1)]
            par = layer1_seq(c, h1s, outs, nfull, par, hb, first_seq=(s == 0))
        k.final_wait("sp", [c.ht[0].b, c.ht[1].b])
        k.emit()
        k.pop()
    return nc


def pack_vecs(inp):
    f = lambda v: np.ascontiguousarray(np.asarray(v).reshape(-1, 128).T)
    vecF = np.zeros((128, 140), np.float32)
    vecF[:, 0:8] = f(inp["even_norm"][0])
    vecF[:, 8:16] = f(inp["ssd_norm"][0])
    vecF[:, 16:48] = np.asarray(inp["lru_conv_w"][0]).reshape(4, 8, 128).transpose(2, 1, 0).reshape(128, 32)
    vecF[:, 48:56] = f(inp["lru_conv_b"][0])
    vecF[:, 56:64] = f(inp["lru_b_a"][0])
    vecF[:, 64:72] = f(inp["lru_b_x"][0])
    vecF[:, 72:80] = f(inp["lru_lambda"][0])
    vecF[:, 80:128] = np.asarray(inp["ssd_conv_w"][0]).reshape(4, 12, 128).transpose(2, 1, 0).reshape(128, 48)
    vecF[:, 128:140] = f(inp["ssd_conv_b"][0])
    vecT = np.concatenate([np.asarray(inp["ssd_dt_bias"][0]), np.asarray(inp["ssd_a_log"][0]), np.asarray(inp["ssd_d"][0])])[None, :].astype(np.float32)
    return vecF, vecT


_NC_CACHE = {}


def kernel(**inp):
    inp = {k_: np.asarray(v, dtype=np.float32) for k_, v in inp.items()}
    x = inp["x"]
    ncores = 8
    vecF, vecT = pack_vecs(inp)
    vec1F = np.ascontiguousarray(inp["odd_norm"][0].reshape(8, 128).T)
    if "f" not in _NC_CACHE:
        _NC_CACHE["f"] = build_fused(NSEQ, NFULL)
    xs = np.split(np.ascontiguousarray(x), ncores, axis=0)
    maps = [{"x": xs[i], "meta": inp["meta"], "w_in": inp["even_w_in"][0], "w_out": inp["even_w_out"][0],
             "wa": inp["lru_w_a"][0], "wx": inp["lru_w_x"][0], "vecF": vecF, "vecT": vecT,
             "w_in1": inp["odd_w_in"][0], "w_out1": inp["odd_w_out"][0], "vec1F": vec1F,
             "fnorm": inp["final_norm"][None, :]} for i in range(ncores)]
    r = run_bass_kernel_spmd(_NC_CACHE["f"], maps, core_ids=list(range(ncores)))
    return np.concatenate([r.results[i]["out"] for i in range(ncores)], axis=0).astype(np.float32)
```

```python
import numpy as np
from contextlib import ExitStack
import concourse.bass as bass
import concourse.mybir as mybir
from concourse.bass_utils import run_bass_kernel_spmd


F32 = mybir.dt.float32
BF16 = mybir.dt.bfloat16
AF = mybir.ActivationFunctionType
ALU = mybir.AluOpType
AX = mybir.AxisListType

ENGS = ("pe", "act", "dve", "pool", "sp")


class Buf:
    __slots__ = ("name", "w", "rs", "dsem", "dcnt")

    def __init__(self, name):
        self.name = name
        self.w = None
        self.rs = []
        self.dsem = None
        self.dcnt = 0


class K:
    def __init__(self, nc, es):
        self.nc = nc
        self.es = es
        self.prog = {e: [] for e in ENGS}
        self.sem = {e: es.enter_context(nc.semaphore("s_" + e)) for e in ENGS}
        self.cnt = {e: 0 for e in ENGS}
        self.waited = {e: {} for e in ENGS}
        self.pending = {e: [] for e in ENGS}
        self.nsem = 5
        self.ninstr = 0
        self.scopes = [es]

    def push(self):
        self.scopes.append(ExitStack())

    def pop(self):
        self.scopes.pop().close()

    def barrier(self, dma_bufs=()):
        for e in ENGS:
            if self.pending[e]:
                raise RuntimeError("barrier with pending unsignalled ops on " + e)
        waits = {}
        for e in ENGS:
            if e != "pool" and self.cnt[e] > 0:
                self._need("pool", (e, self.cnt[e], self.sem[e]), waits)
        for b in dma_bufs:
            self._need("pool", b.w, waits)
            for t in b.rs:
                self._need("pool", t, waits)
        self.cnt["pool"] += 1
        tok = ("pool", self.cnt["pool"], self.sem["pool"])
        self.prog["pool"].append((list(waits.values()), lambda e: e.engine_nop(), (self.sem["pool"], 1)))
        for e in ENGS:
            if e != "pool":
                self.wait_tok(e, tok)

    def sb(self, name, shape, dt):
        return self.scopes[-1].enter_context(self.nc.sbuf_tensor(name, list(shape), dt))

    def ps(self, name, shape, dt=F32):
        return self.scopes[-1].enter_context(self.nc.psum_tensor(name, list(shape), dt))

    def dsem_of(self, buf):
        if buf.dsem is None:
            buf.dsem = self.es.enter_context(self.nc.semaphore("d_" + buf.name))
            self.nsem += 1
        return buf.dsem

    def _need(self, eng, tok, waits):
        if tok is None:
            return
        key, val, semh = tok
        if key == eng and False:
            return
        cur = self.waited[eng].get(key, 0)
        if val > cur:
            self.waited[eng][key] = val
            waits[key] = (semh, val)

    def _deps(self, eng, r, w):
        waits = {}
        for b in r:
            if b.w is not None and b.w[0] == "PENDING":
                raise RuntimeError("read of buffer %s with unsignalled writer" % b.name)
            self._need(eng, b.w, waits)
        for b in w:
            if b.w is not None and b.w[0] == "PENDING":
                if b.w[1] != eng:
                    raise RuntimeError("write of buffer %s with unsignalled writer" % b.name)
            else:
                self._need(eng, b.w, waits)
            for t in b.rs:
                if t[0] == "PENDING":
                    if t[1] != eng:
                        raise RuntimeError("WAR on buffer %s with unsignalled reader" % b.name)
                else:
                    self._need(eng, t, waits)
        return list(waits.values())

    def op(self, eng, fn, r=(), w=(), sig=True):
        waits = self._deps(eng, r, w)
        if sig:
            self.cnt[eng] += 1
            tok = (eng, self.cnt[eng], self.sem[eng])
            semh = self.sem[eng]
            for (b, kind) in self.pending[eng]:
                if kind == "r":
                    b.rs = [t for t in b.rs if not (t[0] == "PENDING" and t[1] == eng)]
                    b.rs.append(tok)
                else:
                    b.w = tok
                    b.rs = [t for t in b.rs if not (t[0] == "PENDING" and t[1] == eng)]
            self.pending[eng] = []
            for b in r:
                b.rs.append(tok)
            for b in w:
                b.w = tok
                b.rs = []
            self.prog[eng].append((waits, fn, (semh, 1)))
        else:
            ptok = ("PENDING", eng)
            for b in r:
                b.rs.append(ptok)
                self.pending[eng].append((b, "r"))
            for b in w:
                b.w = ptok
                b.rs = []
                self.pending[eng].append((b, "w"))
            self.prog[eng].append((waits, fn, None))
        self.ninstr += 1

    def dma(self, eng, out_ap, in_ap, r=(), w=(), sbuf=None, **kw):
        waits = self._deps(eng, r, w)
        semh = self.dsem_of(sbuf)
        sbuf.dcnt += 1
        tok = ("d_" + sbuf.name, 16 * sbuf.dcnt, semh)
        for b in r:
            b.rs.append(tok)
        for b in w:
            b.w = tok
            b.rs = []
        self.prog[eng].append((waits, lambda e: e.dma_start(out=out_ap, in_=in_ap, **kw), (semh, 16)))
        self.ninstr += 1
        return tok

    def wait_tok(self, eng, tok):
        waits = {}
        self._need(eng, tok, waits)
        for (semh, val) in waits.values():
            self.prog[eng].append(([(semh, val)], None, None))

    def final_wait(self, eng, bufs):
        waits = {}
        for b in bufs:
            self._need(eng, b.w, waits)
            for t in b.rs:
                self._need(eng, t, waits)
        if waits:
            self.prog[eng].append((list(waits.values()), None, None))

    def emit(self):
        nc = self.nc
        engmap = {"pe": "tensor", "act": "scalar", "dve": "vector", "pool": "gpsimd", "sp": "sync"}
        with nc.Block() as block:
            for e in ENGS:
                prog = self.prog[e]

                def body(engine, prog=prog):
                    for waits, fn, inc in prog:
                        for (semh, val) in waits:
                            engine.wait_ge(semh, val)
                        if fn is not None:
                            ins = fn(engine)
                            if inc is not None:
                                ins.then_inc(inc[0], inc[1])
                getattr(block, engmap[e])(body)
        self.prog = {e: [] for e in ENGS}


D_MODEL = 1024
EVEN_IN = 4624


class T:
    def __init__(self, k, name, shape, dt, space="sb"):
        self.t = k.sb("t_" + name, shape, dt) if space == "sb" else k.ps("t_" + name, shape, dt)
        self.b = Buf(name)
        self.name = name

    def __getitem__(self, idx):
        return self.t[idx]


class View:
    def __init__(self, ap, b):
        self.ap = ap
        self.b = b

    def __getitem__(self, idx):
        return self.ap[idx]


def _b(xs):
    return [getattr(x, "b", x) for x in xs]


class Ctx:
    pass


def setup_common(k, nc):
    c = Ctx()
    c.k = k
    c.nc = nc
    c.rr = 0
    return c


def OP(c, eng, fn, r=(), w=(), sig=True):
    c.k.op(eng, fn, r=_b(r), w=_b(w), sig=sig)


def ACT(c, out, in_, func, r, w, **kw):
    OP(c, "act", lambda e: e.activation(out=out, in_=in_, func=func, **kw), r, w)


def TT(c, eng, out, in0, in1, op, r, w):
    OP(c, eng, lambda e: e.tensor_tensor(out=out, in0=in0, in1=in1, op=op), r, w)


def TS(c, eng, out, in0, s1, s2, op0, op1, r, w):
    if s2 is None:
        OP(c, eng, lambda e: e.tensor_scalar(out=out, in0=in0, scalar1=s1, scalar2=None, op0=op0), r, w)
    else:
        OP(c, eng, lambda e: e.tensor_scalar(out=out, in0=in0, scalar1=s1, scalar2=s2, op0=op0, op1=op1), r, w)


def STT(c, eng, out, in0, scalar, in1, op0, op1, r, w):
    OP(c, eng, lambda e: e.scalar_tensor_tensor(out=out, in0=in0, scalar=scalar, in1=in1, op0=op0, op1=op1), r, w)


def CP(c, eng, out, in_, r, w):
    if eng == "act":
        ACT(c, out, in_, AF.Copy, r, w)
    else:
        OP(c, eng, lambda e: e.tensor_copy(out=out, in_=in_), r, w)


def MM(c, out, lhsT, rhs, start, stop, r, w, sig, skip=False):
    if skip:
        OP(c, "pe", lambda e: e.matmul(out, lhsT=lhsT, rhs=rhs, start=start, stop=stop, skip_group_check=True), r, w, sig=sig)
    else:
        OP(c, "pe", lambda e: e.matmul(out, lhsT=lhsT, rhs=rhs, start=start, stop=stop), r, w, sig=sig)


def TR(c, out, in_, ident, r, w, sig):
    OP(c, "pe", lambda e: e.transpose(out=out, in_=in_, identity=ident), r, w, sig=sig)


def load_cast_weight(c, dst, dst_slices, dram_rows, width, scale_aps, st, engs=("act", "dve")):
    k = c.k
    for i, (da, ra) in enumerate(zip(dst_slices, dram_rows)):
        s = st[c.rr % len(st)]
        eng = engs[c.rr % len(engs)]
        c.rr += 1
        k.dma("sp", s.t[:, 0:width], ra, w=[s.b], sbuf=s.b)
        sc = scale_aps[i]
        if sc is None:
            CP(c, eng, da, s.t[:, 0:width], [s], [dst])
        else:
            if eng == "act":
                OP(c, "act", lambda e, da=da, s=s, sc=sc: e.activation(out=da, in_=s.t[:, 0:width], func=AF.Copy, scale=sc), [s, c.vecF], [dst])
            else:
                TS(c, eng, da, s.t[:, 0:width], sc, None, ALU.mult, None, [s, c.vecF], [dst])


def setup_layer0(c, D):
    k = c.k
    c.w_in = T(k, "w_in", [128, 8, EVEN_IN], BF16)
    c.w_out = T(k, "w_out", [128, 16, 1024], BF16)
    c.wa = T(k, "wa", [128, 4, 2, 256], BF16)
    c.wx = T(k, "wx", [128, 4, 2, 256], BF16)
    c.vecF = T(k, "vecF", [128, 140], F32)
    c.vecT = T(k, "vecT", [128, 48], F32)
    k.dma("sp", c.vecF[:], D["vecF"][:, :], w=[c.vecF.b], sbuf=c.vecF.b)
    k.dma("sp", c.vecT[:], D["vecT"][0:1, :].partition_broadcast(128), w=[c.vecT.b], sbuf=c.vecT.b)
    st = c.stage
    for (c0, wd) in ((0, 1024), (1024, 1024), (2048, 1024), (3072, 1024), (4096, 528)):
        dsts, rows, scs = [], [], []
        for kc in range(8):
            dsts.append(c.w_in[:, kc, c0:c0 + wd])
            rows.append(D["w_in"][kc * 128:(kc + 1) * 128, c0:c0 + wd])
            scs.append(c.vecF[:, kc:kc + 1])
        load_cast_weight(c, c.w_in, dsts, rows, wd, scs, st)
    dsts, rows, scs = [], [], []
    for kc in range(16):
        dsts.append(c.w_out[:, kc, :])
        rows.append(D["w_out"][kc * 128:(kc + 1) * 128, :])
        scs.append(None if kc < 8 else c.vecF[:, 8 + kc - 8:8 + kc - 8 + 1])
    load_cast_weight(c, c.w_out, dsts, rows, 1024, scs, st)
    for (wt, nm) in ((c.wa, "wa"), (c.wx, "wx")):
        dsts, rows, scs = [], [], []
        for g in range(4):
            for kc in range(2):
                dsts.append(wt[:, g, kc, :])
                rows.append(D[nm][g, kc * 128:(kc + 1) * 128, :])
                scs.append(None)
        load_cast_weight(c, wt, dsts, rows, 256, scs, st)

    c.L1 = T(k, "L1", [128, 128], F32)
    c.L2 = T(k, "L2", [128, 128], F32)
    c.L4 = T(k, "L4", [128, 2, 128], F32)
    c.mle = T(k, "mle", [128, 64], F32)
    OP(c, "pool", lambda e: e.memset(c.L1[:], 0.0), [], [c.L1])
    OP(c, "pool", lambda e: e.memset(c.L2[:], 0.0), [], [c.L2])
    OP(c, "pool", lambda e: e.memset(c.L4[:], 0.0), [], [c.L4])
    OP(c, "pool", lambda e: e.memset(c.mle[:], 1.0), [], [c.mle])
    for h in range(2):
        ps = slice(h * 64, (h + 1) * 64)
        OP(c, "pool", lambda e, ps=ps: e.memset(c.L1[ps, ps], 1.0), [], [c.L1])
        OP(c, "pool", lambda e, ps=ps: e.memset(c.L2[ps, ps], 1.0), [], [c.L2])
        OP(c, "pool", lambda e, ps=ps, h=h: e.memset(c.L4[ps, h, :], 1.0), [], [c.L4])
        OP(c, "pool", lambda e, ps=ps: e.affine_select(out=c.L1[ps, ps], in_=c.L1[ps, ps], pattern=[[1, 64]], compare_op=ALU.is_ge, fill=0.0, base=0, channel_multiplier=-1), [c.L1], [c.L1])
        OP(c, "pool", lambda e, ps=ps: e.affine_select(out=c.L2[ps, ps], in_=c.L2[ps, ps], pattern=[[-1, 64]], compare_op=ALU.is_gt, fill=0.0, base=0, channel_multiplier=1), [c.L2], [c.L2])
        OP(c, "pool", lambda e, ps=ps: e.affine_select(out=c.mle[ps, :], in_=c.mle[ps, :], pattern=[[1, 64]], compare_op=ALU.is_ge, fill=0.0, base=0, channel_multiplier=-1), [c.mle], [c.mle])
    c.pv = T(k, "pv", [128, 64], F32)
    ACT(c, c.pv[:, 0:8], c.vecF[:, 72:80], AF.Exp, [c.vecF], [c.pv], scale=-1.0)
    ACT(c, c.pv[:, 0:8], c.pv[:, 0:8], AF.Ln, [c.pv], [c.pv], bias=1.0)
    TS(c, "dve", c.pv[:, 8:16], c.pv[:, 0:8], -16.0, None, ALU.mult, None, [c.pv], [c.pv])
    TS(c, "dve", c.pv[:, 0:8], c.pv[:, 0:8], -8.0, None, ALU.mult, None, [c.pv], [c.pv])
    TS(c, "dve", c.pv[:, 16:32], c.vecF[:, 56:72], -1.0, None, ALU.mult, None, [c.vecF], [c.pv])
    ACT(c, c.pv[:, 32:48], c.vecT[:, 16:32], AF.Exp, [c.vecT], [c.pv])
    TS(c, "dve", c.pv[:, 32:48], c.pv[:, 32:48], -1.0, None, ALU.mult, None, [c.pv], [c.pv])


def alloc_layer0_work(c):
    k = c.k
    c.xt = [T(k, "xt%d" % i, [128, 1024], F32) for i in range(2)]
    c.st1 = [T(k, "st1_%d" % i, [128, 8], F32) for i in range(2)]
    c.W = [T(k, "W%d" % i, [128, 1024], F32) for i in range(6)]
    c.H = [T(k, "H%d" % i, [128, (256 if i == 1 else (512 if i == 5 else 1024))], BF16) for i in range(6)]
    c.projF = [T(k, "projF%d" % i, [128, 20, 131], F32) for i in range(2)]
    c.sg = [T(k, "sg%d" % i, [128, 1024], BF16) for i in range(2)]
    c.gz = [T(k, "gz%d" % i, [128, 1024], BF16) for i in range(2)]
    c.dts = [T(k, "dts%d" % i, [128, 32], F32) for i in range(2)]
    c.ubP = T(k, "ubP", [128, 1024], BF16)
    c.sgt = View(c.ubP.t[:, :].bitcast(F32), c.ubP.b)
    c.uTP = T(k, "uTP", [128, 1024], BF16)
    c.xbc = T(k, "xbc", [128, 12, 128], F32)
    c.lxcb = [Buf("lxc%d" % i) for i in range(8)]
    c.eacb = [Buf("eac%d" % i) for i in range(8)]
    c.excb = [Buf("exc%d" % i) for i in range(8)]
    c.xbccb = [Buf("xbcc%d" % i) for i in range(12)]
    c.S = T(k, "S", [128, 1024], F32)
    c.Sbf = T(k, "Sbf", [128, 1024], BF16)
    c.hst = T(k, "hst", [128, 8], F32)
    c.S_meta = T(k, "S_meta", [128, 1024], F32)
    c.hst_meta = T(k, "hst_meta", [128, 8], F32)
    c.hist_meta = T(k, "hist_meta", [128, 20, 3], F32)
    c.sm = T(k, "sm", [128, 96], F32)
    c.cbm = T(k, "cbm", [128, 2, 64], F32)
    c.pT = T(k, "pT", [128, 8, 128], BF16, "ps")
    c.pG = T(k, "pG", [128, 512], F32, "ps")
    c.pT2 = View(c.pG[:, 256:384].bitcast(BF16).rearrange("p (c t) -> p c t", c=2), c.pG.b)
    c.pB = [T(k, "pB%d" % i, [128, 512], F32, "ps") for i in range(6)]
    c.stage = [c.W[0], c.W[1], c.W[2], c.W[3]]
    c.pTP = View(c.pB[0][:, :].bitcast(BF16).rearrange("p (c t) -> p c t", c=8), c.pB[0].b)


def seq_reset0(c, par):
    OP(c, "pool", lambda e: e.memset(c.projF[par][:, :, 0:3], 0.0), [], [c.projF[par]])
    OP(c, "pool", lambda e: e.memset(c.S[:], 0.0), [], [c.S])
    OP(c, "pool", lambda e: e.memset(c.Sbf[:], 0.0), [], [c.Sbf])
    OP(c, "pool", lambda e: e.memset(c.hst[:], 0.0), [], [c.hst])


def act_sigmoid_from(c, out, in_, rin, wout, neg_bias=None):
    ACT(c, out, in_, AF.Exp, rin, wout, scale=-1.0)
    ACT(c, out, out, AF.Ln, wout, wout, bias=1.0)
    ACT(c, out, out, AF.Exp, wout, wout, scale=-1.0)


def layer0_P(c, src_ap, nt, par, prev):
    k = c.k
    xt = c.xt[par]
    st1 = c.st1[par]
    P = slice(0, nt)
    identb = c.identb
    projF = c.projF[par]
    k.dma("sp", xt[P, :], src_ap, w=[xt.b], sbuf=xt.b)
    ACT(c, c.ubP[P, :], xt[P, :], AF.Square, [xt], [c.ubP, st1], accum_out=st1[P, 0:1])
    ACT(c, st1[P, 1:2], st1[P, 0:1], AF.Ln, [st1], [st1], scale=1.0 / D_MODEL, bias=1e-6)
    ACT(c, st1[P, 1:2], st1[P, 1:2], AF.Exp, [st1], [st1], scale=-0.5)
    ub = c.ubP
    TS(c, "dve", ub[P, :], xt[P, :], st1[P, 1:2], None, ALU.mult, None, [xt, st1], [ub])
    yield
    for kc in range(8):
        TR(c, c.pTP[:, kc, P], ub[P, kc * 128:(kc + 1) * 128], identb[P, P], [ub, identb], [c.pTP], sig=(kc == 7))
    uT = c.uTP
    uTv = uT[:].rearrange("p (c t) -> p c t", c=8)
    CP(c, "act", uTv[:, :, P], c.pTP[:, :, P], [c.pTP], [uT])
    yield
    if prev == "meta":
        CP(c, "pool", projF[:, :, 0:3], c.hist_meta[:, :, :], [c.hist_meta], [projF])
    elif prev is not None:
        pp, pnt = prev
        CP(c, "pool", projF[:, :, 0:3], c.projF[pp][:, :, pnt:pnt + 3], [c.projF[pp]], [projF])
    pz = (c.pB[0], c.pB[1])
    gz = c.gz[par]
    dts = c.dts[par]
    for kc in range(8):
        MM(c, c.pB[1][P, 0:16], uTv[:, kc, P], c.w_in[:, kc, 4608:4624], kc == 0, kc == 7, [uT, c.w_in], [c.pB[1]], sig=(kc == 7))
    TT(c, "dve", dts[P, 0:16], c.pB[1][P, 0:16], c.vecT[P, 0:16], ALU.add, [c.pB[1], c.vecT], [dts])
    yield
    ACT(c, dts[P, 0:16], dts[P, 0:16], AF.Exp, [dts], [dts])
    ACT(c, dts[P, 0:16], dts[P, 0:16], AF.Ln, [dts], [dts], bias=1.0)
    TT(c, "dve", dts[P, 16:32], dts[P, 0:16], c.pv[P, 32:48], ALU.mult, [dts, c.pv], [dts])
    yield
    for hf in range(2):
        for kc in range(8):
            MM(c, pz[hf][P, :], uTv[:, kc, P], c.w_in[:, kc, 2048 + hf * 512:2048 + (hf + 1) * 512], kc == 0, kc == 7, [uT, c.w_in], [pz[hf]], sig=(kc == 7))
        yield
    for hf in range(2):
        hs = slice(hf * 512, (hf + 1) * 512)
        act_sigmoid_from(c, c.sgt[P, :], pz[hf][P, :], [pz[hf]], [c.sgt])
        yield
        TT(c, "dve", gz[P, hs], c.sgt[P, :], pz[hf][P, :], ALU.mult, [c.sgt, pz[hf]], [gz])
        yield
    sg = c.sg[par]
    sgv = sg[:].rearrange("p (c t) -> p c t", c=8)
    groups = []
    for g4 in range(2):
        groups.append(("x", [g4 * 4 + i for i in range(4)], 0))
    for g4 in range(2):
        groups.append(("g", [g4 * 4 + i for i in range(4)], 1024))
    for g4 in range(3):
        groups.append(("b", [g4 * 4 + i for i in range(4)], 3072))
    for gi, (kind, ocs, colbase) in enumerate(groups):
        pb = c.pB[gi % 2]
        pbv = pb[:].rearrange("p (c t) -> p c t", c=4)
        for i, oc in enumerate(ocs):
            for kc in range(8):
                MM(c, pbv[:, i, P], c.w_in[:, kc, colbase + oc * 128:colbase + (oc + 1) * 128], uTv[:, kc, P], kc == 0, kc == 7, [uT, c.w_in], [pb], sig=(kc == 7))
            yield
        if kind == "x":
            CP(c, "act", projF[:, ocs[0]:ocs[0] + 4, 3:3 + nt], pbv[:, :, P], [pb], [projF])
        elif kind == "b":
            CP(c, "act", projF[:, 8 + ocs[0]:8 + ocs[0] + 4, 3:3 + nt], pbv[:, :, P], [pb], [projF])
        else:
            o = sgv[:, ocs[0]:ocs[0] + 4, P]
            sc = c.sgt[:].rearrange("p (c t) -> p c t", c=4)[:, :, P]
            act_sigmoid_from(c, sc, pbv[:, :, P], [pb], [c.sgt])
            yield
            TT(c, "dve", o, sc, pbv[:, :, P], ALU.mult, [c.sgt, pb], [sg])
        yield


def layer0_M(c, dst_ap, nt, par, dst_buf):
    k = c.k
    xt = c.xt[par]
    st1 = c.st1[par]
    W = c.W
    H = c.H
    chunks = [(0, nt)] if nt <= 64 else [(0, 64), (64, 128)]
    cw = chunks[0][1]
    nch = len(chunks)
    identb = c.identb
    P = slice(0, nt)
    projF = c.projF[par]
    sg = c.sg[par]
    sgv = sg[:].rearrange("p (c t) -> p c t", c=8)
    gz = c.gz[par]
    dts = c.dts[par]
    sm = c.sm

    lx = W[3]
    lxv = lx[:].rearrange("p (c t) -> p c t", c=8)
    lcb = c.lxcb
    ne = 0
    for tp in range(4):
        for ch in range(8):
            o = lxv[:, ch, P]
            if tp == 0:
                TS(c, "dve", o, projF[:, ch, 0:nt], c.vecF[:, 16 + ch * 4:16 + ch * 4 + 1], c.vecF[:, 48 + ch:48 + ch + 1], ALU.mult, ALU.add, [projF, c.vecF], ([lx, lcb[ch]] if ch == 0 else [lcb[ch]]))
            else:
                STT(c, "dve", o, projF[:, ch, tp:tp + nt], c.vecF[:, 16 + ch * 4 + tp:16 + ch * 4 + tp + 1], o, ALU.mult, ALU.add, [projF, c.vecF, lcb[ch]], [lcb[ch]])
            ne += 1
            if ne % 4 == 0:
                yield
    yield
    lxb = H[2]
    lxbv = lxb[:].rearrange("p (c t) -> p c t", c=8)
    CP(c, "act", lxbv[:, :, P], lxv[:, :, P], [lx] + c.lxcb, [lxb])
    ea_ = W[4]
    ex_ = W[5]
    eav = ea_[:].rearrange("p (c t) -> p c t", c=8)
    exv = ex_[:].rearrange("p (c t) -> p c t", c=8)
    for (wt, pbs, ev, boff, et) in ((c.wa, (c.pB[2], c.pB[3]), eav, 16, ea_), (c.wx, (c.pB[4], c.pB[5]), exv, 24, ex_)):
        for oc in range(8):
            g = oc // 2
            pb = pbs[oc // 4]
            pbv = pb[:].rearrange("p (c t) -> p c t", c=4)
            for kc in range(2):
                MM(c, pbv[:, oc % 4, P], wt[:, g, kc, (oc % 2) * 128:(oc % 2 + 1) * 128], lxbv[:, 2 * g + kc, P], kc == 0, kc == 1, [lxb, wt], [pb], sig=(oc % 4 == 3 and kc == 1))
    yield
    xbc = c.xbc
    xcb = c.xbccb
    ne = 0
    for tp in range(4):
        for ch in range(12):
            o = xbc[:, ch, P]
            if tp == 0:
                TS(c, "dve", o, projF[:, 8 + ch, 0:nt], c.vecF[:, 80 + ch * 4:80 + ch * 4 + 1], c.vecF[:, 128 + ch:128 + ch + 1], ALU.mult, ALU.add, [projF, c.vecF], ([xbc, xcb[ch]] if ch == 0 else [xcb[ch]]))
            else:
                STT(c, "dve", o, projF[:, 8 + ch, tp:tp + nt], c.vecF[:, 80 + ch * 4 + tp:80 + ch * 4 + tp + 1], o, ALU.mult, ALU.add, [projF, c.vecF, xcb[ch]], [xcb[ch]])
            ne += 1
            if ne % 4 == 0:
                yield
    gl = ((c.wa, (c.pB[2], c.pB[3]), eav, 16, ea_, c.eacb), (c.wx, (c.pB[4], c.pB[5]), exv, 24, ex_, c.excb))
    for (wt, pbs, ev, boff, et, cb) in gl:
        for oc in range(8):
            pb = pbs[oc // 4]
            pbv = pb[:].rearrange("p (c t) -> p c t", c=4)
            ACT(c, ev[:, oc, P], pbv[:, oc % 4, P], AF.Exp, [pb, c.pv], ([et, cb[0]] if oc == 0 else [cb[oc]]), scale=-1.0, bias=c.pv[:, boff + oc:boff + oc + 1])
            if oc % 4 == 3:
                yield
    for (wt, pbs, ev, boff, et, cb) in gl:
        ACT(c, ev[:, :, P], ev[:, :, P], AF.Ln, [et] + cb, [et], bias=1.0)
    for (wt, pbs, ev, boff, et, cb) in gl:
        ACT(c, ev[:, :, P], ev[:, :, P], AF.Exp, [et], [et], scale=-1.0)
    yield
    e0 = W[0][:].rearrange("p (c t) -> p c t", c=8)
    e1 = W[1][:].rearrange("p (c t) -> p c t", c=8)
    act_sigmoid_from(c, e0[:, :, P], xbc[:, 0:8, P], [xbc] + c.xbccb, [W[0]])
    act_sigmoid_from(c, e1[:, 0:4, P], xbc[:, 8:12, P], [xbc] + c.xbccb, [W[1]])
    yield
    xsT = H[3][:].rearrange("p (c t) -> p c t", c=8)
    bcT = H[5][:].rearrange("p (c t) -> p c t", c=4)
    TT(c, "dve", xsT[:, :, P], e0[:, :, P], xbc[:, 0:8, P], ALU.mult, [W[0], xbc] + c.xbccb, [H[3]])
    TT(c, "dve", bcT[:, 0:4, P], e1[:, 0:4, P], xbc[:, 8:12, P], ALU.mult, [W[1], xbc] + c.xbccb, [H[5]])
    yield
    a_ = W[2]
    av = a_[:].rearrange("p (c t) -> p c t", c=8)
    a2_ = W[1]
    a2v = a2_[:].rearrange("p (c t) -> p c t", c=8)
    for ch in range(8):
        ACT(c, av[:, ch, P], eav[:, ch, P], AF.Exp, [ea_, c.pv], [a_], scale=c.pv[:, ch:ch + 1])
        ACT(c, a2v[:, ch, P], eav[:, ch, P], AF.Exp, [ea_, c.pv], [a2_], scale=c.pv[:, 8 + ch:8 + ch + 1])
        if ch % 4 == 3:
            yield
    for ch in range(8):
        TR(c, c.pT[P, ch, :], xsT[:, ch, P], identb[:, :], [H[3], identb], [c.pT], sig=(ch == 7))
    for ch in range(2):
        TR(c, c.pT2[P, ch, :], bcT[:, ch, P], identb[:, :], [H[5], identb], [c.pT2], sig=(ch == 1))
    yield
    TS(c, "dve", a2v[:, :, P], a2v[:, :, P], -1.0, 1.0, ALU.mult, ALU.add, [a2_], [a2_])
    ACT(c, a2v[:, :, P], a2v[:, :, P], AF.Ln, [a2_], [a2_])
    ACT(c, a2v[:, :, P], a2v[:, :, P], AF.Exp, [a2_], [a2_], scale=0.5)
    yield
    Xps = c.pT[:].rearrange("p c t -> p (c t)")
    Xdt = H[0]
    TT(c, "dve", Xdt[P, :].rearrange("p (h d) -> p h d", h=16), Xps[P, :].rearrange("p (h d) -> p h d", h=16),
       dts[P, 0:16].unsqueeze(2).to_broadcast([nt, 16, 64]), ALU.mult, [c.pT, dts], [Xdt])
    skip = W[0]
    TT(c, "dve", skip[P, :].rearrange("p (h d) -> p h d", h=16), Xps[P, :].rearrange("p (h d) -> p h d", h=16),
       c.vecT[P, 32:48].unsqueeze(2).to_broadcast([nt, 16, 64]), ALU.mult, [c.pT, c.vecT], [skip])
    yield
    Btok = H[1]
    CP(c, "act", Btok[P, 0:256], c.pT2[P, :, :].rearrange("p c t -> p (c t)"), [c.pT2], [Btok])
    MM(c, c.pG[P, 16:32], c.L1[P, P], dts[P, 16:32], True, True, [c.L1, dts], [c.pG], sig=False)
    MM(c, c.pG[P, 32:48], c.L2[P, P], dts[P, 16:32], True, True, [c.L2, dts], [c.pG], sig=False)
    for ci in range(nch):
        MM(c, c.pG[:, 48 + 16 * ci:64 + 16 * ci], c.L4[P, ci, :], dts[P, 16:32], True, True, [c.L4, dts], [c.pG], sig=(ci == nch - 1))
    ACT(c, sm[P, 32:48], c.pG[P, 16:32], AF.Exp, [c.pG], [sm])
    ACT(c, sm[P, 48:64], c.pG[P, 32:48], AF.Exp, [c.pG], [sm])
    ACT(c, sm[:, 64:64 + 16 * nch], c.pG[:, 48:48 + 16 * nch], AF.Exp, [c.pG], [sm])
    yield
    TT(c, "dve", exv[:, :, P], exv[:, :, P], a2v[:, :, P], ALU.mult, [ex_, a2_], [ex_])
    TT(c, "dve", exv[:, :, P], exv[:, :, P], lxv[:, :, P], ALU.mult, [ex_, lx] + c.lxcb, [ex_])
    for ch in range(8):
        OP(c, "dve", lambda e, ch=ch: e.tensor_tensor_scan(out=eav[:, ch, P], data0=av[:, ch, P], data1=exv[:, ch, P], initial=c.hst[:, ch:ch + 1], op0=ALU.mult, op1=ALU.add), [a_, ex_, c.hst], ([ea_, c.eacb[0]] if ch == 0 else [c.eacb[ch]]))
        if ch % 4 == 3:
            yield
    CP(c, "dve", c.hst[:, :], eav[:, :, nt - 1], [ea_] + c.eacb, [c.hst])
    mixA = H[4][:].rearrange("p (c t) -> p c t", c=8)
    mixB = H[2][:].rearrange("p (c t) -> p c t", c=8)
    TT(c, "pool", mixA[:, :, P], eav[:, :, P], sgv[:, :, P], ALU.mult, [ea_, sg] + c.eacb, [H[4]])
    yield
    R1 = W[1]
    R1v = R1[:, 0:16 * cw].rearrange("p (h l) -> p h l", h=16)
    TT(c, "dve", R1v[P, :, :], dts[P, 16:32].unsqueeze(2).to_broadcast([nt, 16, cw]),
       c.mle[P, 0:cw].unsqueeze(1).to_broadcast([nt, 16, cw]), ALU.mult, [dts, c.mle], [R1])
    for hf in range(2):
        pb = c.pB[2 + hf]
        MM(c, pb[P, 0:8 * cw], c.L2[P, P], R1[P, hf * 8 * cw:(hf + 1) * 8 * cw], True, True, [c.L2, R1], [pb], sig=True)
    dec = W[2]
    decv = dec[:, 0:16 * cw].rearrange("p (h l) -> p h l", h=16)
    for hf in range(2):
        pb = c.pB[2 + hf]
        ACT(c, dec[P, hf * 8 * cw:(hf + 1) * 8 * cw], pb[P, 0:8 * cw], AF.Exp, [pb], [dec])
    yield
    cbps = c.pG[:, 128:256].rearrange("p (g l) -> p g l", g=2)
    for ci, (p0, p1) in enumerate(chunks):
        for g in range(2):
            MM(c, cbps[p0:p1, g, 0:cw], bcT[:, g, p0:p1], bcT[:, 2 + g, p0:p1], True, True, [H[5]], [c.pG], sig=(ci == nch - 1 and g == 1))
    TT(c, "dve", c.cbm[P, :, 0:cw], cbps[P, :, 0:cw], c.mle[P, 0:cw].unsqueeze(1).to_broadcast([nt, 2, cw]), ALU.mult, [c.pG, c.mle], [c.cbm])
    yield
    MT = H[3]
    MTv = MT[:, 0:16 * cw].rearrange("p (h l) -> p h l", h=16)
    for g in range(2):
        TT(c, "dve", MTv[P, g * 8:(g + 1) * 8, :], decv[P, g * 8:(g + 1) * 8, :],
           c.cbm[P, g:g + 1, 0:cw].to_broadcast([nt, 8, cw]), ALU.mult, [dec, c.cbm], [MT])
    yield
    Xd = H[2]
    TT(c, "pool", Xd[P, :].rearrange("p (h d) -> p h d", h=16), Xdt[P, :].rearrange("p (h d) -> p h d", h=16),
       sm[P, 48:64].unsqueeze(2).to_broadcast([nt, 16, 64]), ALU.mult, [Xdt, sm], [H[2]])
    for ci, (p0, p1) in enumerate(chunks):
        for h in range(16):
            pb = c.pB[4 + h // 8]
            MM(c, pb[p0:p1, (h % 8) * 64:(h % 8 + 1) * 64], MTv[p0:p1, h, :], Xdt[p0:p1, h * 64:(h + 1) * 64], True, True, [MT, Xdt], [pb],
               sig=(h % 8 == 7))
        yield
    y = W[1]
    for ci, (p0, p1) in enumerate(chunks):
        PC = slice(p0, p1)
        ncw = p1 - p0
        for g in range(2):
            pb = c.pB[2 + g]
            MM(c, pb[p0:p1, :], bcT[:, 2 + g, p0:p1], c.Sbf[:, g * 512:(g + 1) * 512], True, True, [H[5], c.Sbf], [pb], sig=True)
        yield
        for g in range(2):
            gs = slice(g * 512, (g + 1) * 512)
            TT(c, "dve", y[PC, gs].rearrange("p (h d) -> p h d", h=8), c.pB[2 + g][PC, :].rearrange("p (h d) -> p h d", h=8),
               sm[PC, 32 + 8 * g:40 + 8 * g].unsqueeze(2).to_broadcast([ncw, 8, 64]), ALU.mult, [c.pB[2 + g], sm], [y])
        yield
        for g in range(2):
            pb = c.pB[2 + g]
            MM(c, pb[:, :], Btok[p0:p1, g * 128:(g + 1) * 128], Xd[p0:p1, g * 512:(g + 1) * 512], True, True, [Btok, H[2]], [pb], sig=True)
        TT(c, "pool", c.S[:].rearrange("p (h d) -> p h d", h=16), c.S[:].rearrange("p (h d) -> p h d", h=16),
           sm[:, 64 + 16 * ci:80 + 16 * ci].unsqueeze(2).to_broadcast([128, 16, 64]), ALU.mult, [c.S, sm], [c.S])
        yield
        for g in range(2):
            gs = slice(g * 512, (g + 1) * 512)
            TT(c, "dve", c.S[:, gs], c.S[:, gs], c.pB[2 + g][:, :], ALU.add, [c.S, c.pB[2 + g]], [c.S])
        CP(c, "act", c.Sbf[:], c.S[:], [c.S], [c.Sbf])
        yield
    for g in range(2):
        gs = slice(g * 512, (g + 1) * 512)
        TT(c, "dve", y[P, gs], y[P, gs], c.pB[4 + g][P, :], ALU.add, [y, c.pB[4 + g]], [y])
    yield
    yield
    TT(c, "dve", y[P, :], y[P, :], skip[P, :], ALU.add, [y, skip], [y])
    TT(c, "dve", y[P, :], y[P, :], gz[P, :], ALU.mult, [y, gz], [y])
    for g in range(2):
        gs = slice(g * 512, (g + 1) * 512)
        ACT(c, W[4][P, gs], y[P, gs], AF.Square, [y], [W[4], st1], accum_out=st1[P, 2 + g:3 + g])
    ACT(c, st1[P, 4:6], st1[P, 2:4], AF.Ln, [st1], [st1], scale=1.0 / 512, bias=1e-6)
    ACT(c, st1[P, 4:6], st1[P, 4:6], AF.Exp, [st1], [st1], scale=-0.5)
    yield
    yb = H[0]
    for g in range(2):
        gs = slice(g * 512, (g + 1) * 512)
        TS(c, "dve", yb[P, gs], y[P, gs], st1[P, 4 + g:5 + g], None, ALU.mult, None, [y, st1], [yb])
    for ch in range(8):
        TR(c, c.pT[:, ch, P], yb[P, ch * 128:(ch + 1) * 128], identb[P, P], [yb, identb], [c.pT], sig=(ch == 7))
    CP(c, "act", mixB[:, :, P], c.pT[:, :, P], [c.pT], [H[2]])
    yield
    for hf in range(2):
        pb = c.pB[2 + hf]
        for kc in range(16):
            lhs = mixA[:, kc, P] if kc < 8 else mixB[:, kc - 8, P]
            MM(c, pb[P, :], lhs, c.w_out[:, kc, hf * 512:(hf + 1) * 512], kc == 0, kc == 15, [H[4], H[2], c.w_out], [pb], sig=(kc == 15))
    yield
    for hf in range(2):
        hs = slice(hf * 512, (hf + 1) * 512)
        TT(c, "dve", xt[P, hs], xt[P, hs], c.pB[2 + hf][P, :], ALU.add, [xt, c.pB[2 + hf]], [xt])
    k.dma("sp", dst_ap, xt[P, :], r=[xt.b], w=[dst_buf], sbuf=xt.b)


def interleave(gm, gp, ratio=1):
    am, ap = gm is not None, gp is not None
    while am or ap:
        if am:
            for _ in range(ratio):
                try:
                    next(gm)
                except StopIteration:
                    am = False
                    break
        if ap:
            try:
                next(gp)
            except StopIteration:
                ap = False


def layer0_seq(c, tiles, par0, dst_buf, first_seq=True):
    par = par0
    n = len(tiles)
    if first_seq:
        seq_reset0(c, par)
        interleave(None, layer0_P(c, tiles[0][0], tiles[0][2], par, None))
        start = 0
    else:
        CP(c, "pool", c.S[:], c.S_meta[:], [c.S_meta], [c.S])
        CP(c, "act", c.Sbf[:], c.S_meta[:], [c.S_meta], [c.Sbf])
        CP(c, "pool", c.hst[:], c.hst_meta[:], [c.hst_meta], [c.hst])
        interleave(None, layer0_P(c, tiles[1][0], tiles[1][2], par, "meta"))
        start = 1
    for j in range(start, n):
        gp = layer0_P(c, tiles[j + 1][0], tiles[j + 1][2], par ^ 1, (par, tiles[j][2])) if j + 1 < n else None
        gm = layer0_M(c, tiles[j][1], tiles[j][2], par, dst_buf)
        interleave(gm, gp)
        if first_seq and j == 0:
            CP(c, "pool", c.S_meta[:], c.S[:], [c.S], [c.S_meta])
            CP(c, "pool", c.hst_meta[:], c.hst[:], [c.hst], [c.hst_meta])
            CP(c, "pool", c.hist_meta[:, :, :], c.projF[par][:, :, tiles[0][2]:tiles[0][2] + 3], [c.projF[par]], [c.hist_meta])
        par ^= 1
    return par


LTOT = 2064
BIG = 30000.0


def setup_layer1(c, D, L):
    k = c.k
    c.w_in1 = T(k, "w_in1", [128, 8, 4096], BF16)
    c.w_out1 = T(k, "w_out1", [128, 8, 1024], BF16)
    c.vecF = T(k, "vec1F", [128, 8], F32)
    c.fn = T(k, "fn", [128, 1024], F32)
    k.dma("sp", c.vecF[:], D["vec1F"][:, :], w=[c.vecF.b], sbuf=c.vecF.b)
    k.dma("sp", c.fn[:], D["fnorm"][0:1, :].partition_broadcast(128), w=[c.fn.b], sbuf=c.fn.b)
    st = c.stage
    dsts, rows, scs = [], [], []
    for kc in range(8):
        for hf in range(8):
            dsts.append(c.w_in1[:, kc, hf * 512:(hf + 1) * 512])
            rows.append(D["w_in1"][kc * 128:(kc + 1) * 128, hf * 512:(hf + 1) * 512])
            scs.append(c.vecF[:, kc:kc + 1])
    load_cast_weight(c, c.w_in1, dsts, rows, 512, scs, st)
    dsts, rows, scs = [], [], []
    for kc in range(8):
        for hf in range(2):
            dsts.append(c.w_out1[:, kc, hf * 512:(hf + 1) * 512])
            rows.append(D["w_out1"][kc * 128:(kc + 1) * 128, hf * 512:(hf + 1) * 512])
            scs.append(None)
    load_cast_weight(c, c.w_out1, dsts, rows, 512, scs, st)
    c.negm = T(k, "negm", [128, 4, 128], BF16)
    c.tri2 = T(k, "tri2", [128, 128], BF16)
    c.zrow = T(k, "zrow", [1, 256], BF16)
    c.tri = T(k, "tri", [128, 128], BF16)
    c.ones = T(k, "ones", [128, 2], BF16)
    OP(c, "pool", lambda e: e.memset(c.negm[:], 0.0), [], [c.negm])
    OP(c, "pool", lambda e: e.memset(c.tri2[:], 1.0), [], [c.tri2])
    OP(c, "pool", lambda e: e.memset(c.zrow[:], 0.0), [], [c.zrow])
    OP(c, "pool", lambda e: e.memset(c.tri[:], 1.0), [], [c.tri])
    OP(c, "pool", lambda e: e.memset(c.ones[:], 1.0), [], [c.ones])
    for i in range(4):
        OP(c, "pool", lambda e, i=i: e.affine_select(out=c.negm[:, i, :], in_=c.negm[:, i, :], pattern=[[1, 128]], compare_op=ALU.is_gt, fill=-BIG, base=0, channel_multiplier=-1), [c.negm], [c.negm])
    OP(c, "pool", lambda e: e.affine_select(out=c.tri[:], in_=c.tri[:], pattern=[[-1, 128]], compare_op=ALU.is_ge, fill=0.0, base=0, channel_multiplier=1), [c.tri], [c.tri])
    OP(c, "pool", lambda e: e.affine_select(out=c.tri2[:], in_=c.tri2[:], pattern=[[1, 128]], compare_op=ALU.is_gt, fill=0.0, base=0, channel_multiplier=-1), [c.tri2], [c.tri2])
    nkb = 1 + (L - 16) // 128
    c.KT = T(k, "KT", [128, 8, L], BF16)
    c.V = T(k, "V", [128, nkb, 1024], BF16)
    c.KTb = [Buf("KTb%d" % i) for i in range(nkb)]
    c.Vb = [Buf("Vb%d" % i) for i in range(nkb)]


def alloc_layer1_work(c):
    k = c.k
    c.ht = [T(k, "ht%d" % i, [128, 1024], F32) for i in range(2)]
    c.st2 = [T(k, "st2_%d" % i, [128, 8], F32) for i in range(2)]
    c.ub1 = T(k, "ub1", [128, 1024], BF16)
    c.uT1 = T(k, "uT1", [128, 1024], BF16)
    c.QTs = [T(k, "QTs%d" % i, [128, 8, 2, 128], BF16) for i in range(2)]
    for i in range(2):
        OP(c, "pool", lambda e, i=i: e.memset(c.QTs[i][:], 0.0), [], [c.QTs[i]])
    c.sgt1 = T(k, "sgt1", [128, 512], F32)
    c.sgz = [T(k, "sgz%d" % i, [128, 1024], BF16) for i in range(2)]
    c.E = [T(k, "E%d" % i, [128, 512], F32) for i in range(4)]
    c.SP = [T(k, "SP%d" % i, [128, 512], BF16) for i in range(4)]
    c.X = [T(k, "X%d" % i, [128, 512], F32) for i in range(2)]
    c.Wt = [T(k, "Wt%d" % i, [128, 512], BF16) for i in range(2)]
    c.ob = [T(k, "ob%d" % i, [128, 1024], BF16) for i in range(2)]
    c.oT = T(k, "oT", [128, 1024], BF16)
    c.B = [T(k, "B%d" % i, [128, 512], F32, "ps") for i in range(8)]
    c.pT1 = View(c.B[6][:, :].bitcast(BF16).rearrange("p (c t) -> p c t", c=8), c.B[6].b)
    c.pTo = View(c.B[0][:, :].bitcast(BF16).rearrange("p (c t) -> p c t", c=8), c.B[0].b)


def l1_proj_gen(c, src_ap, nt, j, par, src_buf):
    k = c.k
    ht = c.ht[par]
    st2 = c.st2[par]
    P = slice(0, nt)
    identb = c.identb
    pos0 = 0 if j == 0 else 16 + (j - 1) * 128
    B = c.B
    k.dma("sp", ht[P, :], src_ap, r=[src_buf], w=[ht.b], sbuf=ht.b)
    ub = c.ub1
    ACT(c, ub[P, :], ht[P, :], AF.Square, [ht], [ub, st2], accum_out=st2[P, 0:1])
    ACT(c, st2[P, 1:2], st2[P, 0:1], AF.Ln, [st2], [st2], scale=1.0 / 1024, bias=1e-6)
    ACT(c, st2[P, 1:2], st2[P, 1:2], AF.Exp, [st2], [st2], scale=-0.5)
    TS(c, "dve", ub[P, :], ht[P, :], st2[P, 1:2], None, ALU.mult, None, [ht, st2], [ub])
    yield
    for kc in range(8):
        TR(c, c.pT1[:, kc, P], ub[P, kc * 128:(kc + 1) * 128], identb[P, P], [ub, identb], [c.pT1], sig=(kc == 7))
    uTv = c.uT1[:].rearrange("p (c t) -> p c t", c=8)
    CP(c, "dve", uTv[:, :, P], c.pT1[:, :, P], [c.pT1], [c.uT1])
    yield
    w = c.w_in1
    for g4 in range(2):
        pb = B[7 - g4]
        pbv = pb[:].rearrange("p (c t) -> p c t", c=4)
        for i in range(4):
            oc = g4 * 4 + i
            for kc in range(8):
                MM(c, pbv[:, i, P], w[:, kc, 1024 + oc * 128:1024 + (oc + 1) * 128], uTv[:, kc, P], kc == 0, kc == 7, [c.uT1, w], [pb], sig=(kc == 7))
            yield
        CP(c, "dve", c.KT[:, g4 * 4:(g4 + 1) * 4, pos0:pos0 + nt], pbv[:, :, P], [pb], [c.KTb[j]])
        yield
    for hf in range(2):
        pb = B[7 - hf]
        for kc in range(8):
            MM(c, pb[P, :], uTv[:, kc, P], w[:, kc, 2048 + hf * 512:2048 + (hf + 1) * 512], kc == 0, kc == 7, [c.uT1, w], [pb], sig=(kc == 7))
        yield
        CP(c, "dve", c.V[P, j, hf * 512:(hf + 1) * 512], pb[P, :], [pb], [c.Vb[j]])
        yield
    if j == 0:
        return
    QTs = c.QTs[par]
    for g4 in range(2):
        pb = B[7 - g4]
        pbv = pb[:].rearrange("p (c t) -> p c t", c=4)
        for i in range(4):
            oc = g4 * 4 + i
            for kc in range(8):
                MM(c, pbv[:, i, P], w[:, kc, oc * 128:(oc + 1) * 128], uTv[:, kc, P], kc == 0, kc == 7, [c.uT1, w], [pb], sig=(kc == 7))
            yield
        for hf_ in range(2):
            hp = slice(hf_ * 64, (hf_ + 1) * 64)
            TS(c, "dve", QTs[hp, g4 * 4:(g4 + 1) * 4, hf_, P], pbv[hp, :, P], 0.125, None, ALU.mult, None, [pb], [QTs])
        yield
    sgz = c.sgz[par]
    for hf in range(2):
        pb = B[7 - hf]
        hs = slice(hf * 512, (hf + 1) * 512)
        for kc in range(8):
            MM(c, pb[P, :], uTv[:, kc, P], w[:, kc, 3072 + hf * 512:3072 + (hf + 1) * 512], kc == 0, kc == 7, [c.uT1, w], [pb], sig=(kc == 7))
        yield
        ACT(c, c.sgt1[P, :], pb[P, :], AF.Exp, [pb], [c.sgt1], scale=-1.0)
        TS(c, "dve", c.sgt1[P, :], c.sgt1[P, :], 1.0, None, ALU.add, None, [c.sgt1], [c.sgt1])
        OP(c, "dve", lambda e: e.reciprocal(out=c.sgt1[P, :], in_=c.sgt1[P, :]), [c.sgt1], [c.sgt1])
        TT(c, "dve", sgz[P, hs], c.sgt1[P, :], pb[P, :], ALU.mult, [c.sgt1, pb], [sgz])
        yield


def run_gen(g, n=None):
    if g is None:
        return False
    try:
        if n is None:
            while True:
                next(g)
        for _ in range(n):
            next(g)
    except StopIteration:
        return False
    return True


def layer1_attn(c, nt, j, par, gen_next):
    k = c.k
    ht = c.ht[par]
    st2 = c.st2[par]
    P = slice(0, nt)
    identb = c.identb
    B = c.B
    QTs_ = c.QTs[par]
    sgz_ = c.sgz[par]

    units = []
    for pr in range(2):
        for kb in range(j, -1, -1):
            for q in range(2):
                units.append((kb, 2 * pr + q, q))
    nu = len(units)

    def kinfo(kb):
        if kb == 0:
            return 16, 0
        return 128, 16 + (kb - 1) * 128

    def views(u):
        kb, hg, q = units[u]
        ks, kp = kinfo(kb)
        return kb, hg, q, ks, kp

    def stageA(u):
        kb, hg, q, ks, kp = views(u)
        diag = (kb == j)
        z = B[u % 2]
        zv = z[:].rearrange("p (i t) -> p i t", i=4)
        first = True
        if diag:
            MM(c, z[0:ks, :], identb[0:ks, 0:ks], c.negm[0:ks, :, :].rearrange("p i t -> p (i t)"), True, False, [identb, c.negm], [z], sig=False)
            first = False
        for i2 in range(2):
            ch = 2 * hg + i2
            MM(c, z[0:ks, 2 * i2 * 128:(2 * i2 + 2) * 128], c.KT[:, ch, kp:kp + ks], QTs_[:, ch, :, :].rearrange("p a t -> p (a t)"), first, (i2 == 1) or first, [c.KTb[kb], QTs_], [z], sig=(i2 == 1))
        E = c.E[u % 4]
        SP = c.SP[u % 4]
        Ev = E[:].rearrange("p (i t) -> p i t", i=4)
        SPv = SP[:].rearrange("p (i t) -> p i t", i=4)
        ACT(c, Ev[0:ks, :, P], zv[0:ks, :, P], AF.Exp, [z], [E])

    def stageA2(u):
        kb, hg, q, ks, kp = views(u)
        E = c.E[u % 4]
        SP = c.SP[u % 4]
        Ev = E[:].rearrange("p (i t) -> p i t", i=4)
        SPv = SP[:].rearrange("p (i t) -> p i t", i=4)
        ACT(c, SPv[0:ks, :, P], Ev[0:ks, :, P], AF.Ln, [E], [SP], bias=1.0)

    def stageB(u):
        kb, hg, q, ks, kp = views(u)
        tb = B[2 + q]
        tv = tb[:].rearrange("p (i t) -> p i t", i=4)
        SP = c.SP[u % 4]
        SPv = SP[:].rearrange("p (i t) -> p i t", i=4)
        MM(c, tb[0:ks, :], c.tri[0:ks, 0:ks], SP[0:ks, :], kb == j, False, [c.tri, SP], [tb], sig=True, skip=True)
        X = c.X[u % 2]
        Xv = X[:].rearrange("p (i t) -> p i t", i=4)
        ACT(c, Xv[0:ks, :, P], tv[0:ks, :, P], AF.Exp, [tb], [X], scale=-1.0)

    def stageC(u):
        kb, hg, q, ks, kp = views(u)
        tb = B[2 + q]
        tv = tb[:].rearrange("p (i t) -> p i t", i=4)
        SP = c.SP[u % 4]
        SPv = SP[:].rearrange("p (i t) -> p i t", i=4)
        if kb > 0:
            MM(c, tb[0:ks, :], c.tri2[0:ks, 0:ks], SP[0:ks, :], False, False, [c.tri2, SP], [tb], sig=True, skip=True)
        E = c.E[u % 4]
        X = c.X[u % 2]
        Wt = c.Wt[u % 2]
        Ev = E[:].rearrange("p (i t) -> p i t", i=4)
        Xv = X[:].rearrange("p (i t) -> p i t", i=4)
        Wv = Wt[:].rearrange("p (i t) -> p i t", i=4)
        TT(c, "dve", Wv[0:ks, :, P], Ev[0:ks, :, P], Xv[0:ks, :, P], ALU.mult, [E, X], [Wt])

    def stageD(u):
        kb, hg, q, ks, kp = views(u)
        ob_ = B[4 + q]
        Wt = c.Wt[u % 2]
        Wv = Wt[:].rearrange("p (i t) -> p i t", i=4)
        if kb == j:
            MM(c, ob_[P, 0:256], c.zrow[0:1, P], c.zrow[0:1, 0:256], True, False, [c.zrow], [ob_], sig=False)
        for i in range(4):
            hd = 4 * hg + i
            MM(c, ob_[P, i * 64:(i + 1) * 64], Wv[0:ks, i, P], c.V[0:ks, kb, hd * 64:(hd + 1) * 64], False, (kb == 0 and i == 3), [Wt, c.Vb[kb]], [ob_], sig=(i == 3))
        if kb == 0:
            hsl = slice(hg * 256, (hg + 1) * 256)
            TT(c, "dve", c.ob[par][P, hsl], ob_[P, 0:256], sgz_[P, hsl], ALU.mult, [ob_, sgz_], [c.ob[par]])

    per = max(1, -(-52 // max(1, nu - 2)))
    for step in range(nu + 4):
        if step < nu:
            stageA(step)
        if 0 <= step - 2 < nu:
            stageB(step - 2)
        if step < nu:
            stageA2(step)
        if 0 <= step - 3 < nu:
            stageC(step - 3)
        if 0 <= step - 4 < nu:
            stageD(step - 4)
        run_gen(gen_next, per)
    run_gen(gen_next, None)


def l1_tail_gen(c, dst_ap, nt, par):
    k = c.k
    ht = c.ht[par]
    st2 = c.st2[par]
    ob = c.ob[par]
    P = slice(0, nt)
    identb = c.identb
    B = c.B
    for kc in range(8):
        TR(c, c.pT1[:, kc, P], ob[P, kc * 128:(kc + 1) * 128], identb[P, P], [ob, identb], [c.pT1], sig=(kc == 7))
    oTv = c.oT[:].rearrange("p (c t) -> p c t", c=8)
    CP(c, "dve", oTv[:, :, P], c.pT1[:, :, P], [c.pT1], [c.oT])
    yield
    for hf in range(2):
        pb = B[6 + hf]
        for kc in range(8):
            MM(c, pb[P, :], oTv[:, kc, P], c.w_out1[:, kc, hf * 512:(hf + 1) * 512], kc == 0, kc == 7, [c.oT, c.w_out1], [pb], sig=(kc == 7))
        yield
    for hf in range(2):
        hs = slice(hf * 512, (hf + 1) * 512)
        TT(c, "dve", ht[P, hs], ht[P, hs], B[6 + hf][P, :], ALU.add, [ht, B[6 + hf]], [ht])
    yield
    ACT(c, c.oT[P, :], ht[P, :], AF.Square, [ht], [c.oT, st2], accum_out=st2[P, 2:3])
    ACT(c, st2[P, 3:4], st2[P, 2:3], AF.Ln, [st2], [st2], scale=1.0 / 1024, bias=1e-6)
    ACT(c, st2[P, 3:4], st2[P, 3:4], AF.Exp, [st2], [st2], scale=-0.5)
    yield
    STT(c, "dve", ht[P, :], ht[P, :], st2[P, 3:4], c.fn[P, :], ALU.mult, ALU.mult, [ht, st2, c.fn], [ht])
    k.dma("sp", dst_ap, ht[P, :], r=[ht.b], sbuf=ht.b)
    yield


def chain_gens(*gens):
    for g in gens:
        if g is not None:
            yield from g


def layer1_seq(c, h1s, outs, nfull, par0, src_buf, first_seq=True):
    par = par0
    if first_seq:
        run_gen(l1_proj_gen(c, h1s[0], 16, 0, par, src_buf), None)
        par ^= 1
    run_gen(l1_proj_gen(c, h1s[1], 128, 1, par, src_buf), None)
    for j in range(1, nfull + 1):
        gt = l1_tail_gen(c, outs[j - 1], 128, par ^ 1) if j >= 2 else None
        gp = l1_proj_gen(c, h1s[j + 1], 128, j + 1, par ^ 1, src_buf) if j + 1 <= nfull else None
        layer1_attn(c, 128, j, par, chain_gens(gt, gp))
        par ^= 1
    run_gen(l1_tail_gen(c, outs[nfull], 128, par ^ 1), None)
    return par

NSEQ = 4
NFULL = 16
LTOT_ = 16 + NFULL * 128


def common_setup(c, sw=2312):
    k = c.k
    if sw:
        c.stage = [T(k, "stage%d" % i, [128, sw], F32) for i in range(2)]
    ident = T(k, "ident", [128, 128], F32)
    c.identb = T(k, "identb", [128, 128], BF16)
    OP(c, "pool", lambda e: e.memset(ident[:], 1.0), [], [ident])
    OP(c, "pool", lambda e: e.affine_select(out=ident[:], in_=ident[:], pattern=[[-1, 128]], compare_op=ALU.is_equal, fill=0.0, base=0, channel_multiplier=1), [ident], [ident])
    CP(c, "dve", c.identb[:], ident[:], [ident], [c.identb])


def build_l0(nseq, nfull):
    nc = bass.Bass("TRN2", target_bir_lowering=False)
    L = 16 + nfull * 128
    D = {}
    x = nc.dram_tensor("x", [nseq, nfull * 128, 1024], F32, kind="ExternalInput").ap()
    meta = nc.dram_tensor("meta", [16, 1024], F32, kind="ExternalInput").ap()
    D["w_in"] = nc.dram_tensor("w_in", [1024, 4624], F32, kind="ExternalInput").ap()
    D["w_out"] = nc.dram_tensor("w_out", [2048, 1024], F32, kind="ExternalInput").ap()
    D["wa"] = nc.dram_tensor("wa", [4, 256, 256], F32, kind="ExternalInput").ap()
    D["wx"] = nc.dram_tensor("wx", [4, 256, 256], F32, kind="ExternalInput").ap()
    D["vecF"] = nc.dram_tensor("vecF", [128, 140], F32, kind="ExternalInput").ap()
    D["vecT"] = nc.dram_tensor("vecT", [1, 48], F32, kind="ExternalInput").ap()
    h1 = nc.dram_tensor("h1", [nseq, L, 1024], F32, kind="ExternalOutput").ap()
    with ExitStack() as es:
        k = K(nc, es)
        c = setup_common(k, nc)
        common_setup(c, 0)
        alloc_layer0_work(c)
        setup_layer0(c, D)
        hb = Buf("h1dram")
        par = 0
        for s in range(nseq):
            tiles = [(meta[:, :], h1[s, 0:16, :], 16)]
            for j in range(nfull):
                tiles.append((x[s, j * 128:(j + 1) * 128, :], h1[s, 16 + j * 128:16 + (j + 1) * 128, :], 128))
            par = layer0_seq(c, tiles, par, hb, first_seq=(s == 0))
        k.final_wait("sp", [c.xt[0].b, c.xt[1].b])
        k.emit()
    return nc


def build_l1(nseq, nfull):
    nc = bass.Bass("TRN2", target_bir_lowering=False)
    L = 16 + nfull * 128
    D = {}
    h1 = nc.dram_tensor("h1", [nseq, L, 1024], F32, kind="ExternalInput").ap()
    D["w_in1"] = nc.dram_tensor("w_in1", [1024, 4096], F32, kind="ExternalInput").ap()
    D["w_out1"] = nc.dram_tensor("w_out1", [1024, 1024], F32, kind="ExternalInput").ap()
    D["vec1F"] = nc.dram_tensor("vec1F", [128, 8], F32, kind="ExternalInput").ap()
    D["fnorm"] = nc.dram_tensor("fnorm", [1, 1024], F32, kind="ExternalInput").ap()
    out = nc.dram_tensor("out", [nseq, nfull * 128, 1024], F32, kind="ExternalOutput").ap()
    with ExitStack() as es:
        k = K(nc, es)
        c = setup_common(k, nc)
        common_setup(c, 0)
        alloc_layer1_work(c)
        c.stage = list(c.E)
        setup_layer1(c, D, L)
        hb = Buf("h1dram")
        par = 0
        for s in range(nseq):
            h1s = [h1[s, 0:16, :]] + [h1[s, 16 + (j - 1) * 128:16 + j * 128, :] for j in range(1, nfull + 1)]
            outs = [None] + [out[s, (j - 1) * 128:j * 128, :] for j in range(1, nfull + 1)]
            par = layer1_seq(c, h1s, outs, nfull, par, hb, first_seq=(s == 0))
        k.final_wait("sp", [c.ht[0].b, c.ht[1].b])
        k.emit()
    return nc


def build_fused(nseq, nfull):
    nc = bass.Bass("TRN2", target_bir_lowering=False)
    L = 16 + nfull * 128
    D = {}
    x = nc.dram_tensor("x", [nseq, nfull * 128, 1024], F32, kind="ExternalInput").ap()
    meta = nc.dram_tensor("meta", [16, 1024], F32, kind="ExternalInput").ap()
    D["w_in"] = nc.dram_tensor("w_in", [1024, 4624], F32, kind="ExternalInput").ap()
    D["w_out"] = nc.dram_tensor("w_out", [2048, 1024], F32, kind="ExternalInput").ap()
    D["wa"] = nc.dram_tensor("wa", [4, 256, 256], F32, kind="ExternalInput").ap()
    D["wx"] = nc.dram_tensor("wx", [4, 256, 256], F32, kind="ExternalInput").ap()
    D["vecF"] = nc.dram_tensor("vecF", [128, 140], F32, kind="ExternalInput").ap()
    D["vecT"] = nc.dram_tensor("vecT", [1, 48], F32, kind="ExternalInput").ap()
    D["w_in1"] = nc.dram_tensor("w_in1", [1024, 4096], F32, kind="ExternalInput").ap()
    D["w_out1"] = nc.dram_tensor("w_out1", [1024, 1024], F32, kind="ExternalInput").ap()
    D["vec1F"] = nc.dram_tensor("vec1F", [128, 8], F32, kind="ExternalInput").ap()
    D["fnorm"] = nc.dram_tensor("fnorm", [1, 1024], F32, kind="ExternalInput").ap()
    out = nc.dram_tensor("out", [nseq, nfull * 128, 1024], F32, kind="ExternalOutput").ap()
    h1 = nc.dram_tensor("h1s", [nseq, L, 1024], F32, kind="Internal").ap()
    with ExitStack() as es:
        k = K(nc, es)
        c = setup_common(k, nc)
        common_setup(c, 0)
        hb = Buf("h1dram")
        k.push()
        alloc_layer0_work(c)
        setup_layer0(c, D)
        par = 0
        for s in range(nseq):
            tiles = [(meta[:, :], h1[s, 0:16, :], 16)]
            for j in range(nfull):
                tiles.append((x[s, j * 128:(j + 1) * 128, :], h1[s, 16 + j * 128:16 + (j + 1) * 128, :], 128))
            par = layer0_seq(c, tiles, par, hb, first_seq=(s == 0))
        k.barrier([c.xt[0].b, c.xt[1].b, c.vecF.b, c.vecT.b] + [t.b for t in c.stage])
        k.emit()
        k.pop()
        hb = Buf("h1dram2")
        k.push()
        alloc_layer1_work(c)
        c.stage = list(c.E)
        setup_layer1(c, D, L)
        par = 0
        for s in range(nseq):
            h1s = [h1[s, 0:16, :]] + [h1[s, 16 + (j - 1) * 128:16 + j * 128, :] for j in range(1, nfull + 1)]
            outs = [None] + [out[s, (j - 1) * 128:j * 128, :] for j in range(1, nfull + 1)]
            par = layer1_seq(c, h1s, outs, nfull, par, hb, first_seq=(s == 0))
        k.final_wait("sp", [c.ht[0].b, c.ht[1].b])
        k.emit()
        k.pop()
    return nc


def pack_vecs(inp):
    f = lambda v: np.ascontiguousarray(np.asarray(v).reshape(-1, 128).T)
    vecF = np.zeros((128, 140), np.float32)
    vecF[:, 0:8] = f(inp["even_norm"][0])
    vecF[:, 8:16] = f(inp["ssd_norm"][0])
    vecF[:, 16:48] = np.asarray(inp["lru_conv_w"][0]).reshape(4, 8, 128).transpose(2, 1, 0).reshape(128, 32)
    vecF[:, 48:56] = f(inp["lru_conv_b"][0])
    vecF[:, 56:64] = f(inp["lru_b_a"][0])
    vecF[:, 64:72] = f(inp["lru_b_x"][0])
    vecF[:, 72:80] = f(inp["lru_lambda"][0])
    vecF[:, 80:128] = np.asarray(inp["ssd_conv_w"][0]).reshape(4, 12, 128).transpose(2, 1, 0).reshape(128, 48)
    vecF[:, 128:140] = f(inp["ssd_conv_b"][0])
    vecT = np.concatenate([np.asarray(inp["ssd_dt_bias"][0]), np.asarray(inp["ssd_a_log"][0]), np.asarray(inp["ssd_d"][0])])[None, :].astype(np.float32)
    return vecF, vecT


_NC_CACHE = {}


def kernel(**inp):
    inp = {k_: np.asarray(v, dtype=np.float32) for k_, v in inp.items()}
    x = inp["x"]
    ncores = 8
    vecF, vecT = pack_vecs(inp)
    vec1F = np.ascontiguousarray(inp["odd_norm"][0].reshape(8, 128).T)
    if "f" not in _NC_CACHE:
        _NC_CACHE["f"] = build_fused(NSEQ, NFULL)
    xs = np.split(np.ascontiguousarray(x), ncores, axis=0)
    maps = [{"x": xs[i], "meta": inp["meta"], "w_in": inp["even_w_in"][0], "w_out": inp["even_w_out"][0],
             "wa": inp["lru_w_a"][0], "wx": inp["lru_w_x"][0], "vecF": vecF, "vecT": vecT,
             "w_in1": inp["odd_w_in"][0], "w_out1": inp["odd_w_out"][0], "vec1F": vec1F,
             "fnorm": inp["final_norm"][None, :]} for i in range(ncores)]
    r = run_bass_kernel_spmd(_NC_CACHE["f"], maps, core_ids=list(range(ncores)))
    return np.concatenate([r.results[i]["out"] for i in range(ncores)], axis=0).astype(np.float32)
```

```python
import numpy as np
from contextlib import ExitStack
import concourse.bass as bass
import concourse.mybir as mybir
from concourse.bass_utils import run_bass_kernel_spmd


F32 = mybir.dt.float32
BF16 = mybir.dt.bfloat16
AF = mybir.ActivationFunctionType
ALU = mybir.AluOpType
AX = mybir.AxisListType

ENGS = ("pe", "act", "dve", "pool", "sp")


class Buf:
    __slots__ = ("name", "w", "rs", "dsem", "dcnt")

    def __init__(self, name):
        self.name = name
        self.w = None
        self.rs = []
        self.dsem = None
        self.dcnt = 0


class K:
    def __init__(self, nc, es):
        self.nc = nc
        self.es = es
        self.prog = {e: [] for e in ENGS}
        self.sem = {e: es.enter_context(nc.semaphore("s_" + e)) for e in ENGS}
        self.cnt = {e: 0 for e in ENGS}
        self.waited = {e: {} for e in ENGS}
        self.pending = {e: [] for e in ENGS}
        self.nsem = 5
        self.ninstr = 0
        self.scopes = [es]

    def push(self):
        self.scopes.append(ExitStack())

    def pop(self):
        self.scopes.pop().close()

    def barrier(self, dma_bufs=()):
        for e in ENGS:
            if self.pending[e]:
                raise RuntimeError("barrier with pending unsignalled ops on " + e)
        waits = {}
        for e in ENGS:
            if e != "pool" and self.cnt[e] > 0:
                self._need("pool", (e, self.cnt[e], self.sem[e]), waits)
        for b in dma_bufs:
            self._need("pool", b.w, waits)
            for t in b.rs:
                self._need("pool", t, waits)
        self.cnt["pool"] += 1
        tok = ("pool", self.cnt["pool"], self.sem["pool"])
        self.prog["pool"].append((list(waits.values()), lambda e: e.engine_nop(), (self.sem["pool"], 1)))
        for e in ENGS:
            if e != "pool":
                self.wait_tok(e, tok)

    def sb(self, name, shape, dt):
        return self.scopes[-1].enter_context(self.nc.sbuf_tensor(name, list(shape), dt))

    def ps(self, name, shape, dt=F32):
        return self.scopes[-1].enter_context(self.nc.psum_tensor(name, list(shape), dt))

    def dsem_of(self, buf):
        if buf.dsem is None:
            buf.dsem = self.es.enter_context(self.nc.semaphore("d_" + buf.name))
            self.nsem += 1
        return buf.dsem

    def _need(self, eng, tok, waits):
        if tok is None:
            return
        key, val, semh = tok
        if key == eng and False:
            return
        cur = self.waited[eng].get(key, 0)
        if val > cur:
            self.waited[eng][key] = val
            waits[key] = (semh, val)

    def _deps(self, eng, r, w):
        waits = {}
        for b in r:
            if b.w is not None and b.w[0] == "PENDING":
                raise RuntimeError("read of buffer %s with unsignalled writer" % b.name)
            self._need(eng, b.w, waits)
        for b in w:
            if b.w is not None and b.w[0] == "PENDING":
                if b.w[1] != eng:
                    raise RuntimeError("write of buffer %s with unsignalled writer" % b.name)
            else:
                self._need(eng, b.w, waits)
            for t in b.rs:
                if t[0] == "PENDING":
                    if t[1] != eng:
                        raise RuntimeError("WAR on buffer %s with unsignalled reader" % b.name)
                else:
                    self._need(eng, t, waits)
        return list(waits.values())

    def op(self, eng, fn, r=(), w=(), sig=True):
        waits = self._deps(eng, r, w)
        if sig:
            self.cnt[eng] += 1
            tok = (eng, self.cnt[eng], self.sem[eng])
            semh = self.sem[eng]
            for (b, kind) in self.pending[eng]:
                if kind == "r":
                    b.rs = [t for t in b.rs if not (t[0] == "PENDING" and t[1] == eng)]
                    b.rs.append(tok)
                else:
                    b.w = tok
                    b.rs = [t for t in b.rs if not (t[0] == "PENDING" and t[1] == eng)]
            self.pending[eng] = []
            for b in r:
                b.rs.append(tok)
            for b in w:
                b.w = tok
                b.rs = []
            self.prog[eng].append((waits, fn, (semh, 1)))
        else:
            ptok = ("PENDING", eng)
            for b in r:
                b.rs.append(ptok)
                self.pending[eng].append((b, "r"))
            for b in w:
                b.w = ptok
                b.rs = []
                self.pending[eng].append((b, "w"))
            self.prog[eng].append((waits, fn, None))
        self.ninstr += 1

    def dma(self, eng, out_ap, in_ap, r=(), w=(), sbuf=None, **kw):
        waits = self._deps(eng, r, w)
        semh = self.dsem_of(sbuf)
        sbuf.dcnt += 1
        tok = ("d_" + sbuf.name, 16 * sbuf.dcnt, semh)
        for b in r:
            b.rs.append(tok)
        for b in w:
            b.w = tok
            b.rs = []
        self.prog[eng].append((waits, lambda e: e.dma_start(out=out_ap, in_=in_ap, **kw), (semh, 16)))
        self.ninstr += 1
        return tok

    def wait_tok(self, eng, tok):
        waits = {}
        self._need(eng, tok, waits)
        for (semh, val) in waits.values():
            self.prog[eng].append(([(semh, val)], None, None))

    def final_wait(self, eng, bufs):
        waits = {}
        for b in bufs:
            self._need(eng, b.w, waits)
            for t in b.rs:
                self._need(eng, t, waits)
        if waits:
            self.prog[eng].append((list(waits.values()), None, None))

    def emit(self):
        nc = self.nc
        engmap = {"pe": "tensor", "act": "scalar", "dve": "vector", "pool": "gpsimd", "sp": "sync"}
        with nc.Block() as block:
            for e in ENGS:
                prog = self.prog[e]

                def body(engine, prog=prog):
                    for waits, fn, inc in prog:
                        for (semh, val) in waits:
                            engine.wait_ge(semh, val)
                        if fn is not None:
                            ins = fn(engine)
                            if inc is not None:
                                ins.then_inc(inc[0], inc[1])
                getattr(block, engmap[e])(body)
        self.prog = {e: [] for e in ENGS}


D_MODEL = 1024
EVEN_IN = 4624


class T:
    def __init__(self, k, name, shape, dt, space="sb"):
        self.t = k.sb("t_" + name, shape, dt) if space == "sb" else k.ps("t_" + name, shape, dt)
        self.b = Buf(name)
        self.name = name

    def __getitem__(self, idx):
        return self.t[idx]


class View:
    def __init__(self, ap, b):
        self.ap = ap
        self.b = b

    def __getitem__(self, idx):
        return self.ap[idx]


def _b(xs):
    return [getattr(x, "b", x) for x in xs]


class Ctx:
    pass


def setup_common(k, nc):
    c = Ctx()
    c.k = k
    c.nc = nc
    c.rr = 0
    return c


def OP(c, eng, fn, r=(), w=(), sig=True):
    c.k.op(eng, fn, r=_b(r), w=_b(w), sig=sig)


def ACT(c, out, in_, func, r, w, **kw):
    OP(c, "act", lambda e: e.activation(out=out, in_=in_, func=func, **kw), r, w)


def TT(c, eng, out, in0, in1, op, r, w):
    OP(c, eng, lambda e: e.tensor_tensor(out=out, in0=in0, in1=in1, op=op), r, w)


def TS(c, eng, out, in0, s1, s2, op0, op1, r, w):
    if s2 is None:
        OP(c, eng, lambda e: e.tensor_scalar(out=out, in0=in0, scalar1=s1, scalar2=None, op0=op0), r, w)
    else:
        OP(c, eng, lambda e: e.tensor_scalar(out=out, in0=in0, scalar1=s1, scalar2=s2, op0=op0, op1=op1), r, w)


def STT(c, eng, out, in0, scalar, in1, op0, op1, r, w):
    OP(c, eng, lambda e: e.scalar_tensor_tensor(out=out, in0=in0, scalar=scalar, in1=in1, op0=op0, op1=op1), r, w)


def CP(c, eng, out, in_, r, w):
    if eng == "act":
        ACT(c, out, in_, AF.Copy, r, w)
    else:
        OP(c, eng, lambda e: e.tensor_copy(out=out, in_=in_), r, w)


def MM(c, out, lhsT, rhs, start, stop, r, w, sig, skip=False):
    if skip:
        OP(c, "pe", lambda e: e.matmul(out, lhsT=lhsT, rhs=rhs, start=start, stop=stop, skip_group_check=True), r, w, sig=sig)
    else:
        OP(c, "pe", lambda e: e.matmul(out, lhsT=lhsT, rhs=rhs, start=start, stop=stop), r, w, sig=sig)


def TR(c, out, in_, ident, r, w, sig):
    OP(c, "pe", lambda e: e.transpose(out=out, in_=in_, identity=ident), r, w, sig=sig)


def load_cast_weight(c, dst, dst_slices, dram_rows, width, scale_aps, st, engs=("act", "dve")):
    k = c.k
    for i, (da, ra) in enumerate(zip(dst_slices, dram_rows)):
        s = st[c.rr % len(st)]
        eng = engs[c.rr % len(engs)]
        c.rr += 1
        k.dma("sp", s.t[:, 0:width], ra, w=[s.b], sbuf=s.b)
        sc = scale_aps[i]
        if sc is None:
            CP(c, eng, da, s.t[:, 0:width], [s], [dst])
        else:
            if eng == "act":
                OP(c, "act", lambda e, da=da, s=s, sc=sc: e.activation(out=da, in_=s.t[:, 0:width], func=AF.Copy, scale=sc), [s, c.vecF], [dst])
            else:
                TS(c, eng, da, s.t[:, 0:width], sc, None, ALU.mult, None, [s, c.vecF], [dst])


def setup_layer0(c, D):
    k = c.k
    c.w_in = T(k, "w_in", [128, 8, EVEN_IN], BF16)
    c.w_out = T(k, "w_out", [128, 16, 1024], BF16)
    c.wa = T(k, "wa", [128, 4, 2, 256], BF16)
    c.wx = T(k, "wx", [128, 4, 2, 256], BF16)
    c.vecF = T(k, "vecF", [128, 140], F32)
    c.vecT = T(k, "vecT", [128, 48], F32)
    k.dma("sp", c.vecF[:], D["vecF"][:, :], w=[c.vecF.b], sbuf=c.vecF.b)
    k.dma("sp", c.vecT[:], D["vecT"][0:1, :].partition_broadcast(128), w=[c.vecT.b], sbuf=c.vecT.b)
    st = c.stage
    for (c0, wd) in ((0, 1024), (1024, 1024), (2048, 1024), (3072, 1024), (4096, 528)):
        dsts, rows, scs = [], [], []
        for kc in range(8):
            dsts.append(c.w_in[:, kc, c0:c0 + wd])
            rows.append(D["w_in"][kc * 128:(kc + 1) * 128, c0:c0 + wd])
            scs.append(c.vecF[:, kc:kc + 1])
        load_cast_weight(c, c.w_in, dsts, rows, wd, scs, st)
    dsts, rows, scs = [], [], []
    for kc in range(16):
        dsts.append(c.w_out[:, kc, :])
        rows.append(D["w_out"][kc * 128:(kc + 1) * 128, :])
        scs.append(None if kc < 8 else c.vecF[:, 8 + kc - 8:8 + kc - 8 + 1])
    load_cast_weight(c, c.w_out, dsts, rows, 1024, scs, st)
    for (wt, nm) in ((c.wa, "wa"), (c.wx, "wx")):
        dsts, rows, scs = [], [], []
        for g in range(4):
            for kc in range(2):
                dsts.append(wt[:, g, kc, :])
                rows.append(D[nm][g, kc * 128:(kc + 1) * 128, :])
                scs.append(None)
        load_cast_weight(c, wt, dsts, rows, 256, scs, st)

    c.L1 = T(k, "L1", [128, 128], F32)
    c.L2 = T(k, "L2", [128, 128], F32)
    c.L4 = T(k, "L4", [128, 2, 128], F32)
    c.mle = T(k, "mle", [128, 64], F32)
    OP(c, "pool", lambda e: e.memset(c.L1[:], 0.0), [], [c.L1])
    OP(c, "pool", lambda e: e.memset(c.L2[:], 0.0), [], [c.L2])
    OP(c, "pool", lambda e: e.memset(c.L4[:], 0.0), [], [c.L4])
    OP(c, "pool", lambda e: e.memset(c.mle[:], 1.0), [], [c.mle])
    for h in range(2):
        ps = slice(h * 64, (h + 1) * 64)
        OP(c, "pool", lambda e, ps=ps: e.memset(c.L1[ps, ps], 1.0), [], [c.L1])
        OP(c, "pool", lambda e, ps=ps: e.memset(c.L2[ps, ps], 1.0), [], [c.L2])
        OP(c, "pool", lambda e, ps=ps, h=h: e.memset(c.L4[ps, h, :], 1.0), [], [c.L4])
        OP(c, "pool", lambda e, ps=ps: e.affine_select(out=c.L1[ps, ps], in_=c.L1[ps, ps], pattern=[[1, 64]], compare_op=ALU.is_ge, fill=0.0, base=0, channel_multiplier=-1), [c.L1], [c.L1])
        OP(c, "pool", lambda e, ps=ps: e.affine_select(out=c.L2[ps, ps], in_=c.L2[ps, ps], pattern=[[-1, 64]], compare_op=ALU.is_gt, fill=0.0, base=0, channel_multiplier=1), [c.L2], [c.L2])
        OP(c, "pool", lambda e, ps=ps: e.affine_select(out=c.mle[ps, :], in_=c.mle[ps, :], pattern=[[1, 64]], compare_op=ALU.is_ge, fill=0.0, base=0, channel_multiplier=-1), [c.mle], [c.mle])
    c.pv = T(k, "pv", [128, 64], F32)
    ACT(c, c.pv[:, 0:8], c.vecF[:, 72:80], AF.Exp, [c.vecF], [c.pv], scale=-1.0)
    ACT(c, c.pv[:, 0:8], c.pv[:, 0:8], AF.Ln, [c.pv], [c.pv], bias=1.0)
    TS(c, "dve", c.pv[:, 8:16], c.pv[:, 0:8], -16.0, None, ALU.mult, None, [c.pv], [c.pv])
    TS(c, "dve", c.pv[:, 0:8], c.pv[:, 0:8], -8.0, None, ALU.mult, None, [c.pv], [c.pv])
    TS(c, "dve", c.pv[:, 16:32], c.vecF[:, 56:72], -1.0, None, ALU.mult, None, [c.vecF], [c.pv])
    ACT(c, c.pv[:, 32:48], c.vecT[:, 16:32], AF.Exp, [c.vecT], [c.pv])
    TS(c, "dve", c.pv[:, 32:48], c.pv[:, 32:48], -1.0, None, ALU.mult, None, [c.pv], [c.pv])


def alloc_layer0_work(c):
    k = c.k
    c.xt = [T(k, "xt%d" % i, [128, 1024], F32) for i in range(2)]
    c.st1 = [T(k, "st1_%d" % i, [128, 8], F32) for i in range(2)]
    c.W = [T(k, "W%d" % i, [128, 1024], F32) for i in range(6)]
    c.H = [T(k, "H%d" % i, [128, (256 if i == 1 else (512 if i == 5 else 1024))], BF16) for i in range(6)]
    c.projF = [T(k, "projF%d" % i, [128, 20, 131], F32) for i in range(2)]
    c.sg = [T(k, "sg%d" % i, [128, 1024], BF16) for i in range(2)]
    c.gz = [T(k, "gz%d" % i, [128, 1024], BF16) for i in range(2)]
    c.dts = [T(k, "dts%d" % i, [128, 32], F32) for i in range(2)]
    c.ubP = T(k, "ubP", [128, 1024], BF16)
    c.sgt = View(c.ubP.t[:, :].bitcast(F32), c.ubP.b)
    c.uTP = T(k, "uTP", [128, 1024], BF16)
    c.xbc = T(k, "xbc", [128, 12, 128], F32)
    c.lxcb = [Buf("lxc%d" % i) for i in range(8)]
    c.acb = [Buf("ac%d" % i) for i in range(8)]
    c.a2cb = [Buf("a2c%d" % i) for i in range(8)]
    c.eacb = [Buf("eac%d" % i) for i in range(8)]
    c.excb = [Buf("exc%d" % i) for i in range(8)]
    c.xbccb = [Buf("xbcc%d" % i) for i in range(12)]
    c.S = T(k, "S", [128, 1024], F32)
    c.Sbf = T(k, "Sbf", [128, 1024], BF16)
    c.hst = T(k, "hst", [128, 8], F32)
    c.S_meta = T(k, "S_meta", [128, 1024], F32)
    c.hst_meta = T(k, "hst_meta", [128, 8], F32)
    c.hist_meta = T(k, "hist_meta", [128, 20, 3], F32)
    c.sm = T(k, "sm", [128, 96], F32)
    c.cbm = T(k, "cbm", [128, 2, 64], F32)
    c.pT = T(k, "pT", [128, 8, 128], BF16, "ps")
    c.pG = T(k, "pG", [128, 512], F32, "ps")
    c.pT2 = View(c.pG[:, 256:384].bitcast(BF16).rearrange("p (c t) -> p c t", c=2), c.pG.b)
    c.pB = [T(k, "pB%d" % i, [128, 512], F32, "ps") for i in range(6)]
    c.stage = [c.W[0], c.W[1], c.W[2], c.W[3]]
    c.pTP = View(c.pB[0][:, :].bitcast(BF16).rearrange("p (c t) -> p c t", c=8), c.pB[0].b)


def seq_reset0(c, par):
    OP(c, "pool", lambda e: e.memset(c.projF[par][:, :, 0:3], 0.0), [], [c.projF[par]])
    OP(c, "pool", lambda e: e.memset(c.S[:], 0.0), [], [c.S])
    OP(c, "pool", lambda e: e.memset(c.Sbf[:], 0.0), [], [c.Sbf])
    OP(c, "pool", lambda e: e.memset(c.hst[:], 0.0), [], [c.hst])


def act_sigmoid_from(c, out, in_, rin, wout, neg_bias=None):
    ACT(c, out, in_, AF.Exp, rin, wout, scale=-1.0)
    ACT(c, out, out, AF.Ln, wout, wout, bias=1.0)
    ACT(c, out, out, AF.Exp, wout, wout, scale=-1.0)


def layer0_P(c, src_ap, nt, par, prev):
    k = c.k
    xt = c.xt[par]
    st1 = c.st1[par]
    P = slice(0, nt)
    identb = c.identb
    projF = c.projF[par]
    k.dma("sp", xt[P, :], src_ap, w=[xt.b], sbuf=xt.b)
    ACT(c, c.ubP[P, :], xt[P, :], AF.Square, [xt], [c.ubP, st1], accum_out=st1[P, 0:1])
    ACT(c, st1[P, 1:2], st1[P, 0:1], AF.Ln, [st1], [st1], scale=1.0 / D_MODEL, bias=1e-6)
    ACT(c, st1[P, 1:2], st1[P, 1:2], AF.Exp, [st1], [st1], scale=-0.5)
    ub = c.ubP
    TS(c, "dve", ub[P, :], xt[P, :], st1[P, 1:2], None, ALU.mult, None, [xt, st1], [ub])
    yield
    for kc in range(8):
        TR(c, c.pTP[:, kc, P], ub[P, kc * 128:(kc + 1) * 128], identb[P, P], [ub, identb], [c.pTP], sig=(kc == 7))
    uT = c.uTP
    uTv = uT[:].rearrange("p (c t) -> p c t", c=8)
    CP(c, "act", uTv[:, :, P], c.pTP[:, :, P], [c.pTP], [uT])
    yield
    if prev == "meta":
        CP(c, "pool", projF[:, :, 0:3], c.hist_meta[:, :, :], [c.hist_meta], [projF])
    elif prev is not None:
        pp, pnt = prev
        CP(c, "pool", projF[:, :, 0:3], c.projF[pp][:, :, pnt:pnt + 3], [c.projF[pp]], [projF])
    pz = (c.pB[0], c.pB[1])
    gz = c.gz[par]
    dts = c.dts[par]
    for kc in range(8):
        MM(c, c.pB[1][P, 0:16], uTv[:, kc, P], c.w_in[:, kc, 4608:4624], kc == 0, kc == 7, [uT, c.w_in], [c.pB[1]], sig=(kc == 7))
    TT(c, "dve", dts[P, 0:16], c.pB[1][P, 0:16], c.vecT[P, 0:16], ALU.add, [c.pB[1], c.vecT], [dts])
    yield
    ACT(c, dts[P, 0:16], dts[P, 0:16], AF.Exp, [dts], [dts])
    ACT(c, dts[P, 0:16], dts[P, 0:16], AF.Ln, [dts], [dts], bias=1.0)
    TT(c, "dve", dts[P, 16:32], dts[P, 0:16], c.pv[P, 32:48], ALU.mult, [dts, c.pv], [dts])
    yield
    for hf in range(2):
        for kc in range(8):
            MM(c, pz[hf][P, :], uTv[:, kc, P], c.w_in[:, kc, 2048 + hf * 512:2048 + (hf + 1) * 512], kc == 0, kc == 7, [uT, c.w_in], [pz[hf]], sig=(kc == 7))
        yield
    for hf in range(2):
        hs = slice(hf * 512, (hf + 1) * 512)
        act_sigmoid_from(c, c.sgt[P, :], pz[hf][P, :], [pz[hf]], [c.sgt])
        yield
        TT(c, "dve", gz[P, hs], c.sgt[P, :], pz[hf][P, :], ALU.mult, [c.sgt, pz[hf]], [gz])
        yield
    sg = c.sg[par]
    sgv = sg[:].rearrange("p (c t) -> p c t", c=8)
    groups = []
    for g4 in range(2):
        groups.append(("x", [g4 * 4 + i for i in range(4)], 0))
    for g4 in range(2):
        groups.append(("g", [g4 * 4 + i for i in range(4)], 1024))
    for g4 in range(3):
        groups.append(("b", [g4 * 4 + i for i in range(4)], 3072))
    for gi, (kind, ocs, colbase) in enumerate(groups):
        pb = c.pB[gi % 2]
        pbv = pb[:].rearrange("p (c t) -> p c t", c=4)
        for i, oc in enumerate(ocs):
            for kc in range(8):
                MM(c, pbv[:, i, P], c.w_in[:, kc, colbase + oc * 128:colbase + (oc + 1) * 128], uTv[:, kc, P], kc == 0, kc == 7, [uT, c.w_in], [pb], sig=(kc == 7))
            yield
        if kind == "x":
            CP(c, "act", projF[:, ocs[0]:ocs[0] + 4, 3:3 + nt], pbv[:, :, P], [pb], [projF])
        elif kind == "b":
            CP(c, "act", projF[:, 8 + ocs[0]:8 + ocs[0] + 4, 3:3 + nt], pbv[:, :, P], [pb], [projF])
        else:
            o = sgv[:, ocs[0]:ocs[0] + 4, P]
            sc = c.sgt[:].rearrange("p (c t) -> p c t", c=4)[:, :, P]
            act_sigmoid_from(c, sc, pbv[:, :, P], [pb], [c.sgt])
            yield
            TT(c, "dve", o, sc, pbv[:, :, P], ALU.mult, [c.sgt, pb], [sg])
        yield


def layer0_M(c, dst_ap, nt, par, dst_buf):
    k = c.k
    xt = c.xt[par]
    st1 = c.st1[par]
    W = c.W
    H = c.H
    chunks = [(0, nt)] if nt <= 64 else [(0, 64), (64, 128)]
    cw = chunks[0][1]
    nch = len(chunks)
    identb = c.identb
    P = slice(0, nt)
    projF = c.projF[par]
    sg = c.sg[par]
    sgv = sg[:].rearrange("p (c t) -> p c t", c=8)
    gz = c.gz[par]
    dts = c.dts[par]
    sm = c.sm

    lx = W[3]
    lxv = lx[:].rearrange("p (c t) -> p c t", c=8)
    lcb = c.lxcb
    ne = 0
    for tp in range(4):
        for ch in range(8):
            o = lxv[:, ch, P]
            if tp == 0:
                TS(c, "dve", o, projF[:, ch, 0:nt], c.vecF[:, 16 + ch * 4:16 + ch * 4 + 1], c.vecF[:, 48 + ch:48 + ch + 1], ALU.mult, ALU.add, [projF, c.vecF], ([lx, lcb[ch]] if ch == 0 else [lcb[ch]]))
            else:
                STT(c, "dve", o, projF[:, ch, tp:tp + nt], c.vecF[:, 16 + ch * 4 + tp:16 + ch * 4 + tp + 1], o, ALU.mult, ALU.add, [projF, c.vecF, lcb[ch]], [lcb[ch]])
            ne += 1
            if ne % 4 == 0:
                yield
    yield
    lxb = H[2]
    lxbv = lxb[:].rearrange("p (c t) -> p c t", c=8)
    CP(c, "act", lxbv[:, :, P], lxv[:, :, P], [lx] + c.lxcb, [lxb])
    ea_ = W[4]
    ex_ = W[5]
    eav = ea_[:].rearrange("p (c t) -> p c t", c=8)
    exv = ex_[:].rearrange("p (c t) -> p c t", c=8)
    for (wt, pbs, ev, boff, et) in ((c.wa, (c.pB[2], c.pB[3]), eav, 16, ea_), (c.wx, (c.pB[4], c.pB[5]), exv, 24, ex_)):
        for oc in range(8):
            g = oc // 2
            pb = pbs[oc // 4]
            pbv = pb[:].rearrange("p (c t) -> p c t", c=4)
            for kc in range(2):
                MM(c, pbv[:, oc % 4, P], wt[:, g, kc, (oc % 2) * 128:(oc % 2 + 1) * 128], lxbv[:, 2 * g + kc, P], kc == 0, kc == 1, [lxb, wt], [pb], sig=(oc % 4 == 3 and kc == 1))
    yield
    xbc = c.xbc
    xcb = c.xbccb
    ne = 0
    for tp in range(4):
        for ch in range(12):
            o = xbc[:, ch, P]
            if tp == 0:
                TS(c, "dve", o, projF[:, 8 + ch, 0:nt], c.vecF[:, 80 + ch * 4:80 + ch * 4 + 1], c.vecF[:, 128 + ch:128 + ch + 1], ALU.mult, ALU.add, [projF, c.vecF], ([xbc, xcb[ch]] if ch == 0 else [xcb[ch]]))
            else:
                STT(c, "dve", o, projF[:, 8 + ch, tp:tp + nt], c.vecF[:, 80 + ch * 4 + tp:80 + ch * 4 + tp + 1], o, ALU.mult, ALU.add, [projF, c.vecF, xcb[ch]], [xcb[ch]])
            ne += 1
            if ne % 4 == 0:
                yield
    gl = ((c.wa, (c.pB[2], c.pB[3]), eav, 16, ea_, c.eacb), (c.wx, (c.pB[4], c.pB[5]), exv, 24, ex_, c.excb))
    for (wt, pbs, ev, boff, et, cb) in gl:
        for oc in range(8):
            pb = pbs[oc // 4]
            pbv = pb[:].rearrange("p (c t) -> p c t", c=4)
            ACT(c, ev[:, oc, P], pbv[:, oc % 4, P], AF.Exp, [pb, c.pv], ([et, cb[0]] if oc == 0 else [cb[oc]]), scale=-1.0, bias=c.pv[:, boff + oc:boff + oc + 1])
            if oc % 4 == 3:
                yield
    for (wt, pbs, ev, boff, et, cb) in gl:
        ACT(c, ev[:, :, P], ev[:, :, P], AF.Ln, [et] + cb, [et], bias=1.0)
    for (wt, pbs, ev, boff, et, cb) in gl:
        ACT(c, ev[:, :, P], ev[:, :, P], AF.Exp, [et], [et], scale=-1.0)
    yield
    e0 = W[0][:].rearrange("p (c t) -> p c t", c=8)
    e1 = W[1][:].rearrange("p (c t) -> p c t", c=8)
    act_sigmoid_from(c, e0[:, :, P], xbc[:, 0:8, P], [xbc] + c.xbccb, [W[0]])
    act_sigmoid_from(c, e1[:, 0:4, P], xbc[:, 8:12, P], [xbc] + c.xbccb, [W[1]])
    yield
    xsT = H[3][:].rearrange("p (c t) -> p c t", c=8)
    bcT = H[5][:].rearrange("p (c t) -> p c t", c=4)
    TT(c, "dve", xsT[:, :, P], e0[:, :, P], xbc[:, 0:8, P], ALU.mult, [W[0], xbc] + c.xbccb, [H[3]])
    TT(c, "dve", bcT[:, 0:4, P], e1[:, 0:4, P], xbc[:, 8:12, P], ALU.mult, [W[1], xbc] + c.xbccb, [H[5]])
    yield
    a_ = W[2]
    av = a_[:].rearrange("p (c t) -> p c t", c=8)
    a2_ = W[1]
    a2v = a2_[:].rearrange("p (c t) -> p c t", c=8)
    for ch in range(8):
        ACT(c, av[:, ch, P], eav[:, ch, P], AF.Exp, [ea_, c.pv], ([a_, c.acb[0]] if ch == 0 else [c.acb[ch]]), scale=c.pv[:, ch:ch + 1])
        ACT(c, a2v[:, ch, P], eav[:, ch, P], AF.Exp, [ea_, c.pv], ([a2_, c.a2cb[0]] if ch == 0 else [c.a2cb[ch]]), scale=c.pv[:, 8 + ch:8 + ch + 1])
        if ch % 4 == 3:
            yield
    for ch in range(8):
        TR(c, c.pT[P, ch, :], xsT[:, ch, P], identb[:, :], [H[3], identb], [c.pT], sig=(ch == 7))
    for ch in range(2):
        TR(c, c.pT2[P, ch, :], bcT[:, ch, P], identb[:, :], [H[5], identb], [c.pT2], sig=(ch == 1))
    yield
    TS(c, "dve", a2v[:, :, P], a2v[:, :, P], -1.0, 1.0, ALU.mult, ALU.add, [a2_] + c.a2cb, [a2_])
    ACT(c, a2v[:, :, P], a2v[:, :, P], AF.Ln, [a2_], [a2_])
    ACT(c, a2v[:, :, P], a2v[:, :, P], AF.Exp, [a2_], [a2_], scale=0.5)
    yield
    Xps = c.pT[:].rearrange("p c t -> p (c t)")
    Xdt = H[0]
    TT(c, "dve", Xdt[P, :].rearrange("p (h d) -> p h d", h=16), Xps[P, :].rearrange("p (h d) -> p h d", h=16),
       dts[P, 0:16].unsqueeze(2).to_broadcast([nt, 16, 64]), ALU.mult, [c.pT, dts], [Xdt])
    skip = W[0]
    TT(c, "dve", skip[P, :].rearrange("p (h d) -> p h d", h=16), Xps[P, :].rearrange("p (h d) -> p h d", h=16),
       c.vecT[P, 32:48].unsqueeze(2).to_broadcast([nt, 16, 64]), ALU.mult, [c.pT, c.vecT], [skip])
    yield
    Btok = H[1]
    CP(c, "act", Btok[P, 0:256], c.pT2[P, :, :].rearrange("p c t -> p (c t)"), [c.pT2], [Btok])
    MM(c, c.pG[P, 16:32], c.L1[P, P], dts[P, 16:32], True, True, [c.L1, dts], [c.pG], sig=False)
    MM(c, c.pG[P, 32:48], c.L2[P, P], dts[P, 16:32], True, True, [c.L2, dts], [c.pG], sig=False)
    for ci in range(nch):
        MM(c, c.pG[:, 48 + 16 * ci:64 + 16 * ci], c.L4[P, ci, :], dts[P, 16:32], True, True, [c.L4, dts], [c.pG], sig=(ci == nch - 1))
    ACT(c, sm[P, 32:48], c.pG[P, 16:32], AF.Exp, [c.pG], [sm])
    ACT(c, sm[P, 48:64], c.pG[P, 32:48], AF.Exp, [c.pG], [sm])
    ACT(c, sm[:, 64:64 + 16 * nch], c.pG[:, 48:48 + 16 * nch], AF.Exp, [c.pG], [sm])
    yield
    TT(c, "dve", exv[:, :, P], exv[:, :, P], a2v[:, :, P], ALU.mult, [ex_, a2_], [ex_])
    TT(c, "dve", exv[:, :, P], exv[:, :, P], lxv[:, :, P], ALU.mult, [ex_, lx] + c.lxcb, [ex_])
    for ch in range(8):
        OP(c, "dve", lambda e, ch=ch: e.tensor_tensor_scan(out=eav[:, ch, P], data0=av[:, ch, P], data1=exv[:, ch, P], initial=c.hst[:, ch:ch + 1], op0=ALU.mult, op1=ALU.add), [a_, ex_, c.hst] + c.acb, ([ea_, c.eacb[0]] if ch == 0 else [c.eacb[ch]]))
        if ch % 4 == 3:
            yield
    CP(c, "dve", c.hst[:, :], eav[:, :, nt - 1], [ea_] + c.eacb, [c.hst])
    mixA = H[4][:].rearrange("p (c t) -> p c t", c=8)
    mixB = H[2][:].rearrange("p (c t) -> p c t", c=8)
    TT(c, "pool", mixA[:, :, P], eav[:, :, P], sgv[:, :, P], ALU.mult, [ea_, sg] + c.eacb, [H[4]])
    yield
    R1 = W[1]
    R1v = R1[:, 0:16 * cw].rearrange("p (h l) -> p h l", h=16)
    TT(c, "dve", R1v[P, :, :], dts[P, 16:32].unsqueeze(2).to_broadcast([nt, 16, cw]),
       c.mle[P, 0:cw].unsqueeze(1).to_broadcast([nt, 16, cw]), ALU.mult, [dts, c.mle], [R1])
    for hf in range(2):
        pb = c.pB[2 + hf]
        MM(c, pb[P, 0:8 * cw], c.L2[P, P], R1[P, hf * 8 * cw:(hf + 1) * 8 * cw], True, True, [c.L2, R1], [pb], sig=True)
    dec = W[2]
    decv = dec[:, 0:16 * cw].rearrange("p (h l) -> p h l", h=16)
    for hf in range(2):
        pb = c.pB[2 + hf]
        ACT(c, dec[P, hf * 8 * cw:(hf + 1) * 8 * cw], pb[P, 0:8 * cw], AF.Exp, [pb], [dec])
    yield
    cbps = c.pG[:, 128:256].rearrange("p (g l) -> p g l", g=2)
    for ci, (p0, p1) in enumerate(chunks):
        for g in range(2):
            MM(c, cbps[p0:p1, g, 0:cw], bcT[:, g, p0:p1], bcT[:, 2 + g, p0:p1], True, True, [H[5]], [c.pG], sig=(ci == nch - 1 and g == 1))
    TT(c, "dve", c.cbm[P, :, 0:cw], cbps[P, :, 0:cw], c.mle[P, 0:cw].unsqueeze(1).to_broadcast([nt, 2, cw]), ALU.mult, [c.pG, c.mle], [c.cbm])
    yield
    MT = H[3]
    MTv = MT[:, 0:16 * cw].rearrange("p (h l) -> p h l", h=16)
    for g in range(2):
        TT(c, "dve", MTv[P, g * 8:(g + 1) * 8, :], decv[P, g * 8:(g + 1) * 8, :],
           c.cbm[P, g:g + 1, 0:cw].to_broadcast([nt, 8, cw]), ALU.mult, [dec, c.cbm], [MT])
    yield
    Xd = H[2]
    TT(c, "pool", Xd[P, :].rearrange("p (h d) -> p h d", h=16), Xdt[P, :].rearrange("p (h d) -> p h d", h=16),
       sm[P, 48:64].unsqueeze(2).to_broadcast([nt, 16, 64]), ALU.mult, [Xdt, sm], [H[2]])
    for ci, (p0, p1) in enumerate(chunks):
        for h in range(16):
            pb = c.pB[4 + h // 8]
            MM(c, pb[p0:p1, (h % 8) * 64:(h % 8 + 1) * 64], MTv[p0:p1, h, :], Xdt[p0:p1, h * 64:(h + 1) * 64], True, True, [MT, Xdt], [pb],
               sig=(h % 8 == 7))
        yield
    y = W[1]
    for ci, (p0, p1) in enumerate(chunks):
        PC = slice(p0, p1)
        ncw = p1 - p0
        for g in range(2):
            pb = c.pB[2 + g]
            MM(c, pb[p0:p1, :], bcT[:, 2 + g, p0:p1], c.Sbf[:, g * 512:(g + 1) * 512], True, True, [H[5], c.Sbf], [pb], sig=True)
        yield
        for g in range(2):
            gs = slice(g * 512, (g + 1) * 512)
            TT(c, "dve", y[PC, gs].rearrange("p (h d) -> p h d", h=8), c.pB[2 + g][PC, :].rearrange("p (h d) -> p h d", h=8),
               sm[PC, 32 + 8 * g:40 + 8 * g].unsqueeze(2).to_broadcast([ncw, 8, 64]), ALU.mult, [c.pB[2 + g], sm], [y])
        yield
        for g in range(2):
            pb = c.pB[2 + g]
            MM(c, pb[:, :], Btok[p0:p1, g * 128:(g + 1) * 128], Xd[p0:p1, g * 512:(g + 1) * 512], True, True, [Btok, H[2]], [pb], sig=True)
        TT(c, "pool", c.S[:].rearrange("p (h d) -> p h d", h=16), c.S[:].rearrange("p (h d) -> p h d", h=16),
           sm[:, 64 + 16 * ci:80 + 16 * ci].unsqueeze(2).to_broadcast([128, 16, 64]), ALU.mult, [c.S, sm], [c.S])
        yield
        for g in range(2):
            gs = slice(g * 512, (g + 1) * 512)
            TT(c, "dve", c.S[:, gs], c.S[:, gs], c.pB[2 + g][:, :], ALU.add, [c.S, c.pB[2 + g]], [c.S])
        CP(c, "act", c.Sbf[:], c.S[:], [c.S], [c.Sbf])
        yield
    for g in range(2):
        gs = slice(g * 512, (g + 1) * 512)
        TT(c, "dve", y[P, gs], y[P, gs], c.pB[4 + g][P, :], ALU.add, [y, c.pB[4 + g]], [y])
    yield
    yield
    TT(c, "dve", y[P, :], y[P, :], skip[P, :], ALU.add, [y, skip], [y])
    TT(c, "dve", y[P, :], y[P, :], gz[P, :], ALU.mult, [y, gz], [y])
    for g in range(2):
        gs = slice(g * 512, (g + 1) * 512)
        ACT(c, W[4][P, gs], y[P, gs], AF.Square, [y], [W[4], st1], accum_out=st1[P, 2 + g:3 + g])
    ACT(c, st1[P, 4:6], st1[P, 2:4], AF.Ln, [st1], [st1], scale=1.0 / 512, bias=1e-6)
    ACT(c, st1[P, 4:6], st1[P, 4:6], AF.Exp, [st1], [st1], scale=-0.5)
    yield
    yb = H[0]
    for g in range(2):
        gs = slice(g * 512, (g + 1) * 512)
        TS(c, "dve", yb[P, gs], y[P, gs], st1[P, 4 + g:5 + g], None, ALU.mult, None, [y, st1], [yb])
    for ch in range(8):
        TR(c, c.pT[:, ch, P], yb[P, ch * 128:(ch + 1) * 128], identb[P, P], [yb, identb], [c.pT], sig=(ch == 7))
    CP(c, "act", mixB[:, :, P], c.pT[:, :, P], [c.pT], [H[2]])
    yield
    for hf in range(2):
        pb = c.pB[2 + hf]
        for kc in range(16):
            lhs = mixA[:, kc, P] if kc < 8 else mixB[:, kc - 8, P]
            MM(c, pb[P, :], lhs, c.w_out[:, kc, hf * 512:(hf + 1) * 512], kc == 0, kc == 15, [H[4], H[2], c.w_out], [pb], sig=(kc == 15))
    yield
    for hf in range(2):
        hs = slice(hf * 512, (hf + 1) * 512)
        TT(c, "dve", xt[P, hs], xt[P, hs], c.pB[2 + hf][P, :], ALU.add, [xt, c.pB[2 + hf]], [xt])
    k.dma("sp", dst_ap, xt[P, :], r=[xt.b], w=[dst_buf], sbuf=xt.b)


def interleave(gm, gp, ratio=1):
    am, ap = gm is not None, gp is not None
    while am or ap:
        if am:
            for _ in range(ratio):
                try:
                    next(gm)
                except StopIteration:
                    am = False
                    break
        if ap:
            try:
                next(gp)
            except StopIteration:
                ap = False


def layer0_seq(c, tiles, par0, dst_buf, first_seq=True):
    par = par0
    n = len(tiles)
    if first_seq:
        seq_reset0(c, par)
        interleave(None, layer0_P(c, tiles[0][0], tiles[0][2], par, None))
        start = 0
    else:
        CP(c, "pool", c.S[:], c.S_meta[:], [c.S_meta], [c.S])
        CP(c, "act", c.Sbf[:], c.S_meta[:], [c.S_meta], [c.Sbf])
        CP(c, "pool", c.hst[:], c.hst_meta[:], [c.hst_meta], [c.hst])
        interleave(None, layer0_P(c, tiles[1][0], tiles[1][2], par, "meta"))
        start = 1
    for j in range(start, n):
        gp = layer0_P(c, tiles[j + 1][0], tiles[j + 1][2], par ^ 1, (par, tiles[j][2])) if j + 1 < n else None
        gm = layer0_M(c, tiles[j][1], tiles[j][2], par, dst_buf)
        interleave(gm, gp)
        if first_seq and j == 0:
            CP(c, "pool", c.S_meta[:], c.S[:], [c.S], [c.S_meta])
            CP(c, "pool", c.hst_meta[:], c.hst[:], [c.hst], [c.hst_meta])
            CP(c, "pool", c.hist_meta[:, :, :], c.projF[par][:, :, tiles[0][2]:tiles[0][2] + 3], [c.projF[par]], [c.hist_meta])
        par ^= 1
    return par


LTOT = 2064
BIG = 30000.0


def setup_layer1(c, D, L):
    k = c.k
    c.w_in1 = T(k, "w_in1", [128, 8, 4096], BF16)
    c.w_out1 = T(k, "w_out1", [128, 8, 1024], BF16)
    c.vecF = T(k, "vec1F", [128, 8], F32)
    c.fn = T(k, "fn", [128, 1024], F32)
    k.dma("sp", c.vecF[:], D["vec1F"][:, :], w=[c.vecF.b], sbuf=c.vecF.b)
    k.dma("sp", c.fn[:], D["fnorm"][0:1, :].partition_broadcast(128), w=[c.fn.b], sbuf=c.fn.b)
    st = c.stage
    dsts, rows, scs = [], [], []
    for kc in range(8):
        for hf in range(8):
            dsts.append(c.w_in1[:, kc, hf * 512:(hf + 1) * 512])
            rows.append(D["w_in1"][kc * 128:(kc + 1) * 128, hf * 512:(hf + 1) * 512])
            scs.append(c.vecF[:, kc:kc + 1])
    load_cast_weight(c, c.w_in1, dsts, rows, 512, scs, st)
    dsts, rows, scs = [], [], []
    for kc in range(8):
        for hf in range(2):
            dsts.append(c.w_out1[:, kc, hf * 512:(hf + 1) * 512])
            rows.append(D["w_out1"][kc * 128:(kc + 1) * 128, hf * 512:(hf + 1) * 512])
            scs.append(None)
    load_cast_weight(c, c.w_out1, dsts, rows, 512, scs, st)
    c.negm = T(k, "negm", [128, 4, 128], BF16)
    c.tri2 = T(k, "tri2", [128, 128], BF16)
    c.zrow = T(k, "zrow", [1, 256], BF16)
    c.tri = T(k, "tri", [128, 128], BF16)
    c.ones = T(k, "ones", [128, 2], BF16)
    OP(c, "pool", lambda e: e.memset(c.negm[:], 0.0), [], [c.negm])
    OP(c, "pool", lambda e: e.memset(c.tri2[:], 1.0), [], [c.tri2])
    OP(c, "pool", lambda e: e.memset(c.zrow[:], 0.0), [], [c.zrow])
    OP(c, "pool", lambda e: e.memset(c.tri[:], 1.0), [], [c.tri])
    OP(c, "pool", lambda e: e.memset(c.ones[:], 1.0), [], [c.ones])
    for i in range(4):
        OP(c, "pool", lambda e, i=i: e.affine_select(out=c.negm[:, i, :], in_=c.negm[:, i, :], pattern=[[1, 128]], compare_op=ALU.is_gt, fill=-BIG, base=0, channel_multiplier=-1), [c.negm], [c.negm])
    OP(c, "pool", lambda e: e.affine_select(out=c.tri[:], in_=c.tri[:], pattern=[[-1, 128]], compare_op=ALU.is_ge, fill=0.0, base=0, channel_multiplier=1), [c.tri], [c.tri])
    OP(c, "pool", lambda e: e.affine_select(out=c.tri2[:], in_=c.tri2[:], pattern=[[1, 128]], compare_op=ALU.is_gt, fill=0.0, base=0, channel_multiplier=-1), [c.tri2], [c.tri2])
    nkb = 1 + (L - 16) // 128
    c.KT = T(k, "KT", [128, 8, L], BF16)
    c.V = T(k, "V", [128, nkb, 1024], BF16)
    c.KTb = [Buf("KTb%d" % i) for i in range(nkb)]
    c.Vb = [Buf("Vb%d" % i) for i in range(nkb)]


def alloc_layer1_work(c):
    k = c.k
    c.ht = [T(k, "ht%d" % i, [128, 1024], F32) for i in range(2)]
    c.st2 = [T(k, "st2_%d" % i, [128, 8], F32) for i in range(2)]
    c.ub1 = T(k, "ub1", [128, 1024], BF16)
    c.uT1 = T(k, "uT1", [128, 1024], BF16)
    c.QTs = [T(k, "QTs%d" % i, [128, 8, 2, 128], BF16) for i in range(2)]
    for i in range(2):
        OP(c, "pool", lambda e, i=i: e.memset(c.QTs[i][:], 0.0), [], [c.QTs[i]])
    c.sgt1 = T(k, "sgt1", [128, 512], F32)
    c.sgz = [T(k, "sgz%d" % i, [128, 1024], BF16) for i in range(2)]
    c.E = [T(k, "E%d" % i, [128, 512], F32) for i in range(4)]
    c.SP = [T(k, "SP%d" % i, [128, 512], BF16) for i in range(4)]
    c.X = [T(k, "X%d" % i, [128, 512], F32) for i in range(2)]
    c.Wt = [T(k, "Wt%d" % i, [128, 512], BF16) for i in range(2)]
    c.ob = [T(k, "ob%d" % i, [128, 1024], BF16) for i in range(2)]
    c.oT = T(k, "oT", [128, 1024], BF16)
    c.B = [T(k, "B%d" % i, [128, 512], F32, "ps") for i in range(8)]
    c.pT1 = View(c.B[6][:, :].bitcast(BF16).rearrange("p (c t) -> p c t", c=8), c.B[6].b)
    c.pTo = View(c.B[0][:, :].bitcast(BF16).rearrange("p (c t) -> p c t", c=8), c.B[0].b)


def l1_proj_gen(c, src_ap, nt, j, par, src_buf):
    k = c.k
    ht = c.ht[par]
    st2 = c.st2[par]
    P = slice(0, nt)
    identb = c.identb
    pos0 = 0 if j == 0 else 16 + (j - 1) * 128
    B = c.B
    k.dma("sp", ht[P, :], src_ap, r=[src_buf], w=[ht.b], sbuf=ht.b)
    ub = c.ub1
    ACT(c, ub[P, :], ht[P, :], AF.Square, [ht], [ub, st2], accum_out=st2[P, 0:1])
    ACT(c, st2[P, 1:2], st2[P, 0:1], AF.Ln, [st2], [st2], scale=1.0 / 1024, bias=1e-6)
    ACT(c, st2[P, 1:2], st2[P, 1:2], AF.Exp, [st2], [st2], scale=-0.5)
    TS(c, "dve", ub[P, :], ht[P, :], st2[P, 1:2], None, ALU.mult, None, [ht, st2], [ub])
    yield
    for kc in range(8):
        TR(c, c.pT1[:, kc, P], ub[P, kc * 128:(kc + 1) * 128], identb[P, P], [ub, identb], [c.pT1], sig=(kc == 7))
    uTv = c.uT1[:].rearrange("p (c t) -> p c t", c=8)
    CP(c, "dve", uTv[:, :, P], c.pT1[:, :, P], [c.pT1], [c.uT1])
    yield
    w = c.w_in1
    for g4 in range(2):
        pb = B[7 - g4]
        pbv = pb[:].rearrange("p (c t) -> p c t", c=4)
        for i in range(4):
            oc = g4 * 4 + i
            for kc in range(8):
                MM(c, pbv[:, i, P], w[:, kc, 1024 + oc * 128:1024 + (oc + 1) * 128], uTv[:, kc, P], kc == 0, kc == 7, [c.uT1, w], [pb], sig=(kc == 7))
            yield
        CP(c, "dve", c.KT[:, g4 * 4:(g4 + 1) * 4, pos0:pos0 + nt], pbv[:, :, P], [pb], [c.KTb[j]])
        yield
    for hf in range(2):
        pb = B[7 - hf]
        for kc in range(8):
            MM(c, pb[P, :], uTv[:, kc, P], w[:, kc, 2048 + hf * 512:2048 + (hf + 1) * 512], kc == 0, kc == 7, [c.uT1, w], [pb], sig=(kc == 7))
        yield
        CP(c, "dve", c.V[P, j, hf * 512:(hf + 1) * 512], pb[P, :], [pb], [c.Vb[j]])
        yield
    if j == 0:
        return
    QTs = c.QTs[par]
    for g4 in range(2):
        pb = B[7 - g4]
        pbv = pb[:].rearrange("p (c t) -> p c t", c=4)
        for i in range(4):
            oc = g4 * 4 + i
            for kc in range(8):
                MM(c, pbv[:, i, P], w[:, kc, oc * 128:(oc + 1) * 128], uTv[:, kc, P], kc == 0, kc == 7, [c.uT1, w], [pb], sig=(kc == 7))
            yield
        for hf_ in range(2):
            hp = slice(hf_ * 64, (hf_ + 1) * 64)
            TS(c, "dve", QTs[hp, g4 * 4:(g4 + 1) * 4, hf_, P], pbv[hp, :, P], 0.125, None, ALU.mult, None, [pb], [QTs])
        yield
    sgz = c.sgz[par]
    for hf in range(2):
        pb = B[7 - hf]
        hs = slice(hf * 512, (hf + 1) * 512)
        for kc in range(8):
            MM(c, pb[P, :], uTv[:, kc, P], w[:, kc, 3072 + hf * 512:3072 + (hf + 1) * 512], kc == 0, kc == 7, [c.uT1, w], [pb], sig=(kc == 7))
        yield
        ACT(c, c.sgt1[P, :], pb[P, :], AF.Exp, [pb], [c.sgt1], scale=-1.0)
        TS(c, "dve", c.sgt1[P, :], c.sgt1[P, :], 1.0, None, ALU.add, None, [c.sgt1], [c.sgt1])
        OP(c, "dve", lambda e: e.reciprocal(out=c.sgt1[P, :], in_=c.sgt1[P, :]), [c.sgt1], [c.sgt1])
        TT(c, "dve", sgz[P, hs], c.sgt1[P, :], pb[P, :], ALU.mult, [c.sgt1, pb], [sgz])
        yield


def run_gen(g, n=None):
    if g is None:
        return False
    try:
        if n is None:
            while True:
                next(g)
        for _ in range(n):
            next(g)
    except StopIteration:
        return False
    return True


def layer1_attn(c, nt, j, par, gen_next):
    k = c.k
    ht = c.ht[par]
    st2 = c.st2[par]
    P = slice(0, nt)
    identb = c.identb
    B = c.B
    QTs_ = c.QTs[par]
    sgz_ = c.sgz[par]

    units = []
    for pr in range(2):
        for kb in range(j, -1, -1):
            for q in range(2):
                units.append((kb, 2 * pr + q, q))
    nu = len(units)

    def kinfo(kb):
        if kb == 0:
            return 16, 0
        return 128, 16 + (kb - 1) * 128

    def views(u):
        kb, hg, q = units[u]
        ks, kp = kinfo(kb)
        return kb, hg, q, ks, kp

    def stageA(u):
        kb, hg, q, ks, kp = views(u)
        diag = (kb == j)
        z = B[u % 2]
        zv = z[:].rearrange("p (i t) -> p i t", i=4)
        first = True
        if diag:
            MM(c, z[0:ks, :], identb[0:ks, 0:ks], c.negm[0:ks, :, :].rearrange("p i t -> p (i t)"), True, False, [identb, c.negm], [z], sig=False)
            first = False
        for i2 in range(2):
            ch = 2 * hg + i2
            MM(c, z[0:ks, 2 * i2 * 128:(2 * i2 + 2) * 128], c.KT[:, ch, kp:kp + ks], QTs_[:, ch, :, :].rearrange("p a t -> p (a t)"), first, (i2 == 1) or first, [c.KTb[kb], QTs_], [z], sig=(i2 == 1))
        E = c.E[u % 4]
        SP = c.SP[u % 4]
        Ev = E[:].rearrange("p (i t) -> p i t", i=4)
        SPv = SP[:].rearrange("p (i t) -> p i t", i=4)
        ACT(c, Ev[0:ks, :, P], zv[0:ks, :, P], AF.Exp, [z], [E])

    def stageA2(u):
        kb, hg, q, ks, kp = views(u)
        E = c.E[u % 4]
        SP = c.SP[u % 4]
        Ev = E[:].rearrange("p (i t) -> p i t", i=4)
        SPv = SP[:].rearrange("p (i t) -> p i t", i=4)
        ACT(c, SPv[0:ks, :, P], Ev[0:ks, :, P], AF.Ln, [E], [SP], bias=1.0)

    def stageB(u):
        kb, hg, q, ks, kp = views(u)
        tb = B[2 + q]
        tv = tb[:].rearrange("p (i t) -> p i t", i=4)
        SP = c.SP[u % 4]
        SPv = SP[:].rearrange("p (i t) -> p i t", i=4)
        MM(c, tb[0:ks, :], c.tri[0:ks, 0:ks], SP[0:ks, :], kb == j, False, [c.tri, SP], [tb], sig=True, skip=True)
        X = c.X[u % 2]
        Xv = X[:].rearrange("p (i t) -> p i t", i=4)
        ACT(c, Xv[0:ks, :, P], tv[0:ks, :, P], AF.Exp, [tb], [X], scale=-1.0)

    def stageC(u):
        kb, hg, q, ks, kp = views(u)
        tb = B[2 + q]
        tv = tb[:].rearrange("p (i t) -> p i t", i=4)
        SP = c.SP[u % 4]
        SPv = SP[:].rearrange("p (i t) -> p i t", i=4)
        if kb > 0:
            MM(c, tb[0:ks, :], c.tri2[0:ks, 0:ks], SP[0:ks, :], False, False, [c.tri2, SP], [tb], sig=True, skip=True)
        E = c.E[u % 4]
        X = c.X[u % 2]
        Wt = c.Wt[u % 2]
        Ev = E[:].rearrange("p (i t) -> p i t", i=4)
        Xv = X[:].rearrange("p (i t) -> p i t", i=4)
        Wv = Wt[:].rearrange("p (i t) -> p i t", i=4)
        TT(c, "dve", Wv[0:ks, :, P], Ev[0:ks, :, P], Xv[0:ks, :, P], ALU.mult, [E, X], [Wt])

    def stageD(u):
        kb, hg, q, ks, kp = views(u)
        ob_ = B[4 + q]
        Wt = c.Wt[u % 2]
        Wv = Wt[:].rearrange("p (i t) -> p i t", i=4)
        if kb == j:
            MM(c, ob_[P, 0:256], c.zrow[0:1, P], c.zrow[0:1, 0:256], True, False, [c.zrow], [ob_], sig=False)
        for i in range(4):
            hd = 4 * hg + i
            MM(c, ob_[P, i * 64:(i + 1) * 64], Wv[0:ks, i, P], c.V[0:ks, kb, hd * 64:(hd + 1) * 64], False, (kb == 0 and i == 3), [Wt, c.Vb[kb]], [ob_], sig=(i == 3))
        if kb == 0:
            hsl = slice(hg * 256, (hg + 1) * 256)
            TT(c, "dve", c.ob[par][P, hsl], ob_[P, 0:256], sgz_[P, hsl], ALU.mult, [ob_, sgz_], [c.ob[par]])

    per = max(1, -(-52 // max(1, nu - 2)))
    for step in range(nu + 4):
        if step < nu:
            stageA(step)
        if 0 <= step - 2 < nu:
            stageB(step - 2)
        if step < nu:
            stageA2(step)
        if 0 <= step - 3 < nu:
            stageC(step - 3)
        if 0 <= step - 4 < nu:
            stageD(step - 4)
        run_gen(gen_next, per)
    run_gen(gen_next, None)


def l1_tail_gen(c, dst_ap, nt, par):
    k = c.k
    ht = c.ht[par]
    st2 = c.st2[par]
    ob = c.ob[par]
    P = slice(0, nt)
    identb = c.identb
    B = c.B
    for kc in range(8):
        TR(c, c.pT1[:, kc, P], ob[P, kc * 128:(kc + 1) * 128], identb[P, P], [ob, identb], [c.pT1], sig=(kc == 7))
    oTv = c.oT[:].rearrange("p (c t) -> p c t", c=8)
    CP(c, "dve", oTv[:, :, P], c.pT1[:, :, P], [c.pT1], [c.oT])
    yield
    for hf in range(2):
        pb = B[6 + hf]
        for kc in range(8):
            MM(c, pb[P, :], oTv[:, kc, P], c.w_out1[:, kc, hf * 512:(hf + 1) * 512], kc == 0, kc == 7, [c.oT, c.w_out1], [pb], sig=(kc == 7))
        yield
    for hf in range(2):
        hs = slice(hf * 512, (hf + 1) * 512)
        TT(c, "dve", ht[P, hs], ht[P, hs], B[6 + hf][P, :], ALU.add, [ht, B[6 + hf]], [ht])
    yield
    ACT(c, c.oT[P, :], ht[P, :], AF.Square, [ht], [c.oT, st2], accum_out=st2[P, 2:3])
    ACT(c, st2[P, 3:4], st2[P, 2:3], AF.Ln, [st2], [st2], scale=1.0 / 1024, bias=1e-6)
    ACT(c, st2[P, 3:4], st2[P, 3:4], AF.Exp, [st2], [st2], scale=-0.5)
    yield
    STT(c, "dve", ht[P, :], ht[P, :], st2[P, 3:4], c.fn[P, :], ALU.mult, ALU.mult, [ht, st2, c.fn], [ht])
    k.dma("sp", dst_ap, ht[P, :], r=[ht.b], sbuf=ht.b)
    yield


def chain_gens(*gens):
    for g in gens:
        if g is not None:
            yield from g


def layer1_seq(c, h1s, outs, nfull, par0, src_buf, first_seq=True):
    par = par0
    if first_seq:
        run_gen(l1_proj_gen(c, h1s[0], 16, 0, par, src_buf), None)
        par ^= 1
    run_gen(l1_proj_gen(c, h1s[1], 128, 1, par, src_buf), None)
    for j in range(1, nfull + 1):
        gt = l1_tail_gen(c, outs[j - 1], 128, par ^ 1) if j >= 2 else None
        gp = l1_proj_gen(c, h1s[j + 1], 128, j + 1, par ^ 1, src_buf) if j + 1 <= nfull else None
        layer1_attn(c, 128, j, par, chain_gens(gt, gp))
        par ^= 1
    run_gen(l1_tail_gen(c, outs[nfull], 128, par ^ 1), None)
    return par

NSEQ = 4
NFULL = 16
LTOT_ = 16 + NFULL * 128


def common_setup(c, sw=2312):
    k = c.k
    if sw:
        c.stage = [T(k, "stage%d" % i, [128, sw], F32) for i in range(2)]
    ident = T(k, "ident", [128, 128], F32)
    c.identb = T(k, "identb", [128, 128], BF16)
    OP(c, "pool", lambda e: e.memset(ident[:], 1.0), [], [ident])
    OP(c, "pool", lambda e: e.affine_select(out=ident[:], in_=ident[:], pattern=[[-1, 128]], compare_op=ALU.is_equal, fill=0.0, base=0, channel_multiplier=1), [ident], [ident])
    CP(c, "dve", c.identb[:], ident[:], [ident], [c.identb])


def build_l0(nseq, nfull):
    nc = bass.Bass("TRN2", target_bir_lowering=False)
    L = 16 + nfull * 128
    D = {}
    x = nc.dram_tensor("x", [nseq, nfull * 128, 1024], F32, kind="ExternalInput").ap()
    meta = nc.dram_tensor("meta", [16, 1024], F32, kind="ExternalInput").ap()
    D["w_in"] = nc.dram_tensor("w_in", [1024, 4624], F32, kind="ExternalInput").ap()
    D["w_out"] = nc.dram_tensor("w_out", [2048, 1024], F32, kind="ExternalInput").ap()
    D["wa"] = nc.dram_tensor("wa", [4, 256, 256], F32, kind="ExternalInput").ap()
    D["wx"] = nc.dram_tensor("wx", [4, 256, 256], F32, kind="ExternalInput").ap()
    D["vecF"] = nc.dram_tensor("vecF", [128, 140], F32, kind="ExternalInput").ap()
    D["vecT"] = nc.dram_tensor("vecT", [1, 48], F32, kind="ExternalInput").ap()
    h1 = nc.dram_tensor("h1", [nseq, L, 1024], F32, kind="ExternalOutput").ap()
    with ExitStack() as es:
        k = K(nc, es)
        c = setup_common(k, nc)
        common_setup(c, 0)
        alloc_layer0_work(c)
        setup_layer0(c, D)
        hb = Buf("h1dram")
        par = 0
        for s in range(nseq):
            tiles = [(meta[:, :], h1[s, 0:16, :], 16)]
            for j in range(nfull):
                tiles.append((x[s, j * 128:(j + 1) * 128, :], h1[s, 16 + j * 128:16 + (j + 1) * 128, :], 128))
            par = layer0_seq(c, tiles, par, hb, first_seq=(s == 0))
        k.final_wait("sp", [c.xt[0].b, c.xt[1].b])
        k.emit()
    return nc


def build_l1(nseq, nfull):
    nc = bass.Bass("TRN2", target_bir_lowering=False)
    L = 16 + nfull * 128
    D = {}
    h1 = nc.dram_tensor("h1", [nseq, L, 1024], F32, kind="ExternalInput").ap()
    D["w_in1"] = nc.dram_tensor("w_in1", [1024, 4096], F32, kind="ExternalInput").ap()
    D["w_out1"] = nc.dram_tensor("w_out1", [1024, 1024], F32, kind="ExternalInput").ap()
    D["vec1F"] = nc.dram_tensor("vec1F", [128, 8], F32, kind="ExternalInput").ap()
    D["fnorm"] = nc.dram_tensor("fnorm", [1, 1024], F32, kind="ExternalInput").ap()
    out = nc.dram_tensor("out", [nseq, nfull * 128, 1024], F32, kind="ExternalOutput").ap()
    with ExitStack() as es:
        k = K(nc, es)
        c = setup_common(k, nc)
        common_setup(c, 0)
        alloc_layer1_work(c)
        c.stage = list(c.E)
        setup_layer1(c, D, L)
        hb = Buf("h1dram")
        par = 0
        for s in range(nseq):
            h1s = [h1[s, 0:16, :]] + [h1[s, 16 + (j - 1) * 128:16 + j * 128, :] for j in range(1, nfull + 1)]
            outs = [None] + [out[s, (j - 1) * 128:j * 128, :] for j in range(1, nfull + 1)]
            par = layer1_seq(c, h1s, outs, nfull, par, hb, first_seq=(s == 0))
        k.final_wait("sp", [c.ht[0].b, c.ht[1].b])
        k.emit()
    return nc


def build_fused(nseq, nfull):
    nc = bass.Bass("TRN2", target_bir_lowering=False)
    L = 16 + nfull * 128
    D = {}
    x = nc.dram_tensor("x", [nseq, nfull * 128, 1024], F32, kind="ExternalInput").ap()
    meta = nc.dram_tensor("meta", [16, 1024], F32, kind="ExternalInput").ap()
    D["w_in"] = nc.dram_tensor("w_in", [1024, 4624], F32, kind="ExternalInput").ap()
    D["w_out"] = nc.dram_tensor("w_out", [2048, 1024], F32, kind="ExternalInput").ap()
    D["wa"] = nc.dram_tensor("wa", [4, 256, 256], F32, kind="ExternalInput").ap()
    D["wx"] = nc.dram_tensor("wx", [4, 256, 256], F32, kind="ExternalInput").ap()
    D["vecF"] = nc.dram_tensor("vecF", [128, 140], F32, kind="ExternalInput").ap()
    D["vecT"] = nc.dram_tensor("vecT", [1, 48], F32, kind="ExternalInput").ap()
    D["w_in1"] = nc.dram_tensor("w_in1", [1024, 4096], F32, kind="ExternalInput").ap()
    D["w_out1"] = nc.dram_tensor("w_out1", [1024, 1024], F32, kind="ExternalInput").ap()
    D["vec1F"] = nc.dram_tensor("vec1F", [128, 8], F32, kind="ExternalInput").ap()
    D["fnorm"] = nc.dram_tensor("fnorm", [1, 1024], F32, kind="ExternalInput").ap()
    out = nc.dram_tensor("out", [nseq, nfull * 128, 1024], F32, kind="ExternalOutput").ap()
    h1 = nc.dram_tensor("h1s", [nseq, L, 1024], F32, kind="Internal").ap()
    with ExitStack() as es:
        k = K(nc, es)
        c = setup_common(k, nc)
        common_setup(c, 0)
        hb = Buf("h1dram")
        k.push()
        alloc_layer0_work(c)
        setup_layer0(c, D)
        par = 0
        for s in range(nseq):
            tiles = [(meta[:, :], h1[s, 0:16, :], 16)]
            for j in range(nfull):
                tiles.append((x[s, j * 128:(j + 1) * 128, :], h1[s, 16 + j * 128:16 + (j + 1) * 128, :], 128))
            par = layer0_seq(c, tiles, par, hb, first_seq=(s == 0))
        k.barrier([c.xt[0].b, c.xt[1].b, c.vecF.b, c.vecT.b] + [t.b for t in c.stage])
        k.emit()
        k.pop()
        hb = Buf("h1dram2")
        k.push()
        alloc_layer1_work(c)
        c.stage = list(c.E)
        setup_layer1(c, D, L)
        par = 0
        for s in range(nseq):
            h1s = [h1[s, 0:16, :]] + [h1[s, 16 + (j - 1) * 128:16 + j * 128, :] for j in range(1, nfull + 1)]
            outs = [None] + [out[s, (j - 1) * 128:j * 128, :] for j in range(1, nfull + 1)]
            par = layer1_seq(c, h1s, outs, nfull, par, hb, first_seq=(s == 0))
        k.final_wait("sp", [c.ht[0].b, c.ht[1].b])
        k.emit()
        k.pop()
    return nc


def pack_vecs(inp):
    f = lambda v: np.ascontiguousarray(np.asarray(v).reshape(-1, 128).T)
    vecF = np.zeros((128, 140), np.float32)
    vecF[:, 0:8] = f(inp["even_norm"][0])
    vecF[:, 8:16] = f(inp["ssd_norm"][0])
    vecF[:, 16:48] = np.asarray(inp["lru_conv_w"][0]).reshape(4, 8, 128).transpose(2, 1, 0).reshape(128, 32)
    vecF[:, 48:56] = f(inp["lru_conv_b"][0])
    vecF[:, 56:64] = f(inp["lru_b_a"][0])
    vecF[:, 64:72] = f(inp["lru_b_x"][0])
    vecF[:, 72:80] = f(inp["lru_lambda"][0])
    vecF[:, 80:128] = np.asarray(inp["ssd_conv_w"][0]).reshape(4, 12, 128).transpose(2, 1, 0).reshape(128, 48)
    vecF[:, 128:140] = f(inp["ssd_conv_b"][0])
    vecT = np.concatenate([np.asarray(inp["ssd_dt_bias"][0]), np.asarray(inp["ssd_a_log"][0]), np.asarray(inp["ssd_d"][0])])[None, :].astype(np.float32)
    return vecF, vecT


_NC_CACHE = {}


def kernel(**inp):
    inp = {k_: np.asarray(v, dtype=np.float32) for k_, v in inp.items()}
    x = inp["x"]
    ncores = 8
    vecF, vecT = pack_vecs(inp)
    vec1F = np.ascontiguousarray(inp["odd_norm"][0].reshape(8, 128).T)
    if "f" not in _NC_CACHE:
        _NC_CACHE["f"] = build_fused(NSEQ, NFULL)
    xs = np.split(np.ascontiguousarray(x), ncores, axis=0)
    maps = [{"x": xs[i], "meta": inp["meta"], "w_in": inp["even_w_in"][0], "w_out": inp["even_w_out"][0],
             "wa": inp["lru_w_a"][0], "wx": inp["lru_w_x"][0], "vecF": vecF, "vecT": vecT,
             "w_in1": inp["odd_w_in"][0], "w_out1": inp["odd_w_out"][0], "vec1F": vec1F,
             "fnorm": inp["final_norm"][None, :]} for i in range(ncores)]
    r = run_bass_kernel_spmd(_NC_CACHE["f"], maps, core_ids=list(range(ncores)))
    return np.concatenate([r.results[i]["out"] for i in range(ncores)], axis=0).astype(np.float32)
```
